# Optimizing a Trainium2 kernel written in Bass

```python
import jax, jax.numpy as jnp
from jax import lax
import numpy as np

D_MODEL = 1024
BATCH = 4
SEQ = 8192
DEPTH = 4

CHUNK = 64

TM_HEAD = 64
TM_WIDTH = D_MODEL // 2
TM_HEADS = TM_WIDTH // TM_HEAD
TM_DECAY_RANK = 64
TM_AICL_RANK = 64
TM_LN_EPS = 64e-5
SC_WIDTH = D_MODEL // 2
SC_KERNEL = 3
CF_WIDTH = D_MODEL // 2
CF_KERNEL = 31
SSD_HEAD = 64
SSD_WIDTH = D_MODEL // 2
SSD_HEADS = SSD_WIDTH // SSD_HEAD
SSD_STATE = 128
SSD_GROUPS = 2
SSD_HPG = SSD_HEADS // SSD_GROUPS
SSD_CONV = 4
SSD_XBC = SSD_WIDTH + 2 * SSD_GROUPS * SSD_STATE

TM_COLS = 4 * TM_WIDTH + TM_DECAY_RANK + TM_AICL_RANK
SC_COLS = 4 * SC_WIDTH
EVEN_COLS = TM_COLS + SC_COLS
CF_COLS = 3 * CF_WIDTH
SSD_COLS = SSD_WIDTH + SSD_XBC + SSD_HEADS
ODD_COLS = CF_COLS + SSD_COLS

NORM_EPS = 1e-6

kernel_name = "hybrid_rwkv7_shortconv_conformer_ssd_trunk"


def rms_norm(x, g):
    xf = x.astype(jnp.float32)
    y = xf * lax.rsqrt(jnp.mean(xf * xf, axis=-1, keepdims=True) + NORM_EPS)
    return (y * g.astype(jnp.float32)).astype(x.dtype)


def layer_norm(x, g, b, eps=1e-5):
    xf = x.astype(jnp.float32)
    mu = jnp.mean(xf, axis=-1, keepdims=True)
    var = jnp.mean(jnp.square(xf - mu), axis=-1, keepdims=True)
    y = (xf - mu) * lax.rsqrt(var + eps)
    return (y * g.astype(jnp.float32) + b.astype(jnp.float32)).astype(x.dtype)


def causal_dwconv(u, w):
    k, ch = w.shape
    return lax.conv_general_dilated(
        u, w[:, None, :].astype(u.dtype), window_strides=(1,), padding=[(k - 1, 0)],
        dimension_numbers=("NWC", "WIO", "NWC"), feature_group_count=ch)


def wkv7_scan(r, w, k, v, a, b):
    bsz, _, h, n = r.shape

    def step(s, inp):
        rt, wt, kt, vt, at, bt = inp
        sa = jnp.einsum("bhij,bhj->bhi", s, at)
        s = s * wt[:, :, None, :] + sa[..., None] * bt[:, :, None, :] + vt[..., None] * kt[:, :, None, :]
        return s, jnp.einsum("bhij,bhj->bhi", s, rt)

    xs = tuple(jnp.moveaxis(t, 1, 0) for t in (r, w, k, v, a, b))
    s0 = jnp.zeros((bsz, h, n, n), jnp.float32)
    _, ys = lax.scan(step, s0, xs)
    return jnp.moveaxis(ys, 0, 1)


def rwkv7_mix(p, mu, w0, w2, a0, a2, k_k, k_a, r_k, lnx_g, lnx_b):
    bsz, t, _ = p.shape
    dt_out = p.dtype
    p_prev = jnp.pad(p, ((0, 0), (1, 0), (0, 0)))[:, :-1]
    p = p + (p_prev - p) * mu
    W = TM_WIDTH
    r, k, v, g, wd, ad = jnp.split(p, [W, 2 * W, 3 * W, 4 * W, 4 * W + TM_DECAY_RANK], axis=-1)
    w = -jax.nn.softplus(-(w0 + jnp.tanh(wd) @ w2)) - 0.5
    a = jax.nn.sigmoid(a0 + ad @ a2)
    heads = lambda z: z.astype(jnp.float32).reshape(bsz, t, TM_HEADS, TM_HEAD)
    r, w, k, v, a = heads(r), heads(w), heads(k), heads(v), heads(a)
    kk = k * k_k.astype(jnp.float32)
    kk = kk * lax.rsqrt(jnp.maximum(jnp.sum(kk * kk, axis=-1, keepdims=True), 1e-24))
    k = k * (1.0 + (a - 1.0) * k_a.astype(jnp.float32))
    decay = jnp.exp(-jnp.exp(w))
    y = wkv7_scan(r, decay, k, v, -kk, kk * a)
    mean = jnp.mean(y, axis=-1, keepdims=True)
    var = jnp.mean(jnp.square(y - mean), axis=-1, keepdims=True)
    y = ((y - mean) * lax.rsqrt(var + TM_LN_EPS)).reshape(bsz, t, W)
    y = y * lnx_g.astype(jnp.float32) + lnx_b.astype(jnp.float32)
    bonus = jnp.sum(r * k * r_k.astype(jnp.float32), axis=-1, keepdims=True) * v
    y = y + bonus.reshape(bsz, t, W)
    return (y * jax.nn.silu(g.astype(jnp.float32))).astype(dt_out)


def short_conv_mix(p, conv_w):
    b_gate, c_gate, h, g = jnp.split(p, 4, axis=-1)
    y = b_gate * causal_dwconv(c_gate * h, conv_w)
    return y * jax.nn.silu(g)


def conformer_conv_mix(p, conv_w, conv_b, ln_g, ln_b):
    val, glu_gate, g = jnp.split(p, 3, axis=-1)
    u = val * jax.nn.sigmoid(glu_gate)
    u = causal_dwconv(u, conv_w) + conv_b
    u = jax.nn.silu(layer_norm(u, ln_g, ln_b))
    return u * jax.nn.silu(g)


def mamba2_ssd_mix(p, conv_w, conv_b, dt_bias, a_log, d_skip, norm_g):
    bsz, t, _ = p.shape
    dt_out = p.dtype
    nc = t // CHUNK
    G, E, P, N, L = SSD_GROUPS, SSD_HPG, SSD_HEAD, SSD_STATE, CHUNK
    z, xbc, dt = jnp.split(p, [SSD_WIDTH, SSD_WIDTH + SSD_XBC], axis=-1)
    xbc = jax.nn.silu(causal_dwconv(xbc, conv_w) + conv_b).astype(jnp.float32)
    xs, bm, cm = jnp.split(xbc, [SSD_WIDTH, SSD_WIDTH + G * N], axis=-1)
    dt = jax.nn.softplus(dt.astype(jnp.float32) + dt_bias.astype(jnp.float32))
    a = -jnp.exp(a_log.astype(jnp.float32))
    xh = xs.reshape(bsz, nc, L, G, E, P)
    dtc = dt.reshape(bsz, nc, L, G, E)
    x_dt = xh * dtc[..., None]
    da = jnp.transpose(dtc * a.reshape(G, E), (0, 3, 4, 1, 2))
    bm = bm.reshape(bsz, nc, L, G, N)
    cm = cm.reshape(bsz, nc, L, G, N)
    a_cs = jnp.cumsum(da, axis=-1)
    seg = a_cs[..., :, None] - a_cs[..., None, :]
    causal = jnp.tril(jnp.ones((L, L), dtype=bool))
    lmat = jnp.exp(jnp.where(causal, seg, -jnp.inf))
    cb = jnp.einsum("bclgn,bcsgn->bcgls", cm, bm)
    y_diag = jnp.einsum("bcgls,bgecls,bcsgep->bclgep", cb, lmat, x_dt)
    decay_states = jnp.exp(a_cs[..., -1:] - a_cs)
    states = jnp.einsum("bclgn,bgecl,bclgep->bcgepn", bm, decay_states, x_dt)
    chunk_decay = jnp.exp(a_cs[..., -1])

    def step(s, inp):
        st, dec = inp
        return s * dec[..., None, None] + st, s

    s0 = jnp.zeros((bsz, G, E, P, N), jnp.float32)
    _, prev = lax.scan(step, s0, (jnp.moveaxis(states, 1, 0), jnp.moveaxis(chunk_decay, -1, 0)))
    prev = jnp.moveaxis(prev, 0, 1)
    y_off = jnp.einsum("bclgn,bcgepn,bgecl->bclgep", cm, prev, jnp.exp(a_cs))
    y = y_diag + y_off + xh * d_skip.astype(jnp.float32).reshape(G, E, 1)
    y = y.reshape(bsz, t, SSD_WIDTH)
    y = rms_norm(y * jax.nn.silu(z.astype(jnp.float32)), norm_g)
    return y.astype(dt_out)


def setup_inputs(seed: int = 0) -> dict:
    key = jax.random.key(seed)
    ks = iter(jax.random.split(key, 40))
    nrm = lambda shape, s: jax.random.normal(next(ks), shape, jnp.float32) * s
    uni = lambda shape, lo, hi: jax.random.uniform(next(ks), shape, jnp.float32, lo, hi)
    D = D_MODEL
    NE, NO = (DEPTH + 1) // 2, DEPTH // 2
    dt_init = jnp.exp(uni((NO, SSD_HEADS), float(np.log(1e-3)), float(np.log(1e-1))))
    return {
        "x": nrm((BATCH, SEQ, D), 1.0),
        "c": nrm((BATCH, D), 1.0),
        "ada_w": nrm((DEPTH, D, 3 * D), 0.5 * D ** -0.5),
        "ada_b": nrm((DEPTH, 3 * D), 0.02),
        "norm_pre": 1.0 + nrm((DEPTH, D), 0.02),
        "norm_post": 1.0 + nrm((DEPTH, D), 0.02),
        "ev_w_in": nrm((NE, D, EVEN_COLS), D ** -0.5),
        "ev_w_out": nrm((NE, TM_WIDTH + SC_WIDTH, D), (TM_WIDTH + SC_WIDTH) ** -0.5),
        "tm_mu": uni((NE, TM_COLS), 0.0, 1.0),
        "tm_w0": uni((NE, TM_WIDTH), -6.0, -1.0),
        "tm_w2": nrm((NE, TM_DECAY_RANK, TM_WIDTH), TM_DECAY_RANK ** -0.5),
        "tm_a0": nrm((NE, TM_WIDTH), 0.1),
        "tm_a2": nrm((NE, TM_AICL_RANK, TM_WIDTH), TM_AICL_RANK ** -0.5),
        "tm_k_k": 0.85 + nrm((NE, TM_HEADS, TM_HEAD), 0.02),
        "tm_k_a": 1.0 + nrm((NE, TM_HEADS, TM_HEAD), 0.02),
        "tm_r_k": nrm((NE, TM_HEADS, TM_HEAD), 0.1),
        "tm_lnx_g": 1.0 + nrm((NE, TM_WIDTH), 0.02),
        "tm_lnx_b": nrm((NE, TM_WIDTH), 0.02),
        "sc_conv_w": nrm((NE, SC_KERNEL, SC_WIDTH), SC_KERNEL ** -0.5),
        "od_w_in": nrm((NO, D, ODD_COLS), D ** -0.5),
        "od_w_out": nrm((NO, CF_WIDTH + SSD_WIDTH, D), (CF_WIDTH + SSD_WIDTH) ** -0.5),
        "cf_conv_w": nrm((NO, CF_KERNEL, CF_WIDTH), CF_KERNEL ** -0.5),
        "cf_conv_b": nrm((NO, CF_WIDTH), 0.02),
        "cf_ln_g": 1.0 + nrm((NO, CF_WIDTH), 0.02),
        "cf_ln_b": nrm((NO, CF_WIDTH), 0.02),
        "ssd_conv_w": nrm((NO, SSD_CONV, SSD_XBC), SSD_CONV ** -0.5),
        "ssd_conv_b": nrm((NO, SSD_XBC), 0.02),
        "ssd_dt_bias": dt_init + jnp.log(-jnp.expm1(-dt_init)),
        "ssd_a_log": jnp.log(uni((NO, SSD_HEADS), 1.0, 16.0)),
        "ssd_d": 1.0 + nrm((NO, SSD_HEADS), 0.1),
        "ssd_norm_g": 1.0 + nrm((NO, SSD_WIDTH), 0.02),
    }


def reference(x, c, ada_w, ada_b, norm_pre, norm_post,
              ev_w_in, ev_w_out, tm_mu, tm_w0, tm_w2, tm_a0, tm_a2, tm_k_k, tm_k_a, tm_r_k,
              tm_lnx_g, tm_lnx_b, sc_conv_w,
              od_w_in, od_w_out, cf_conv_w, cf_conv_b, cf_ln_g, cf_ln_b,
              ssd_conv_w, ssd_conv_b, ssd_dt_bias, ssd_a_log, ssd_d, ssd_norm_g):
    c_act = jax.nn.silu(c)
    for i in range(DEPTH):
        shift, scale, gate = jnp.split(c_act @ ada_w[i] + ada_b[i], 3, axis=-1)
        h = rms_norm(x, norm_pre[i]) * (1.0 + scale[:, None, :]) + shift[:, None, :]
        j = i // 2
        if i % 2 == 0:
            p = h @ ev_w_in[j]
            y_a = rwkv7_mix(p[..., :TM_COLS], tm_mu[j], tm_w0[j], tm_w2[j], tm_a0[j], tm_a2[j],
                            tm_k_k[j], tm_k_a[j], tm_r_k[j], tm_lnx_g[j], tm_lnx_b[j])
            y_b = short_conv_mix(p[..., TM_COLS:], sc_conv_w[j])
            y = jnp.concatenate([y_a, y_b], axis=-1) @ ev_w_out[j]
        else:
            p = h @ od_w_in[j]
            y_c = conformer_conv_mix(p[..., :CF_COLS], cf_conv_w[j], cf_conv_b[j], cf_ln_g[j], cf_ln_b[j])
            y_d = mamba2_ssd_mix(p[..., CF_COLS:], ssd_conv_w[j], ssd_conv_b[j], ssd_dt_bias[j],
                                 ssd_a_log[j], ssd_d[j], ssd_norm_g[j])
            y = jnp.concatenate([y_c, y_d], axis=-1) @ od_w_out[j]
        x = x + gate[:, None, :] * rms_norm(y, norm_post[i])
    return x
```

```python
import numpy as np
import concourse.bass as bass
import concourse.mybir as mybir
from concourse.bass_utils import run_bass_kernel_spmd

F32 = mybir.dt.float32
BF16 = mybir.dt.bfloat16
AF = mybir.ActivationFunctionType
ALU = mybir.AluOpType

D = 1024
SEQ = 8192
TT = 512
NDS = 24
EVEN_COLS = 4224
ODD_COLS = 3080


class Buf:
    __slots__ = ("w", "r")

    def __init__(self):
        self.w = {}
        self.r = {}


class Tile:
    def __init__(self, h, nb=1):
        self.h = h
        self.bufs = [Buf() for _ in range(nb)]

    def __getitem__(self, k):
        return self.h[k]

    @property
    def b(self):
        return self.bufs[0]


class Prog:
    def __init__(self, nc):
        self.nc = nc
        self.eng = {"pe": nc.tensor, "act": nc.scalar, "dve": nc.vector, "pool": nc.gpsimd, "sp": nc.sync}
        self.sem = {}
        self.cnt = {}
        self.cur = {}
        self.qeng = {}
        self.epochs = {}
        for q in ("pe", "act", "dve", "pool"):
            self._new_epoch(q)
        for i in range(NDS):
            self.sem["d%d" % i] = nc.alloc_semaphore("d%d" % i)
            self.cnt["d%d" % i] = 0
        self.dnext = 0
        self.known = {e: {} for e in self.eng}
        self.snap = {}
        self.nwait = 0
        self.nins = 0

    SEM_LIMIT = 30000

    def _new_epoch(self, e):
        k = self.epochs.get(e, 0)
        self.epochs[e] = k + 1
        key = "%s#%d" % (e, k)
        self.sem[key] = self.nc.alloc_semaphore("s_%s_%d" % (e, k))
        self.cnt[key] = 0
        self.cur[e] = key
        self.qeng[key] = e

    def sb(self, name, shape, dt, nb=1):
        return Tile(self.nc.alloc_sbuf_tensor(name, list(shape), dt), nb)

    def ps(self, name, shape, dt=F32, nb=1):
        return Tile(self.nc.alloc_psum_tensor(name, list(shape), dt), nb)

    def _wait(self, e, evs, same_ok=True):
        kn = self.known[e]
        need = {}
        for (q, v) in evs:
            if same_ok and self.qeng.get(q) == e:
                continue
            if kn.get(q, 0) >= v:
                continue
            if need.get(q, 0) < v:
                need[q] = v
        for q, v in need.items():
            if kn.get(q, 0) >= v:
                continue
            self.eng[e].wait_ge(self.sem[q], v)
            self.nwait += 1
            kn[q] = v
            s = self.snap.get((q, v))
            if s:
                for q2, v2 in s.items():
                    if kn.get(q2, 0) < v2:
                        kn[q2] = v2

    def _deps(self, reads, writes):
        ev = []
        for b in reads:
            ev.extend(b.w.items())
        for b in writes:
            ev.extend(b.w.items())
            ev.extend(b.r.items())
        return ev

    def op(self, e, fn, reads=(), writes=()):
        self._wait(e, self._deps(reads, writes), same_ok=(e == "pe"))
        ins = fn(self.eng[e])
        key = self.cur[e]
        self.cnt[key] += 1
        n = self.cnt[key]
        ins.then_inc(self.sem[key], 1)
        self.nins += 1
        self.snap[(key, n)] = dict(self.known[e])
        for b in reads:
            b.r[key] = n
        for b in writes:
            b.w[key] = n
            b.r = {}
        if n >= self.SEM_LIMIT:
            self._new_epoch(e)

    def dma(self, e, out, in_, reads=(), writes=(), **kw):
        i = self.dnext
        self.dnext = (i + 1) % NDS
        q = "d%d" % i
        ev = self._deps(reads, writes)
        if self.cnt[q] > 0:
            ev.append((q, self.cnt[q]))
        self._wait(e, ev, same_ok=False)
        self.eng[e].dma_start(out=out, in_=in_, **kw).then_inc(self.sem[q], 16)
        self.cnt[q] += 16
        n = self.cnt[q]
        self.nins += 1
        self.snap[(q, n)] = dict(self.known[e])
        for b in reads:
            b.r[q] = n
        for b in writes:
            b.w[q] = n
            b.r = {}

    def finish(self):
        self._wait("sp", [(q, v) for q, v in self.cnt.items() if v > 0], same_ok=False)

    def act(self, out, in_, func, reads, writes, **kw):
        self.op("act", lambda e: e.activation(out=out, in_=in_, func=func, **kw), reads, writes)

    def mm(self, out, lhsT, rhs, start, stop, reads, writes):
        self.op("pe", lambda e: e.matmul(out, lhsT=lhsT, rhs=rhs, start=start, stop=stop), reads, writes)

    def tr(self, out, in_, ident, reads, writes):
        self.op("pe", lambda e: e.transpose(out, in_, ident), reads, writes)

    def ts(self, eng, out, in0, s1, s2, op0, op1, reads, writes):
        if s2 is None:
            self.op(eng, lambda e: e.tensor_scalar(out=out, in0=in0, scalar1=s1, scalar2=None, op0=op0), reads, writes)
        else:
            self.op(eng, lambda e: e.tensor_scalar(out=out, in0=in0, scalar1=s1, scalar2=s2, op0=op0, op1=op1), reads, writes)

    def tt(self, eng, out, in0, in1, op, reads, writes):
        self.op(eng, lambda e: e.tensor_tensor(out=out, in0=in0, in1=in1, op=op), reads, writes)

    def stt(self, eng, out, in0, s, in1, op0, op1, reads, writes):
        self.op(eng, lambda e: e.scalar_tensor_tensor(out=out, in0=in0, scalar=s, in1=in1, op0=op0, op1=op1), reads, writes)

    def cp(self, eng, out, in_, reads, writes):
        if eng == "act":
            self.op(eng, lambda e: e.activation(out=out, in_=in_, func=AF.Copy), reads, writes)
        else:
            self.op(eng, lambda e: e.tensor_copy(out=out, in_=in_), reads, writes)

    def rsqrt(self, ap, tile):
        self.op("act", lambda e: e.activation(out=ap, in_=ap, func=AF.Ln), [tile.b], [tile.b])
        self.op("act", lambda e: e.activation(out=ap, in_=ap, func=AF.Exp, scale=-0.5), [tile.b], [tile.b])

    def ms(self, eng, ap, val, writes):
        self.op(eng, lambda e: e.memset(ap, val), (), writes)


CW = 2688
C_ID = 0
C_SU = 128
C_IU = 192
C_SL = 256
C_BLK = 384
C_ONE = 512
C_TRI2 = 640
C_MB2 = 768
C_SEL0 = 896
C_SEL1 = 1024
C_RST = 1152
CWF = 1664
C_PAR0 = 1664
C_PAR1 = 2176
CB_PAR0 = 640
CB_PAR1 = 1152


def make_consts():
    c = np.zeros((128, CW), np.float32)
    c[:, 0:128] = np.eye(128, dtype=np.float32)
    s = np.arange(64)[:, None]
    t = np.arange(64)[None, :]
    c[0:64, C_SU:C_SU + 64] = (s < t)
    c[0:64, C_IU:C_IU + 64] = (s <= t)
    c[0:64, C_SL:C_SL + 64] = (s > t)
    c[0:64, C_BLK:C_BLK + 64] = 1.0
    c[64:128, C_BLK + 64:C_BLK + 128] = 1.0
    c[:, C_ONE:C_ONE + 128] = 1.0
    r = np.arange(128)[:, None]
    q = np.arange(128)[None, :]
    same = (r // 64) == (q // 64)
    c[:, C_TRI2:C_TRI2 + 128] = (same & (r <= q))
    c[:, C_MB2:C_MB2 + 128] = np.where(same & (r <= q), 0.0, -1.0e5)
    c[0:64, C_SEL0:C_SEL0 + 128] = 1.0
    c[64:128, C_SEL1:C_SEL1 + 128] = 1.0
    tk = np.arange(512)
    c[:, C_PAR0:C_PAR0 + 512] = ((tk // 64) % 2 == 0)[None, :]
    c[:, C_PAR1:C_PAR1 + 512] = ((tk // 64) % 2 == 1)[None, :]
    c[:, C_RST:C_RST + 512] = (tk % 64 != 0)[None, :]
    return c


class Builder:
    def __init__(self, n_layers=4, n_tiles=16, mixers="ABCD", dbg=False):
        self.NL = n_layers
        self.NT = n_tiles
        self.mixers = mixers
        self.dbg = dbg
        self.INV_DT = BF16
        self.wseq = None
        self.wrec = []
        self.wptr = 0
        self.wissued = {}
        self.prefetch = True
        self.rw_stop = 0
        self.nc = bass.Bass("TRN2", target_bir_lowering=False)
        self.p = Prog(self.nc)

    def declare(self):
        nc = self.nc
        di = lambda name, shape: nc.dram_tensor(name, list(shape), F32, kind="ExternalInput").ap()
        self.x = di("x", (SEQ, D))
        self.c = di("c", (8, 128))
        self.consts = di("consts", (128, CW))
        self.ada_w = di("ada_w", (4, D, 3 * D))
        self.ada_b = di("ada_b", (4, 3 * D))
        self.norm_pre = di("norm_pre", (4, D))
        self.norm_post = di("norm_post", (4, D))
        self.ev_w_in = di("ev_w_in", (2, D, EVEN_COLS))
        self.ev_w_out = di("ev_w_out", (2, D, D))
        self.tm_mu = di("tm_mu", (2, 2176))
        self.tm_w0 = di("tm_w0", (2, 512))
        self.tm_w2 = di("tm_w2", (2, 64, 512))
        self.tm_a0 = di("tm_a0", (2, 512))
        self.tm_a2 = di("tm_a2", (2, 64, 512))
        self.tm_k_k = di("tm_k_k", (2, 512))
        self.tm_k_a = di("tm_k_a", (2, 512))
        self.tm_r_k = di("tm_r_k", (2, 512))
        self.tm_lnx_g = di("tm_lnx_g", (2, 512))
        self.tm_lnx_b = di("tm_lnx_b", (2, 512))
        self.sc_conv_w = di("sc_conv_w", (2, 3, 512))
        self.od_w_in = di("od_w_in", (2, D, ODD_COLS))
        self.od_w_out = di("od_w_out", (2, D, D))
        self.cf_conv_w = di("cf_conv_w", (2, 31, 512))
        self.cf_conv_b = di("cf_conv_b", (2, 512))
        self.cf_ln_g = di("cf_ln_g", (2, 512))
        self.cf_ln_b = di("cf_ln_b", (2, 512))
        self.ssd_conv_w = di("ssd_conv_w", (2, 4, 1024))
        self.ssd_conv_b = di("ssd_conv_b", (2, 1024))
        self.ssd_dt_bias = di("ssd_dt_bias", (2, 8))
        self.ssd_a_log = di("ssd_a_log", (2, 8))
        self.ssd_d = di("ssd_d", (2, 8))
        self.ssd_norm_g = di("ssd_norm_g", (2, 512))
        self.out = nc.dram_tensor("out", [SEQ, D], F32, kind="ExternalOutput").ap()
        self.wblocks = []
        for L in range(self.NL):
            ncols = EVEN_COLS if L % 2 == 0 else ODD_COLS
            blks = []
            c0 = 0
            bi = 0
            while c0 < ncols + D:
                if c0 < ncols:
                    n = min(512, ncols - c0)
                    src = (self.ev_w_in if L % 2 == 0 else self.od_w_in)[L // 2][:, c0:c0 + n]
                else:
                    n = 512
                    src = (self.ev_w_out if L % 2 == 0 else self.od_w_out)[L // 2][:, c0 - ncols:c0 - ncols + n]
                sc = nc.dram_tensor("wsc_%d_%d" % (L, bi), [128, 8 * 512], BF16).ap()
                blks.append((sc, n, Buf(), src))
                c0 += n
                bi += 1
            self.wblocks.append(blks)

    def alloc(self):
        p = self.p
        self.cst = p.sb("cst", (128, CWF), F32)
        self.cstb = p.sb("cstb", (128, 1664), BF16)
        self.xs = [p.sb("xs0", (128, 4, D), F32)]
        self.xs.append(self.xs[0])
        self.xT = p.sb("xT", (128, 8, TT), F32)
        self.yo = p.sb("yo", (128, 8, TT + 3), F32)
        self.sq = p.sb("sq", (128, 8, TT), BF16, nb=8)
        self.rb = p.sb("rb", (128, TT), F32)
        self.wst = self.yo
        self.g2c = p.sb("g2c", (128, 4, 8), F32)
        self.npo = p.sb("npo", (128, 4, 8), F32)
        self.hT = p.sb("hT", (128, 8, TT), BF16)
        self.yT = p.sb("yT", (128, 8, TT), BF16, nb=8)
        self.wring = [p.sb("wr%d" % i, (128, 8, 512), BF16) for i in range(2)]
        self.wnext = 0
        self.ss = p.sb("ss", (128, 8), F32)
        self.rstd = p.sb("rstd", (128, 8), F32)
        self.g1c = p.sb("g1c", (128, 4, 8), F32)
        self.shc = p.sb("shc", (128, 4, 8), F32)
        self.cT = p.sb("cT", (128, 8), F32)
        self.abc = p.sb("abc", (128, 4, 24), F32)
        self.npc = p.sb("npc", (128, 4, 8), F32)
        self.F = [p.sb("F%d" % i, (128, 4, TT), F32) for i in range(6)]
        self.cfw = p.sb("cfw", (128, 2, 4, 31), F32)
        self.cfp = p.sb("cfp", (128, 2, 3, 4), F32)
        self.cf_tail = p.sb("cf_tail", (128, 2, 4, 30), F32)
        self.uext = p.sb("uext", (128, 4, 30 + TT), F32)
        self.sdw = p.sb("sdw", (128, 2, 8, 4), F32)
        self.sdb = p.sb("sdb", (128, 2, 8), F32)
        self.sd_tail = p.sb("sd_tail", (128, 2, 8, 3), F32)
        self.sdng = p.sb("sdng", (128, 2, 4), F32)
        self.hrow = p.sb("hrow", (128, 2, 3, 8), F32)
        self.xext = self.yo
        self.xbcT = p.sb("xbcT", (128, 8, TT), BF16)
        self.ctm = p.sb("ctm", (128, 2, 2, TT), BF16)
        self.S = p.sb("S", (128, 2, 8, 64), F32)
        self.Sb = p.sb("Sb", (128, 2, 8, 64), BF16, nb=2)
        self.tokx = p.sb("tokx", (128, 512), F32)
        self.xdt = p.sb("xdt", (128, 8, 64), BF16)
        self.xdd = p.sb("xdd", (128, 8, 64), BF16)
        self.btok = p.sb("btok", (128, 256), BF16)
        self.dts = p.sb("dts", (128, 6, 8), F32)
        self.cdb = p.sb("cdb", (128, 2, 8), F32)
        self.Dall = p.sb("Dall", (128, 8, 128), F32)
        self.dec = p.sb("dec", (128, 8, 128), F32)
        self.cbt = p.sb("cbt", (128, 2, 128), F32)
        self.MT = p.sb("MT", (128, 8, 128), BF16)
        self.ytok = p.sb("ytok", (128, 512), F32)

        self.tmp = p.sb("tmp", (128, D), F32)
        self.pp = [p.ps("pp%d" % i, (128, 512), F32) for i in range(4)]
        self.ppn = 0
        self.pt = [Tile(self.nc.alloc_psum_tensor("ptb%d" % i, [128, 1024], BF16)[:, 0:512]) for i in range(2)]
        pcb_ = self.nc.alloc_psum_tensor("pcb", [128, 512], F32)
        self.pcb = Tile(pcb_[:, 0:256].rearrange("p (g n) -> p g n", g=2))
        psm_ = self.nc.alloc_psum_tensor("psm", [128, 512], F32)
        self.psm = Tile(psm_[:, 0:64])
        self.pp6 = list(self.pp) + [Tile(pcb_[:, :]), Tile(psm_[:, :])]
        self.pp6[4].bufs = self.pcb.bufs
        self.pp6[5].bufs = self.psm.bufs
        self.pp6n = 0
        self.ptn = 0
        self.mu = p.sb("mu", (128, 2, 17), F32)
        self.omu = p.sb("omu", (128, 2, 17), F32)
        self.scw = p.sb("scw", (128, 2, 4, 3), F32)
        self.sc_tail = p.sb("sc_tail", (128, 2, 4, 2), F32)
        self.tm_prev = p.sb("tm_prev", (128, 2, 17), F32)
        self.tmc = p.sb("tmc", (128, 2, 7, 4), F32)
        self.tmk1 = p.sb("tmk1", (128, 2, 4), F32)
        self.w2a2 = p.sb("w2a2", (128, 2, 512), BF16)
        self.Hs = p.sb("Hs", (128, 2, 4, 64), F32)
        self.Hb = p.sb("Hb", (128, 4, 64), BF16)
        self.gl = p.sb("gl", (128, 4, 8), F32)

    def next_pp(self):
        t = self.pp[self.ppn]
        self.ppn = (self.ppn + 1) % len(self.pp)
        return t

    def next_pp6(self):
        t = self.pp6[self.pp6n]
        self.pp6n = (self.pp6n + 1) % len(self.pp6)
        return t

    def next_pt(self):
        t = self.pt[self.ptn]
        self.ptn = (self.ptn + 1) % len(self.pt)
        return t

    def dump(self, name, ap, tile, n):
        if not self.dbg:
            return
        p = self.p
        d = self.nc.dram_tensor("dbg_" + name, [128, n], F32, kind="ExternalOutput").ap()
        sc = self.tmp
        p.cp("dve", sc[:, 0:n], ap, list(tile.bufs), [sc.b])
        p.dma("sp", d, sc[:, 0:n], [sc.b], ())

    def prologue(self):
        p = self.p
        nc = self.nc
        p.dma("sp", self.cst[:], self.consts[:, 0:CWF], (), [self.cst.b])
        p.cp("dve", self.cstb[:, 0:640], self.cst[:, 0:640], [self.cst.b], [self.cstb.b])
        p.dma("sp", self.tmp[:], self.consts[:, C_PAR0:C_PAR0 + 1024], (), [self.tmp.b])
        p.cp("dve", self.cstb[:, 640:1664], self.tmp[:], [self.tmp.b], [self.cstb.b])
        self.ident_b = self.cstb[:, 0:128]
        self.ones_b = self.cstb[:, C_ONE:C_ONE + 128]
        self.ident_f = self.cst[:, 0:128]
        k = 0
        for L in range(self.NL):
            for (sc, n, buf, src) in self.wblocks[L]:
                p.dma("sp", self.wst[:, :, 0:n], src.rearrange("(kc p) n -> p kc n", p=128), (), [self.wst.b])
                wr = self.wring[k % 2]
                eng = ("dve", "act", "pool")[k % 3]
                if eng == "act":
                    p.act(wr[:, :, 0:n], self.wst[:, :, 0:n], AF.Copy, [self.wst.b], [wr.b])
                else:
                    p.cp(eng, wr[:, :, 0:n], self.wst[:, :, 0:n], [self.wst.b], [wr.b])
                p.dma("act", sc.rearrange("p (kc n) -> p kc n", kc=8)[:, :, 0:n], wr[:, :, 0:n], [wr.b], [buf])
                k += 1
        stg = self.tmp

        def cols(src_rows, R, W, dst_fn, dst_tile):
            for w0 in range(0, W, 1024):
                wn = min(1024, W - w0)
                p.dma("sp", stg[0:R, 0:wn], src_rows[:, w0:w0 + wn], (), [stg.b])
                for c_ in range(wn // 128):
                    ps = self.next_pp()
                    p.tr(ps[:, 0:R], stg[0:R, c_ * 128:(c_ + 1) * 128], self.cst[0:R, 0:R], [stg.b, self.cst.b], [ps.b])
                    p.cp("dve", dst_fn(w0 // 128 + c_), ps[:, 0:R], [ps.b], [dst_tile.b])

        cols(self.c, 8, 128, lambda cc: self.cT[:, :], self.cT)
        p.act(self.cT[:], self.cT[:], AF.Silu, [self.cT.b], [self.cT.b])
        cols(self.ada_b, 4, 3072, lambda cc: self.abc[:, :, cc], self.abc)
        cols(self.norm_pre, 4, 1024, lambda cc: self.npc[:, :, cc], self.npc)
        cols(self.norm_post, 4, 1024, lambda cc: self.npo[:, :, cc], self.npo)
        cols(self.tm_mu, 2, 2176, lambda cc: self.mu[:, :, cc], self.mu)
        p.ts("dve", self.omu[:], self.mu[:], -1.0, 1.0, ALU.mult, ALU.add, [self.mu.b], [self.omu.b])
        for l in range(2):
            for i_, src in enumerate((self.tm_w0, self.tm_a0, self.tm_k_k, self.tm_k_a, self.tm_r_k, self.tm_lnx_g, self.tm_lnx_b)):
                cols(src[l:l + 1, :], 1, 512, lambda cc, l=l, i_=i_: self.tmc[:, l, i_, cc:cc + 1], self.tmc)
            p.ts("dve", self.tmk1[:, l, :], self.tmc[:, l, 3, :], -1.0, 1.0, ALU.mult, ALU.add, [self.tmc.b], [self.tmk1.b])
            p.dma("sp", stg[0:64, 0:512], self.tm_w2[l], (), [stg.b])
            p.dma("sp", stg[64:128, 0:512], self.tm_a2[l], (), [stg.b])
            p.cp("dve", self.w2a2[:, l, :], stg[:, 0:512], [stg.b], [self.w2a2.b])
            cols(self.sc_conv_w[l], 3, 512, lambda cc, l=l: self.scw[:, l, cc, :], self.scw)
            cols(self.cf_conv_w[l], 31, 512, lambda cc, l=l: self.cfw[:, l, cc, :], self.cfw)
            for i_, src in enumerate((self.cf_conv_b, self.cf_ln_g, self.cf_ln_b)):
                cols(src[l:l + 1, :], 1, 512, lambda cc, l=l, i_=i_: self.cfp[:, l, i_, cc:cc + 1], self.cfp)
            cols(self.ssd_conv_w[l], 4, 1024, lambda cc, l=l: self.sdw[:, l, cc, :], self.sdw)
            cols(self.ssd_conv_b[l:l + 1, :], 1, 1024, lambda cc, l=l: self.sdb[:, l, cc:cc + 1], self.sdb)
            cols(self.ssd_norm_g[l:l + 1, :], 1, 512, lambda cc, l=l: self.sdng[:, l, cc:cc + 1], self.sdng)
            for i_, src in enumerate((self.ssd_dt_bias, self.ssd_a_log, self.ssd_d)):
                p.dma("sp", stg[0:1, 0:8], src[l:l + 1, :], (), [stg.b])
                ps = self.next_pp()
                p.mm(ps[:, 0:8], self.cst[0:1, C_ONE:C_ONE + 128], stg[0:1, 0:8], True, True, [self.cst.b, stg.b], [ps.b])
                p.cp("dve", self.hrow[:, l, i_, :], ps[:, 0:8], [ps.b], [self.hrow.b])
            p.act(self.hrow[:, l, 1, :], self.hrow[:, l, 1, :], AF.Exp, [self.hrow.b], [self.hrow.b])
            p.ts("dve", self.hrow[:, l, 1, :], self.hrow[:, l, 1, :], -1.0, None, ALU.mult, None, [self.hrow.b], [self.hrow.b])
        p.ms("dve", self.Hs[:], 0.0, [self.Hs.b])
        p.ms("dve", self.cf_tail[:], 0.0, [self.cf_tail.b])
        p.ms("dve", self.sd_tail[:], 0.0, [self.sd_tail.b])
        p.ms("dve", self.S[:], 0.0, [self.S.b])
        p.ms("dve", self.sc_tail[:], 0.0, [self.sc_tail.b])
        p.ms("dve", self.tm_prev[:], 0.0, [self.tm_prev.b])
        for L in range(self.NL):
            aw = self.ada_w[L]
            pcol = self.next_pp()
            for blk in range(4):
                p.dma("sp", self.wst[:, :, 0:512], aw[:, blk * 512:(blk + 1) * 512].rearrange("(kc p) n -> p kc n", p=128), (), [self.wst.b])
                for jj in range(4):
                    j = blk * 4 + jj
                    for kc in range(8):
                        p.mm(pcol[:, j:j + 1], self.wst[:, kc, jj * 128:(jj + 1) * 128], self.cT[:, kc:kc + 1], kc == 0, kc == 7,
                             [self.wst.b, self.cT.b], [pcol.b])
            p.tt("dve", self.shc[:, L, :], pcol[:, 0:8], self.abc[:, L, 0:8], ALU.add, [pcol.b, self.abc.b], [self.shc.b])
            p.tt("dve", self.g1c[:, L, :], pcol[:, 8:16], self.abc[:, L, 8:16], ALU.add, [pcol.b, self.abc.b], [self.g1c.b])
            p.stt("dve", self.g1c[:, L, :], self.g1c[:, L, :], 1.0, self.npc[:, L, :], ALU.add, ALU.mult, [self.g1c.b, self.npc.b], [self.g1c.b])
            pcol2 = self.next_pp()
            for blk in range(2):
                p.dma("sp", self.wst[:, :, 0:512], aw[:, 2 * D + blk * 512:2 * D + (blk + 1) * 512].rearrange("(kc p) n -> p kc n", p=128), (), [self.wst.b])
                for jj in range(4):
                    j = blk * 4 + jj
                    for kc in range(8):
                        p.mm(pcol2[:, j:j + 1], self.wst[:, kc, jj * 128:(jj + 1) * 128], self.cT[:, kc:kc + 1], kc == 0, kc == 7,
                             [self.wst.b, self.cT.b], [pcol2.b])
            p.tt("dve", self.g2c[:, L, :], pcol2[:, 0:8], self.abc[:, L, 16:24], ALU.add, [pcol2.b, self.abc.b], [self.g2c.b])
            p.tt("dve", self.g2c[:, L, :], self.g2c[:, L, :], self.npo[:, L, :], ALU.mult, [self.g2c.b, self.npo.b], [self.g2c.b])

    def _issue_w(self, k):
        L, bi = self.wseq[k]
        sc, n, buf, _ = self.wblocks[L][bi]
        wr = self.wring[k % len(self.wring)]
        self.p.dma("sp", wr[:, :, 0:n], sc.rearrange("p (kc n) -> p kc n", kc=8)[:, :, 0:n], [buf], [wr.b])
        self.wissued[k] = wr

    def load_w(self, L, bi):
        if self.wseq is None:
            self.wrec.append((L, bi))
            sc, n, buf, _ = self.wblocks[L][bi]
            wr = self.wring[self.wnext]
            self.wnext = (self.wnext + 1) % len(self.wring)
            self.p.dma("sp", wr[:, :, 0:n], sc.rearrange("p (kc n) -> p kc n", kc=8)[:, :, 0:n], [buf], [wr.b])
            return wr
        k = self.wptr
        assert self.wseq[k] == (L, bi), (k, self.wseq[k], (L, bi))
        if k not in self.wissued:
            self._issue_w(k)
        wr = self.wissued.pop(k)
        self.wptr += 1
        if self.wptr < len(self.wseq):
            self._issue_w(self.wptr)
        return wr

    def load_tile(self, t):
        p = self.p
        xs = self.xs[t % 2]
        for kc in range(8):
            ps = self.next_pp()
            for s in range(4):
                p.tr(ps[:, s * 128:(s + 1) * 128], xs[:, s, kc * 128:(kc + 1) * 128], self.ident_f, [xs.b, self.cst.b], [ps.b])
            if kc % 2 == 0:
                p.act(self.xT[:, kc, :], ps[:], AF.Copy, [ps.b], [self.xT.b])
            else:
                p.cp("dve", self.xT[:, kc, :], ps[:], [ps.b], [self.xT.b])

    def store_tile(self, t, orr):
        p = self.p
        xs = self.xs[t % 2]
        for s in range(4):
            for half in range(2):
                ps = self.next_pp()
                for q in range(4):
                    kc = half * 4 + q
                    p.tr(ps[:, q * 128:(q + 1) * 128], self.xT[:, kc, s * 128:(s + 1) * 128], self.ident_f, [self.xT.b, self.cst.b], [ps.b])
                if half == 0:
                    p.act(xs[:, s, 0:512], ps[:], AF.Copy, [ps.b], [xs.b])
                else:
                    p.cp("dve", xs[:, s, 512:1024], ps[:], [ps.b], [xs.b])
        p.dma("sp", orr[t], xs[:], [xs.b], ())

    def rms_bcast(self, src):
        p = self.p
        for kc in range(8):
            p.act(self.sq[:, kc, :], src[:, kc, 0:TT], AF.Square, [src.b], [self.sq.bufs[kc]])
        ps = self.next_pp()
        for kc in range(8):
            p.mm(ps[:], self.ones_b, self.sq[:, kc, :], kc == 0, kc == 7, [self.cstb.b, self.sq.bufs[kc]], [ps.b])
        p.ts("dve", self.rb[:], ps[:], 1.0 / D, 1e-6, ALU.mult, ALU.add, [ps.b], [self.rb.b])
        p.op("act", lambda e: e.activation(out=self.rb[:], in_=self.rb[:], func=AF.Ln), [self.rb.b], [self.rb.b])
        p.op("act", lambda e: e.activation(out=self.rb[:], in_=self.rb[:], func=AF.Exp, scale=-0.5), [self.rb.b], [self.rb.b])

    def pre_norm(self, L):
        p = self.p
        if L == 0 and not getattr(self, "_d0", False):
            self.dump("xT0", self.xT[:, 0, :], self.xT, 512)
            self.dump("g1c", self.g1c[:, 0, :], self.g1c, 8)
            self.dump("shc", self.shc[:, 0, :], self.shc, 8)
            self.dump("cT", self.cT[:], self.cT, 8)
            self.dump("npc", self.npc[:, 0, :], self.npc, 8)
            self.dump("abc", self.abc[:, 0, :], self.abc, 24)
            self.dump("xT7", self.xT[:, 7, :], self.xT, 512)
        self.rms_bcast(self.xT)
        if L == 0 and not getattr(self, "_d0", False):
            self._d0 = True
            self._d0b = True
            self.dump("rb0", self.rb[:], self.rb, 512)
            self.dump("sq0", self.sq[:, 0, :], self.sq, 512)
        for kc in range(8):
            p.tt("dve", self.tmp[:, 0:512], self.xT[:, kc, :], self.rb[:], ALU.mult, [self.xT.b, self.rb.b], [self.tmp.b])
            p.ts("dve", self.hT[:, kc, :], self.tmp[:, 0:512], self.g1c[:, L, kc:kc + 1], self.shc[:, L, kc:kc + 1], ALU.mult, ALU.add,
                 [self.tmp.b, self.g1c.b, self.shc.b], [self.hT.b])
        if getattr(self, "_d0b", False):
            self._d0b = False
            self.dump("hT", self.hT[:, 0, :], self.hT, 512)
            self.dump("hT7", self.hT[:, 7, :], self.hT, 512)

    def proj_fm(self, wr, lc, ps):
        p = self.p
        for kc in range(8):
            p.mm(ps[:], wr[:, kc, lc * 128:(lc + 1) * 128], self.hT[:, kc, :], kc == 0, kc == 7, [wr.b, self.hT.b], [ps.b])

    def out_proj_post(self, L):
        p = self.p
        nb = len(self.wblocks[L])
        allY = self.yT.bufs
        for half in range(2):
            wr = self.load_w(L, nb - 2 + half)
            for q in range(4):
                dmc = half * 4 + q
                ps = self.next_pp()
                for cc in range(8):
                    p.mm(ps[:], wr[:, cc, q * 128:(q + 1) * 128], self.yT[:, cc, :], cc == 0, cc == 7, [wr.b] + allY, [ps.b])
                p.act(self.yo[:, dmc, 0:TT], ps[:], AF.Copy, [ps.b], [self.yo.b])
        if not getattr(self, "_d1", False):
            self._d1 = True
            self.dump("yo0", self.yo[:, 0, 0:TT], self.yo, 512)
            self.dump("yo7", self.yo[:, 7, 0:TT], self.yo, 512)
        self.rms_bcast(self.yo)
        for dmc in range(8):
            p.stt("dve", self.tmp[:, 0:512], self.yo[:, dmc, 0:TT], self.g2c[:, L, dmc:dmc + 1], self.rb[:], ALU.mult, ALU.mult,
                  [self.yo.b, self.g2c.b, self.rb.b], [self.tmp.b])
            p.tt("dve", self.xT[:, dmc, :], self.xT[:, dmc, :], self.tmp[:, 0:512], ALU.add, [self.xT.b, self.tmp.b], [self.xT.b])

    def even_layer(self, L):
        p = self.p
        j = L // 2
        self.pre_norm(L)
        wrs = {}
        loaded = {}

        def get(ch):
            bi = ch // 4
            if bi not in loaded:
                loaded[bi] = self.load_w(L, bi)
            return (loaded[bi], ch % 4)

        self._cur_blk = None
        if "B" in self.mixers:
            self.short_conv_stream(L, j)
        else:
            for cc in range(4, 8):
                p.ms("dve", self.yT[:, cc, :], 0.0, [self.yT.bufs[cc]])
        if "A" in self.mixers:
            self.rwkv(L, j)
        else:
            for cc in range(4):
                p.ms("dve", self.yT[:, cc, :], 0.0, [self.yT.bufs[cc]])
        if not getattr(self, "_yd", False):
            self._yd = True
            self.dump("yT4", self.yT[:, 4, :], self.yT, 512)
        self.out_proj_post(L)

    def short_conv_stream(self, L, j):
        p = self.p
        F = self.F
        dest = {}
        for cc in range(4):
            dest[17 + cc] = (F[3], cc, AF.Copy)
            dest[21 + cc] = (F[0], cc, AF.Copy)
            dest[25 + cc] = (F[1], cc, AF.Copy)
            dest[29 + cc] = (F[4], cc, AF.Silu)
        cur_bi = None
        wr = None
        for ch in range(17, 33):
            bi = ch // 4
            if bi != cur_bi:
                wr = self.load_w(L, bi)
                cur_bi = bi
            ps = self.next_pp()
            self.proj_fm(wr, ch % 4, ps)
            t, cc, fn = dest[ch]
            if ch % 2 == 0:
                p.act(t[:, cc, :], ps[:], fn, [ps.b], [t.b])
            else:
                if fn == AF.Copy:
                    p.cp("dve", t[:, cc, :], ps[:], [ps.b], [t.b])
                else:
                    p.act(t[:, cc, :], ps[:], fn, [ps.b], [t.b])
        u = F[0]
        acc = F[2]
        w = self.scw
        tl = self.sc_tail
        p.tt("dve", u[:], F[0][:], F[1][:], ALU.mult, [F[0].b, F[1].b], [u.b])
        for cc in range(4):
            p.ts("dve", acc[:, cc, :], u[:, cc, :], w[:, j, cc, 2:3], None, ALU.mult, None, [u.b, w.b], [acc.b])
            p.stt("dve", acc[:, cc, 1:TT], u[:, cc, 0:TT - 1], w[:, j, cc, 1:2], acc[:, cc, 1:TT], ALU.mult, ALU.add, [u.b, w.b, acc.b], [acc.b])
            p.stt("dve", acc[:, cc, 2:TT], u[:, cc, 0:TT - 2], w[:, j, cc, 0:1], acc[:, cc, 2:TT], ALU.mult, ALU.add, [u.b, w.b, acc.b], [acc.b])
            p.stt("dve", acc[:, cc, 0:1], tl[:, j, cc, 1:2], w[:, j, cc, 1:2], acc[:, cc, 0:1], ALU.mult, ALU.add, [tl.b, w.b, acc.b], [acc.b])
            p.stt("dve", acc[:, cc, 0:2], tl[:, j, cc, 0:2], w[:, j, cc, 0:1], acc[:, cc, 0:2], ALU.mult, ALU.add, [tl.b, w.b, acc.b], [acc.b])
            p.cp("dve", tl[:, j, cc, :], u[:, cc, TT - 2:TT], [u.b, acc.b], [tl.b])
        p.tt("dve", acc[:], acc[:], F[3][:], ALU.mult, [acc.b, F[3].b], [acc.b])
        for cc in range(4):
            p.tt("dve", self.yT[:, 4 + cc, :], acc[:, cc, :], F[4][:, cc, :], ALU.mult, [acc.b, F[4].b], [self.yT.bufs[4 + cc]])

    def proj_to(self, L, ch, dst_ap, dst_bufs, fn=None, eng="act"):
        p = self.p
        bi = ch // 4
        if self._cur_blk != (L, bi):
            self._cur_wr = self.load_w(L, bi)
            self._cur_blk = (L, bi)
        ps = self.next_pp()
        self.proj_fm(self._cur_wr, ch % 4, ps)
        if eng == "act":
            p.act(dst_ap, ps[:], fn or AF.Copy, [ps.b], dst_bufs)
        else:
            p.cp(eng, dst_ap, ps[:], [ps.b], dst_bufs)

    def conformer(self, L, j):
        p = self.p
        F = self.F
        ue = self.uext
        p.cp("dve", ue[:, :, 0:30], self.cf_tail[:, j, :, :], [self.cf_tail.b], [ue.b])
        for cc in range(4):
            self.proj_to(L, cc, F[0][:, cc, :], [F[0].b], eng="dve")
        for cc in range(4):
            self.proj_to(L, 4 + cc, F[1][:, cc, :], [F[1].b], AF.Sigmoid)
        p.tt("dve", ue[:, :, 30:30 + TT], F[0][:], F[1][:], ALU.mult, [F[0].b, F[1].b], [ue.b])
        p.cp("dve", self.cf_tail[:, j, :, :], ue[:, :, TT:TT + 30], [ue.b], [self.cf_tail.b])
        for cc in range(4):
            self.proj_to(L, 8 + cc, F[1][:, cc, :], [F[1].b], AF.Silu)
        acc = F[2]
        w = self.cfw
        for cc in range(4):
            eng = "dve"
            p.ts(eng, acc[:, cc, :], ue[:, cc, 30:30 + TT], w[:, j, cc, 30:31], self.cfp[:, j, 0, cc:cc + 1], ALU.mult, ALU.add,
                 [ue.b, w.b, self.cfp.b], [acc.bufs[0]])
            for k in range(30):
                p.stt(eng, acc[:, cc, :], ue[:, cc, k:k + TT], w[:, j, cc, k:k + 1], acc[:, cc, :], ALU.mult, ALU.add,
                      [ue.b, w.b, acc.b], [acc.b])
        ones_f = self.cst[:, C_ONE:C_ONE + 128]
        pm = self.next_pp()
        for cc in range(4):
            p.mm(pm[:], ones_f, acc[:, cc, :], cc == 0, cc == 3, [self.cst.b, acc.b], [pm.b])
        for cc in range(4):
            p.act(F[0][:, cc, :], acc[:, cc, :], AF.Square, [acc.b], [F[0].b])
        pq = self.next_pp()
        for cc in range(4):
            p.mm(pq[:], ones_f, F[0][:, cc, :], cc == 0, cc == 3, [self.cst.b, F[0].b], [pq.b])
        mean = F[3][:, 0, :]
        var = F[3][:, 1, :]
        p.ts("dve", mean, pm[:], 1.0 / 512, None, ALU.mult, None, [pm.b], [F[3].b])
        p.tt("dve", F[3][:, 2, :], mean, mean, ALU.mult, [F[3].b], [F[3].b])
        p.stt("dve", var, pq[:], 1.0 / 512, F[3][:, 2, :], ALU.mult, ALU.subtract, [pq.b, F[3].b], [F[3].b])
        p.ts("dve", var, var, 1e-5, None, ALU.add, None, [F[3].b], [F[3].b])
        p.op("act", lambda e: e.activation(out=var, in_=var, func=AF.Ln), [F[3].b], [F[3].b])
        p.op("act", lambda e: e.activation(out=var, in_=var, func=AF.Exp, scale=-0.5), [F[3].b], [F[3].b])
        if not getattr(self, "_cfd", False):
            self._cfd = True
            self.dump("cf_hT", self.hT[:, 0, :], self.hT, 512)
            self.dump("cf_xT", self.xT[:, 0, :], self.xT, 512)
            self.dump("cf_g1c", self.g1c[:, 1, :], self.g1c, 8)
            self.dump("cf_shc", self.shc[:, 1, :], self.shc, 8)
            self.dump("cf_rb", self.rb[:], self.rb, 512)
            self.dump("cf_u", ue[:, 0, 30:30 + TT], ue, 512)
            self.dump("cf_acc", acc[:, 0, :], acc, 512)
            self.dump("cf_mean", mean, F[3], 512)
            self.dump("cf_rstd", var, F[3], 512)
            self.dump("cf_sg", F[1][:, 0, :], F[1], 512)
        for cc in range(4):
            p.tt("dve", F[0][:, cc, :], acc[:, cc, :], mean, ALU.subtract, [acc.b, F[3].b], [F[0].b])
            p.tt("dve", F[0][:, cc, :], F[0][:, cc, :], var, ALU.mult, [F[0].b, F[3].b], [F[0].b])
            p.act(F[0][:, cc, :], F[0][:, cc, :], AF.Silu, [F[0].b, self.cfp.b], [F[0].b],
                  scale=self.cfp[:, j, 1, cc:cc + 1], bias=self.cfp[:, j, 2, cc:cc + 1])
            p.tt("dve", self.yT[:, cc, :], F[0][:, cc, :], F[1][:, cc, :], ALU.mult, [F[0].b, F[1].b], [self.yT.bufs[cc]])

    def ssd(self, L, j):
        p = self.p
        F = self.F
        xe = self.xext
        zs = F[4]
        for cc in range(4):
            self.proj_to(L, 12 + cc, zs[:, cc, :], [zs.b], AF.Silu)
        p.cp("dve", xe[:, :, 0:3], self.sd_tail[:, j, :, :], [self.sd_tail.b], [xe.b])
        for cc in range(8):
            self.proj_to(L, 16 + cc, xe[:, cc, 3:3 + TT], [xe.b], eng="act")
        p.cp("dve", self.sd_tail[:, j, :, :], xe[:, :, TT:TT + 3], [xe.b], [self.sd_tail.b])
        wdt = self.load_w(L, 6)
        self._cur_blk = None
        w = self.sdw
        for cc in range(8):
            eng = "dve"
            acc = F[5][:, cc % 4, :]
            p.ts(eng, acc, xe[:, cc, 3:3 + TT], w[:, j, cc, 3:4], None, ALU.mult, None, [xe.b, w.b], [F[5].bufs[0]])
            for k in range(3):
                p.stt(eng, acc, xe[:, cc, k:k + TT], w[:, j, cc, k:k + 1], acc, ALU.mult, ALU.add, [xe.b, w.b, F[5].b], [F[5].b])
            p.act(self.xbcT[:, cc, :], acc, AF.Silu, [F[5].b, self.sdb.b], [self.xbcT.b], bias=self.sdb[:, j, cc:cc + 1])
        for g in range(2):
            for e in range(2):
                par = self.cstb[:, (CB_PAR0 if e == 0 else CB_PAR1):(CB_PAR0 if e == 0 else CB_PAR1) + TT]
                p.tt("dve", self.ctm[:, g, e, :], self.xbcT[:, 6 + g, :], par, ALU.mult, [self.xbcT.b, self.cstb.b], [self.ctm.b])
        y2T = F[0]
        for s in range(4):
            tsl = slice(s * 128, (s + 1) * 128)
            for kc in range(8):
                p.mm(self.psm[:, 0:8], self.hT[:, kc, tsl], wdt[:, kc, 0:8], kc == 0, kc == 7, [self.hT.b, wdt.b], [self.psm.b])
            d = self.dts
            p.tt("dve", d[:, 0, :], self.psm[:, 0:8], self.hrow[:, j, 0, :], ALU.add, [self.psm.b, self.hrow.b], [d.b])
            p.act(d[:, 0, :], d[:, 0, :], AF.Exp, [d.b], [d.b])
            p.ts("dve", d[:, 0, :], d[:, 0, :], 1.0, None, ALU.add, None, [d.b], [d.b])
            p.act(d[:, 0, :], d[:, 0, :], AF.Ln, [d.b], [d.b])
            p.tt("dve", d[:, 1, :], d[:, 0, :], self.hrow[:, j, 1, :], ALU.mult, [d.b, self.hrow.b], [d.b])
            p.mm(self.psm[:, 8:16], self.cst[:, C_TRI2:C_TRI2 + 128], d[:, 1, :], True, True, [self.cst.b, d.b], [self.psm.b])
            p.mm(self.psm[:, 16:24], self.cst[:, C_BLK:C_BLK + 128], d[:, 1, :], True, True, [self.cst.b, d.b], [self.psm.b])
            p.mm(self.psm[:, 24:32], self.cst[:, C_SEL0:C_SEL0 + 128], d[:, 1, :], True, True, [self.cst.b, d.b], [self.psm.b])
            p.mm(self.psm[:, 32:40], self.cst[:, C_SEL1:C_SEL1 + 128], d[:, 1, :], True, True, [self.cst.b, d.b], [self.psm.b])
            p.cp("dve", d[:, 2, :], self.psm[:, 8:16], [self.psm.b], [d.b])
            p.tt("dve", d[:, 4, :], self.psm[:, 16:24], d[:, 2, :], ALU.subtract, [self.psm.b, d.b], [d.b])
            p.act(d[:, 4, :], d[:, 4, :], AF.Exp, [d.b], [d.b])
            p.act(d[:, 5, :], d[:, 2, :], AF.Exp, [d.b], [d.b])
            p.act(self.cdb[:, 0, :], self.psm[:, 24:32], AF.Exp, [self.psm.b], [self.cdb.b])
            p.act(self.cdb[:, 1, :], self.psm[:, 32:40], AF.Exp, [self.psm.b], [self.cdb.b])
            p.ts("dve", d[:, 3, :], d[:, 2, :], -1.0, None, ALU.mult, None, [d.b], [d.b])
            pt = self.next_pt()
            for cc in range(4):
                p.tr(pt[:, cc * 128:(cc + 1) * 128], self.xbcT[:, cc, tsl], self.ident_b, [self.xbcT.b, self.cstb.b], [pt.b])
            p.cp("act", self.tokx[:], pt[:], [pt.b], [self.tokx.b])
            pt2 = self.next_pt()
            for cc in range(2):
                p.tr(pt2[:, cc * 128:(cc + 1) * 128], self.xbcT[:, 4 + cc, tsl], self.ident_b, [self.xbcT.b, self.cstb.b], [pt2.b])
            p.cp("act", self.btok[:], pt2[:, 0:256], [pt2.b], [self.btok.b])
            tx3 = self.tokx[:].rearrange("p (h q) -> p h q", h=8)
            p.tt("dve", self.xdt[:], tx3, d[:, 0, :].unsqueeze(2).to_broadcast([128, 8, 64]), ALU.mult, [self.tokx.b, d.b], [self.xdt.b])
            p.tt("dve", self.xdd[:], self.xdt[:], d[:, 4, :].unsqueeze(2).to_broadcast([128, 8, 64]), ALU.mult, [self.xdt.b, d.b], [self.xdd.b])
            for g in range(2):
                p.mm(self.pcb[:, g, :], self.xbcT[:, 4 + g, tsl], self.xbcT[:, 6 + g, tsl], True, True, [self.xbcT.b], [self.pcb.b])
            p.cp("act", self.cbt[:], self.pcb[:], [self.pcb.b], [self.cbt.b])
            p.cp("dve", self.Dall[:], d[:, 1, :].unsqueeze(2).to_broadcast([128, 8, 128]), [d.b], [self.Dall.b])
            for hh in range(2):
                pd = self.next_pp()
                for q in range(4):
                    h = hh * 4 + q
                    p.mm(pd[:, q * 128:(q + 1) * 128], self.Dall[:, h, :], self.cst[:, C_TRI2:C_TRI2 + 128], True, False, [self.Dall.b, self.cst.b], [pd.b])
                    p.mm(pd[:, q * 128:(q + 1) * 128], self.cst[:, C_ID:C_ID + 128], self.cst[:, C_MB2:C_MB2 + 128], False, True, [self.cst.b], [pd.b])
                for q in range(4):
                    h = hh * 4 + q
                    p.act(self.dec[:, h, :], pd[:, q * 128:(q + 1) * 128], AF.Exp, [pd.b, d.b], [self.dec.b], bias=d[:, 3, h:h + 1])
            for g in range(2):
                p.tt("dve", self.MT[:, g * 4:(g + 1) * 4, :], self.dec[:, g * 4:(g + 1) * 4, :],
                     self.cbt[:, g, :].unsqueeze(1).to_broadcast([128, 4, 128]), ALU.mult, [self.dec.b, self.cbt.b], [self.MT.b])
            pyd = self.next_pp()
            for h in range(8):
                p.mm(pyd[:, h * 64:(h + 1) * 64], self.MT[:, h, :], self.xdt[:, h, :], True, True, [self.MT.b, self.xdt.b], [pyd.b])
            for e in range(2):
                rs = slice(e * 64, (e + 1) * 64)
                p.cp("act", self.Sb[:, e, :, :], self.S[:, j, :, :], [self.S.b], [self.Sb.bufs[e]])
                pst = self.next_pp()
                for g in range(2):
                    p.mm(pst[:, g * 256:(g + 1) * 256], self.btok[rs, g * 128:(g + 1) * 128],
                         self.xdd[rs, g * 4:(g + 1) * 4, :].rearrange("p h q -> p (h q)"), True, True, [self.btok.b, self.xdd.b], [pst.b])
                p.tt("dve", self.S[:, j, :, :], self.S[:, j, :, :], self.cdb[:, e, :].unsqueeze(2).to_broadcast([128, 8, 64]), ALU.mult,
                     [self.S.b, self.cdb.b], [self.S.b])
                p.tt("dve", self.S[:, j, :, :], self.S[:, j, :, :], pst[:].rearrange("p (h q) -> p h q", h=8), ALU.add, [self.S.b, pst.b], [self.S.b])
            pyo = self.next_pp()
            for g in range(2):
                for e in range(2):
                    p.mm(pyo[:, g * 256:(g + 1) * 256], self.ctm[:, g, e, tsl], self.Sb[:, e, g * 4:(g + 1) * 4, :].rearrange("p h q -> p (h q)"),
                         e == 0, e == 1, [self.ctm.b, self.Sb.bufs[e]], [pyo.b])
            yt = self.ytok
            yt3 = yt[:].rearrange("p (h q) -> p h q", h=8)
            p.tt("dve", yt3, pyo[:].rearrange("p (h q) -> p h q", h=8), d[:, 5, :].unsqueeze(2).to_broadcast([128, 8, 64]), ALU.mult, [pyo.b, d.b], [yt.b])
            p.tt("dve", yt[:], yt[:], pyd[:], ALU.add, [yt.b, pyd.b], [yt.b])
            p.tt("dve", self.tokx[:].rearrange("p (h q) -> p h q", h=8), tx3, self.hrow[:, j, 2, :].unsqueeze(2).to_broadcast([128, 8, 64]), ALU.mult,
                 [self.tokx.b, self.hrow.b], [self.tokx.b])
            p.tt("dve", yt[:], yt[:], self.tokx[:], ALU.add, [yt.b, self.tokx.b], [yt.b])
            py = self.next_pp()
            for cc in range(4):
                p.tr(py[:, cc * 128:(cc + 1) * 128], yt[:, cc * 128:(cc + 1) * 128], self.ident_f, [yt.b, self.cst.b], [py.b])
            p.tt("dve", y2T[:, :, tsl], py[:].rearrange("p (c t) -> p c t", c=4), zs[:, :, tsl], ALU.mult, [py.b, zs.b], [y2T.b])
        for cc in range(4):
            p.act(self.sq[:, cc, :], y2T[:, cc, :], AF.Square, [y2T.b], [self.sq.bufs[cc]])
        pr = self.next_pp()
        for cc in range(4):
            p.mm(pr[:], self.ones_b, self.sq[:, cc, :], cc == 0, cc == 3, [self.cstb.b, self.sq.bufs[cc]], [pr.b])
        p.ts("dve", self.rb[:], pr[:], 1.0 / 512, 1e-6, ALU.mult, ALU.add, [pr.b], [self.rb.b])
        p.op("act", lambda e: e.activation(out=self.rb[:], in_=self.rb[:], func=AF.Ln), [self.rb.b], [self.rb.b])
        p.op("act", lambda e: e.activation(out=self.rb[:], in_=self.rb[:], func=AF.Exp, scale=-0.5), [self.rb.b], [self.rb.b])
        for cc in range(4):
            p.stt("dve", self.yT[:, 4 + cc, :], y2T[:, cc, :], self.sdng[:, j, cc:cc + 1], self.rb[:], ALU.mult, ALU.mult,
                  [y2T.b, self.sdng.b, self.rb.b], [self.yT.bufs[4 + cc]])

    def odd_layer(self, L):
        p = self.p
        j = L // 2
        self.pre_norm(L)
        self._cur_blk = None
        if "C" in self.mixers:
            self.conformer(L, j)
        else:
            for cc in range(4):
                p.ms("dve", self.yT[:, cc, :], 0.0, [self.yT.bufs[cc]])
        if not getattr(self, "_cfd2", False):
            self._cfd2 = True
            self.dump("cf_y", self.yT[:, 0, :], self.yT, 512)
        if "D" in self.mixers:
            self.ssd(L, j)
        else:
            for cc in range(4, 8):
                p.ms("dve", self.yT[:, cc, :], 0.0, [self.yT.bufs[cc]])
        self.out_proj_post(L)

    def rwkv(self, L, j):
        p = self.p
        F = self.F
        LW = -0.6065306597126334
        INV = self.INV_DT
        cst = self.cst
        rT, kT, vT, sgT, bonT, aT = F[0], F[1], F[2], F[3], F[4], F[5]
        yob = self.yo[:].rearrange("p a b -> p (a b)").bitcast(BF16)
        AR = Tile(yob[:, 0:4096].rearrange("p (c n q d) -> p c n q d", c=4, n=8, q=2))
        AR.bufs = self.yo.bufs
        BK = Tile(yob[:, 4096:8192].rearrange("p (c n q d) -> p c n q d", c=4, n=8, q=2))
        BK.bufs = self.yo.bufs
        Vt, Bt, Kt = self.hT, self.sq, self.xbcT
        Ytok = Tile(self.xs[0][:].rearrange("p s (a d) -> p (s a) d", a=2))
        Ytok.bufs = self.xs[0].bufs
        if INV == F32:
            Wp = [self.Dall, self.dec]
            NTp = [Tile(self.uext[:, 0, 0:512].rearrange("p (h d) -> p h d", h=8)), Tile(self.uext[:, 1, 0:512].rearrange("p (h d) -> p h d", h=8))]
            Xs = Tile(self.uext[:, 2, 0:512].rearrange("p (h d) -> p h d", h=8))
        else:
            Wp = []
            for t_ in (self.Dall, self.dec):
                w_ = Tile(t_[:].rearrange("p h d -> p (h d)").bitcast(BF16)[:, 0:1024].rearrange("p (h d) -> p h d", h=8))
                w_.bufs = t_.bufs
                Wp.append(w_)
            NTp = [Tile(self.uext[:, i_, 0:512].bitcast(BF16)[:, 0:512].rearrange("p (h d) -> p h d", h=8)) for i_ in range(2)]
            Xs = Tile(self.uext[:, 2, 0:512].bitcast(BF16)[:, 0:512].rearrange("p (h d) -> p h d", h=8))
        for t_ in NTp + [Xs]:
            t_.bufs = self.uext.bufs
        Us = self.xdd
        Srb = Tile(self.Sb[:, 0, :, :])
        Srb.bufs = [self.Sb.bufs[0]]
        Sk = self.MT
        wdad = Tile(self.xdt[:].rearrange("p h d -> p (h d)"))
        wdad.bufs = self.xdt.bufs
        sig, cs = self.tokx, self.ytok
        e1 = self.tmp[:, 0:512]
        e2 = self.tmp[:, 512:1024]
        tmpb = self.tmp.b
        kk = self.rb
        mu, omu, prev = self.mu, self.omu, self.tm_prev
        tmc = self.tmc

        def shifted(ch, dst_ap, dst_tile):
            bi = ch // 4
            if self._cur_blk != (L, bi):
                self._cur_wr = self.load_w(L, bi)
                self._cur_blk = (L, bi)
            ps = self.next_pp()
            self.proj_fm(self._cur_wr, ch % 4, ps)
            p.act(dst_ap, ps[:], AF.Copy, [ps.b, omu.b], [dst_tile.b], scale=omu[:, j, ch:ch + 1])
            p.stt("dve", dst_ap[:, 1:TT], ps[:, 0:TT - 1], mu[:, j, ch:ch + 1], dst_ap[:, 1:TT], ALU.mult, ALU.add, [ps.b, mu.b, dst_tile.b], [dst_tile.b])
            p.stt("dve", dst_ap[:, 0:1], prev[:, j, ch:ch + 1], mu[:, j, ch:ch + 1], dst_ap[:, 0:1], ALU.mult, ALU.add, [prev.b, mu.b, dst_tile.b], [dst_tile.b])
            p.cp("dve", prev[:, j, ch:ch + 1], ps[:, TT - 1:TT], [ps.b], [prev.b])

        for cc in range(4):
            shifted(cc, rT[:, cc, :], rT)
        for cc in range(4):
            shifted(4 + cc, kT[:, cc, :], kT)
        for cc in range(4):
            shifted(8 + cc, vT[:, cc, :], vT)
        for cc in range(4):
            shifted(12 + cc, sgT[:, cc, :], sgT)
        shifted(16, e1, self.tmp)
        p.act(wdad[0:64, :], e1[0:64, :], AF.Tanh, [tmpb], [wdad.b])
        p.cp("dve", wdad[64:128, :], e1[64:128, :], [tmpb], [wdad.b])
        for cc in range(4):
            p.act(sgT[:, cc, :], sgT[:, cc, :], AF.Silu, [sgT.b], [sgT.b])
        if self.rw_stop == 1:
            for cc in range(4):
                p.ms("dve", self.yT[:, cc, :], 0.0, [self.yT.bufs[cc]])
            return
        for cc in range(4):
            csl = slice(cc * 128, (cc + 1) * 128)
            pz = self.next_pp()
            p.mm(pz[:], self.w2a2[0:64, j, csl], wdad[0:64, :], True, True, [self.w2a2.b, wdad.b], [pz.b])
            p.act(sig[:], pz[:], AF.Sigmoid, [pz.b, tmc.b], [sig.b], bias=tmc[:, j, 0, cc:cc + 1])
            pa = self.next_pp()
            p.mm(pa[:], self.w2a2[64:128, j, csl], wdad[64:128, :], True, True, [self.w2a2.b, wdad.b], [pa.b])
            p.act(aT[:, cc, :], pa[:], AF.Sigmoid, [pa.b, tmc.b], [aT.b], bias=tmc[:, j, 1, cc:cc + 1])
            p.op("dve", lambda e: e.tensor_tensor_scan(out=cs[:], data0=cst[:, C_RST:C_RST + TT], data1=sig[:], initial=0.0,
                                                        op0=ALU.mult, op1=ALU.add), [cst.b, sig.b], [cs.b])
            p.act(self.gl[:, cc, :], cs[:].rearrange("p (n d) -> p n d", d=64)[:, :, 63], AF.Exp, [cs.b], [self.gl.b], scale=LW)
            p.act(e1, cs[:], AF.Exp, [cs.b], [tmpb], scale=LW)
            p.tt("dve", AR[:, cc, :, 1, :], rT[:, cc, :].rearrange("p (n d) -> p n d", d=64), e1.rearrange("p (n d) -> p n d", d=64), ALU.mult,
                 [rT.b, tmpb], [AR.b])
            p.ts("dve", kk[:], kT[:, cc, :], tmc[:, j, 2, cc:cc + 1], None, ALU.mult, None, [kT.b, tmc.b], [kk.b])
            p.tt("dve", e2, kk[:], kk[:], ALU.mult, [kk.b], [tmpb])
            pn = self.next_pp()
            p.mm(pn[:], cst[:, C_BLK:C_BLK + 128], e2, True, True, [cst.b, tmpb], [pn.b])
            p.ts("dve", e2, pn[:], 1e-24, None, ALU.max, None, [pn.b], [tmpb])
            p.op("act", lambda e: e.activation(out=e2, in_=e2, func=AF.Ln), [tmpb], [tmpb])
            p.op("act", lambda e: e.activation(out=e2, in_=e2, func=AF.Exp, scale=-0.5), [tmpb], [tmpb])
            p.tt("dve", kk[:], kk[:], e2, ALU.mult, [kk.b, tmpb], [kk.b])
            p.tt("dve", e2, cs[:], sig[:], ALU.subtract, [cs.b, sig.b], [tmpb])
            p.act(e2, e2, AF.Exp, [tmpb], [tmpb], scale=LW)
            p.stt("dve", AR[:, cc, :, 0, :], kk[:].rearrange("p (n d) -> p n d", d=64), -1.0, e2.rearrange("p (n d) -> p n d", d=64),
                  ALU.mult, ALU.mult, [kk.b, tmpb], [AR.b])
            p.act(e1, cs[:], AF.Exp, [cs.b], [tmpb], scale=-LW)
            p.tt("dve", e2, kk[:], aT[:, cc, :], ALU.mult, [kk.b, aT.b], [tmpb])
            p.tt("dve", BK[:, cc, :, 0, :], e2.rearrange("p (n d) -> p n d", d=64), e1.rearrange("p (n d) -> p n d", d=64), ALU.mult, [tmpb], [BK.b])
            p.ts("dve", e2, aT[:, cc, :], tmc[:, j, 3, cc:cc + 1], self.tmk1[:, j, cc:cc + 1], ALU.mult, ALU.add, [aT.b, tmc.b, self.tmk1.b], [tmpb])
            p.tt("dve", kT[:, cc, :], kT[:, cc, :], e2, ALU.mult, [kT.b, tmpb], [kT.b])
            p.tt("dve", BK[:, cc, :, 1, :], kT[:, cc, :].rearrange("p (n d) -> p n d", d=64), e1.rearrange("p (n d) -> p n d", d=64), ALU.mult,
                 [kT.b, tmpb], [BK.b])
            p.stt("dve", e2, rT[:, cc, :], tmc[:, j, 4, cc:cc + 1], kT[:, cc, :], ALU.mult, ALU.mult, [rT.b, tmc.b, kT.b], [tmpb])
            pbn = self.next_pp()
            p.mm(pbn[:], cst[:, C_BLK:C_BLK + 128], e2, True, True, [cst.b, tmpb], [pbn.b])
            p.tt("dve", bonT[:, cc, :], pbn[:], vT[:, cc, :], ALU.mult, [pbn.b, vT.b], [bonT.b])
        if self.rw_stop == 2:
            for cc in range(4):
                p.ms("dve", self.yT[:, cc, :], 0.0, [self.yT.bufs[cc]])
            return
        vb = Tile(self.ctm[:].rearrange("p g e t -> p (g e) t"))
        vb.bufs = self.ctm.bufs
        p.cp("act", vb[:], vT[:], [vT.b], [vb.b])
        for c in range(8):
            tsl = slice(c * 64, (c + 1) * 64)
            for ii, (dst, getsrc, srct) in enumerate(((Vt, lambda cc: vb[:, cc, tsl], vb), (Bt, lambda cc: BK[:, cc, c, 0, :], BK), (Kt, lambda cc: BK[:, cc, c, 1, :], BK))):
                if self.rw_stop == 31 and ii > 0:
                    continue
                if self.rw_stop == 33:
                    continue
                if self.rw_stop == 32 and ii != 1:
                    continue
                pt = self.next_pp()
                for cc in range(4):
                    p.mm(pt[0:64, cc * 128:(cc + 1) * 128], getsrc(cc), self.ident_b, True, True, [srct.b, self.cstb.b], [pt.b])
                if c % 2 == 0:
                    p.cp("act", dst[0:64, c, :], pt[0:64, :], [pt.b], [dst.b] if dst is not self.sq else list(self.sq.bufs))
                else:
                    p.cp("dve", dst[0:64, c, :], pt[0:64, :], [pt.b], [dst.b] if dst is not self.sq else list(self.sq.bufs))
        if self.rw_stop in (3, 31, 32, 33):
            for cc in range(4):
                p.ms("dve", self.yT[:, cc, :], 0.0, [self.yT.bufs[cc]])
            return
        Btb = list(self.sq.bufs)
        p.cp("act", self.Hb[:], self.Hs[:, j, :, :], [self.Hs.b], [self.Hb.b])
        msk_su = cst[0:64, C_SU:C_SU + 64]
        msk_iu = cst[0:64, C_IU:C_IU + 64]
        msk_sl = cst[0:64, C_SL:C_SL + 64]
        idn = cst[0:64, 0:64]
        bc = lambda ap, n: ap.unsqueeze(1).to_broadcast([64, n, 64])
        npp = self.next_pp6
        SrbP = [Tile(self.Sb[:, 0, :, :]), Tile(self.Sb[:, 1, :, :])]
        SrbP[0].bufs = [self.Sb.bufs[0]]
        SrbP[1].bufs = [self.Sb.bufs[1]]
        SkP = [self.MT, Tile(self.Dall[:].rearrange("p h d -> p (h d)").bitcast(BF16)[:, 1024:2048].rearrange("p (h d) -> p h d", h=8))]
        TtP = [Tile(self.cbt[:].rearrange("p g n -> p (g n)").bitcast(BF16)[:, 0:512].rearrange("p (h d) -> p h d", h=8)),
               Tile(self.dec[:].rearrange("p h d -> p (h d)").bitcast(BF16)[:, 1024:1536].rearrange("p (h d) -> p h d", h=8))]
        TtP[0].bufs = self.cbt.bufs

        def prep_units(c):
            Srb, Sk, Tt = SrbP[c % 2], SkP[c % 2], TtP[c % 2]
            W0, NT0 = Wp[0], NTp[0]
            units = []

            def scores(g):
                hs = slice(g * 4, (g + 1) * 4)
                rows = slice(g * 64, (g + 1) * 64)
                Pb, Pk, Pa = npp(), npp(), npp()
                for q in range(4):
                    ar = AR[rows, q, c, :, :].rearrange("p q d -> p (q d)")
                    p.mm(Pb[0:64, q * 128:(q + 1) * 128], BK[rows, q, c, 0, :], ar, True, True, [BK.b, AR.b], [Pb.b])
                    p.mm(Pk[0:64, q * 128:(q + 1) * 128], BK[rows, q, c, 1, :], ar, True, True, [BK.b, AR.b], [Pk.b])
                    p.mm(Pa[0:64, q * 64:(q + 1) * 64], AR[rows, q, c, 0, :], BK[rows, q, c, 0, :], True, True, [BK.b, AR.b], [Pa.b])
                Pb3 = Pb[0:64, :].rearrange("p (h d) -> p h d", h=4)
                Pk3 = Pk[0:64, :].rearrange("p (h d) -> p h d", h=4)
                p.tt("dve", W0[0:64, hs, 0:64], Pb3[:, :, 0:64], bc(msk_su, 4), ALU.mult, [Pb.b, cst.b], [W0.b])
                p.tt("dve", Srb[0:64, hs, :], Pb3[:, :, 64:128], bc(msk_iu, 4), ALU.mult, [Pb.b, cst.b], [Srb.b])
                p.tt("dve", Sk[0:64, hs, 0:64], Pk3[:, :, 0:64], bc(msk_su, 4), ALU.mult, [Pk.b, cst.b], [Sk.b])
                p.tt("dve", Sk[0:64, hs, 64:128], Pk3[:, :, 64:128], bc(msk_iu, 4), ALU.mult, [Pk.b, cst.b], [Sk.b])
                p.tt("dve", NT0[0:64, hs, :], Pa[0:64, 0:256].rearrange("p (h d) -> p h d", h=4), bc(msk_sl, 4), ALU.mult, [Pa.b, cst.b], [NT0.b])

            def level(lvl, g):
                cur = lvl % 2
                Wc, NTc = Wp[cur], NTp[cur]
                Wn, NTn = Wp[1 - cur], NTp[1 - cur]
                last = (lvl == 5)
                hs = slice(g * 4, (g + 1) * 4)
                P1 = npp()
                if not last:
                    P2 = npp()
                for q in range(4):
                    h = g * 4 + q
                    if lvl == 0:
                        p.mm(P1[0:64, q * 128:q * 128 + 64], NTc[0:64, h, :], Wc[0:64, h, 0:64], True, True, [NTc.b, Wc.b], [P1.b])
                    elif not last:
                        p.mm(P1[0:64, q * 128:(q + 1) * 128], NTc[0:64, h, :], Wc[0:64, h, :], True, True, [NTc.b, Wc.b], [P1.b])
                    else:
                        p.mm(P1[0:64, q * 128 + 64:(q + 1) * 128], NTc[0:64, h, :], Wc[0:64, h, 64:128], True, True, [NTc.b, Wc.b], [P1.b])
                    if not last:
                        p.mm(P2[0:64, q * 64:(q + 1) * 64], Wc[0:64, h, 0:64], NTc[0:64, h, :], True, True, [NTc.b, Wc.b], [P2.b])
                P13 = P1[0:64, :].rearrange("p (h d) -> p h d", h=4)
                if not last:
                    p.cp("act", Wn[0:64, hs, 0:64], P13[:, :, 0:64], [P1.b], [Wn.b])
                    p.cp("act", NTn[0:64, hs, :], P2[0:64, 0:256].rearrange("p (h d) -> p h d", h=4), [P2.b], [NTn.b])
                if lvl == 0:
                    p.tt("dve", Wn[0:64, hs, 64:128], Wc[0:64, hs, 0:64], bc(idn, 4), ALU.add, [Wc.b, cst.b], [Wn.b])
                elif not last:
                    p.tt("dve", Wn[0:64, hs, 64:128], P13[:, :, 64:128], Wc[0:64, hs, 64:128], ALU.add, [P1.b, Wc.b], [Wn.b])
                else:
                    p.tt("dve", Tt[0:64, hs, :], P13[:, :, 64:128], Wc[0:64, hs, 64:128], ALU.add, [P1.b, Wc.b], [Tt.b])

            for g in range(2):
                units.append(lambda g=g: scores(g))
            for lvl in range(6):
                for g in range(2):
                    units.append(lambda lvl=lvl, g=g: level(lvl, g))
            return units

        def chain_units(c):
            Srb, Sk, Tt = SrbP[c % 2], SkP[c % 2], TtP[c % 2]
            st = {}

            def hterm(PS, qd):
                for q in range(4):
                    p.mm(PS[0:64, q * 64:(q + 1) * 64], AR[64:128, q, c, qd, :], self.Hb[64:128, q, :], True, True, [AR.b, self.Hb.b], [PS.b])

            def ux():
                PX, PXo = npp(), npp()
                hterm(PXo, 0)
                for hh in range(8):
                    g, q = hh // 4, hh % 4
                    h = 2 * q + g
                    o = PX[0:64, hh * 64:(hh + 1) * 64]
                    if g == 0:
                        p.mm(o, AR[0:64, q, c, 0, :], self.Hb[0:64, q, :], True, False, [AR.b, self.Hb.b], [PX.b])
                    p.mm(o, Sk[0:64, hh, 0:64], Vt[0:64, c, h * 64:(h + 1) * 64], g == 1, True, [Sk.b, Vt.b], [PX.b])
                p.cp("act", Xs[0:64, 0:4, :], PX[0:64, 0:256].rearrange("p (h d) -> p h d", h=4), [PX.b], [Xs.b])
                p.cp("act", Xs[0:64, 4:8, :], PXo[0:64, 0:256].rearrange("p (h d) -> p h d", h=4), [PXo.b], [Xs.b])
                p.tt("dve", Xs[0:64, 4:8, :], Xs[0:64, 4:8, :], PX[0:64, 256:512].rearrange("p (h d) -> p h d", h=4), ALU.add, [Xs.b, PX.b], [Xs.b])

            def uu():
                PU = npp()
                for hh in range(8):
                    p.mm(PU[0:64, hh * 64:(hh + 1) * 64], Tt[0:64, hh, :], Xs[0:64, hh, :], True, True, [Tt.b, Xs.b], [PU.b])
                p.cp("act", Us[0:64, :, :].rearrange("p (q g) d -> p g q d", g=2), PU[0:64, :].rearrange("p (g q d) -> p g q d", g=2, q=4), [PU.b], [Us.b])

            def uy():
                PY, PYo = npp(), npp()
                hterm(PYo, 1)
                for hh in range(8):
                    g, q = hh // 4, hh % 4
                    h = 2 * q + g
                    o = PY[0:64, hh * 64:(hh + 1) * 64]
                    if g == 0:
                        p.mm(o, AR[0:64, q, c, 1, :], self.Hb[0:64, q, :], True, False, [AR.b, self.Hb.b], [PY.b])
                    p.mm(o, Srb[0:64, hh, :], Us[0:64, h, :], g == 1, False, [Srb.b, Us.b], [PY.b])
                    p.mm(o, Sk[0:64, hh, 64:128], Vt[0:64, c, h * 64:(h + 1) * 64], False, True, [Sk.b, Vt.b], [PY.b])
                Yc = Ytok[0:64, c, :].rearrange("p (q g d) -> p g q d", g=2, d=64)
                p.cp("dve", Yc, PY[0:64, :].rearrange("p (g q d) -> p g q d", g=2, q=4), [PY.b], [Ytok.b])
                p.tt("dve", Yc[:, 1, :, :], Yc[:, 1, :, :], PYo[0:64, 0:256].rearrange("p (q d) -> p q d", q=4), ALU.add, [Ytok.b, PYo.b], [Ytok.b])

            def uh():
                PH = npp()
                for cc in range(4):
                    o = PH[:, cc * 128:(cc + 1) * 128]
                    p.mm(o, Bt[0:64, c, cc * 128:(cc + 1) * 128], Us[0:64, 2 * cc:2 * cc + 2, :].rearrange("p h d -> p (h d)"), True, False, Btb + [Us.b], [PH.b])
                    p.mm(o, Kt[0:64, c, cc * 128:(cc + 1) * 128], Vt[0:64, c, cc * 128:(cc + 1) * 128], False, True, [Kt.b, Vt.b], [PH.b])
                PH3 = PH[:].rearrange("p (c d) -> p c d", c=4)
                for hl in range(2):
                    rws = slice(hl * 64, (hl + 1) * 64)
                    p.tt("dve", self.Hs[rws, j, :, :], self.Hs[rws, j, :, :], PH3[rws, :, hl * 64:(hl + 1) * 64], ALU.add, [self.Hs.b, PH.b], [self.Hs.b])
                    p.tt("dve", self.Hs[rws, j, :, :], self.Hs[rws, j, :, :], self.gl[rws, :, c].unsqueeze(2).to_broadcast([64, 4, 64]), ALU.mult,
                         [self.Hs.b, self.gl.b], [self.Hs.b])
                p.cp("act", self.Hb[:], self.Hs[:, j, :, :], [self.Hs.b], [self.Hb.b])

            return [ux, uu, uy, uh]

        prev_chain = []
        for c in range(9):
            pu = prep_units(c) if c < 8 else []
            cu = prev_chain
            i_ = j_ = 0
            while i_ < len(pu) or j_ < len(cu):
                for _ in range(4):
                    if i_ < len(pu):
                        pu[i_]()
                        i_ += 1
                if j_ < len(cu):
                    cu[j_]()
                    j_ += 1
            prev_chain = chain_units(c) if c < 8 else []
        if self.rw_stop in (4, 5, 6):
            for cc in range(4):
                p.ms("dve", self.yT[:, cc, :], 0.0, [self.yT.bufs[cc]])
            return
        Y3 = Ytok[0:64, :, :].rearrange("p c (h d) -> p (c h) d", d=64)
        st = Tile(self.uext[:, 3, 0:512])
        st.bufs = self.uext.bufs
        s1 = st[0:64, 0:64]
        s2 = st[0:64, 64:128]
        s3 = st[0:64, 128:192]
        p.op("dve", lambda e: e.reduce_sum(out=s1, in_=Y3, axis=mybir.AxisListType.X), [Ytok.b], [st.b])
        sqt = Tile(self.F[5][:].rearrange("p c t -> p (c t)")[:, 0:2048].rearrange("p (a b) -> p a b", b=64))
        sqt.bufs = self.F[5].bufs
        for hf in range(2):
            ysl = Y3[:, hf * 32:(hf + 1) * 32, :]
            p.tt("dve", sqt[0:64, :, :], ysl, ysl, ALU.mult, [Ytok.b], [sqt.b])
            p.op("dve", lambda e, hf=hf: e.reduce_sum(out=s2[:, hf * 32:(hf + 1) * 32], in_=sqt[0:64, :, :], axis=mybir.AxisListType.X), [sqt.b], [st.b])
        p.ts("dve", s1, s1, 1.0 / 64, None, ALU.mult, None, [st.b], [st.b])
        p.tt("dve", s3, s1, s1, ALU.mult, [st.b], [st.b])
        p.stt("dve", s2, s2, 1.0 / 64, s3, ALU.mult, ALU.subtract, [st.b], [st.b])
        p.ts("dve", s2, s2, 64e-5, None, ALU.add, None, [st.b], [st.b])
        p.op("act", lambda e: e.activation(out=s2, in_=s2, func=AF.Ln), [st.b], [st.b])
        p.op("act", lambda e: e.activation(out=s2, in_=s2, func=AF.Exp, scale=-0.5), [st.b], [st.b])
        p.tt("dve", Y3, Y3, s1.unsqueeze(2).to_broadcast([64, 64, 64]), ALU.subtract, [Ytok.b, st.b], [Ytok.b])
        p.tt("dve", Y3, Y3, s2.unsqueeze(2).to_broadcast([64, 64, 64]), ALU.mult, [Ytok.b, st.b], [Ytok.b])
        for cc in range(4):
            py = self.next_pp()
            for c in range(8):
                p.tr(py[:, c * 64:(c + 1) * 64], Ytok[0:64, c, cc * 128:(cc + 1) * 128], cst[0:64, 0:64], [Ytok.b, cst.b], [py.b])
            p.act(e1, py[:], AF.Identity, [py.b, tmc.b], [tmpb], scale=tmc[:, j, 5, cc:cc + 1], bias=tmc[:, j, 6, cc:cc + 1])
            p.tt("dve", e1, e1, bonT[:, cc, :], ALU.add, [tmpb, bonT.b], [tmpb])
            p.tt("dve", self.yT[:, cc, :], e1, sgT[:, cc, :], ALU.mult, [tmpb, sgT.b], [self.yT.bufs[cc]])

    def build(self):
        if self.prefetch and self.wseq is None and not getattr(self, "_recording", False):
            rec = Builder(self.NL, self.NT, self.mixers, False)
            rec._recording = True
            rec.rw_stop = self.rw_stop
            rec.build()
            self.wseq = rec.wrec
        p = self.p
        self.declare()
        self.alloc()
        self.prologue()
        xr = self.x.rearrange("(t s p) d -> t p s d", p=128, s=4)
        orr = self.out.rearrange("(t s p) d -> t p s d", p=128, s=4)
        for t in range(self.NT):
            p.dma("sp", self.xs[0][:], xr[t], (), [self.xs[0].b])
            self.load_tile(t)
            for L in range(self.NL):
                if L % 2 == 0:
                    self.even_layer(L)
                else:
                    self.odd_layer(L)
            self.store_tile(t, orr)
        p.finish()
        return self.nc


def make_in_maps(inputs):
    consts = make_consts()
    maps = []
    for core in range(8):
        b = core % 4
        m = {}
        for k, v in inputs.items():
            v = np.asarray(v)
            if k == "x":
                m[k] = np.ascontiguousarray(v[b])
            elif k == "c":
                m[k] = np.ascontiguousarray(v[b].reshape(8, 128))
            elif k in ("tm_k_k", "tm_k_a", "tm_r_k"):
                m[k] = np.ascontiguousarray(v.reshape(2, 512))
            else:
                m[k] = np.ascontiguousarray(v)
        m["consts"] = consts
        maps.append(m)
    return maps


def kernel(**inputs):
    import os
    bld = Builder(mixers=os.environ.get("K_MIXERS", "ABCD"))
    bld.rw_stop = int(os.environ.get("K_RWSTOP", "0"))
    nc = bld.build()
    maps = make_in_maps(inputs)
    res = run_bass_kernel_spmd(nc, maps, core_ids=list(range(8)))
    out = np.stack([res.results[b]["out"] for b in range(4)], axis=0)
    return out.astype(np.float32)
```

```python
import numpy as np
import concourse.bass as bass
import concourse.mybir as mybir
from concourse.bass_utils import run_bass_kernel_spmd

F32 = mybir.dt.float32
BF16 = mybir.dt.bfloat16
AF = mybir.ActivationFunctionType
ALU = mybir.AluOpType

D = 1024
SEQ = 8192
TT = 512
NDS = 24
EVEN_COLS = 4224
ODD_COLS = 3080


class Buf:
    __slots__ = ("w", "r")

    def __init__(self):
        self.w = {}
        self.r = {}


class Tile:
    def __init__(self, h, nb=1):
        self.h = h
        self.bufs = [Buf() for _ in range(nb)]

    def __getitem__(self, k):
        return self.h[k]

    @property
    def b(self):
        return self.bufs[0]


class Prog:
    def __init__(self, nc):
        self.nc = nc
        self.eng = {"pe": nc.tensor, "act": nc.scalar, "dve": nc.vector, "pool": nc.gpsimd, "sp": nc.sync}
        self.sem = {}
        self.cnt = {}
        self.cur = {}
        self.qeng = {}
        self.epochs = {}
        for q in ("pe", "act", "dve", "pool"):
            self._new_epoch(q)
        for i in range(NDS):
            self.sem["d%d" % i] = nc.alloc_semaphore("d%d" % i)
            self.cnt["d%d" % i] = 0
        self.dnext = 0
        self.known = {e: {} for e in self.eng}
        self.snap = {}
        self.nwait = 0
        self.nins = 0
        import os
        self.raw_only = os.environ.get("K_RAWONLY", "0") == "1"

    SEM_LIMIT = 30000

    def _new_epoch(self, e):
        k = self.epochs.get(e, 0)
        self.epochs[e] = k + 1
        key = "%s#%d" % (e, k)
        self.sem[key] = self.nc.alloc_semaphore("s_%s_%d" % (e, k))
        self.cnt[key] = 0
        self.cur[e] = key
        self.qeng[key] = e

    def sb(self, name, shape, dt, nb=1):
        return Tile(self.nc.alloc_sbuf_tensor(name, list(shape), dt), nb)

    def ps(self, name, shape, dt=F32, nb=1):
        return Tile(self.nc.alloc_psum_tensor(name, list(shape), dt), nb)

    def _wait(self, e, evs, same_ok=True):
        kn = self.known[e]
        need = {}
        for (q, v) in evs:
            if same_ok and self.qeng.get(q) == e:
                continue
            if kn.get(q, 0) >= v:
                continue
            if need.get(q, 0) < v:
                need[q] = v
        for q, v in need.items():
            if kn.get(q, 0) >= v:
                continue
            self.eng[e].wait_ge(self.sem[q], v)
            self.nwait += 1
            kn[q] = v
            s = self.snap.get((q, v))
            if s:
                for q2, v2 in s.items():
                    if kn.get(q2, 0) < v2:
                        kn[q2] = v2

    def _deps(self, reads, writes):
        ev = []
        for b in reads:
            ev.extend(b.w.items())
        for b in writes:
            ev.extend(b.w.items())
            ev.extend(b.r.items())
        return ev

    def op(self, e, fn, reads=(), writes=()):
        if e != "pe" and self.raw_only:
            raw = []
            for b in reads:
                for q, v in b.w.items():
                    if self.qeng.get(q) == e:
                        raw.append((q, v))
            if raw:
                self._wait(e, raw, same_ok=False)
            self._wait(e, self._deps(reads, writes), same_ok=True)
        else:
            self._wait(e, self._deps(reads, writes), same_ok=(e == "pe"))
        ins = fn(self.eng[e])
        key = self.cur[e]
        self.cnt[key] += 1
        n = self.cnt[key]
        ins.then_inc(self.sem[key], 1)
        self.nins += 1
        self.snap[(key, n)] = dict(self.known[e])
        for b in reads:
            b.r[key] = n
        for b in writes:
            b.w[key] = n
            b.r = {}
        if n >= self.SEM_LIMIT:
            self._new_epoch(e)

    def dma(self, e, out, in_, reads=(), writes=(), **kw):
        i = self.dnext
        self.dnext = (i + 1) % NDS
        q = "d%d" % i
        ev = self._deps(reads, writes)
        if self.cnt[q] > 0:
            ev.append((q, self.cnt[q]))
        self._wait(e, ev, same_ok=False)
        self.eng[e].dma_start(out=out, in_=in_, **kw).then_inc(self.sem[q], 16)
        self.cnt[q] += 16
        n = self.cnt[q]
        self.nins += 1
        self.snap[(q, n)] = dict(self.known[e])
        for b in reads:
            b.r[q] = n
        for b in writes:
            b.w[q] = n
            b.r = {}

    def finish(self):
        self._wait("sp", [(q, v) for q, v in self.cnt.items() if v > 0], same_ok=False)

    def act(self, out, in_, func, reads, writes, **kw):
        self.op("act", lambda e: e.activation(out=out, in_=in_, func=func, **kw), reads, writes)

    def mm(self, out, lhsT, rhs, start, stop, reads, writes):
        self.op("pe", lambda e: e.matmul(out, lhsT=lhsT, rhs=rhs, start=start, stop=stop), reads, writes)

    def tr(self, out, in_, ident, reads, writes):
        self.op("pe", lambda e: e.transpose(out, in_, ident), reads, writes)

    def ts(self, eng, out, in0, s1, s2, op0, op1, reads, writes):
        if s2 is None:
            self.op(eng, lambda e: e.tensor_scalar(out=out, in0=in0, scalar1=s1, scalar2=None, op0=op0), reads, writes)
        else:
            self.op(eng, lambda e: e.tensor_scalar(out=out, in0=in0, scalar1=s1, scalar2=s2, op0=op0, op1=op1), reads, writes)

    def tt(self, eng, out, in0, in1, op, reads, writes):
        self.op(eng, lambda e: e.tensor_tensor(out=out, in0=in0, in1=in1, op=op), reads, writes)

    def stt(self, eng, out, in0, s, in1, op0, op1, reads, writes):
        self.op(eng, lambda e: e.scalar_tensor_tensor(out=out, in0=in0, scalar=s, in1=in1, op0=op0, op1=op1), reads, writes)

    def cp(self, eng, out, in_, reads, writes):
        if eng == "act":
            self.op(eng, lambda e: e.activation(out=out, in_=in_, func=AF.Copy), reads, writes)
        else:
            self.op(eng, lambda e: e.tensor_copy(out=out, in_=in_), reads, writes)

    def rsqrt(self, ap, tile):
        self.op("act", lambda e: e.activation(out=ap, in_=ap, func=AF.Ln), [tile.b], [tile.b])
        self.op("act", lambda e: e.activation(out=ap, in_=ap, func=AF.Exp, scale=-0.5), [tile.b], [tile.b])

    def ms(self, eng, ap, val, writes):
        self.op(eng, lambda e: e.memset(ap, val), (), writes)


CW = 2688
C_ID = 0
C_SU = 128
C_IU = 192
C_SL = 256
C_BLK = 384
C_ONE = 512
C_TRI2 = 640
C_MB2 = 768
C_SEL0 = 896
C_SEL1 = 1024
C_RST = 1152
CWF = 1664
C_PAR0 = 1664
C_PAR1 = 2176
CB_PAR0 = 640
CB_PAR1 = 1152


def make_consts():
    c = np.zeros((128, CW), np.float32)
    c[:, 0:128] = np.eye(128, dtype=np.float32)
    s = np.arange(64)[:, None]
    t = np.arange(64)[None, :]
    c[0:64, C_SU:C_SU + 64] = (s < t)
    c[0:64, C_IU:C_IU + 64] = (s <= t)
    c[0:64, C_SL:C_SL + 64] = (s > t)
    c[0:64, C_BLK:C_BLK + 64] = 1.0
    c[64:128, C_BLK + 64:C_BLK + 128] = 1.0
    c[:, C_ONE:C_ONE + 128] = 1.0
    r = np.arange(128)[:, None]
    q = np.arange(128)[None, :]
    same = (r // 64) == (q // 64)
    c[:, C_TRI2:C_TRI2 + 128] = (same & (r <= q))
    c[:, C_MB2:C_MB2 + 128] = np.where(same & (r <= q), 0.0, -1.0e5)
    c[0:64, C_SEL0:C_SEL0 + 128] = 1.0
    c[64:128, C_SEL1:C_SEL1 + 128] = 1.0
    tk = np.arange(512)
    c[:, C_PAR0:C_PAR0 + 512] = ((tk // 64) % 2 == 0)[None, :]
    c[:, C_PAR1:C_PAR1 + 512] = ((tk // 64) % 2 == 1)[None, :]
    c[:, C_RST:C_RST + 512] = (tk % 64 != 0)[None, :]
    return c


class Builder:
    def __init__(self, n_layers=4, n_tiles=16, mixers="ABCD", dbg=False):
        self.NL = n_layers
        self.NT = n_tiles
        self.mixers = mixers
        self.dbg = dbg
        self.INV_DT = BF16
        self.wseq = None
        self.wrec = []
        self.wptr = 0
        self.wissued = {}
        self.prefetch = True
        self.rw_stop = 0
        self.nc = bass.Bass("TRN2", target_bir_lowering=False)
        self.p = Prog(self.nc)

    def declare(self):
        nc = self.nc
        di = lambda name, shape: nc.dram_tensor(name, list(shape), F32, kind="ExternalInput").ap()
        self.x = di("x", (SEQ, D))
        self.c = di("c", (8, 128))
        self.consts = di("consts", (128, CW))
        self.ada_w = di("ada_w", (4, D, 3 * D))
        self.ada_b = di("ada_b", (4, 3 * D))
        self.norm_pre = di("norm_pre", (4, D))
        self.norm_post = di("norm_post", (4, D))
        self.ev_w_in = di("ev_w_in", (2, D, EVEN_COLS))
        self.ev_w_out = di("ev_w_out", (2, D, D))
        self.tm_mu = di("tm_mu", (2, 2176))
        self.tm_w0 = di("tm_w0", (2, 512))
        self.tm_w2 = di("tm_w2", (2, 64, 512))
        self.tm_a0 = di("tm_a0", (2, 512))
        self.tm_a2 = di("tm_a2", (2, 64, 512))
        self.tm_k_k = di("tm_k_k", (2, 512))
        self.tm_k_a = di("tm_k_a", (2, 512))
        self.tm_r_k = di("tm_r_k", (2, 512))
        self.tm_lnx_g = di("tm_lnx_g", (2, 512))
        self.tm_lnx_b = di("tm_lnx_b", (2, 512))
        self.sc_conv_w = di("sc_conv_w", (2, 3, 512))
        self.od_w_in = di("od_w_in", (2, D, ODD_COLS))
        self.od_w_out = di("od_w_out", (2, D, D))
        self.cf_conv_w = di("cf_conv_w", (2, 31, 512))
        self.cf_conv_b = di("cf_conv_b", (2, 512))
        self.cf_ln_g = di("cf_ln_g", (2, 512))
        self.cf_ln_b = di("cf_ln_b", (2, 512))
        self.ssd_conv_w = di("ssd_conv_w", (2, 4, 1024))
        self.ssd_conv_b = di("ssd_conv_b", (2, 1024))
        self.ssd_dt_bias = di("ssd_dt_bias", (2, 8))
        self.ssd_a_log = di("ssd_a_log", (2, 8))
        self.ssd_d = di("ssd_d", (2, 8))
        self.ssd_norm_g = di("ssd_norm_g", (2, 512))
        self.out = nc.dram_tensor("out", [SEQ, D], F32, kind="ExternalOutput").ap()
        self.wblocks = []
        for L in range(self.NL):
            ncols = EVEN_COLS if L % 2 == 0 else ODD_COLS
            blks = []
            c0 = 0
            bi = 0
            while c0 < ncols + D:
                if c0 < ncols:
                    n = min(512, ncols - c0)
                    src = (self.ev_w_in if L % 2 == 0 else self.od_w_in)[L // 2][:, c0:c0 + n]
                else:
                    n = 512
                    src = (self.ev_w_out if L % 2 == 0 else self.od_w_out)[L // 2][:, c0 - ncols:c0 - ncols + n]
                sc = nc.dram_tensor("wsc_%d_%d" % (L, bi), [128, 8 * 512], BF16).ap()
                blks.append((sc, n, Buf(), src))
                c0 += n
                bi += 1
            self.wblocks.append(blks)

    def alloc(self):
        p = self.p
        self.cst = p.sb("cst", (128, CWF), F32)
        self.cstb = p.sb("cstb", (128, 1664), BF16)
        self.xs = [p.sb("xs0", (128, 4, D), F32)]
        self.xs.append(self.xs[0])
        self.xT = p.sb("xT", (128, 8, TT), F32)
        self.yo = p.sb("yo", (128, 8, TT + 3), F32)
        self.sq = p.sb("sq", (128, 8, TT), BF16, nb=8)
        self.rb = p.sb("rb", (128, TT), F32)
        self.wst = self.yo
        self.g2c = p.sb("g2c", (128, 4, 8), F32)
        self.npo = p.sb("npo", (128, 4, 8), F32)
        self.hT = p.sb("hT", (128, 8, TT), BF16)
        self.yT = p.sb("yT", (128, 8, TT), BF16, nb=8)
        self.wring = [p.sb("wr%d" % i, (128, 8, 512), BF16) for i in range(2)]
        self.wnext = 0
        self.ss = p.sb("ss", (128, 8), F32)
        self.rstd = p.sb("rstd", (128, 8), F32)
        self.g1c = p.sb("g1c", (128, 4, 8), F32)
        self.shc = p.sb("shc", (128, 4, 8), F32)
        self.cT = p.sb("cT", (128, 8), F32)
        self.abc = p.sb("abc", (128, 4, 24), F32)
        self.npc = p.sb("npc", (128, 4, 8), F32)
        self.F = [p.sb("F%d" % i, (128, 4, TT), F32) for i in range(6)]
        self.cfw = p.sb("cfw", (128, 2, 4, 31), F32)
        self.cfp = p.sb("cfp", (128, 2, 3, 4), F32)
        self.cf_tail = p.sb("cf_tail", (128, 2, 4, 30), F32)
        self.uext = p.sb("uext", (128, 4, 30 + TT), F32)
        self.sdw = p.sb("sdw", (128, 2, 8, 4), F32)
        self.sdb = p.sb("sdb", (128, 2, 8), F32)
        self.sd_tail = p.sb("sd_tail", (128, 2, 8, 3), F32)
        self.sdng = p.sb("sdng", (128, 2, 4), F32)
        self.hrow = p.sb("hrow", (128, 2, 3, 8), F32)
        self.xext = self.yo
        self.xbcT = p.sb("xbcT", (128, 8, TT), BF16)
        self.ctm = p.sb("ctm", (128, 2, 2, TT), BF16)
        self.S = p.sb("S", (128, 2, 8, 64), F32)
        self.Sb = p.sb("Sb", (128, 2, 8, 64), BF16, nb=2)
        self.tokx = p.sb("tokx", (128, 512), F32)
        self.xdt = p.sb("xdt", (128, 8, 64), BF16)
        self.xdd = p.sb("xdd", (128, 8, 64), BF16)
        self.btok = p.sb("btok", (128, 256), BF16)
        self.dts = p.sb("dts", (128, 6, 8), F32)
        self.cdb = p.sb("cdb", (128, 2, 8), F32)
        self.Dall = p.sb("Dall", (128, 8, 128), F32)
        self.dec = p.sb("dec", (128, 8, 128), F32)
        self.cbt = p.sb("cbt", (128, 2, 128), F32)
        self.MT = p.sb("MT", (128, 8, 128), BF16)
        self.ytok = p.sb("ytok", (128, 512), F32)

        self.tmp = p.sb("tmp", (128, D), F32)
        self.pp = [p.ps("pp%d" % i, (128, 512), F32) for i in range(4)]
        self.ppn = 0
        self.pt = [Tile(self.nc.alloc_psum_tensor("ptb%d" % i, [128, 1024], BF16)[:, 0:512]) for i in range(2)]
        pcb_ = self.nc.alloc_psum_tensor("pcb", [128, 512], F32)
        self.pcb = Tile(pcb_[:, 0:256].rearrange("p (g n) -> p g n", g=2))
        psm_ = self.nc.alloc_psum_tensor("psm", [128, 512], F32)
        self.psm = Tile(psm_[:, 0:64])
        self.pp6 = list(self.pp) + [Tile(pcb_[:, :]), Tile(psm_[:, :])]
        self.pp6[4].bufs = self.pcb.bufs
        self.pp6[5].bufs = self.psm.bufs
        self.pp6n = 0
        self.ptn = 0
        self.mu = p.sb("mu", (128, 2, 17), F32)
        self.omu = p.sb("omu", (128, 2, 17), F32)
        self.scw = p.sb("scw", (128, 2, 4, 3), F32)
        self.sc_tail = p.sb("sc_tail", (128, 2, 4, 2), F32)
        self.tm_prev = p.sb("tm_prev", (128, 2, 17), F32)
        self.tmc = p.sb("tmc", (128, 2, 7, 4), F32)
        self.tmk1 = p.sb("tmk1", (128, 2, 4), F32)
        self.w2a2 = p.sb("w2a2", (128, 2, 512), BF16)
        self.Hs = p.sb("Hs", (128, 2, 4, 64), F32)
        self.Hb = p.sb("Hb", (128, 4, 64), BF16)
        self.gl = p.sb("gl", (128, 4, 8), F32)

    def next_pp(self):
        t = self.pp[self.ppn]
        self.ppn = (self.ppn + 1) % len(self.pp)
        return t

    def next_pp6(self):
        t = self.pp6[self.pp6n]
        self.pp6n = (self.pp6n + 1) % len(self.pp6)
        return t

    def next_pt(self):
        t = self.pt[self.ptn]
        self.ptn = (self.ptn + 1) % len(self.pt)
        return t

    def dump(self, name, ap, tile, n):
        if not self.dbg:
            return
        p = self.p
        d = self.nc.dram_tensor("dbg_" + name, [128, n], F32, kind="ExternalOutput").ap()
        sc = self.tmp
        p.cp("dve", sc[:, 0:n], ap, list(tile.bufs), [sc.b])
        p.dma("sp", d, sc[:, 0:n], [sc.b], ())

    def prologue(self):
        p = self.p
        nc = self.nc
        p.dma("sp", self.cst[:], self.consts[:, 0:CWF], (), [self.cst.b])
        p.cp("dve", self.cstb[:, 0:640], self.cst[:, 0:640], [self.cst.b], [self.cstb.b])
        p.dma("sp", self.tmp[:], self.consts[:, C_PAR0:C_PAR0 + 1024], (), [self.tmp.b])
        p.cp("dve", self.cstb[:, 640:1664], self.tmp[:], [self.tmp.b], [self.cstb.b])
        self.ident_b = self.cstb[:, 0:128]
        self.ones_b = self.cstb[:, C_ONE:C_ONE + 128]
        self.ident_f = self.cst[:, 0:128]
        k = 0
        for L in range(self.NL):
            for (sc, n, buf, src) in self.wblocks[L]:
                p.dma("sp", self.wst[:, :, 0:n], src.rearrange("(kc p) n -> p kc n", p=128), (), [self.wst.b])
                wr = self.wring[k % 2]
                eng = ("dve", "act", "pool")[k % 3]
                if eng == "act":
                    p.act(wr[:, :, 0:n], self.wst[:, :, 0:n], AF.Copy, [self.wst.b], [wr.b])
                else:
                    p.cp(eng, wr[:, :, 0:n], self.wst[:, :, 0:n], [self.wst.b], [wr.b])
                p.dma("act", sc.rearrange("p (kc n) -> p kc n", kc=8)[:, :, 0:n], wr[:, :, 0:n], [wr.b], [buf])
                k += 1
        stg = self.tmp

        def cols(src_rows, R, W, dst_fn, dst_tile):
            for w0 in range(0, W, 1024):
                wn = min(1024, W - w0)
                p.dma("sp", stg[0:R, 0:wn], src_rows[:, w0:w0 + wn], (), [stg.b])
                for c_ in range(wn // 128):
                    ps = self.next_pp()
                    p.tr(ps[:, 0:R], stg[0:R, c_ * 128:(c_ + 1) * 128], self.cst[0:R, 0:R], [stg.b, self.cst.b], [ps.b])
                    p.cp("dve", dst_fn(w0 // 128 + c_), ps[:, 0:R], [ps.b], [dst_tile.b])

        cols(self.c, 8, 128, lambda cc: self.cT[:, :], self.cT)
        p.act(self.cT[:], self.cT[:], AF.Silu, [self.cT.b], [self.cT.b])
        cols(self.ada_b, 4, 3072, lambda cc: self.abc[:, :, cc], self.abc)
        cols(self.norm_pre, 4, 1024, lambda cc: self.npc[:, :, cc], self.npc)
        cols(self.norm_post, 4, 1024, lambda cc: self.npo[:, :, cc], self.npo)
        cols(self.tm_mu, 2, 2176, lambda cc: self.mu[:, :, cc], self.mu)
        p.ts("dve", self.omu[:], self.mu[:], -1.0, 1.0, ALU.mult, ALU.add, [self.mu.b], [self.omu.b])
        for l in range(2):
            for i_, src in enumerate((self.tm_w0, self.tm_a0, self.tm_k_k, self.tm_k_a, self.tm_r_k, self.tm_lnx_g, self.tm_lnx_b)):
                cols(src[l:l + 1, :], 1, 512, lambda cc, l=l, i_=i_: self.tmc[:, l, i_, cc:cc + 1], self.tmc)
            p.ts("dve", self.tmk1[:, l, :], self.tmc[:, l, 3, :], -1.0, 1.0, ALU.mult, ALU.add, [self.tmc.b], [self.tmk1.b])
            p.dma("sp", stg[0:64, 0:512], self.tm_w2[l], (), [stg.b])
            p.dma("sp", stg[64:128, 0:512], self.tm_a2[l], (), [stg.b])
            p.cp("dve", self.w2a2[:, l, :], stg[:, 0:512], [stg.b], [self.w2a2.b])
            cols(self.sc_conv_w[l], 3, 512, lambda cc, l=l: self.scw[:, l, cc, :], self.scw)
            cols(self.cf_conv_w[l], 31, 512, lambda cc, l=l: self.cfw[:, l, cc, :], self.cfw)
            for i_, src in enumerate((self.cf_conv_b, self.cf_ln_g, self.cf_ln_b)):
                cols(src[l:l + 1, :], 1, 512, lambda cc, l=l, i_=i_: self.cfp[:, l, i_, cc:cc + 1], self.cfp)
            cols(self.ssd_conv_w[l], 4, 1024, lambda cc, l=l: self.sdw[:, l, cc, :], self.sdw)
            cols(self.ssd_conv_b[l:l + 1, :], 1, 1024, lambda cc, l=l: self.sdb[:, l, cc:cc + 1], self.sdb)
            cols(self.ssd_norm_g[l:l + 1, :], 1, 512, lambda cc, l=l: self.sdng[:, l, cc:cc + 1], self.sdng)
            for i_, src in enumerate((self.ssd_dt_bias, self.ssd_a_log, self.ssd_d)):
                p.dma("sp", stg[0:1, 0:8], src[l:l + 1, :], (), [stg.b])
                ps = self.next_pp()
                p.mm(ps[:, 0:8], self.cst[0:1, C_ONE:C_ONE + 128], stg[0:1, 0:8], True, True, [self.cst.b, stg.b], [ps.b])
                p.cp("dve", self.hrow[:, l, i_, :], ps[:, 0:8], [ps.b], [self.hrow.b])
            p.act(self.hrow[:, l, 1, :], self.hrow[:, l, 1, :], AF.Exp, [self.hrow.b], [self.hrow.b])
            p.ts("dve", self.hrow[:, l, 1, :], self.hrow[:, l, 1, :], -1.0, None, ALU.mult, None, [self.hrow.b], [self.hrow.b])
        p.ms("dve", self.Hs[:], 0.0, [self.Hs.b])
        p.ms("dve", self.cf_tail[:], 0.0, [self.cf_tail.b])
        p.ms("dve", self.sd_tail[:], 0.0, [self.sd_tail.b])
        p.ms("dve", self.S[:], 0.0, [self.S.b])
        p.ms("dve", self.sc_tail[:], 0.0, [self.sc_tail.b])
        p.ms("dve", self.tm_prev[:], 0.0, [self.tm_prev.b])
        for L in range(self.NL):
            aw = self.ada_w[L]
            pcol = self.next_pp()
            for blk in range(4):
                p.dma("sp", self.wst[:, :, 0:512], aw[:, blk * 512:(blk + 1) * 512].rearrange("(kc p) n -> p kc n", p=128), (), [self.wst.b])
                for jj in range(4):
                    j = blk * 4 + jj
                    for kc in range(8):
                        p.mm(pcol[:, j:j + 1], self.wst[:, kc, jj * 128:(jj + 1) * 128], self.cT[:, kc:kc + 1], kc == 0, kc == 7,
                             [self.wst.b, self.cT.b], [pcol.b])
            p.tt("dve", self.shc[:, L, :], pcol[:, 0:8], self.abc[:, L, 0:8], ALU.add, [pcol.b, self.abc.b], [self.shc.b])
            p.tt("dve", self.g1c[:, L, :], pcol[:, 8:16], self.abc[:, L, 8:16], ALU.add, [pcol.b, self.abc.b], [self.g1c.b])
            p.stt("dve", self.g1c[:, L, :], self.g1c[:, L, :], 1.0, self.npc[:, L, :], ALU.add, ALU.mult, [self.g1c.b, self.npc.b], [self.g1c.b])
            pcol2 = self.next_pp()
            for blk in range(2):
                p.dma("sp", self.wst[:, :, 0:512], aw[:, 2 * D + blk * 512:2 * D + (blk + 1) * 512].rearrange("(kc p) n -> p kc n", p=128), (), [self.wst.b])
                for jj in range(4):
                    j = blk * 4 + jj
                    for kc in range(8):
                        p.mm(pcol2[:, j:j + 1], self.wst[:, kc, jj * 128:(jj + 1) * 128], self.cT[:, kc:kc + 1], kc == 0, kc == 7,
                             [self.wst.b, self.cT.b], [pcol2.b])
            p.tt("dve", self.g2c[:, L, :], pcol2[:, 0:8], self.abc[:, L, 16:24], ALU.add, [pcol2.b, self.abc.b], [self.g2c.b])
            p.tt("dve", self.g2c[:, L, :], self.g2c[:, L, :], self.npo[:, L, :], ALU.mult, [self.g2c.b, self.npo.b], [self.g2c.b])

    def _issue_w(self, k):
        L, bi = self.wseq[k]
        sc, n, buf, _ = self.wblocks[L][bi]
        wr = self.wring[k % len(self.wring)]
        self.p.dma("sp", wr[:, :, 0:n], sc.rearrange("p (kc n) -> p kc n", kc=8)[:, :, 0:n], [buf], [wr.b])
        self.wissued[k] = wr

    def load_w(self, L, bi):
        if self.wseq is None:
            self.wrec.append((L, bi))
            sc, n, buf, _ = self.wblocks[L][bi]
            wr = self.wring[self.wnext]
            self.wnext = (self.wnext + 1) % len(self.wring)
            self.p.dma("sp", wr[:, :, 0:n], sc.rearrange("p (kc n) -> p kc n", kc=8)[:, :, 0:n], [buf], [wr.b])
            return wr
        k = self.wptr
        assert self.wseq[k] == (L, bi), (k, self.wseq[k], (L, bi))
        if k not in self.wissued:
            self._issue_w(k)
        wr = self.wissued.pop(k)
        self.wptr += 1
        if self.wptr < len(self.wseq):
            self._issue_w(self.wptr)
        return wr

    def load_tile(self, t):
        p = self.p
        xs = self.xs[t % 2]
        for kc in range(8):
            ps = self.next_pp()
            for s in range(4):
                p.tr(ps[:, s * 128:(s + 1) * 128], xs[:, s, kc * 128:(kc + 1) * 128], self.ident_f, [xs.b, self.cst.b], [ps.b])
            if kc % 2 == 0:
                p.act(self.xT[:, kc, :], ps[:], AF.Copy, [ps.b], [self.xT.b])
            else:
                p.cp("dve", self.xT[:, kc, :], ps[:], [ps.b], [self.xT.b])

    def store_tile(self, t, orr):
        p = self.p
        xs = self.xs[t % 2]
        for s in range(4):
            for half in range(2):
                ps = self.next_pp()
                for q in range(4):
                    kc = half * 4 + q
                    p.tr(ps[:, q * 128:(q + 1) * 128], self.xT[:, kc, s * 128:(s + 1) * 128], self.ident_f, [self.xT.b, self.cst.b], [ps.b])
                if half == 0:
                    p.act(xs[:, s, 0:512], ps[:], AF.Copy, [ps.b], [xs.b])
                else:
                    p.cp("dve", xs[:, s, 512:1024], ps[:], [ps.b], [xs.b])
        p.dma("sp", orr[t], xs[:], [xs.b], ())

    def rms_bcast(self, src):
        p = self.p
        for kc in range(8):
            p.act(self.sq[:, kc, :], src[:, kc, 0:TT], AF.Square, [src.b], [self.sq.bufs[kc]])
        ps = self.next_pp()
        for kc in range(8):
            p.mm(ps[:], self.ones_b, self.sq[:, kc, :], kc == 0, kc == 7, [self.cstb.b, self.sq.bufs[kc]], [ps.b])
        p.ts("dve", self.rb[:], ps[:], 1.0 / D, 1e-6, ALU.mult, ALU.add, [ps.b], [self.rb.b])
        p.op("act", lambda e: e.activation(out=self.rb[:], in_=self.rb[:], func=AF.Ln), [self.rb.b], [self.rb.b])
        p.op("act", lambda e: e.activation(out=self.rb[:], in_=self.rb[:], func=AF.Exp, scale=-0.5), [self.rb.b], [self.rb.b])

    def pre_norm(self, L):
        p = self.p
        if L == 0 and not getattr(self, "_d0", False):
            self.dump("xT0", self.xT[:, 0, :], self.xT, 512)
            self.dump("g1c", self.g1c[:, 0, :], self.g1c, 8)
            self.dump("shc", self.shc[:, 0, :], self.shc, 8)
            self.dump("cT", self.cT[:], self.cT, 8)
            self.dump("npc", self.npc[:, 0, :], self.npc, 8)
            self.dump("abc", self.abc[:, 0, :], self.abc, 24)
            self.dump("xT7", self.xT[:, 7, :], self.xT, 512)
        self.rms_bcast(self.xT)
        if L == 0 and not getattr(self, "_d0", False):
            self._d0 = True
            self._d0b = True
            self.dump("rb0", self.rb[:], self.rb, 512)
            self.dump("sq0", self.sq[:, 0, :], self.sq, 512)
        for kc in range(8):
            p.tt("dve", self.tmp[:, 0:512], self.xT[:, kc, :], self.rb[:], ALU.mult, [self.xT.b, self.rb.b], [self.tmp.b])
            p.ts("dve", self.hT[:, kc, :], self.tmp[:, 0:512], self.g1c[:, L, kc:kc + 1], self.shc[:, L, kc:kc + 1], ALU.mult, ALU.add,
                 [self.tmp.b, self.g1c.b, self.shc.b], [self.hT.b])
        if getattr(self, "_d0b", False):
            self._d0b = False
            self.dump("hT", self.hT[:, 0, :], self.hT, 512)
            self.dump("hT7", self.hT[:, 7, :], self.hT, 512)

    def proj_fm(self, wr, lc, ps):
        p = self.p
        for kc in range(8):
            p.mm(ps[:], wr[:, kc, lc * 128:(lc + 1) * 128], self.hT[:, kc, :], kc == 0, kc == 7, [wr.b, self.hT.b], [ps.b])

    def out_proj_post(self, L):
        p = self.p
        nb = len(self.wblocks[L])
        allY = self.yT.bufs
        for half in range(2):
            wr = self.load_w(L, nb - 2 + half)
            for q in range(4):
                dmc = half * 4 + q
                ps = self.next_pp()
                for cc in range(8):
                    p.mm(ps[:], wr[:, cc, q * 128:(q + 1) * 128], self.yT[:, cc, :], cc == 0, cc == 7, [wr.b] + allY, [ps.b])
                p.act(self.yo[:, dmc, 0:TT], ps[:], AF.Copy, [ps.b], [self.yo.b])
        if not getattr(self, "_d1", False):
            self._d1 = True
            self.dump("yo0", self.yo[:, 0, 0:TT], self.yo, 512)
            self.dump("yo7", self.yo[:, 7, 0:TT], self.yo, 512)
        self.rms_bcast(self.yo)
        for dmc in range(8):
            p.stt("dve", self.tmp[:, 0:512], self.yo[:, dmc, 0:TT], self.g2c[:, L, dmc:dmc + 1], self.rb[:], ALU.mult, ALU.mult,
                  [self.yo.b, self.g2c.b, self.rb.b], [self.tmp.b])
            p.tt("dve", self.xT[:, dmc, :], self.xT[:, dmc, :], self.tmp[:, 0:512], ALU.add, [self.xT.b, self.tmp.b], [self.xT.b])

    def even_layer(self, L):
        p = self.p
        j = L // 2
        self.pre_norm(L)
        wrs = {}
        loaded = {}

        def get(ch):
            bi = ch // 4
            if bi not in loaded:
                loaded[bi] = self.load_w(L, bi)
            return (loaded[bi], ch % 4)

        self._cur_blk = None
        if "B" in self.mixers:
            self.short_conv_stream(L, j)
        else:
            for cc in range(4, 8):
                p.ms("dve", self.yT[:, cc, :], 0.0, [self.yT.bufs[cc]])
        if "A" in self.mixers:
            self.rwkv(L, j)
        else:
            for cc in range(4):
                p.ms("dve", self.yT[:, cc, :], 0.0, [self.yT.bufs[cc]])
        if not getattr(self, "_yd", False):
            self._yd = True
            self.dump("yT4", self.yT[:, 4, :], self.yT, 512)
        self.out_proj_post(L)

    def short_conv_stream(self, L, j):
        p = self.p
        F = self.F
        dest = {}
        for cc in range(4):
            dest[17 + cc] = (F[3], cc, AF.Copy)
            dest[21 + cc] = (F[0], cc, AF.Copy)
            dest[25 + cc] = (F[1], cc, AF.Copy)
            dest[29 + cc] = (F[4], cc, AF.Silu)
        cur_bi = None
        wr = None
        for ch in range(17, 33):
            bi = ch // 4
            if bi != cur_bi:
                wr = self.load_w(L, bi)
                cur_bi = bi
            ps = self.next_pp()
            self.proj_fm(wr, ch % 4, ps)
            t, cc, fn = dest[ch]
            if ch % 2 == 0:
                p.act(t[:, cc, :], ps[:], fn, [ps.b], [t.b])
            else:
                if fn == AF.Copy:
                    p.cp("dve", t[:, cc, :], ps[:], [ps.b], [t.b])
                else:
                    p.act(t[:, cc, :], ps[:], fn, [ps.b], [t.b])
        u = F[0]
        acc = F[2]
        w = self.scw
        tl = self.sc_tail
        p.tt("dve", u[:], F[0][:], F[1][:], ALU.mult, [F[0].b, F[1].b], [u.b])
        for cc in range(4):
            p.ts("dve", acc[:, cc, :], u[:, cc, :], w[:, j, cc, 2:3], None, ALU.mult, None, [u.b, w.b], [acc.b])
            p.stt("dve", acc[:, cc, 1:TT], u[:, cc, 0:TT - 1], w[:, j, cc, 1:2], acc[:, cc, 1:TT], ALU.mult, ALU.add, [u.b, w.b, acc.b], [acc.b])
            p.stt("dve", acc[:, cc, 2:TT], u[:, cc, 0:TT - 2], w[:, j, cc, 0:1], acc[:, cc, 2:TT], ALU.mult, ALU.add, [u.b, w.b, acc.b], [acc.b])
            p.stt("dve", acc[:, cc, 0:1], tl[:, j, cc, 1:2], w[:, j, cc, 1:2], acc[:, cc, 0:1], ALU.mult, ALU.add, [tl.b, w.b, acc.b], [acc.b])
            p.stt("dve", acc[:, cc, 0:2], tl[:, j, cc, 0:2], w[:, j, cc, 0:1], acc[:, cc, 0:2], ALU.mult, ALU.add, [tl.b, w.b, acc.b], [acc.b])
            p.cp("dve", tl[:, j, cc, :], u[:, cc, TT - 2:TT], [u.b, acc.b], [tl.b])
        p.tt("dve", acc[:], acc[:], F[3][:], ALU.mult, [acc.b, F[3].b], [acc.b])
        for cc in range(4):
            p.tt("dve", self.yT[:, 4 + cc, :], acc[:, cc, :], F[4][:, cc, :], ALU.mult, [acc.b, F[4].b], [self.yT.bufs[4 + cc]])

    def proj_to(self, L, ch, dst_ap, dst_bufs, fn=None, eng="act"):
        p = self.p
        bi = ch // 4
        if self._cur_blk != (L, bi):
            self._cur_wr = self.load_w(L, bi)
            self._cur_blk = (L, bi)
        ps = self.next_pp()
        self.proj_fm(self._cur_wr, ch % 4, ps)
        if eng == "act":
            p.act(dst_ap, ps[:], fn or AF.Copy, [ps.b], dst_bufs)
        else:
            p.cp(eng, dst_ap, ps[:], [ps.b], dst_bufs)

    def conformer(self, L, j):
        p = self.p
        F = self.F
        ue = self.uext
        p.cp("dve", ue[:, :, 0:30], self.cf_tail[:, j, :, :], [self.cf_tail.b], [ue.b])
        for cc in range(4):
            self.proj_to(L, cc, F[0][:, cc, :], [F[0].b], eng="dve")
        for cc in range(4):
            self.proj_to(L, 4 + cc, F[1][:, cc, :], [F[1].b], AF.Sigmoid)
        p.tt("dve", ue[:, :, 30:30 + TT], F[0][:], F[1][:], ALU.mult, [F[0].b, F[1].b], [ue.b])
        p.cp("dve", self.cf_tail[:, j, :, :], ue[:, :, TT:TT + 30], [ue.b], [self.cf_tail.b])
        for cc in range(4):
            self.proj_to(L, 8 + cc, F[1][:, cc, :], [F[1].b], AF.Silu)
        acc = F[2]
        w = self.cfw
        ub = Tile(self.sq[:].rearrange("p a b -> p (a b)")[:, 0:4 * (30 + TT)].rearrange("p (c t) -> p c t", c=4))
        ub.bufs = list(self.sq.bufs)
        p.cp("act", ub[:], ue[:, :, 0:30 + TT], [ue.b], list(ub.bufs))
        ring = []
        for t_ in (self.xdt, self.xdd):
            v_ = t_[:].rearrange("p h d -> p (h d)")
            for i_ in range(4):
                ring.append(Tile(v_[:, i_ * 128:(i_ + 1) * 128]))
        n_ = 0
        for cc in range(4):
            ps = self.next_pp()
            for k in range(31):
                dg = ring[n_ % 8]
                p.ts("dve", dg[:], self.ident_b, w[:, j, cc, k:k + 1], None, ALU.mult, None, [self.cstb.b, w.b], [dg.b])
                p.mm(ps[:], dg[:], ub[:, cc, k:k + TT], k == 0, k == 30, [dg.b] + list(ub.bufs), [ps.b])
                n_ += 1
            p.act(acc[:, cc, :], ps[:], AF.Identity, [ps.b, self.cfp.b], [acc.b], bias=self.cfp[:, j, 0, cc:cc + 1])
        p.ms("dve", self.xdt[0:1, 0, 0:1], 0.0, [r_.b for r_ in ring] + [self.xdt.b, self.xdd.b])
        ones_f = self.cst[:, C_ONE:C_ONE + 128]
        pm = self.next_pp()
        for cc in range(4):
            p.mm(pm[:], ones_f, acc[:, cc, :], cc == 0, cc == 3, [self.cst.b, acc.b], [pm.b])
        for cc in range(4):
            p.act(F[0][:, cc, :], acc[:, cc, :], AF.Square, [acc.b], [F[0].b])
        pq = self.next_pp()
        for cc in range(4):
            p.mm(pq[:], ones_f, F[0][:, cc, :], cc == 0, cc == 3, [self.cst.b, F[0].b], [pq.b])
        mean = F[3][:, 0, :]
        var = F[3][:, 1, :]
        p.ts("dve", mean, pm[:], 1.0 / 512, None, ALU.mult, None, [pm.b], [F[3].b])
        p.tt("dve", F[3][:, 2, :], mean, mean, ALU.mult, [F[3].b], [F[3].b])
        p.stt("dve", var, pq[:], 1.0 / 512, F[3][:, 2, :], ALU.mult, ALU.subtract, [pq.b, F[3].b], [F[3].b])
        p.ts("dve", var, var, 1e-5, None, ALU.add, None, [F[3].b], [F[3].b])
        p.op("act", lambda e: e.activation(out=var, in_=var, func=AF.Ln), [F[3].b], [F[3].b])
        p.op("act", lambda e: e.activation(out=var, in_=var, func=AF.Exp, scale=-0.5), [F[3].b], [F[3].b])
        if not getattr(self, "_cfd", False):
            self._cfd = True
            self.dump("cf_hT", self.hT[:, 0, :], self.hT, 512)
            self.dump("cf_xT", self.xT[:, 0, :], self.xT, 512)
            self.dump("cf_g1c", self.g1c[:, 1, :], self.g1c, 8)
            self.dump("cf_shc", self.shc[:, 1, :], self.shc, 8)
            self.dump("cf_rb", self.rb[:], self.rb, 512)
            self.dump("cf_u", ue[:, 0, 30:30 + TT], ue, 512)
            self.dump("cf_acc", acc[:, 0, :], acc, 512)
            self.dump("cf_mean", mean, F[3], 512)
            self.dump("cf_rstd", var, F[3], 512)
            self.dump("cf_sg", F[1][:, 0, :], F[1], 512)
        for cc in range(4):
            p.tt("dve", F[0][:, cc, :], acc[:, cc, :], mean, ALU.subtract, [acc.b, F[3].b], [F[0].b])
            p.tt("dve", F[0][:, cc, :], F[0][:, cc, :], var, ALU.mult, [F[0].b, F[3].b], [F[0].b])
            p.act(F[0][:, cc, :], F[0][:, cc, :], AF.Silu, [F[0].b, self.cfp.b], [F[0].b],
                  scale=self.cfp[:, j, 1, cc:cc + 1], bias=self.cfp[:, j, 2, cc:cc + 1])
            p.tt("dve", self.yT[:, cc, :], F[0][:, cc, :], F[1][:, cc, :], ALU.mult, [F[0].b, F[1].b], [self.yT.bufs[cc]])

    def ssd(self, L, j):
        p = self.p
        F = self.F
        xe = self.xext
        zs = F[4]
        for cc in range(4):
            self.proj_to(L, 12 + cc, zs[:, cc, :], [zs.b], AF.Silu)
        p.cp("dve", xe[:, :, 0:3], self.sd_tail[:, j, :, :], [self.sd_tail.b], [xe.b])
        for cc in range(8):
            self.proj_to(L, 16 + cc, xe[:, cc, 3:3 + TT], [xe.b], eng="act")
        p.cp("dve", self.sd_tail[:, j, :, :], xe[:, :, TT:TT + 3], [xe.b], [self.sd_tail.b])
        wdt = self.load_w(L, 6)
        self._cur_blk = None
        w = self.sdw
        for cc in range(8):
            eng = "dve"
            acc = F[5][:, cc % 4, :]
            p.ts(eng, acc, xe[:, cc, 3:3 + TT], w[:, j, cc, 3:4], None, ALU.mult, None, [xe.b, w.b], [F[5].bufs[0]])
            for k in range(3):
                p.stt(eng, acc, xe[:, cc, k:k + TT], w[:, j, cc, k:k + 1], acc, ALU.mult, ALU.add, [xe.b, w.b, F[5].b], [F[5].b])
            p.act(self.xbcT[:, cc, :], acc, AF.Silu, [F[5].b, self.sdb.b], [self.xbcT.b], bias=self.sdb[:, j, cc:cc + 1])
        for g in range(2):
            for e in range(2):
                par = self.cstb[:, (CB_PAR0 if e == 0 else CB_PAR1):(CB_PAR0 if e == 0 else CB_PAR1) + TT]
                p.tt("dve", self.ctm[:, g, e, :], self.xbcT[:, 6 + g, :], par, ALU.mult, [self.xbcT.b, self.cstb.b], [self.ctm.b])
        y2T = F[0]
        for s in range(4):
            tsl = slice(s * 128, (s + 1) * 128)
            for kc in range(8):
                p.mm(self.psm[:, 0:8], self.hT[:, kc, tsl], wdt[:, kc, 0:8], kc == 0, kc == 7, [self.hT.b, wdt.b], [self.psm.b])
            d = self.dts
            p.tt("dve", d[:, 0, :], self.psm[:, 0:8], self.hrow[:, j, 0, :], ALU.add, [self.psm.b, self.hrow.b], [d.b])
            p.act(d[:, 0, :], d[:, 0, :], AF.Exp, [d.b], [d.b])
            p.ts("dve", d[:, 0, :], d[:, 0, :], 1.0, None, ALU.add, None, [d.b], [d.b])
            p.act(d[:, 0, :], d[:, 0, :], AF.Ln, [d.b], [d.b])
            p.tt("dve", d[:, 1, :], d[:, 0, :], self.hrow[:, j, 1, :], ALU.mult, [d.b, self.hrow.b], [d.b])
            p.mm(self.psm[:, 8:16], self.cst[:, C_TRI2:C_TRI2 + 128], d[:, 1, :], True, True, [self.cst.b, d.b], [self.psm.b])
            p.mm(self.psm[:, 16:24], self.cst[:, C_BLK:C_BLK + 128], d[:, 1, :], True, True, [self.cst.b, d.b], [self.psm.b])
            p.mm(self.psm[:, 24:32], self.cst[:, C_SEL0:C_SEL0 + 128], d[:, 1, :], True, True, [self.cst.b, d.b], [self.psm.b])
            p.mm(self.psm[:, 32:40], self.cst[:, C_SEL1:C_SEL1 + 128], d[:, 1, :], True, True, [self.cst.b, d.b], [self.psm.b])
            p.cp("dve", d[:, 2, :], self.psm[:, 8:16], [self.psm.b], [d.b])
            p.tt("dve", d[:, 4, :], self.psm[:, 16:24], d[:, 2, :], ALU.subtract, [self.psm.b, d.b], [d.b])
            p.act(d[:, 4, :], d[:, 4, :], AF.Exp, [d.b], [d.b])
            p.act(d[:, 5, :], d[:, 2, :], AF.Exp, [d.b], [d.b])
            p.act(self.cdb[:, 0, :], self.psm[:, 24:32], AF.Exp, [self.psm.b], [self.cdb.b])
            p.act(self.cdb[:, 1, :], self.psm[:, 32:40], AF.Exp, [self.psm.b], [self.cdb.b])
            p.ts("dve", d[:, 3, :], d[:, 2, :], -1.0, None, ALU.mult, None, [d.b], [d.b])
            pt = self.next_pt()
            for cc in range(4):
                p.tr(pt[:, cc * 128:(cc + 1) * 128], self.xbcT[:, cc, tsl], self.ident_b, [self.xbcT.b, self.cstb.b], [pt.b])
            p.cp("act", self.tokx[:], pt[:], [pt.b], [self.tokx.b])
            pt2 = self.next_pt()
            for cc in range(2):
                p.tr(pt2[:, cc * 128:(cc + 1) * 128], self.xbcT[:, 4 + cc, tsl], self.ident_b, [self.xbcT.b, self.cstb.b], [pt2.b])
            p.cp("act", self.btok[:], pt2[:, 0:256], [pt2.b], [self.btok.b])
            tx3 = self.tokx[:].rearrange("p (h q) -> p h q", h=8)
            p.tt("dve", self.xdt[:], tx3, d[:, 0, :].unsqueeze(2).to_broadcast([128, 8, 64]), ALU.mult, [self.tokx.b, d.b], [self.xdt.b])
            p.tt("dve", self.xdd[:], self.xdt[:], d[:, 4, :].unsqueeze(2).to_broadcast([128, 8, 64]), ALU.mult, [self.xdt.b, d.b], [self.xdd.b])
            for g in range(2):
                p.mm(self.pcb[:, g, :], self.xbcT[:, 4 + g, tsl], self.xbcT[:, 6 + g, tsl], True, True, [self.xbcT.b], [self.pcb.b])
            p.cp("act", self.cbt[:], self.pcb[:], [self.pcb.b], [self.cbt.b])
            p.cp("dve", self.Dall[:], d[:, 1, :].unsqueeze(2).to_broadcast([128, 8, 128]), [d.b], [self.Dall.b])
            for hh in range(2):
                pd = self.next_pp()
                for q in range(4):
                    h = hh * 4 + q
                    p.mm(pd[:, q * 128:(q + 1) * 128], self.Dall[:, h, :], self.cst[:, C_TRI2:C_TRI2 + 128], True, False, [self.Dall.b, self.cst.b], [pd.b])
                    p.mm(pd[:, q * 128:(q + 1) * 128], self.cst[:, C_ID:C_ID + 128], self.cst[:, C_MB2:C_MB2 + 128], False, True, [self.cst.b], [pd.b])
                for q in range(4):
                    h = hh * 4 + q
                    p.act(self.dec[:, h, :], pd[:, q * 128:(q + 1) * 128], AF.Exp, [pd.b, d.b], [self.dec.b], bias=d[:, 3, h:h + 1])
            for g in range(2):
                p.tt("dve", self.MT[:, g * 4:(g + 1) * 4, :], self.dec[:, g * 4:(g + 1) * 4, :],
                     self.cbt[:, g, :].unsqueeze(1).to_broadcast([128, 4, 128]), ALU.mult, [self.dec.b, self.cbt.b], [self.MT.b])
            pyd = self.next_pp()
            for h in range(8):
                p.mm(pyd[:, h * 64:(h + 1) * 64], self.MT[:, h, :], self.xdt[:, h, :], True, True, [self.MT.b, self.xdt.b], [pyd.b])
            for e in range(2):
                rs = slice(e * 64, (e + 1) * 64)
                p.cp("act", self.Sb[:, e, :, :], self.S[:, j, :, :], [self.S.b], [self.Sb.bufs[e]])
                pst = self.next_pp()
                for g in range(2):
                    p.mm(pst[:, g * 256:(g + 1) * 256], self.btok[rs, g * 128:(g + 1) * 128],
                         self.xdd[rs, g * 4:(g + 1) * 4, :].rearrange("p h q -> p (h q)"), True, True, [self.btok.b, self.xdd.b], [pst.b])
                p.tt("dve", self.S[:, j, :, :], self.S[:, j, :, :], self.cdb[:, e, :].unsqueeze(2).to_broadcast([128, 8, 64]), ALU.mult,
                     [self.S.b, self.cdb.b], [self.S.b])
                p.tt("dve", self.S[:, j, :, :], self.S[:, j, :, :], pst[:].rearrange("p (h q) -> p h q", h=8), ALU.add, [self.S.b, pst.b], [self.S.b])
            pyo = self.next_pp()
            for g in range(2):
                for e in range(2):
                    p.mm(pyo[:, g * 256:(g + 1) * 256], self.ctm[:, g, e, tsl], self.Sb[:, e, g * 4:(g + 1) * 4, :].rearrange("p h q -> p (h q)"),
                         e == 0, e == 1, [self.ctm.b, self.Sb.bufs[e]], [pyo.b])
            yt = self.ytok
            yt3 = yt[:].rearrange("p (h q) -> p h q", h=8)
            p.tt("dve", yt3, pyo[:].rearrange("p (h q) -> p h q", h=8), d[:, 5, :].unsqueeze(2).to_broadcast([128, 8, 64]), ALU.mult, [pyo.b, d.b], [yt.b])
            p.tt("dve", yt[:], yt[:], pyd[:], ALU.add, [yt.b, pyd.b], [yt.b])
            p.tt("dve", self.tokx[:].rearrange("p (h q) -> p h q", h=8), tx3, self.hrow[:, j, 2, :].unsqueeze(2).to_broadcast([128, 8, 64]), ALU.mult,
                 [self.tokx.b, self.hrow.b], [self.tokx.b])
            p.tt("dve", yt[:], yt[:], self.tokx[:], ALU.add, [yt.b, self.tokx.b], [yt.b])
            py = self.next_pp()
            for cc in range(4):
                p.tr(py[:, cc * 128:(cc + 1) * 128], yt[:, cc * 128:(cc + 1) * 128], self.ident_f, [yt.b, self.cst.b], [py.b])
            p.tt("dve", y2T[:, :, tsl], py[:].rearrange("p (c t) -> p c t", c=4), zs[:, :, tsl], ALU.mult, [py.b, zs.b], [y2T.b])
        for cc in range(4):
            p.act(self.sq[:, cc, :], y2T[:, cc, :], AF.Square, [y2T.b], [self.sq.bufs[cc]])
        pr = self.next_pp()
        for cc in range(4):
            p.mm(pr[:], self.ones_b, self.sq[:, cc, :], cc == 0, cc == 3, [self.cstb.b, self.sq.bufs[cc]], [pr.b])
        p.ts("dve", self.rb[:], pr[:], 1.0 / 512, 1e-6, ALU.mult, ALU.add, [pr.b], [self.rb.b])
        p.op("act", lambda e: e.activation(out=self.rb[:], in_=self.rb[:], func=AF.Ln), [self.rb.b], [self.rb.b])
        p.op("act", lambda e: e.activation(out=self.rb[:], in_=self.rb[:], func=AF.Exp, scale=-0.5), [self.rb.b], [self.rb.b])
        for cc in range(4):
            p.stt("dve", self.yT[:, 4 + cc, :], y2T[:, cc, :], self.sdng[:, j, cc:cc + 1], self.rb[:], ALU.mult, ALU.mult,
                  [y2T.b, self.sdng.b, self.rb.b], [self.yT.bufs[4 + cc]])

    def odd_layer(self, L):
        p = self.p
        j = L // 2
        self.pre_norm(L)
        self._cur_blk = None
        if "C" in self.mixers:
            self.conformer(L, j)
        else:
            for cc in range(4):
                p.ms("dve", self.yT[:, cc, :], 0.0, [self.yT.bufs[cc]])
        if not getattr(self, "_cfd2", False):
            self._cfd2 = True
            self.dump("cf_y", self.yT[:, 0, :], self.yT, 512)
        if "D" in self.mixers:
            self.ssd(L, j)
        else:
            for cc in range(4, 8):
                p.ms("dve", self.yT[:, cc, :], 0.0, [self.yT.bufs[cc]])
        self.out_proj_post(L)

    def rwkv(self, L, j):
        p = self.p
        F = self.F
        LW = -0.6065306597126334
        INV = self.INV_DT
        cst = self.cst
        rT, kT, vT, sgT, bonT, aT = F[0], F[1], F[2], F[3], F[4], F[5]
        yob = self.yo[:].rearrange("p a b -> p (a b)").bitcast(BF16)
        AR = Tile(yob[:, 0:4096].rearrange("p (c n q d) -> p c n q d", c=4, n=8, q=2))
        AR.bufs = self.yo.bufs
        BK = Tile(yob[:, 4096:8192].rearrange("p (c n q d) -> p c n q d", c=4, n=8, q=2))
        BK.bufs = self.yo.bufs
        Vt, Bt, Kt = self.hT, self.sq, self.xbcT
        Ytok = Tile(self.xs[0][:].rearrange("p s (a d) -> p (s a) d", a=2))
        Ytok.bufs = self.xs[0].bufs
        if INV == F32:
            Wp = [self.Dall, self.dec]
            NTp = [Tile(self.uext[:, 0, 0:512].rearrange("p (h d) -> p h d", h=8)), Tile(self.uext[:, 1, 0:512].rearrange("p (h d) -> p h d", h=8))]
            Xs = Tile(self.uext[:, 2, 0:512].rearrange("p (h d) -> p h d", h=8))
        else:
            Wp = []
            for t_ in (self.Dall, self.dec):
                w_ = Tile(t_[:].rearrange("p h d -> p (h d)").bitcast(BF16)[:, 0:1024].rearrange("p (h d) -> p h d", h=8))
                w_.bufs = t_.bufs
                Wp.append(w_)
            NTp = [Tile(self.uext[:, i_, 0:512].bitcast(BF16)[:, 0:512].rearrange("p (h d) -> p h d", h=8)) for i_ in range(2)]
            Xs = Tile(self.uext[:, 2, 0:512].bitcast(BF16)[:, 0:512].rearrange("p (h d) -> p h d", h=8))
        for t_ in NTp + [Xs]:
            t_.bufs = self.uext.bufs
        Us = self.xdd
        Srb = Tile(self.Sb[:, 0, :, :])
        Srb.bufs = [self.Sb.bufs[0]]
        Sk = self.MT
        wdad = Tile(self.xdt[:].rearrange("p h d -> p (h d)"))
        wdad.bufs = self.xdt.bufs
        sig, cs = self.tokx, self.ytok
        e1 = self.tmp[:, 0:512]
        e2 = self.tmp[:, 512:1024]
        tmpb = self.tmp.b
        kk = self.rb
        mu, omu, prev = self.mu, self.omu, self.tm_prev
        tmc = self.tmc

        def shifted(ch, dst_ap, dst_tile):
            bi = ch // 4
            if self._cur_blk != (L, bi):
                self._cur_wr = self.load_w(L, bi)
                self._cur_blk = (L, bi)
            ps = self.next_pp()
            self.proj_fm(self._cur_wr, ch % 4, ps)
            p.act(dst_ap, ps[:], AF.Copy, [ps.b, omu.b], [dst_tile.b], scale=omu[:, j, ch:ch + 1])
            p.stt("dve", dst_ap[:, 1:TT], ps[:, 0:TT - 1], mu[:, j, ch:ch + 1], dst_ap[:, 1:TT], ALU.mult, ALU.add, [ps.b, mu.b, dst_tile.b], [dst_tile.b])
            p.stt("dve", dst_ap[:, 0:1], prev[:, j, ch:ch + 1], mu[:, j, ch:ch + 1], dst_ap[:, 0:1], ALU.mult, ALU.add, [prev.b, mu.b, dst_tile.b], [dst_tile.b])
            p.cp("dve", prev[:, j, ch:ch + 1], ps[:, TT - 1:TT], [ps.b], [prev.b])

        for cc in range(4):
            shifted(cc, rT[:, cc, :], rT)
        for cc in range(4):
            shifted(4 + cc, kT[:, cc, :], kT)
        for cc in range(4):
            shifted(8 + cc, vT[:, cc, :], vT)
        for cc in range(4):
            shifted(12 + cc, sgT[:, cc, :], sgT)
        shifted(16, e1, self.tmp)
        p.act(wdad[0:64, :], e1[0:64, :], AF.Tanh, [tmpb], [wdad.b])
        p.cp("dve", wdad[64:128, :], e1[64:128, :], [tmpb], [wdad.b])
        for cc in range(4):
            p.act(sgT[:, cc, :], sgT[:, cc, :], AF.Silu, [sgT.b], [sgT.b])
        if self.rw_stop == 1:
            for cc in range(4):
                p.ms("dve", self.yT[:, cc, :], 0.0, [self.yT.bufs[cc]])
            return
        for cc in range(4):
            csl = slice(cc * 128, (cc + 1) * 128)
            pz = self.next_pp()
            p.mm(pz[:], self.w2a2[0:64, j, csl], wdad[0:64, :], True, True, [self.w2a2.b, wdad.b], [pz.b])
            p.act(sig[:], pz[:], AF.Sigmoid, [pz.b, tmc.b], [sig.b], bias=tmc[:, j, 0, cc:cc + 1])
            pa = self.next_pp()
            p.mm(pa[:], self.w2a2[64:128, j, csl], wdad[64:128, :], True, True, [self.w2a2.b, wdad.b], [pa.b])
            p.act(aT[:, cc, :], pa[:], AF.Sigmoid, [pa.b, tmc.b], [aT.b], bias=tmc[:, j, 1, cc:cc + 1])
            p.op("dve", lambda e: e.tensor_tensor_scan(out=cs[:], data0=cst[:, C_RST:C_RST + TT], data1=sig[:], initial=0.0,
                                                        op0=ALU.mult, op1=ALU.add), [cst.b, sig.b], [cs.b])
            p.act(self.gl[:, cc, :], cs[:].rearrange("p (n d) -> p n d", d=64)[:, :, 63], AF.Exp, [cs.b], [self.gl.b], scale=LW)
            p.act(e1, cs[:], AF.Exp, [cs.b], [tmpb], scale=LW)
            p.tt("dve", AR[:, cc, :, 1, :], rT[:, cc, :].rearrange("p (n d) -> p n d", d=64), e1.rearrange("p (n d) -> p n d", d=64), ALU.mult,
                 [rT.b, tmpb], [AR.b])
            p.ts("dve", kk[:], kT[:, cc, :], tmc[:, j, 2, cc:cc + 1], None, ALU.mult, None, [kT.b, tmc.b], [kk.b])
            p.tt("dve", e2, kk[:], kk[:], ALU.mult, [kk.b], [tmpb])
            pn = self.next_pp()
            p.mm(pn[:], cst[:, C_BLK:C_BLK + 128], e2, True, True, [cst.b, tmpb], [pn.b])
            p.ts("dve", e2, pn[:], 1e-24, None, ALU.max, None, [pn.b], [tmpb])
            p.op("act", lambda e: e.activation(out=e2, in_=e2, func=AF.Ln), [tmpb], [tmpb])
            p.op("act", lambda e: e.activation(out=e2, in_=e2, func=AF.Exp, scale=-0.5), [tmpb], [tmpb])
            p.tt("dve", kk[:], kk[:], e2, ALU.mult, [kk.b, tmpb], [kk.b])
            p.tt("dve", e2, cs[:], sig[:], ALU.subtract, [cs.b, sig.b], [tmpb])
            p.act(e2, e2, AF.Exp, [tmpb], [tmpb], scale=LW)
            p.stt("dve", AR[:, cc, :, 0, :], kk[:].rearrange("p (n d) -> p n d", d=64), -1.0, e2.rearrange("p (n d) -> p n d", d=64),
                  ALU.mult, ALU.mult, [kk.b, tmpb], [AR.b])
            p.act(e1, cs[:], AF.Exp, [cs.b], [tmpb], scale=-LW)
            p.tt("dve", e2, kk[:], aT[:, cc, :], ALU.mult, [kk.b, aT.b], [tmpb])
            p.tt("dve", BK[:, cc, :, 0, :], e2.rearrange("p (n d) -> p n d", d=64), e1.rearrange("p (n d) -> p n d", d=64), ALU.mult, [tmpb], [BK.b])
            p.ts("dve", e2, aT[:, cc, :], tmc[:, j, 3, cc:cc + 1], self.tmk1[:, j, cc:cc + 1], ALU.mult, ALU.add, [aT.b, tmc.b, self.tmk1.b], [tmpb])
            p.tt("dve", kT[:, cc, :], kT[:, cc, :], e2, ALU.mult, [kT.b, tmpb], [kT.b])
            p.tt("dve", BK[:, cc, :, 1, :], kT[:, cc, :].rearrange("p (n d) -> p n d", d=64), e1.rearrange("p (n d) -> p n d", d=64), ALU.mult,
                 [kT.b, tmpb], [BK.b])
            p.stt("dve", e2, rT[:, cc, :], tmc[:, j, 4, cc:cc + 1], kT[:, cc, :], ALU.mult, ALU.mult, [rT.b, tmc.b, kT.b], [tmpb])
            pbn = self.next_pp()
            p.mm(pbn[:], cst[:, C_BLK:C_BLK + 128], e2, True, True, [cst.b, tmpb], [pbn.b])
            p.tt("dve", bonT[:, cc, :], pbn[:], vT[:, cc, :], ALU.mult, [pbn.b, vT.b], [bonT.b])
        if self.rw_stop == 2:
            for cc in range(4):
                p.ms("dve", self.yT[:, cc, :], 0.0, [self.yT.bufs[cc]])
            return
        vb = Tile(self.ctm[:].rearrange("p g e t -> p (g e) t"))
        vb.bufs = self.ctm.bufs
        p.cp("act", vb[:], vT[:], [vT.b], [vb.b])
        for c in range(8):
            tsl = slice(c * 64, (c + 1) * 64)
            for ii, (dst, getsrc, srct) in enumerate(((Vt, lambda cc: vb[:, cc, tsl], vb), (Bt, lambda cc: BK[:, cc, c, 0, :], BK), (Kt, lambda cc: BK[:, cc, c, 1, :], BK))):
                if self.rw_stop == 31 and ii > 0:
                    continue
                if self.rw_stop == 33:
                    continue
                if self.rw_stop == 32 and ii != 1:
                    continue
                pt = self.next_pp()
                for cc in range(4):
                    p.mm(pt[0:64, cc * 128:(cc + 1) * 128], getsrc(cc), self.ident_b, True, True, [srct.b, self.cstb.b], [pt.b])
                if c % 2 == 0:
                    p.cp("act", dst[0:64, c, :], pt[0:64, :], [pt.b], [dst.b] if dst is not self.sq else list(self.sq.bufs))
                else:
                    p.cp("dve", dst[0:64, c, :], pt[0:64, :], [pt.b], [dst.b] if dst is not self.sq else list(self.sq.bufs))
        if self.rw_stop in (3, 31, 32, 33):
            for cc in range(4):
                p.ms("dve", self.yT[:, cc, :], 0.0, [self.yT.bufs[cc]])
            return
        Btb = list(self.sq.bufs)
        p.cp("act", self.Hb[:], self.Hs[:, j, :, :], [self.Hs.b], [self.Hb.b])
        msk_su = cst[0:64, C_SU:C_SU + 64]
        msk_iu = cst[0:64, C_IU:C_IU + 64]
        msk_sl = cst[0:64, C_SL:C_SL + 64]
        idn = cst[0:64, 0:64]
        bc = lambda ap, n: ap.unsqueeze(1).to_broadcast([64, n, 64])
        npp = self.next_pp6
        SrbP = [Tile(self.Sb[:, 0, :, :]), Tile(self.Sb[:, 1, :, :])]
        SrbP[0].bufs = [self.Sb.bufs[0]]
        SrbP[1].bufs = [self.Sb.bufs[1]]
        SkP = [self.MT, Tile(self.Dall[:].rearrange("p h d -> p (h d)").bitcast(BF16)[:, 1024:2048].rearrange("p (h d) -> p h d", h=8))]
        TtP = [Tile(self.cbt[:].rearrange("p g n -> p (g n)").bitcast(BF16)[:, 0:512].rearrange("p (h d) -> p h d", h=8)),
               Tile(self.dec[:].rearrange("p h d -> p (h d)").bitcast(BF16)[:, 1024:1536].rearrange("p (h d) -> p h d", h=8))]
        TtP[0].bufs = self.cbt.bufs

        def prep_units(c):
            Srb, Sk, Tt = SrbP[c % 2], SkP[c % 2], TtP[c % 2]
            W0, NT0 = Wp[0], NTp[0]
            units = []

            def scores(g):
                hs = slice(g * 4, (g + 1) * 4)
                rows = slice(g * 64, (g + 1) * 64)
                Pb, Pk, Pa = npp(), npp(), npp()
                for q in range(4):
                    ar = AR[rows, q, c, :, :].rearrange("p q d -> p (q d)")
                    p.mm(Pb[0:64, q * 128:(q + 1) * 128], BK[rows, q, c, 0, :], ar, True, True, [BK.b, AR.b], [Pb.b])
                    p.mm(Pk[0:64, q * 128:(q + 1) * 128], BK[rows, q, c, 1, :], ar, True, True, [BK.b, AR.b], [Pk.b])
                    p.mm(Pa[0:64, q * 64:(q + 1) * 64], AR[rows, q, c, 0, :], BK[rows, q, c, 0, :], True, True, [BK.b, AR.b], [Pa.b])
                Pb3 = Pb[0:64, :].rearrange("p (h d) -> p h d", h=4)
                Pk3 = Pk[0:64, :].rearrange("p (h d) -> p h d", h=4)
                p.tt("dve", W0[0:64, hs, 0:64], Pb3[:, :, 0:64], bc(msk_su, 4), ALU.mult, [Pb.b, cst.b], [W0.b])
                p.tt("dve", Srb[0:64, hs, :], Pb3[:, :, 64:128], bc(msk_iu, 4), ALU.mult, [Pb.b, cst.b], [Srb.b])
                p.tt("dve", Sk[0:64, hs, 0:64], Pk3[:, :, 0:64], bc(msk_su, 4), ALU.mult, [Pk.b, cst.b], [Sk.b])
                p.tt("dve", Sk[0:64, hs, 64:128], Pk3[:, :, 64:128], bc(msk_iu, 4), ALU.mult, [Pk.b, cst.b], [Sk.b])
                p.tt("dve", NT0[0:64, hs, :], Pa[0:64, 0:256].rearrange("p (h d) -> p h d", h=4), bc(msk_sl, 4), ALU.mult, [Pa.b, cst.b], [NT0.b])

            def level(lvl, g):
                cur = lvl % 2
                Wc, NTc = Wp[cur], NTp[cur]
                Wn, NTn = Wp[1 - cur], NTp[1 - cur]
                last = (lvl == 5)
                hs = slice(g * 4, (g + 1) * 4)
                P1 = npp()
                if not last:
                    P2 = npp()
                for q in range(4):
                    h = g * 4 + q
                    if lvl == 0:
                        p.mm(P1[0:64, q * 128:q * 128 + 64], NTc[0:64, h, :], Wc[0:64, h, 0:64], True, True, [NTc.b, Wc.b], [P1.b])
                    elif not last:
                        p.mm(P1[0:64, q * 128:(q + 1) * 128], NTc[0:64, h, :], Wc[0:64, h, :], True, True, [NTc.b, Wc.b], [P1.b])
                    else:
                        p.mm(P1[0:64, q * 128 + 64:(q + 1) * 128], NTc[0:64, h, :], Wc[0:64, h, 64:128], True, True, [NTc.b, Wc.b], [P1.b])
                    if not last:
                        p.mm(P2[0:64, q * 64:(q + 1) * 64], Wc[0:64, h, 0:64], NTc[0:64, h, :], True, True, [NTc.b, Wc.b], [P2.b])
                P13 = P1[0:64, :].rearrange("p (h d) -> p h d", h=4)
                if not last:
                    p.cp("act", Wn[0:64, hs, 0:64], P13[:, :, 0:64], [P1.b], [Wn.b])
                    p.cp("act", NTn[0:64, hs, :], P2[0:64, 0:256].rearrange("p (h d) -> p h d", h=4), [P2.b], [NTn.b])
                if lvl == 0:
                    p.tt("dve", Wn[0:64, hs, 64:128], Wc[0:64, hs, 0:64], bc(idn, 4), ALU.add, [Wc.b, cst.b], [Wn.b])
                elif not last:
                    p.tt("dve", Wn[0:64, hs, 64:128], P13[:, :, 64:128], Wc[0:64, hs, 64:128], ALU.add, [P1.b, Wc.b], [Wn.b])
                else:
                    p.tt("dve", Tt[0:64, hs, :], P13[:, :, 64:128], Wc[0:64, hs, 64:128], ALU.add, [P1.b, Wc.b], [Tt.b])

            for g in range(2):
                units.append(lambda g=g: scores(g))
            for lvl in range(6):
                for g in range(2):
                    units.append(lambda lvl=lvl, g=g: level(lvl, g))
            return units

        def chain_units(c):
            Srb, Sk, Tt = SrbP[c % 2], SkP[c % 2], TtP[c % 2]
            st = {}

            def hterm(PS, qd):
                for q in range(4):
                    p.mm(PS[0:64, q * 64:(q + 1) * 64], AR[64:128, q, c, qd, :], self.Hb[64:128, q, :], True, True, [AR.b, self.Hb.b], [PS.b])

            def ux():
                PX, PXo = npp(), npp()
                hterm(PXo, 0)
                for hh in range(8):
                    g, q = hh // 4, hh % 4
                    h = 2 * q + g
                    o = PX[0:64, hh * 64:(hh + 1) * 64]
                    if g == 0:
                        p.mm(o, AR[0:64, q, c, 0, :], self.Hb[0:64, q, :], True, False, [AR.b, self.Hb.b], [PX.b])
                    p.mm(o, Sk[0:64, hh, 0:64], Vt[0:64, c, h * 64:(h + 1) * 64], g == 1, True, [Sk.b, Vt.b], [PX.b])
                p.cp("act", Xs[0:64, 0:4, :], PX[0:64, 0:256].rearrange("p (h d) -> p h d", h=4), [PX.b], [Xs.b])
                p.cp("act", Xs[0:64, 4:8, :], PXo[0:64, 0:256].rearrange("p (h d) -> p h d", h=4), [PXo.b], [Xs.b])
                p.tt("dve", Xs[0:64, 4:8, :], Xs[0:64, 4:8, :], PX[0:64, 256:512].rearrange("p (h d) -> p h d", h=4), ALU.add, [Xs.b, PX.b], [Xs.b])

            def uu():
                PU = npp()
                for hh in range(8):
                    p.mm(PU[0:64, hh * 64:(hh + 1) * 64], Tt[0:64, hh, :], Xs[0:64, hh, :], True, True, [Tt.b, Xs.b], [PU.b])
                p.cp("act", Us[0:64, :, :].rearrange("p (q g) d -> p g q d", g=2), PU[0:64, :].rearrange("p (g q d) -> p g q d", g=2, q=4), [PU.b], [Us.b])

            def uy():
                PY, PYo = npp(), npp()
                hterm(PYo, 1)
                for hh in range(8):
                    g, q = hh // 4, hh % 4
                    h = 2 * q + g
                    o = PY[0:64, hh * 64:(hh + 1) * 64]
                    if g == 0:
                        p.mm(o, AR[0:64, q, c, 1, :], self.Hb[0:64, q, :], True, False, [AR.b, self.Hb.b], [PY.b])
                    p.mm(o, Srb[0:64, hh, :], Us[0:64, h, :], g == 1, False, [Srb.b, Us.b], [PY.b])
                    p.mm(o, Sk[0:64, hh, 64:128], Vt[0:64, c, h * 64:(h + 1) * 64], False, True, [Sk.b, Vt.b], [PY.b])
                Yc = Ytok[0:64, c, :].rearrange("p (q g d) -> p g q d", g=2, d=64)
                p.cp("dve", Yc, PY[0:64, :].rearrange("p (g q d) -> p g q d", g=2, q=4), [PY.b], [Ytok.b])
                p.tt("dve", Yc[:, 1, :, :], Yc[:, 1, :, :], PYo[0:64, 0:256].rearrange("p (q d) -> p q d", q=4), ALU.add, [Ytok.b, PYo.b], [Ytok.b])

            def uh():
                PH = npp()
                for cc in range(4):
                    o = PH[:, cc * 128:(cc + 1) * 128]
                    p.mm(o, Bt[0:64, c, cc * 128:(cc + 1) * 128], Us[0:64, 2 * cc:2 * cc + 2, :].rearrange("p h d -> p (h d)"), True, False, Btb + [Us.b], [PH.b])
                    p.mm(o, Kt[0:64, c, cc * 128:(cc + 1) * 128], Vt[0:64, c, cc * 128:(cc + 1) * 128], False, True, [Kt.b, Vt.b], [PH.b])
                PH3 = PH[:].rearrange("p (c d) -> p c d", c=4)
                for hl in range(2):
                    rws = slice(hl * 64, (hl + 1) * 64)
                    p.tt("dve", self.Hs[rws, j, :, :], self.Hs[rws, j, :, :], PH3[rws, :, hl * 64:(hl + 1) * 64], ALU.add, [self.Hs.b, PH.b], [self.Hs.b])
                    p.tt("dve", self.Hs[rws, j, :, :], self.Hs[rws, j, :, :], self.gl[rws, :, c].unsqueeze(2).to_broadcast([64, 4, 64]), ALU.mult,
                         [self.Hs.b, self.gl.b], [self.Hs.b])
                p.cp("act", self.Hb[:], self.Hs[:, j, :, :], [self.Hs.b], [self.Hb.b])

            return [ux, uu, uy, uh]

        prev_chain = []
        for c in range(9):
            pu = prep_units(c) if c < 8 else []
            cu = prev_chain
            i_ = j_ = 0
            while i_ < len(pu) or j_ < len(cu):
                for _ in range(4):
                    if i_ < len(pu):
                        pu[i_]()
                        i_ += 1
                if j_ < len(cu):
                    cu[j_]()
                    j_ += 1
            prev_chain = chain_units(c) if c < 8 else []
        if self.rw_stop in (4, 5, 6):
            for cc in range(4):
                p.ms("dve", self.yT[:, cc, :], 0.0, [self.yT.bufs[cc]])
            return
        Y3 = Ytok[0:64, :, :].rearrange("p c (h d) -> p (c h) d", d=64)
        st = Tile(self.uext[:, 3, 0:512])
        st.bufs = self.uext.bufs
        s1 = st[0:64, 0:64]
        s2 = st[0:64, 64:128]
        s3 = st[0:64, 128:192]
        p.op("dve", lambda e: e.reduce_sum(out=s1, in_=Y3, axis=mybir.AxisListType.X), [Ytok.b], [st.b])
        sqt = Tile(self.F[5][:].rearrange("p c t -> p (c t)")[:, 0:2048].rearrange("p (a b) -> p a b", b=64))
        sqt.bufs = self.F[5].bufs
        for hf in range(2):
            ysl = Y3[:, hf * 32:(hf + 1) * 32, :]
            p.tt("dve", sqt[0:64, :, :], ysl, ysl, ALU.mult, [Ytok.b], [sqt.b])
            p.op("dve", lambda e, hf=hf: e.reduce_sum(out=s2[:, hf * 32:(hf + 1) * 32], in_=sqt[0:64, :, :], axis=mybir.AxisListType.X), [sqt.b], [st.b])
        p.ts("dve", s1, s1, 1.0 / 64, None, ALU.mult, None, [st.b], [st.b])
        p.tt("dve", s3, s1, s1, ALU.mult, [st.b], [st.b])
        p.stt("dve", s2, s2, 1.0 / 64, s3, ALU.mult, ALU.subtract, [st.b], [st.b])
        p.ts("dve", s2, s2, 64e-5, None, ALU.add, None, [st.b], [st.b])
        p.op("act", lambda e: e.activation(out=s2, in_=s2, func=AF.Ln), [st.b], [st.b])
        p.op("act", lambda e: e.activation(out=s2, in_=s2, func=AF.Exp, scale=-0.5), [st.b], [st.b])
        p.tt("dve", Y3, Y3, s1.unsqueeze(2).to_broadcast([64, 64, 64]), ALU.subtract, [Ytok.b, st.b], [Ytok.b])
        p.tt("dve", Y3, Y3, s2.unsqueeze(2).to_broadcast([64, 64, 64]), ALU.mult, [Ytok.b, st.b], [Ytok.b])
        for cc in range(4):
            py = self.next_pp()
            for c in range(8):
                p.tr(py[:, c * 64:(c + 1) * 64], Ytok[0:64, c, cc * 128:(cc + 1) * 128], cst[0:64, 0:64], [Ytok.b, cst.b], [py.b])
            p.act(e1, py[:], AF.Identity, [py.b, tmc.b], [tmpb], scale=tmc[:, j, 5, cc:cc + 1], bias=tmc[:, j, 6, cc:cc + 1])
            p.tt("dve", e1, e1, bonT[:, cc, :], ALU.add, [tmpb, bonT.b], [tmpb])
            p.tt("dve", self.yT[:, cc, :], e1, sgT[:, cc, :], ALU.mult, [tmpb, sgT.b], [self.yT.bufs[cc]])

    def build(self):
        if self.prefetch and self.wseq is None and not getattr(self, "_recording", False):
            rec = Builder(self.NL, self.NT, self.mixers, False)
            rec._recording = True
            rec.rw_stop = self.rw_stop
            rec.build()
            self.wseq = rec.wrec
        p = self.p
        self.declare()
        self.alloc()
        self.prologue()
        xr = self.x.rearrange("(t s p) d -> t p s d", p=128, s=4)
        orr = self.out.rearrange("(t s p) d -> t p s d", p=128, s=4)
        for t in range(self.NT):
            p.dma("sp", self.xs[0][:], xr[t], (), [self.xs[0].b])
            self.load_tile(t)
            for L in range(self.NL):
                if L % 2 == 0:
                    self.even_layer(L)
                else:
                    self.odd_layer(L)
            self.store_tile(t, orr)
        p.finish()
        return self.nc


def make_in_maps(inputs):
    consts = make_consts()
    maps = []
    for core in range(8):
        b = core % 4
        m = {}
        for k, v in inputs.items():
            v = np.asarray(v)
            if k == "x":
                m[k] = np.ascontiguousarray(v[b])
            elif k == "c":
                m[k] = np.ascontiguousarray(v[b].reshape(8, 128))
            elif k in ("tm_k_k", "tm_k_a", "tm_r_k"):
                m[k] = np.ascontiguousarray(v.reshape(2, 512))
            else:
                m[k] = np.ascontiguousarray(v)
        m["consts"] = consts
        maps.append(m)
    return maps


def kernel(**inputs):
    import os
    bld = Builder(mixers=os.environ.get("K_MIXERS", "ABCD"))
    bld.rw_stop = int(os.environ.get("K_RWSTOP", "0"))
    nc = bld.build()
    maps = make_in_maps(inputs)
    res = run_bass_kernel_spmd(nc, maps, core_ids=list(range(8)))
    out = np.stack([res.results[b]["out"] for b in range(4)], axis=0)
    return out.astype(np.float32)
```

```python
import numpy as np
import concourse.bass as bass
import concourse.mybir as mybir
from concourse.bass_utils import run_bass_kernel_spmd

F32 = mybir.dt.float32
BF16 = mybir.dt.bfloat16
AF = mybir.ActivationFunctionType
ALU = mybir.AluOpType

D = 1024
SEQ = 8192
TT = 512
NDS = 24
EVEN_COLS = 4224
ODD_COLS = 3080


class Buf:
    __slots__ = ("w", "r")

    def __init__(self):
        self.w = {}
        self.r = {}


class Tile:
    def __init__(self, h, nb=1):
        self.h = h
        self.bufs = [Buf() for _ in range(nb)]

    def __getitem__(self, k):
        return self.h[k]

    @property
    def b(self):
        return self.bufs[0]


class Prog:
    def __init__(self, nc):
        self.nc = nc
        self.eng = {"pe": nc.tensor, "act": nc.scalar, "dve": nc.vector, "pool": nc.gpsimd, "sp": nc.sync}
        self.sem = {}
        self.cnt = {}
        self.cur = {}
        self.qeng = {}
        self.epochs = {}
        for q in ("pe", "act", "dve", "pool"):
            self._new_epoch(q)
        for i in range(NDS):
            self.sem["d%d" % i] = nc.alloc_semaphore("d%d" % i)
            self.cnt["d%d" % i] = 0
        self.dnext = 0
        self.known = {e: {} for e in self.eng}
        self.snap = {}
        self.nwait = 0
        self.nins = 0
        import os
        self.raw_only = os.environ.get("K_RAWONLY", "0") == "1"

    SEM_LIMIT = 30000

    def _new_epoch(self, e):
        k = self.epochs.get(e, 0)
        self.epochs[e] = k + 1
        key = "%s#%d" % (e, k)
        self.sem[key] = self.nc.alloc_semaphore("s_%s_%d" % (e, k))
        self.cnt[key] = 0
        self.cur[e] = key
        self.qeng[key] = e

    def sb(self, name, shape, dt, nb=1):
        return Tile(self.nc.alloc_sbuf_tensor(name, list(shape), dt), nb)

    def ps(self, name, shape, dt=F32, nb=1):
        return Tile(self.nc.alloc_psum_tensor(name, list(shape), dt), nb)

    def _wait(self, e, evs, same_ok=True):
        kn = self.known[e]
        need = {}
        for (q, v) in evs:
            if same_ok and self.qeng.get(q) == e:
                continue
            if kn.get(q, 0) >= v:
                continue
            if need.get(q, 0) < v:
                need[q] = v
        for q, v in need.items():
            if kn.get(q, 0) >= v:
                continue
            self.eng[e].wait_ge(self.sem[q], v)
            self.nwait += 1
            kn[q] = v
            s = self.snap.get((q, v))
            if s:
                for q2, v2 in s.items():
                    if kn.get(q2, 0) < v2:
                        kn[q2] = v2

    def _deps(self, reads, writes):
        ev = []
        for b in reads:
            ev.extend(b.w.items())
        for b in writes:
            ev.extend(b.w.items())
            ev.extend(b.r.items())
        return ev

    def op(self, e, fn, reads=(), writes=()):
        if e != "pe" and self.raw_only:
            raw = []
            for b in reads:
                for q, v in b.w.items():
                    if self.qeng.get(q) == e:
                        raw.append((q, v))
            if raw:
                self._wait(e, raw, same_ok=False)
            self._wait(e, self._deps(reads, writes), same_ok=True)
        else:
            self._wait(e, self._deps(reads, writes), same_ok=(e == "pe"))
        ins = fn(self.eng[e])
        key = self.cur[e]
        self.cnt[key] += 1
        n = self.cnt[key]
        ins.then_inc(self.sem[key], 1)
        self.nins += 1
        self.snap[(key, n)] = dict(self.known[e])
        for b in reads:
            b.r[key] = n
        for b in writes:
            b.w[key] = n
            b.r = {}
        if n >= self.SEM_LIMIT:
            self._new_epoch(e)

    def dma(self, e, out, in_, reads=(), writes=(), **kw):
        i = self.dnext
        self.dnext = (i + 1) % NDS
        q = "d%d" % i
        ev = self._deps(reads, writes)
        if self.cnt[q] > 0:
            ev.append((q, self.cnt[q]))
        self._wait(e, ev, same_ok=False)
        self.eng[e].dma_start(out=out, in_=in_, **kw).then_inc(self.sem[q], 16)
        self.cnt[q] += 16
        n = self.cnt[q]
        self.nins += 1
        self.snap[(q, n)] = dict(self.known[e])
        for b in reads:
            b.r[q] = n
        for b in writes:
            b.w[q] = n
            b.r = {}

    def finish(self):
        self._wait("sp", [(q, v) for q, v in self.cnt.items() if v > 0], same_ok=False)

    def act(self, out, in_, func, reads, writes, **kw):
        self.op("act", lambda e: e.activation(out=out, in_=in_, func=func, **kw), reads, writes)

    def mm(self, out, lhsT, rhs, start, stop, reads, writes):
        self.op("pe", lambda e: e.matmul(out, lhsT=lhsT, rhs=rhs, start=start, stop=stop), reads, writes)

    def tr(self, out, in_, ident, reads, writes):
        self.op("pe", lambda e: e.transpose(out, in_, ident), reads, writes)

    def ts(self, eng, out, in0, s1, s2, op0, op1, reads, writes):
        if s2 is None:
            self.op(eng, lambda e: e.tensor_scalar(out=out, in0=in0, scalar1=s1, scalar2=None, op0=op0), reads, writes)
        else:
            self.op(eng, lambda e: e.tensor_scalar(out=out, in0=in0, scalar1=s1, scalar2=s2, op0=op0, op1=op1), reads, writes)

    def tt(self, eng, out, in0, in1, op, reads, writes):
        self.op(eng, lambda e: e.tensor_tensor(out=out, in0=in0, in1=in1, op=op), reads, writes)

    def stt(self, eng, out, in0, s, in1, op0, op1, reads, writes):
        self.op(eng, lambda e: e.scalar_tensor_tensor(out=out, in0=in0, scalar=s, in1=in1, op0=op0, op1=op1), reads, writes)

    def cp(self, eng, out, in_, reads, writes):
        if eng == "act":
            self.op(eng, lambda e: e.activation(out=out, in_=in_, func=AF.Copy), reads, writes)
        else:
            self.op(eng, lambda e: e.tensor_copy(out=out, in_=in_), reads, writes)

    def rsqrt(self, ap, tile):
        self.op("act", lambda e: e.activation(out=ap, in_=ap, func=AF.Ln), [tile.b], [tile.b])
        self.op("act", lambda e: e.activation(out=ap, in_=ap, func=AF.Exp, scale=-0.5), [tile.b], [tile.b])

    def ms(self, eng, ap, val, writes):
        self.op(eng, lambda e: e.memset(ap, val), (), writes)


CW = 2688
C_ID = 0
C_SU = 128
C_IU = 192
C_SL = 256
C_BLK = 384
C_ONE = 512
C_TRI2 = 640
C_MB2 = 768
C_SEL0 = 896
C_SEL1 = 1024
C_RST = 1152
CWF = 1664
C_PAR0 = 1664
C_PAR1 = 2176
CB_PAR0 = 640
CB_PAR1 = 1152


def make_consts():
    c = np.zeros((128, CW), np.float32)
    c[:, 0:128] = np.eye(128, dtype=np.float32)
    s = np.arange(64)[:, None]
    t = np.arange(64)[None, :]
    c[0:64, C_SU:C_SU + 64] = (s < t)
    c[0:64, C_IU:C_IU + 64] = (s <= t)
    c[0:64, C_SL:C_SL + 64] = (s > t)
    c[0:64, C_BLK:C_BLK + 64] = 1.0
    c[64:128, C_BLK + 64:C_BLK + 128] = 1.0
    c[:, C_ONE:C_ONE + 128] = 1.0
    r = np.arange(128)[:, None]
    q = np.arange(128)[None, :]
    same = (r // 64) == (q // 64)
    c[:, C_TRI2:C_TRI2 + 128] = (same & (r <= q))
    c[:, C_MB2:C_MB2 + 128] = np.where(same & (r <= q), 0.0, -1.0e5)
    c[0:64, C_SEL0:C_SEL0 + 128] = 1.0
    c[64:128, C_SEL1:C_SEL1 + 128] = 1.0
    tk = np.arange(512)
    c[:, C_PAR0:C_PAR0 + 512] = ((tk // 64) % 2 == 0)[None, :]
    c[:, C_PAR1:C_PAR1 + 512] = ((tk // 64) % 2 == 1)[None, :]
    c[:, C_RST:C_RST + 512] = (tk % 64 != 0)[None, :]
    return c


class Builder:
    def __init__(self, n_layers=4, n_tiles=16, mixers="ABCD", dbg=False):
        self.NL = n_layers
        self.NT = n_tiles
        self.mixers = mixers
        self.dbg = dbg
        self.INV_DT = BF16
        self.wseq = None
        self.wrec = []
        self.wptr = 0
        self.wissued = {}
        self.prefetch = True
        self.rw_stop = 0
        self.nc = bass.Bass("TRN2", target_bir_lowering=False)
        self.p = Prog(self.nc)

    def declare(self):
        nc = self.nc
        di = lambda name, shape: nc.dram_tensor(name, list(shape), F32, kind="ExternalInput").ap()
        self.x = di("x", (SEQ, D))
        self.c = di("c", (8, 128))
        self.consts = di("consts", (128, CW))
        self.ada_w = di("ada_w", (4, D, 3 * D))
        self.ada_b = di("ada_b", (4, 3 * D))
        self.norm_pre = di("norm_pre", (4, D))
        self.norm_post = di("norm_post", (4, D))
        self.ev_w_in = di("ev_w_in", (2, D, EVEN_COLS))
        self.ev_w_out = di("ev_w_out", (2, D, D))
        self.tm_mu = di("tm_mu", (2, 2176))
        self.tm_w0 = di("tm_w0", (2, 512))
        self.tm_w2 = di("tm_w2", (2, 64, 512))
        self.tm_a0 = di("tm_a0", (2, 512))
        self.tm_a2 = di("tm_a2", (2, 64, 512))
        self.tm_k_k = di("tm_k_k", (2, 512))
        self.tm_k_a = di("tm_k_a", (2, 512))
        self.tm_r_k = di("tm_r_k", (2, 512))
        self.tm_lnx_g = di("tm_lnx_g", (2, 512))
        self.tm_lnx_b = di("tm_lnx_b", (2, 512))
        self.sc_conv_w = di("sc_conv_w", (2, 3, 512))
        self.od_w_in = di("od_w_in", (2, D, ODD_COLS))
        self.od_w_out = di("od_w_out", (2, D, D))
        self.cf_conv_w = di("cf_conv_w", (2, 31, 512))
        self.cf_conv_b = di("cf_conv_b", (2, 512))
        self.cf_ln_g = di("cf_ln_g", (2, 512))
        self.cf_ln_b = di("cf_ln_b", (2, 512))
        self.ssd_conv_w = di("ssd_conv_w", (2, 4, 1024))
        self.ssd_conv_b = di("ssd_conv_b", (2, 1024))
        self.ssd_dt_bias = di("ssd_dt_bias", (2, 8))
        self.ssd_a_log = di("ssd_a_log", (2, 8))
        self.ssd_d = di("ssd_d", (2, 8))
        self.ssd_norm_g = di("ssd_norm_g", (2, 512))
        self.out = nc.dram_tensor("out", [SEQ, D], F32, kind="ExternalOutput").ap()
        self.wblocks = []
        for L in range(self.NL):
            ncols = EVEN_COLS if L % 2 == 0 else ODD_COLS
            blks = []
            c0 = 0
            bi = 0
            while c0 < ncols + D:
                if c0 < ncols:
                    n = min(512, ncols - c0)
                    src = (self.ev_w_in if L % 2 == 0 else self.od_w_in)[L // 2][:, c0:c0 + n]
                else:
                    n = 512
                    src = (self.ev_w_out if L % 2 == 0 else self.od_w_out)[L // 2][:, c0 - ncols:c0 - ncols + n]
                sc = nc.dram_tensor("wsc_%d_%d" % (L, bi), [128, 8 * 512], BF16).ap()
                blks.append((sc, n, Buf(), src))
                c0 += n
                bi += 1
            self.wblocks.append(blks)

    def alloc(self):
        p = self.p
        self.cst = p.sb("cst", (128, CWF), F32)
        self.cstb = p.sb("cstb", (128, 1664), BF16)
        self.xs = [p.sb("xs0", (128, 4, D), F32)]
        self.xs.append(self.xs[0])
        self.xT = p.sb("xT", (128, 8, TT), F32)
        self.yo = p.sb("yo", (128, 8, TT + 3), F32)
        self.sq = p.sb("sq", (128, 8, TT), BF16, nb=8)
        self.rb = p.sb("rb", (128, TT), F32)
        self.wst = self.yo
        self.g2c = p.sb("g2c", (128, 4, 8), F32)
        self.npo = p.sb("npo", (128, 4, 8), F32)
        self.hT = p.sb("hT", (128, 8, TT), BF16)
        self.yT = p.sb("yT", (128, 8, TT), BF16, nb=8)
        self.wring = [p.sb("wr%d" % i, (128, 8, 512), BF16) for i in range(2)]
        self.wnext = 0
        self.ss = p.sb("ss", (128, 8), F32)
        self.rstd = p.sb("rstd", (128, 8), F32)
        self.g1c = p.sb("g1c", (128, 4, 8), F32)
        self.shc = p.sb("shc", (128, 4, 8), F32)
        self.cT = p.sb("cT", (128, 8), F32)
        self.abc = p.sb("abc", (128, 4, 24), F32)
        self.npc = p.sb("npc", (128, 4, 8), F32)
        self.F = [p.sb("F%d" % i, (128, 4, TT), F32) for i in range(6)]
        self.cfw = p.sb("cfw", (128, 2, 4, 31), F32)
        self.cfp = p.sb("cfp", (128, 2, 3, 4), F32)
        self.cf_tail = p.sb("cf_tail", (128, 2, 4, 30), F32)
        self.uext = p.sb("uext", (128, 4, 30 + TT), F32)
        self.sdw = p.sb("sdw", (128, 2, 8, 4), F32)
        self.sdb = p.sb("sdb", (128, 2, 8), F32)
        self.sd_tail = p.sb("sd_tail", (128, 2, 8, 3), F32)
        self.sdng = p.sb("sdng", (128, 2, 4), F32)
        self.hrow = p.sb("hrow", (128, 2, 3, 8), F32)
        self.xext = self.yo
        self.xbcT = p.sb("xbcT", (128, 8, TT), BF16)
        self.ctm = p.sb("ctm", (128, 2, 2, TT), BF16)
        self.S = p.sb("S", (128, 2, 8, 64), F32)
        self.Sb = p.sb("Sb", (128, 2, 8, 64), BF16, nb=2)
        self.tokx = p.sb("tokx", (128, 512), F32)
        self.xdt = p.sb("xdt", (128, 8, 64), BF16)
        self.xdd = p.sb("xdd", (128, 8, 64), BF16)
        self.btok = p.sb("btok", (128, 256), BF16)
        self.dts = p.sb("dts", (128, 6, 8), F32)
        self.cdb = p.sb("cdb", (128, 2, 8), F32)
        self.Dall = p.sb("Dall", (128, 8, 128), F32)
        self.dec = p.sb("dec", (128, 8, 128), F32)
        self.cbt = p.sb("cbt", (128, 2, 128), F32)
        self.MT = p.sb("MT", (128, 8, 128), BF16)
        self.ytok = p.sb("ytok", (128, 512), F32)

        self.tmp = p.sb("tmp", (128, D), F32)
        self.pp = [p.ps("pp%d" % i, (128, 512), F32) for i in range(4)]
        self.ppn = 0
        self.pt = [Tile(self.nc.alloc_psum_tensor("ptb%d" % i, [128, 1024], BF16)[:, 0:512]) for i in range(2)]
        pcb_ = self.nc.alloc_psum_tensor("pcb", [128, 512], F32)
        self.pcb = Tile(pcb_[:, 0:256].rearrange("p (g n) -> p g n", g=2))
        psm_ = self.nc.alloc_psum_tensor("psm", [128, 512], F32)
        self.psm = Tile(psm_[:, 0:64])
        self.pp6 = list(self.pp) + [Tile(pcb_[:, :]), Tile(psm_[:, :])]
        self.pp6[4].bufs = self.pcb.bufs
        self.pp6[5].bufs = self.psm.bufs
        self.pp6n = 0
        self.ptn = 0
        self.mu = p.sb("mu", (128, 2, 17), F32)
        self.omu = p.sb("omu", (128, 2, 17), F32)
        self.scw = p.sb("scw", (128, 2, 4, 3), F32)
        self.sc_tail = p.sb("sc_tail", (128, 2, 4, 2), F32)
        self.tm_prev = p.sb("tm_prev", (128, 2, 17), F32)
        self.tmc = p.sb("tmc", (128, 2, 7, 4), F32)
        self.tmk1 = p.sb("tmk1", (128, 2, 4), F32)
        self.w2a2 = p.sb("w2a2", (128, 2, 512), BF16)
        self.Hs = p.sb("Hs", (128, 2, 4, 64), F32)
        self.Hb = p.sb("Hb", (128, 4, 64), BF16)
        self.gl = p.sb("gl", (128, 4, 8), F32)

    def next_pp(self):
        t = self.pp[self.ppn]
        self.ppn = (self.ppn + 1) % len(self.pp)
        return t

    def next_pp6(self):
        t = self.pp6[self.pp6n]
        self.pp6n = (self.pp6n + 1) % len(self.pp6)
        return t

    def next_pt(self):
        t = self.pt[self.ptn]
        self.ptn = (self.ptn + 1) % len(self.pt)
        return t

    def dump(self, name, ap, tile, n):
        if not self.dbg:
            return
        p = self.p
        d = self.nc.dram_tensor("dbg_" + name, [128, n], F32, kind="ExternalOutput").ap()
        sc = self.tmp
        p.cp("dve", sc[:, 0:n], ap, list(tile.bufs), [sc.b])
        p.dma("sp", d, sc[:, 0:n], [sc.b], ())

    def prologue(self):
        p = self.p
        nc = self.nc
        p.dma("sp", self.cst[:], self.consts[:, 0:CWF], (), [self.cst.b])
        p.cp("dve", self.cstb[:, 0:640], self.cst[:, 0:640], [self.cst.b], [self.cstb.b])
        p.dma("sp", self.tmp[:], self.consts[:, C_PAR0:C_PAR0 + 1024], (), [self.tmp.b])
        p.cp("dve", self.cstb[:, 640:1664], self.tmp[:], [self.tmp.b], [self.cstb.b])
        self.ident_b = self.cstb[:, 0:128]
        self.ones_b = self.cstb[:, C_ONE:C_ONE + 128]
        self.ident_f = self.cst[:, 0:128]
        k = 0
        for L in range(self.NL):
            for (sc, n, buf, src) in self.wblocks[L]:
                p.dma("sp", self.wst[:, :, 0:n], src.rearrange("(kc p) n -> p kc n", p=128), (), [self.wst.b])
                wr = self.wring[k % 2]
                eng = ("dve", "act", "pool")[k % 3]
                if eng == "act":
                    p.act(wr[:, :, 0:n], self.wst[:, :, 0:n], AF.Copy, [self.wst.b], [wr.b])
                else:
                    p.cp(eng, wr[:, :, 0:n], self.wst[:, :, 0:n], [self.wst.b], [wr.b])
                p.dma("act", sc.rearrange("p (kc n) -> p kc n", kc=8)[:, :, 0:n], wr[:, :, 0:n], [wr.b], [buf])
                k += 1
        stg = self.tmp

        def cols(src_rows, R, W, dst_fn, dst_tile):
            for w0 in range(0, W, 1024):
                wn = min(1024, W - w0)
                p.dma("sp", stg[0:R, 0:wn], src_rows[:, w0:w0 + wn], (), [stg.b])
                for c_ in range(wn // 128):
                    ps = self.next_pp()
                    p.tr(ps[:, 0:R], stg[0:R, c_ * 128:(c_ + 1) * 128], self.cst[0:R, 0:R], [stg.b, self.cst.b], [ps.b])
                    p.cp("dve", dst_fn(w0 // 128 + c_), ps[:, 0:R], [ps.b], [dst_tile.b])

        cols(self.c, 8, 128, lambda cc: self.cT[:, :], self.cT)
        p.act(self.cT[:], self.cT[:], AF.Silu, [self.cT.b], [self.cT.b])
        cols(self.ada_b, 4, 3072, lambda cc: self.abc[:, :, cc], self.abc)
        cols(self.norm_pre, 4, 1024, lambda cc: self.npc[:, :, cc], self.npc)
        cols(self.norm_post, 4, 1024, lambda cc: self.npo[:, :, cc], self.npo)
        cols(self.tm_mu, 2, 2176, lambda cc: self.mu[:, :, cc], self.mu)
        p.ts("dve", self.omu[:], self.mu[:], -1.0, 1.0, ALU.mult, ALU.add, [self.mu.b], [self.omu.b])
        for l in range(2):
            for i_, src in enumerate((self.tm_w0, self.tm_a0, self.tm_k_k, self.tm_k_a, self.tm_r_k, self.tm_lnx_g, self.tm_lnx_b)):
                cols(src[l:l + 1, :], 1, 512, lambda cc, l=l, i_=i_: self.tmc[:, l, i_, cc:cc + 1], self.tmc)
            p.ts("dve", self.tmk1[:, l, :], self.tmc[:, l, 3, :], -1.0, 1.0, ALU.mult, ALU.add, [self.tmc.b], [self.tmk1.b])
            p.dma("sp", stg[0:64, 0:512], self.tm_w2[l], (), [stg.b])
            p.dma("sp", stg[64:128, 0:512], self.tm_a2[l], (), [stg.b])
            p.cp("dve", self.w2a2[:, l, :], stg[:, 0:512], [stg.b], [self.w2a2.b])
            cols(self.sc_conv_w[l], 3, 512, lambda cc, l=l: self.scw[:, l, cc, :], self.scw)
            cols(self.cf_conv_w[l], 31, 512, lambda cc, l=l: self.cfw[:, l, cc, :], self.cfw)
            for i_, src in enumerate((self.cf_conv_b, self.cf_ln_g, self.cf_ln_b)):
                cols(src[l:l + 1, :], 1, 512, lambda cc, l=l, i_=i_: self.cfp[:, l, i_, cc:cc + 1], self.cfp)
            cols(self.ssd_conv_w[l], 4, 1024, lambda cc, l=l: self.sdw[:, l, cc, :], self.sdw)
            cols(self.ssd_conv_b[l:l + 1, :], 1, 1024, lambda cc, l=l: self.sdb[:, l, cc:cc + 1], self.sdb)
            cols(self.ssd_norm_g[l:l + 1, :], 1, 512, lambda cc, l=l: self.sdng[:, l, cc:cc + 1], self.sdng)
            for i_, src in enumerate((self.ssd_dt_bias, self.ssd_a_log, self.ssd_d)):
                p.dma("sp", stg[0:1, 0:8], src[l:l + 1, :], (), [stg.b])
                ps = self.next_pp()
                p.mm(ps[:, 0:8], self.cst[0:1, C_ONE:C_ONE + 128], stg[0:1, 0:8], True, True, [self.cst.b, stg.b], [ps.b])
                p.cp("dve", self.hrow[:, l, i_, :], ps[:, 0:8], [ps.b], [self.hrow.b])
            p.act(self.hrow[:, l, 1, :], self.hrow[:, l, 1, :], AF.Exp, [self.hrow.b], [self.hrow.b])
            p.ts("dve", self.hrow[:, l, 1, :], self.hrow[:, l, 1, :], -1.0, None, ALU.mult, None, [self.hrow.b], [self.hrow.b])
        p.ms("dve", self.Hs[:], 0.0, [self.Hs.b])
        p.ms("dve", self.cf_tail[:], 0.0, [self.cf_tail.b])
        p.ms("dve", self.sd_tail[:], 0.0, [self.sd_tail.b])
        p.ms("dve", self.S[:], 0.0, [self.S.b])
        p.ms("dve", self.sc_tail[:], 0.0, [self.sc_tail.b])
        p.ms("dve", self.tm_prev[:], 0.0, [self.tm_prev.b])
        for L in range(self.NL):
            aw = self.ada_w[L]
            pcol = self.next_pp()
            for blk in range(4):
                p.dma("sp", self.wst[:, :, 0:512], aw[:, blk * 512:(blk + 1) * 512].rearrange("(kc p) n -> p kc n", p=128), (), [self.wst.b])
                for jj in range(4):
                    j = blk * 4 + jj
                    for kc in range(8):
                        p.mm(pcol[:, j:j + 1], self.wst[:, kc, jj * 128:(jj + 1) * 128], self.cT[:, kc:kc + 1], kc == 0, kc == 7,
                             [self.wst.b, self.cT.b], [pcol.b])
            p.tt("dve", self.shc[:, L, :], pcol[:, 0:8], self.abc[:, L, 0:8], ALU.add, [pcol.b, self.abc.b], [self.shc.b])
            p.tt("dve", self.g1c[:, L, :], pcol[:, 8:16], self.abc[:, L, 8:16], ALU.add, [pcol.b, self.abc.b], [self.g1c.b])
            p.stt("dve", self.g1c[:, L, :], self.g1c[:, L, :], 1.0, self.npc[:, L, :], ALU.add, ALU.mult, [self.g1c.b, self.npc.b], [self.g1c.b])
            pcol2 = self.next_pp()
            for blk in range(2):
                p.dma("sp", self.wst[:, :, 0:512], aw[:, 2 * D + blk * 512:2 * D + (blk + 1) * 512].rearrange("(kc p) n -> p kc n", p=128), (), [self.wst.b])
                for jj in range(4):
                    j = blk * 4 + jj
                    for kc in range(8):
                        p.mm(pcol2[:, j:j + 1], self.wst[:, kc, jj * 128:(jj + 1) * 128], self.cT[:, kc:kc + 1], kc == 0, kc == 7,
                             [self.wst.b, self.cT.b], [pcol2.b])
            p.tt("dve", self.g2c[:, L, :], pcol2[:, 0:8], self.abc[:, L, 16:24], ALU.add, [pcol2.b, self.abc.b], [self.g2c.b])
            p.tt("dve", self.g2c[:, L, :], self.g2c[:, L, :], self.npo[:, L, :], ALU.mult, [self.g2c.b, self.npo.b], [self.g2c.b])

    def _issue_w(self, k):
        L, bi = self.wseq[k]
        sc, n, buf, _ = self.wblocks[L][bi]
        wr = self.wring[k % len(self.wring)]
        self.p.dma("sp", wr[:, :, 0:n], sc.rearrange("p (kc n) -> p kc n", kc=8)[:, :, 0:n], [buf], [wr.b])
        self.wissued[k] = wr

    def load_w(self, L, bi):
        if self.wseq is None:
            self.wrec.append((L, bi))
            sc, n, buf, _ = self.wblocks[L][bi]
            wr = self.wring[self.wnext]
            self.wnext = (self.wnext + 1) % len(self.wring)
            self.p.dma("sp", wr[:, :, 0:n], sc.rearrange("p (kc n) -> p kc n", kc=8)[:, :, 0:n], [buf], [wr.b])
            return wr
        k = self.wptr
        assert self.wseq[k] == (L, bi), (k, self.wseq[k], (L, bi))
        if k not in self.wissued:
            self._issue_w(k)
        wr = self.wissued.pop(k)
        self.wptr += 1
        if self.wptr < len(self.wseq):
            self._issue_w(self.wptr)
        return wr

    def load_tile(self, t):
        p = self.p
        xs = self.xs[t % 2]
        for kc in range(8):
            ps = self.next_pp()
            for s in range(4):
                p.tr(ps[:, s * 128:(s + 1) * 128], xs[:, s, kc * 128:(kc + 1) * 128], self.ident_f, [xs.b, self.cst.b], [ps.b])
            if kc % 2 == 0:
                p.act(self.xT[:, kc, :], ps[:], AF.Copy, [ps.b], [self.xT.b])
            else:
                p.cp("dve", self.xT[:, kc, :], ps[:], [ps.b], [self.xT.b])

    def store_tile(self, t, orr):
        p = self.p
        xs = self.xs[t % 2]
        for s in range(4):
            for half in range(2):
                ps = self.next_pp()
                for q in range(4):
                    kc = half * 4 + q
                    p.tr(ps[:, q * 128:(q + 1) * 128], self.xT[:, kc, s * 128:(s + 1) * 128], self.ident_f, [self.xT.b, self.cst.b], [ps.b])
                if half == 0:
                    p.act(xs[:, s, 0:512], ps[:], AF.Copy, [ps.b], [xs.b])
                else:
                    p.cp("dve", xs[:, s, 512:1024], ps[:], [ps.b], [xs.b])
        p.dma("sp", orr[t], xs[:], [xs.b], ())

    def rms_bcast(self, src):
        p = self.p
        for kc in range(8):
            p.act(self.sq[:, kc, :], src[:, kc, 0:TT], AF.Square, [src.b], [self.sq.bufs[kc]])
        ps = self.next_pp()
        for kc in range(8):
            p.mm(ps[:], self.ones_b, self.sq[:, kc, :], kc == 0, kc == 7, [self.cstb.b, self.sq.bufs[kc]], [ps.b])
        p.ts("dve", self.rb[:], ps[:], 1.0 / D, 1e-6, ALU.mult, ALU.add, [ps.b], [self.rb.b])
        p.op("act", lambda e: e.activation(out=self.rb[:], in_=self.rb[:], func=AF.Ln), [self.rb.b], [self.rb.b])
        p.op("act", lambda e: e.activation(out=self.rb[:], in_=self.rb[:], func=AF.Exp, scale=-0.5), [self.rb.b], [self.rb.b])

    def pre_norm(self, L):
        p = self.p
        if L == 0 and not getattr(self, "_d0", False):
            self.dump("xT0", self.xT[:, 0, :], self.xT, 512)
            self.dump("g1c", self.g1c[:, 0, :], self.g1c, 8)
            self.dump("shc", self.shc[:, 0, :], self.shc, 8)
            self.dump("cT", self.cT[:], self.cT, 8)
            self.dump("npc", self.npc[:, 0, :], self.npc, 8)
            self.dump("abc", self.abc[:, 0, :], self.abc, 24)
            self.dump("xT7", self.xT[:, 7, :], self.xT, 512)
        self.rms_bcast(self.xT)
        if L == 0 and not getattr(self, "_d0", False):
            self._d0 = True
            self._d0b = True
            self.dump("rb0", self.rb[:], self.rb, 512)
            self.dump("sq0", self.sq[:, 0, :], self.sq, 512)
        for kc in range(8):
            p.tt("dve", self.tmp[:, 0:512], self.xT[:, kc, :], self.rb[:], ALU.mult, [self.xT.b, self.rb.b], [self.tmp.b])
            p.ts("dve", self.hT[:, kc, :], self.tmp[:, 0:512], self.g1c[:, L, kc:kc + 1], self.shc[:, L, kc:kc + 1], ALU.mult, ALU.add,
                 [self.tmp.b, self.g1c.b, self.shc.b], [self.hT.b])
        if getattr(self, "_d0b", False):
            self._d0b = False
            self.dump("hT", self.hT[:, 0, :], self.hT, 512)
            self.dump("hT7", self.hT[:, 7, :], self.hT, 512)

    def proj_fm(self, wr, lc, ps):
        p = self.p
        for kc in range(8):
            p.mm(ps[:], wr[:, kc, lc * 128:(lc + 1) * 128], self.hT[:, kc, :], kc == 0, kc == 7, [wr.b, self.hT.b], [ps.b])

    def out_proj_post(self, L):
        p = self.p
        nb = len(self.wblocks[L])
        allY = self.yT.bufs
        for half in range(2):
            wr = self.load_w(L, nb - 2 + half)
            for q in range(4):
                dmc = half * 4 + q
                ps = self.next_pp()
                for cc in range(8):
                    p.mm(ps[:], wr[:, cc, q * 128:(q + 1) * 128], self.yT[:, cc, :], cc == 0, cc == 7, [wr.b] + allY, [ps.b])
                p.act(self.yo[:, dmc, 0:TT], ps[:], AF.Copy, [ps.b], [self.yo.b])
        if not getattr(self, "_d1", False):
            self._d1 = True
            self.dump("yo0", self.yo[:, 0, 0:TT], self.yo, 512)
            self.dump("yo7", self.yo[:, 7, 0:TT], self.yo, 512)
        self.rms_bcast(self.yo)
        for dmc in range(8):
            p.stt("dve", self.tmp[:, 0:512], self.yo[:, dmc, 0:TT], self.g2c[:, L, dmc:dmc + 1], self.rb[:], ALU.mult, ALU.mult,
                  [self.yo.b, self.g2c.b, self.rb.b], [self.tmp.b])
            p.tt("dve", self.xT[:, dmc, :], self.xT[:, dmc, :], self.tmp[:, 0:512], ALU.add, [self.xT.b, self.tmp.b], [self.xT.b])

    def even_layer(self, L):
        p = self.p
        j = L // 2
        self.pre_norm(L)
        wrs = {}
        loaded = {}

        def get(ch):
            bi = ch // 4
            if bi not in loaded:
                loaded[bi] = self.load_w(L, bi)
            return (loaded[bi], ch % 4)

        self._cur_blk = None
        if "B" in self.mixers:
            self.short_conv_stream(L, j)
        else:
            for cc in range(4, 8):
                p.ms("dve", self.yT[:, cc, :], 0.0, [self.yT.bufs[cc]])
        if "A" in self.mixers:
            self.rwkv(L, j)
        else:
            for cc in range(4):
                p.ms("dve", self.yT[:, cc, :], 0.0, [self.yT.bufs[cc]])
        if not getattr(self, "_yd", False):
            self._yd = True
            self.dump("yT4", self.yT[:, 4, :], self.yT, 512)
        self.out_proj_post(L)

    def short_conv_stream(self, L, j):
        p = self.p
        F = self.F
        dest = {}
        for cc in range(4):
            dest[17 + cc] = (F[3], cc, AF.Copy)
            dest[21 + cc] = (F[0], cc, AF.Copy)
            dest[25 + cc] = (F[1], cc, AF.Copy)
            dest[29 + cc] = (F[4], cc, AF.Silu)
        cur_bi = None
        wr = None
        for ch in range(17, 33):
            bi = ch // 4
            if bi != cur_bi:
                wr = self.load_w(L, bi)
                cur_bi = bi
            ps = self.next_pp()
            self.proj_fm(wr, ch % 4, ps)
            t, cc, fn = dest[ch]
            if ch % 2 == 0:
                p.act(t[:, cc, :], ps[:], fn, [ps.b], [t.b])
            else:
                if fn == AF.Copy:
                    p.cp("dve", t[:, cc, :], ps[:], [ps.b], [t.b])
                else:
                    p.act(t[:, cc, :], ps[:], fn, [ps.b], [t.b])
        u = F[0]
        acc = F[2]
        w = self.scw
        tl = self.sc_tail
        p.tt("dve", u[:], F[0][:], F[1][:], ALU.mult, [F[0].b, F[1].b], [u.b])
        for cc in range(4):
            p.ts("dve", acc[:, cc, :], u[:, cc, :], w[:, j, cc, 2:3], None, ALU.mult, None, [u.b, w.b], [acc.b])
            p.stt("dve", acc[:, cc, 1:TT], u[:, cc, 0:TT - 1], w[:, j, cc, 1:2], acc[:, cc, 1:TT], ALU.mult, ALU.add, [u.b, w.b, acc.b], [acc.b])
            p.stt("dve", acc[:, cc, 2:TT], u[:, cc, 0:TT - 2], w[:, j, cc, 0:1], acc[:, cc, 2:TT], ALU.mult, ALU.add, [u.b, w.b, acc.b], [acc.b])
            p.stt("dve", acc[:, cc, 0:1], tl[:, j, cc, 1:2], w[:, j, cc, 1:2], acc[:, cc, 0:1], ALU.mult, ALU.add, [tl.b, w.b, acc.b], [acc.b])
            p.stt("dve", acc[:, cc, 0:2], tl[:, j, cc, 0:2], w[:, j, cc, 0:1], acc[:, cc, 0:2], ALU.mult, ALU.add, [tl.b, w.b, acc.b], [acc.b])
            p.cp("dve", tl[:, j, cc, :], u[:, cc, TT - 2:TT], [u.b, acc.b], [tl.b])
        p.tt("dve", acc[:], acc[:], F[3][:], ALU.mult, [acc.b, F[3].b], [acc.b])
        for cc in range(4):
            p.tt("dve", self.yT[:, 4 + cc, :], acc[:, cc, :], F[4][:, cc, :], ALU.mult, [acc.b, F[4].b], [self.yT.bufs[4 + cc]])

    def proj_to(self, L, ch, dst_ap, dst_bufs, fn=None, eng="act"):
        p = self.p
        bi = ch // 4
        if self._cur_blk != (L, bi):
            self._cur_wr = self.load_w(L, bi)
            self._cur_blk = (L, bi)
        ps = self.next_pp()
        self.proj_fm(self._cur_wr, ch % 4, ps)
        if eng == "act":
            p.act(dst_ap, ps[:], fn or AF.Copy, [ps.b], dst_bufs)
        else:
            p.cp(eng, dst_ap, ps[:], [ps.b], dst_bufs)

    def conformer(self, L, j):
        p = self.p
        F = self.F
        ue = self.uext
        p.cp("dve", ue[:, :, 0:30], self.cf_tail[:, j, :, :], [self.cf_tail.b], [ue.b])
        for cc in range(4):
            self.proj_to(L, cc, F[0][:, cc, :], [F[0].b], eng="dve")
        for cc in range(4):
            self.proj_to(L, 4 + cc, F[1][:, cc, :], [F[1].b], AF.Sigmoid)
        p.tt("dve", ue[:, :, 30:30 + TT], F[0][:], F[1][:], ALU.mult, [F[0].b, F[1].b], [ue.b])
        p.cp("dve", self.cf_tail[:, j, :, :], ue[:, :, TT:TT + 30], [ue.b], [self.cf_tail.b])
        for cc in range(4):
            self.proj_to(L, 8 + cc, F[1][:, cc, :], [F[1].b], AF.Silu)
        yield "proj"
        acc = F[2]
        w = self.cfw
        ub = Tile(self.sq[:].rearrange("p a b -> p (a b)")[:, 0:4 * (30 + TT)].rearrange("p (c t) -> p c t", c=4))
        ub.bufs = list(self.sq.bufs)
        p.cp("act", ub[:], ue[:, :, 0:30 + TT], [ue.b], list(ub.bufs))
        ring = []
        for c4 in range(4):
            for i_ in range(4):
                r_ = Tile(self.yT[:, 4 + c4, i_ * 128:(i_ + 1) * 128])
                ring.append(r_)
        n_ = 0
        for cc in range(4):
            ps = self.next_pp()
            for k in range(31):
                dg = ring[n_ % 16]
                p.ts("dve", dg[:], self.ident_b, w[:, j, cc, k:k + 1], None, ALU.mult, None, [self.cstb.b, w.b], [dg.b])
                p.mm(ps[:], dg[:], ub[:, cc, k:k + TT], k == 0, k == 30, [dg.b] + list(ub.bufs), [ps.b])
                n_ += 1
            p.act(acc[:, cc, :], ps[:], AF.Identity, [ps.b, self.cfp.b], [acc.b], bias=self.cfp[:, j, 0, cc:cc + 1])
            yield "cc"
        for c4 in range(4):
            p.ms("dve", self.yT[0:1, 4 + c4, 0:1], 0.0, [r_.b for r_ in ring[c4 * 4:(c4 + 1) * 4]] + [self.yT.bufs[4 + c4]])
        yield "conv"
        ones_f = self.cst[:, C_ONE:C_ONE + 128]
        pm = self.next_pp()
        for cc in range(4):
            p.mm(pm[:], ones_f, acc[:, cc, :], cc == 0, cc == 3, [self.cst.b, acc.b], [pm.b])
        for cc in range(4):
            p.act(F[0][:, cc, :], acc[:, cc, :], AF.Square, [acc.b], [F[0].b])
        pq = self.next_pp()
        for cc in range(4):
            p.mm(pq[:], ones_f, F[0][:, cc, :], cc == 0, cc == 3, [self.cst.b, F[0].b], [pq.b])
        mean = F[3][:, 0, :]
        var = F[3][:, 1, :]
        yield "stats"
        p.ts("dve", mean, pm[:], 1.0 / 512, None, ALU.mult, None, [pm.b], [F[3].b])
        p.tt("dve", F[3][:, 2, :], mean, mean, ALU.mult, [F[3].b], [F[3].b])
        p.stt("dve", var, pq[:], 1.0 / 512, F[3][:, 2, :], ALU.mult, ALU.subtract, [pq.b, F[3].b], [F[3].b])
        p.ts("dve", var, var, 1e-5, None, ALU.add, None, [F[3].b], [F[3].b])
        p.op("act", lambda e: e.activation(out=var, in_=var, func=AF.Ln), [F[3].b], [F[3].b])
        p.op("act", lambda e: e.activation(out=var, in_=var, func=AF.Exp, scale=-0.5), [F[3].b], [F[3].b])
        if not getattr(self, "_cfd", False):
            self._cfd = True
            self.dump("cf_hT", self.hT[:, 0, :], self.hT, 512)
            self.dump("cf_xT", self.xT[:, 0, :], self.xT, 512)
            self.dump("cf_g1c", self.g1c[:, 1, :], self.g1c, 8)
            self.dump("cf_shc", self.shc[:, 1, :], self.shc, 8)
            self.dump("cf_rb", self.rb[:], self.rb, 512)
            self.dump("cf_u", ue[:, 0, 30:30 + TT], ue, 512)
            self.dump("cf_acc", acc[:, 0, :], acc, 512)
            self.dump("cf_mean", mean, F[3], 512)
            self.dump("cf_rstd", var, F[3], 512)
            self.dump("cf_sg", F[1][:, 0, :], F[1], 512)
        for cc in range(4):
            p.tt("dve", F[0][:, cc, :], acc[:, cc, :], mean, ALU.subtract, [acc.b, F[3].b], [F[0].b])
            p.tt("dve", F[0][:, cc, :], F[0][:, cc, :], var, ALU.mult, [F[0].b, F[3].b], [F[0].b])
            p.act(F[0][:, cc, :], F[0][:, cc, :], AF.Silu, [F[0].b, self.cfp.b], [F[0].b],
                  scale=self.cfp[:, j, 1, cc:cc + 1], bias=self.cfp[:, j, 2, cc:cc + 1])
            p.tt("dve", self.yT[:, cc, :], F[0][:, cc, :], F[1][:, cc, :], ALU.mult, [F[0].b, F[1].b], [self.yT.bufs[cc]])

    def ssd(self, L, j):
        p = self.p
        F = self.F
        xe = self.xext
        zs = F[4]
        for cc in range(4):
            self.proj_to(L, 12 + cc, zs[:, cc, :], [zs.b], AF.Silu)
        p.cp("dve", xe[:, :, 0:3], self.sd_tail[:, j, :, :], [self.sd_tail.b], [xe.b])
        for cc in range(8):
            self.proj_to(L, 16 + cc, xe[:, cc, 3:3 + TT], [xe.b], eng="act")
        p.cp("dve", self.sd_tail[:, j, :, :], xe[:, :, TT:TT + 3], [xe.b], [self.sd_tail.b])
        wdt = self.load_w(L, 6)
        self._cur_blk = None
        yield "proj"
        w = self.sdw
        for cc in range(8):
            eng = "dve"
            acc = F[5][:, cc % 4, :]
            p.ts(eng, acc, xe[:, cc, 3:3 + TT], w[:, j, cc, 3:4], None, ALU.mult, None, [xe.b, w.b], [F[5].bufs[0]])
            for k in range(3):
                p.stt(eng, acc, xe[:, cc, k:k + TT], w[:, j, cc, k:k + 1], acc, ALU.mult, ALU.add, [xe.b, w.b, F[5].b], [F[5].b])
            p.act(self.xbcT[:, cc, :], acc, AF.Silu, [F[5].b, self.sdb.b], [self.xbcT.b], bias=self.sdb[:, j, cc:cc + 1])
            if cc % 2 == 1:
                yield "conv"
        for g in range(2):
            for e in range(2):
                par = self.cstb[:, (CB_PAR0 if e == 0 else CB_PAR1):(CB_PAR0 if e == 0 else CB_PAR1) + TT]
                p.tt("dve", self.ctm[:, g, e, :], self.xbcT[:, 6 + g, :], par, ALU.mult, [self.xbcT.b, self.cstb.b], [self.ctm.b])
        y2T = Tile(self.yo[:, 0:4, 0:TT])
        y2T.bufs = self.yo.bufs
        for s in range(4):
            tsl = slice(s * 128, (s + 1) * 128)
            for kc in range(8):
                p.mm(self.psm[:, 0:8], self.hT[:, kc, tsl], wdt[:, kc, 0:8], kc == 0, kc == 7, [self.hT.b, wdt.b], [self.psm.b])
            d = self.dts
            p.tt("dve", d[:, 0, :], self.psm[:, 0:8], self.hrow[:, j, 0, :], ALU.add, [self.psm.b, self.hrow.b], [d.b])
            p.act(d[:, 0, :], d[:, 0, :], AF.Exp, [d.b], [d.b])
            p.ts("dve", d[:, 0, :], d[:, 0, :], 1.0, None, ALU.add, None, [d.b], [d.b])
            p.act(d[:, 0, :], d[:, 0, :], AF.Ln, [d.b], [d.b])
            p.tt("dve", d[:, 1, :], d[:, 0, :], self.hrow[:, j, 1, :], ALU.mult, [d.b, self.hrow.b], [d.b])
            p.mm(self.psm[:, 8:16], self.cst[:, C_TRI2:C_TRI2 + 128], d[:, 1, :], True, True, [self.cst.b, d.b], [self.psm.b])
            p.mm(self.psm[:, 16:24], self.cst[:, C_BLK:C_BLK + 128], d[:, 1, :], True, True, [self.cst.b, d.b], [self.psm.b])
            p.mm(self.psm[:, 24:32], self.cst[:, C_SEL0:C_SEL0 + 128], d[:, 1, :], True, True, [self.cst.b, d.b], [self.psm.b])
            p.mm(self.psm[:, 32:40], self.cst[:, C_SEL1:C_SEL1 + 128], d[:, 1, :], True, True, [self.cst.b, d.b], [self.psm.b])
            p.cp("dve", d[:, 2, :], self.psm[:, 8:16], [self.psm.b], [d.b])
            p.tt("dve", d[:, 4, :], self.psm[:, 16:24], d[:, 2, :], ALU.subtract, [self.psm.b, d.b], [d.b])
            p.act(d[:, 4, :], d[:, 4, :], AF.Exp, [d.b], [d.b])
            p.act(d[:, 5, :], d[:, 2, :], AF.Exp, [d.b], [d.b])
            p.act(self.cdb[:, 0, :], self.psm[:, 24:32], AF.Exp, [self.psm.b], [self.cdb.b])
            p.act(self.cdb[:, 1, :], self.psm[:, 32:40], AF.Exp, [self.psm.b], [self.cdb.b])
            p.ts("dve", d[:, 3, :], d[:, 2, :], -1.0, None, ALU.mult, None, [d.b], [d.b])
            yield "st"
            pt = self.next_pt()
            for cc in range(4):
                p.tr(pt[:, cc * 128:(cc + 1) * 128], self.xbcT[:, cc, tsl], self.ident_b, [self.xbcT.b, self.cstb.b], [pt.b])
            p.cp("act", self.tokx[:], pt[:], [pt.b], [self.tokx.b])
            pt2 = self.next_pt()
            for cc in range(2):
                p.tr(pt2[:, cc * 128:(cc + 1) * 128], self.xbcT[:, 4 + cc, tsl], self.ident_b, [self.xbcT.b, self.cstb.b], [pt2.b])
            p.cp("act", self.btok[:], pt2[:, 0:256], [pt2.b], [self.btok.b])
            tx3 = self.tokx[:].rearrange("p (h q) -> p h q", h=8)
            p.tt("dve", self.xdt[:], tx3, d[:, 0, :].unsqueeze(2).to_broadcast([128, 8, 64]), ALU.mult, [self.tokx.b, d.b], [self.xdt.b])
            p.tt("dve", self.xdd[:], self.xdt[:], d[:, 4, :].unsqueeze(2).to_broadcast([128, 8, 64]), ALU.mult, [self.xdt.b, d.b], [self.xdd.b])
            yield "st"
            for g in range(2):
                p.mm(self.pcb[:, g, :], self.xbcT[:, 4 + g, tsl], self.xbcT[:, 6 + g, tsl], True, True, [self.xbcT.b], [self.pcb.b])
            p.cp("act", self.cbt[:], self.pcb[:], [self.pcb.b], [self.cbt.b])
            p.cp("dve", self.Dall[:], d[:, 1, :].unsqueeze(2).to_broadcast([128, 8, 128]), [d.b], [self.Dall.b])
            for hh in range(2):
                pd = self.next_pp()
                for q in range(4):
                    h = hh * 4 + q
                    p.mm(pd[:, q * 128:(q + 1) * 128], self.Dall[:, h, :], self.cst[:, C_TRI2:C_TRI2 + 128], True, False, [self.Dall.b, self.cst.b], [pd.b])
                    p.mm(pd[:, q * 128:(q + 1) * 128], self.cst[:, C_ID:C_ID + 128], self.cst[:, C_MB2:C_MB2 + 128], False, True, [self.cst.b], [pd.b])
                for q in range(4):
                    h = hh * 4 + q
                    p.act(self.dec[:, h, :], pd[:, q * 128:(q + 1) * 128], AF.Exp, [pd.b, d.b], [self.dec.b], bias=d[:, 3, h:h + 1])
            for g in range(2):
                p.tt("dve", self.MT[:, g * 4:(g + 1) * 4, :], self.dec[:, g * 4:(g + 1) * 4, :],
                     self.cbt[:, g, :].unsqueeze(1).to_broadcast([128, 4, 128]), ALU.mult, [self.dec.b, self.cbt.b], [self.MT.b])
            yield "st"
            pyd = self.next_pp()
            for h in range(8):
                p.mm(pyd[:, h * 64:(h + 1) * 64], self.MT[:, h, :], self.xdt[:, h, :], True, True, [self.MT.b, self.xdt.b], [pyd.b])
            yield "st"
            for e in range(2):
                rs = slice(e * 64, (e + 1) * 64)
                p.cp("act", self.Sb[:, e, :, :], self.S[:, j, :, :], [self.S.b], [self.Sb.bufs[e]])
                pst = self.next_pp()
                for g in range(2):
                    p.mm(pst[:, g * 256:(g + 1) * 256], self.btok[rs, g * 128:(g + 1) * 128],
                         self.xdd[rs, g * 4:(g + 1) * 4, :].rearrange("p h q -> p (h q)"), True, True, [self.btok.b, self.xdd.b], [pst.b])
                p.tt("dve", self.S[:, j, :, :], self.S[:, j, :, :], self.cdb[:, e, :].unsqueeze(2).to_broadcast([128, 8, 64]), ALU.mult,
                     [self.S.b, self.cdb.b], [self.S.b])
                p.tt("dve", self.S[:, j, :, :], self.S[:, j, :, :], pst[:].rearrange("p (h q) -> p h q", h=8), ALU.add, [self.S.b, pst.b], [self.S.b])
            pyo = self.next_pp()
            for g in range(2):
                for e in range(2):
                    p.mm(pyo[:, g * 256:(g + 1) * 256], self.ctm[:, g, e, tsl], self.Sb[:, e, g * 4:(g + 1) * 4, :].rearrange("p h q -> p (h q)"),
                         e == 0, e == 1, [self.ctm.b, self.Sb.bufs[e]], [pyo.b])
            yt = self.ytok
            yt3 = yt[:].rearrange("p (h q) -> p h q", h=8)
            p.tt("dve", yt3, pyo[:].rearrange("p (h q) -> p h q", h=8), d[:, 5, :].unsqueeze(2).to_broadcast([128, 8, 64]), ALU.mult, [pyo.b, d.b], [yt.b])
            p.tt("dve", yt[:], yt[:], pyd[:], ALU.add, [yt.b, pyd.b], [yt.b])
            p.tt("dve", self.tokx[:].rearrange("p (h q) -> p h q", h=8), tx3, self.hrow[:, j, 2, :].unsqueeze(2).to_broadcast([128, 8, 64]), ALU.mult,
                 [self.tokx.b, self.hrow.b], [self.tokx.b])
            p.tt("dve", yt[:], yt[:], self.tokx[:], ALU.add, [yt.b, self.tokx.b], [yt.b])
            yield "st"
            py = self.next_pp()
            for cc in range(4):
                p.tr(py[:, cc * 128:(cc + 1) * 128], yt[:, cc * 128:(cc + 1) * 128], self.ident_f, [yt.b, self.cst.b], [py.b])
            p.tt("dve", y2T[:, :, tsl], py[:].rearrange("p (c t) -> p c t", c=4), zs[:, :, tsl], ALU.mult, [py.b, zs.b], [y2T.b])
        yield "loop"
        for cc in range(4):
            p.act(self.sq[:, cc, :], y2T[:, cc, :], AF.Square, [y2T.b], [self.sq.bufs[cc]])
        pr = self.next_pp()
        for cc in range(4):
            p.mm(pr[:], self.ones_b, self.sq[:, cc, :], cc == 0, cc == 3, [self.cstb.b, self.sq.bufs[cc]], [pr.b])
        p.ts("dve", self.rb[:], pr[:], 1.0 / 512, 1e-6, ALU.mult, ALU.add, [pr.b], [self.rb.b])
        p.op("act", lambda e: e.activation(out=self.rb[:], in_=self.rb[:], func=AF.Ln), [self.rb.b], [self.rb.b])
        p.op("act", lambda e: e.activation(out=self.rb[:], in_=self.rb[:], func=AF.Exp, scale=-0.5), [self.rb.b], [self.rb.b])
        for cc in range(4):
            p.stt("dve", self.yT[:, 4 + cc, :], y2T[:, cc, :], self.sdng[:, j, cc:cc + 1], self.rb[:], ALU.mult, ALU.mult,
                  [y2T.b, self.sdng.b, self.rb.b], [self.yT.bufs[4 + cc]])

    def odd_layer(self, L):
        p = self.p
        j = L // 2
        self.pre_norm(L)
        self._cur_blk = None
        gens = []
        if "C" in self.mixers:
            gc = self.conformer(L, j)
            next(gc)
            gens.append(gc)
        else:
            for cc in range(4):
                p.ms("dve", self.yT[:, cc, :], 0.0, [self.yT.bufs[cc]])
        if "D" in self.mixers:
            gd = self.ssd(L, j)
            next(gd)
            gens.append(gd)
        else:
            for cc in range(4, 8):
                p.ms("dve", self.yT[:, cc, :], 0.0, [self.yT.bufs[cc]])
        while gens:
            for g_ in list(gens):
                try:
                    next(g_)
                except StopIteration:
                    gens.remove(g_)
        self.out_proj_post(L)

    def rwkv(self, L, j):
        p = self.p
        F = self.F
        LW = -0.6065306597126334
        INV = self.INV_DT
        cst = self.cst
        rT, kT, vT, sgT, bonT, aT = F[0], F[1], F[2], F[3], F[4], F[5]
        yob = self.yo[:].rearrange("p a b -> p (a b)").bitcast(BF16)
        AR = Tile(yob[:, 0:4096].rearrange("p (c n q d) -> p c n q d", c=4, n=8, q=2))
        AR.bufs = self.yo.bufs
        BK = Tile(yob[:, 4096:8192].rearrange("p (c n q d) -> p c n q d", c=4, n=8, q=2))
        BK.bufs = self.yo.bufs
        Vt, Bt, Kt = self.hT, self.sq, self.xbcT
        Ytok = Tile(self.xs[0][:].rearrange("p s (a d) -> p (s a) d", a=2))
        Ytok.bufs = self.xs[0].bufs
        if INV == F32:
            Wp = [self.Dall, self.dec]
            NTp = [Tile(self.uext[:, 0, 0:512].rearrange("p (h d) -> p h d", h=8)), Tile(self.uext[:, 1, 0:512].rearrange("p (h d) -> p h d", h=8))]
            Xs = Tile(self.uext[:, 2, 0:512].rearrange("p (h d) -> p h d", h=8))
        else:
            Wp = []
            for t_ in (self.Dall, self.dec):
                w_ = Tile(t_[:].rearrange("p h d -> p (h d)").bitcast(BF16)[:, 0:1024].rearrange("p (h d) -> p h d", h=8))
                w_.bufs = t_.bufs
                Wp.append(w_)
            NTp = [Tile(self.uext[:, i_, 0:512].bitcast(BF16)[:, 0:512].rearrange("p (h d) -> p h d", h=8)) for i_ in range(2)]
            Xs = Tile(self.uext[:, 2, 0:512].bitcast(BF16)[:, 0:512].rearrange("p (h d) -> p h d", h=8))
        for t_ in NTp + [Xs]:
            t_.bufs = self.uext.bufs
        Us = self.xdd
        Srb = Tile(self.Sb[:, 0, :, :])
        Srb.bufs = [self.Sb.bufs[0]]
        Sk = self.MT
        wdad = Tile(self.xdt[:].rearrange("p h d -> p (h d)"))
        wdad.bufs = self.xdt.bufs
        sig, cs = self.tokx, self.ytok
        e1 = self.tmp[:, 0:512]
        e2 = self.tmp[:, 512:1024]
        tmpb = self.tmp.b
        kk = self.rb
        mu, omu, prev = self.mu, self.omu, self.tm_prev
        tmc = self.tmc

        def shifted(ch, dst_ap, dst_tile):
            bi = ch // 4
            if self._cur_blk != (L, bi):
                self._cur_wr = self.load_w(L, bi)
                self._cur_blk = (L, bi)
            ps = self.next_pp()
            self.proj_fm(self._cur_wr, ch % 4, ps)
            p.act(dst_ap, ps[:], AF.Copy, [ps.b, omu.b], [dst_tile.b], scale=omu[:, j, ch:ch + 1])
            p.stt("dve", dst_ap[:, 1:TT], ps[:, 0:TT - 1], mu[:, j, ch:ch + 1], dst_ap[:, 1:TT], ALU.mult, ALU.add, [ps.b, mu.b, dst_tile.b], [dst_tile.b])
            p.stt("dve", dst_ap[:, 0:1], prev[:, j, ch:ch + 1], mu[:, j, ch:ch + 1], dst_ap[:, 0:1], ALU.mult, ALU.add, [prev.b, mu.b, dst_tile.b], [dst_tile.b])
            p.cp("dve", prev[:, j, ch:ch + 1], ps[:, TT - 1:TT], [ps.b], [prev.b])

        for cc in range(4):
            shifted(cc, rT[:, cc, :], rT)
        for cc in range(4):
            shifted(4 + cc, kT[:, cc, :], kT)
        for cc in range(4):
            shifted(8 + cc, vT[:, cc, :], vT)
        for cc in range(4):
            shifted(12 + cc, sgT[:, cc, :], sgT)
        shifted(16, e1, self.tmp)
        p.act(wdad[0:64, :], e1[0:64, :], AF.Tanh, [tmpb], [wdad.b])
        p.cp("dve", wdad[64:128, :], e1[64:128, :], [tmpb], [wdad.b])
        for cc in range(4):
            p.act(sgT[:, cc, :], sgT[:, cc, :], AF.Silu, [sgT.b], [sgT.b])
        if self.rw_stop == 1:
            for cc in range(4):
                p.ms("dve", self.yT[:, cc, :], 0.0, [self.yT.bufs[cc]])
            return
        for cc in range(4):
            csl = slice(cc * 128, (cc + 1) * 128)
            pz = self.next_pp()
            p.mm(pz[:], self.w2a2[0:64, j, csl], wdad[0:64, :], True, True, [self.w2a2.b, wdad.b], [pz.b])
            p.act(sig[:], pz[:], AF.Sigmoid, [pz.b, tmc.b], [sig.b], bias=tmc[:, j, 0, cc:cc + 1])
            pa = self.next_pp()
            p.mm(pa[:], self.w2a2[64:128, j, csl], wdad[64:128, :], True, True, [self.w2a2.b, wdad.b], [pa.b])
            p.act(aT[:, cc, :], pa[:], AF.Sigmoid, [pa.b, tmc.b], [aT.b], bias=tmc[:, j, 1, cc:cc + 1])
            p.op("dve", lambda e: e.tensor_tensor_scan(out=cs[:], data0=cst[:, C_RST:C_RST + TT], data1=sig[:], initial=0.0,
                                                        op0=ALU.mult, op1=ALU.add), [cst.b, sig.b], [cs.b])
            p.act(self.gl[:, cc, :], cs[:].rearrange("p (n d) -> p n d", d=64)[:, :, 63], AF.Exp, [cs.b], [self.gl.b], scale=LW)
            p.act(e1, cs[:], AF.Exp, [cs.b], [tmpb], scale=LW)
            p.tt("dve", AR[:, cc, :, 1, :], rT[:, cc, :].rearrange("p (n d) -> p n d", d=64), e1.rearrange("p (n d) -> p n d", d=64), ALU.mult,
                 [rT.b, tmpb], [AR.b])
            p.ts("dve", kk[:], kT[:, cc, :], tmc[:, j, 2, cc:cc + 1], None, ALU.mult, None, [kT.b, tmc.b], [kk.b])
            p.tt("dve", e2, kk[:], kk[:], ALU.mult, [kk.b], [tmpb])
            pn = self.next_pp()
            p.mm(pn[:], cst[:, C_BLK:C_BLK + 128], e2, True, True, [cst.b, tmpb], [pn.b])
            p.ts("dve", e2, pn[:], 1e-24, None, ALU.max, None, [pn.b], [tmpb])
            p.op("act", lambda e: e.activation(out=e2, in_=e2, func=AF.Ln), [tmpb], [tmpb])
            p.op("act", lambda e: e.activation(out=e2, in_=e2, func=AF.Exp, scale=-0.5), [tmpb], [tmpb])
            p.tt("dve", kk[:], kk[:], e2, ALU.mult, [kk.b, tmpb], [kk.b])
            p.tt("dve", e2, cs[:], sig[:], ALU.subtract, [cs.b, sig.b], [tmpb])
            p.act(e2, e2, AF.Exp, [tmpb], [tmpb], scale=LW)
            p.stt("dve", AR[:, cc, :, 0, :], kk[:].rearrange("p (n d) -> p n d", d=64), -1.0, e2.rearrange("p (n d) -> p n d", d=64),
                  ALU.mult, ALU.mult, [kk.b, tmpb], [AR.b])
            p.act(e1, cs[:], AF.Exp, [cs.b], [tmpb], scale=-LW)
            p.tt("dve", e2, kk[:], aT[:, cc, :], ALU.mult, [kk.b, aT.b], [tmpb])
            p.tt("dve", BK[:, cc, :, 0, :], e2.rearrange("p (n d) -> p n d", d=64), e1.rearrange("p (n d) -> p n d", d=64), ALU.mult, [tmpb], [BK.b])
            p.ts("dve", e2, aT[:, cc, :], tmc[:, j, 3, cc:cc + 1], self.tmk1[:, j, cc:cc + 1], ALU.mult, ALU.add, [aT.b, tmc.b, self.tmk1.b], [tmpb])
            p.tt("dve", kT[:, cc, :], kT[:, cc, :], e2, ALU.mult, [kT.b, tmpb], [kT.b])
            p.tt("dve", BK[:, cc, :, 1, :], kT[:, cc, :].rearrange("p (n d) -> p n d", d=64), e1.rearrange("p (n d) -> p n d", d=64), ALU.mult,
                 [kT.b, tmpb], [BK.b])
            p.stt("dve", e2, rT[:, cc, :], tmc[:, j, 4, cc:cc + 1], kT[:, cc, :], ALU.mult, ALU.mult, [rT.b, tmc.b, kT.b], [tmpb])
            pbn = self.next_pp()
            p.mm(pbn[:], cst[:, C_BLK:C_BLK + 128], e2, True, True, [cst.b, tmpb], [pbn.b])
            p.tt("dve", bonT[:, cc, :], pbn[:], vT[:, cc, :], ALU.mult, [pbn.b, vT.b], [bonT.b])
        if self.rw_stop == 2:
            for cc in range(4):
                p.ms("dve", self.yT[:, cc, :], 0.0, [self.yT.bufs[cc]])
            return
        vb = Tile(self.ctm[:].rearrange("p g e t -> p (g e) t"))
        vb.bufs = self.ctm.bufs
        p.cp("act", vb[:], vT[:], [vT.b], [vb.b])
        for c in range(8):
            tsl = slice(c * 64, (c + 1) * 64)
            for ii, (dst, getsrc, srct) in enumerate(((Vt, lambda cc: vb[:, cc, tsl], vb), (Bt, lambda cc: BK[:, cc, c, 0, :], BK), (Kt, lambda cc: BK[:, cc, c, 1, :], BK))):
                if self.rw_stop == 31 and ii > 0:
                    continue
                if self.rw_stop == 33:
                    continue
                if self.rw_stop == 32 and ii != 1:
                    continue
                pt = self.next_pp()
                for cc in range(4):
                    p.mm(pt[0:64, cc * 128:(cc + 1) * 128], getsrc(cc), self.ident_b, True, True, [srct.b, self.cstb.b], [pt.b])
                if c % 2 == 0:
                    p.cp("act", dst[0:64, c, :], pt[0:64, :], [pt.b], [dst.b] if dst is not self.sq else list(self.sq.bufs))
                else:
                    p.cp("dve", dst[0:64, c, :], pt[0:64, :], [pt.b], [dst.b] if dst is not self.sq else list(self.sq.bufs))
        if self.rw_stop in (3, 31, 32, 33):
            for cc in range(4):
                p.ms("dve", self.yT[:, cc, :], 0.0, [self.yT.bufs[cc]])
            return
        Btb = list(self.sq.bufs)
        p.cp("act", self.Hb[:], self.Hs[:, j, :, :], [self.Hs.b], [self.Hb.b])
        msk_su = cst[0:64, C_SU:C_SU + 64]
        msk_iu = cst[0:64, C_IU:C_IU + 64]
        msk_sl = cst[0:64, C_SL:C_SL + 64]
        idn = cst[0:64, 0:64]
        bc = lambda ap, n: ap.unsqueeze(1).to_broadcast([64, n, 64])
        npp = self.next_pp6
        SrbP = [Tile(self.Sb[:, 0, :, :]), Tile(self.Sb[:, 1, :, :])]
        SrbP[0].bufs = [self.Sb.bufs[0]]
        SrbP[1].bufs = [self.Sb.bufs[1]]
        SkP = [self.MT, Tile(self.Dall[:].rearrange("p h d -> p (h d)").bitcast(BF16)[:, 1024:2048].rearrange("p (h d) -> p h d", h=8))]
        TtP = [Tile(self.cbt[:].rearrange("p g n -> p (g n)").bitcast(BF16)[:, 0:512].rearrange("p (h d) -> p h d", h=8)),
               Tile(self.dec[:].rearrange("p h d -> p (h d)").bitcast(BF16)[:, 1024:1536].rearrange("p (h d) -> p h d", h=8))]
        TtP[0].bufs = self.cbt.bufs

        def prep_units(c):
            Srb, Sk, Tt = SrbP[c % 2], SkP[c % 2], TtP[c % 2]
            W0, NT0 = Wp[0], NTp[0]
            units = []

            def scores(g):
                hs = slice(g * 4, (g + 1) * 4)
                rows = slice(g * 64, (g + 1) * 64)
                Pb, Pk, Pa = npp(), npp(), npp()
                for q in range(4):
                    ar = AR[rows, q, c, :, :].rearrange("p q d -> p (q d)")
                    p.mm(Pb[0:64, q * 128:(q + 1) * 128], BK[rows, q, c, 0, :], ar, True, True, [BK.b, AR.b], [Pb.b])
                    p.mm(Pk[0:64, q * 128:(q + 1) * 128], BK[rows, q, c, 1, :], ar, True, True, [BK.b, AR.b], [Pk.b])
                    p.mm(Pa[0:64, q * 64:(q + 1) * 64], AR[rows, q, c, 0, :], BK[rows, q, c, 0, :], True, True, [BK.b, AR.b], [Pa.b])
                Pb3 = Pb[0:64, :].rearrange("p (h d) -> p h d", h=4)
                Pk3 = Pk[0:64, :].rearrange("p (h d) -> p h d", h=4)
                p.tt("dve", W0[0:64, hs, 0:64], Pb3[:, :, 0:64], bc(msk_su, 4), ALU.mult, [Pb.b, cst.b], [W0.b])
                p.tt("dve", Srb[0:64, hs, :], Pb3[:, :, 64:128], bc(msk_iu, 4), ALU.mult, [Pb.b, cst.b], [Srb.b])
                p.tt("dve", Sk[0:64, hs, 0:64], Pk3[:, :, 0:64], bc(msk_su, 4), ALU.mult, [Pk.b, cst.b], [Sk.b])
                p.tt("dve", Sk[0:64, hs, 64:128], Pk3[:, :, 64:128], bc(msk_iu, 4), ALU.mult, [Pk.b, cst.b], [Sk.b])
                p.tt("dve", NT0[0:64, hs, :], Pa[0:64, 0:256].rearrange("p (h d) -> p h d", h=4), bc(msk_sl, 4), ALU.mult, [Pa.b, cst.b], [NT0.b])

            def level(lvl, g):
                cur = lvl % 2
                Wc, NTc = Wp[cur], NTp[cur]
                Wn, NTn = Wp[1 - cur], NTp[1 - cur]
                last = (lvl == 5)
                hs = slice(g * 4, (g + 1) * 4)
                P1 = npp()
                if not last:
                    P2 = npp()
                for q in range(4):
                    h = g * 4 + q
                    if lvl == 0:
                        p.mm(P1[0:64, q * 128:q * 128 + 64], NTc[0:64, h, :], Wc[0:64, h, 0:64], True, True, [NTc.b, Wc.b], [P1.b])
                    elif not last:
                        p.mm(P1[0:64, q * 128:(q + 1) * 128], NTc[0:64, h, :], Wc[0:64, h, :], True, True, [NTc.b, Wc.b], [P1.b])
                    else:
                        p.mm(P1[0:64, q * 128 + 64:(q + 1) * 128], NTc[0:64, h, :], Wc[0:64, h, 64:128], True, True, [NTc.b, Wc.b], [P1.b])
                    if not last:
                        p.mm(P2[0:64, q * 64:(q + 1) * 64], Wc[0:64, h, 0:64], NTc[0:64, h, :], True, True, [NTc.b, Wc.b], [P2.b])
                P13 = P1[0:64, :].rearrange("p (h d) -> p h d", h=4)
                if not last:
                    p.cp("act", Wn[0:64, hs, 0:64], P13[:, :, 0:64], [P1.b], [Wn.b])
                    p.cp("act", NTn[0:64, hs, :], P2[0:64, 0:256].rearrange("p (h d) -> p h d", h=4), [P2.b], [NTn.b])
                if lvl == 0:
                    p.tt("dve", Wn[0:64, hs, 64:128], Wc[0:64, hs, 0:64], bc(idn, 4), ALU.add, [Wc.b, cst.b], [Wn.b])
                elif not last:
                    p.tt("dve", Wn[0:64, hs, 64:128], P13[:, :, 64:128], Wc[0:64, hs, 64:128], ALU.add, [P1.b, Wc.b], [Wn.b])
                else:
                    p.tt("dve", Tt[0:64, hs, :], P13[:, :, 64:128], Wc[0:64, hs, 64:128], ALU.add, [P1.b, Wc.b], [Tt.b])

            for g in range(2):
                units.append(lambda g=g: scores(g))
            for lvl in range(6):
                for g in range(2):
                    units.append(lambda lvl=lvl, g=g: level(lvl, g))
            return units

        def chain_units(c):
            Srb, Sk, Tt = SrbP[c % 2], SkP[c % 2], TtP[c % 2]
            st = {}

            def hterm(PS, qd):
                for q in range(4):
                    p.mm(PS[0:64, q * 64:(q + 1) * 64], AR[64:128, q, c, qd, :], self.Hb[64:128, q, :], True, True, [AR.b, self.Hb.b], [PS.b])

            def ux():
                PX, PXo = npp(), npp()
                hterm(PXo, 0)
                for hh in range(8):
                    g, q = hh // 4, hh % 4
                    h = 2 * q + g
                    o = PX[0:64, hh * 64:(hh + 1) * 64]
                    if g == 0:
                        p.mm(o, AR[0:64, q, c, 0, :], self.Hb[0:64, q, :], True, False, [AR.b, self.Hb.b], [PX.b])
                    p.mm(o, Sk[0:64, hh, 0:64], Vt[0:64, c, h * 64:(h + 1) * 64], g == 1, True, [Sk.b, Vt.b], [PX.b])
                p.cp("act", Xs[0:64, 0:4, :], PX[0:64, 0:256].rearrange("p (h d) -> p h d", h=4), [PX.b], [Xs.b])
                p.cp("act", Xs[0:64, 4:8, :], PXo[0:64, 0:256].rearrange("p (h d) -> p h d", h=4), [PXo.b], [Xs.b])
                p.tt("dve", Xs[0:64, 4:8, :], Xs[0:64, 4:8, :], PX[0:64, 256:512].rearrange("p (h d) -> p h d", h=4), ALU.add, [Xs.b, PX.b], [Xs.b])

            def uu():
                PU = npp()
                for hh in range(8):
                    p.mm(PU[0:64, hh * 64:(hh + 1) * 64], Tt[0:64, hh, :], Xs[0:64, hh, :], True, True, [Tt.b, Xs.b], [PU.b])
                p.cp("act", Us[0:64, :, :].rearrange("p (q g) d -> p g q d", g=2), PU[0:64, :].rearrange("p (g q d) -> p g q d", g=2, q=4), [PU.b], [Us.b])

            def uy():
                PY, PYo = npp(), npp()
                hterm(PYo, 1)
                for hh in range(8):
                    g, q = hh // 4, hh % 4
                    h = 2 * q + g
                    o = PY[0:64, hh * 64:(hh + 1) * 64]
                    if g == 0:
                        p.mm(o, AR[0:64, q, c, 1, :], self.Hb[0:64, q, :], True, False, [AR.b, self.Hb.b], [PY.b])
                    p.mm(o, Srb[0:64, hh, :], Us[0:64, h, :], g == 1, False, [Srb.b, Us.b], [PY.b])
                    p.mm(o, Sk[0:64, hh, 64:128], Vt[0:64, c, h * 64:(h + 1) * 64], False, True, [Sk.b, Vt.b], [PY.b])
                Yc = Ytok[0:64, c, :].rearrange("p (q g d) -> p g q d", g=2, d=64)
                p.cp("dve", Yc, PY[0:64, :].rearrange("p (g q d) -> p g q d", g=2, q=4), [PY.b], [Ytok.b])
                p.tt("dve", Yc[:, 1, :, :], Yc[:, 1, :, :], PYo[0:64, 0:256].rearrange("p (q d) -> p q d", q=4), ALU.add, [Ytok.b, PYo.b], [Ytok.b])

            def uh():
                PH = npp()
                for cc in range(4):
                    o = PH[:, cc * 128:(cc + 1) * 128]
                    p.mm(o, Bt[0:64, c, cc * 128:(cc + 1) * 128], Us[0:64, 2 * cc:2 * cc + 2, :].rearrange("p h d -> p (h d)"), True, False, Btb + [Us.b], [PH.b])
                    p.mm(o, Kt[0:64, c, cc * 128:(cc + 1) * 128], Vt[0:64, c, cc * 128:(cc + 1) * 128], False, True, [Kt.b, Vt.b], [PH.b])
                PH3 = PH[:].rearrange("p (c d) -> p c d", c=4)
                for hl in range(2):
                    rws = slice(hl * 64, (hl + 1) * 64)
                    p.tt("dve", self.Hs[rws, j, :, :], self.Hs[rws, j, :, :], PH3[rws, :, hl * 64:(hl + 1) * 64], ALU.add, [self.Hs.b, PH.b], [self.Hs.b])
                    p.tt("dve", self.Hs[rws, j, :, :], self.Hs[rws, j, :, :], self.gl[rws, :, c].unsqueeze(2).to_broadcast([64, 4, 64]), ALU.mult,
                         [self.Hs.b, self.gl.b], [self.Hs.b])
                p.cp("act", self.Hb[:], self.Hs[:, j, :, :], [self.Hs.b], [self.Hb.b])

            return [ux, uu, uy, uh]

        prev_chain = []
        for c in range(9):
            pu = prep_units(c) if c < 8 else []
            cu = prev_chain
            i_ = j_ = 0
            while i_ < len(pu) or j_ < len(cu):
                for _ in range(4):
                    if i_ < len(pu):
                        pu[i_]()
                        i_ += 1
                if j_ < len(cu):
                    cu[j_]()
                    j_ += 1
            prev_chain = chain_units(c) if c < 8 else []
        if self.rw_stop in (4, 5, 6):
            for cc in range(4):
                p.ms("dve", self.yT[:, cc, :], 0.0, [self.yT.bufs[cc]])
            return
        Y3 = Ytok[0:64, :, :].rearrange("p c (h d) -> p (c h) d", d=64)
        st = Tile(self.uext[:, 3, 0:512])
        st.bufs = self.uext.bufs
        s1 = st[0:64, 0:64]
        s2 = st[0:64, 64:128]
        s3 = st[0:64, 128:192]
        p.op("dve", lambda e: e.reduce_sum(out=s1, in_=Y3, axis=mybir.AxisListType.X), [Ytok.b], [st.b])
        sqt = Tile(self.F[5][:].rearrange("p c t -> p (c t)")[:, 0:2048].rearrange("p (a b) -> p a b", b=64))
        sqt.bufs = self.F[5].bufs
        for hf in range(2):
            ysl = Y3[:, hf * 32:(hf + 1) * 32, :]
            p.tt("dve", sqt[0:64, :, :], ysl, ysl, ALU.mult, [Ytok.b], [sqt.b])
            p.op("dve", lambda e, hf=hf: e.reduce_sum(out=s2[:, hf * 32:(hf + 1) * 32], in_=sqt[0:64, :, :], axis=mybir.AxisListType.X), [sqt.b], [st.b])
        p.ts("dve", s1, s1, 1.0 / 64, None, ALU.mult, None, [st.b], [st.b])
        p.tt("dve", s3, s1, s1, ALU.mult, [st.b], [st.b])
        p.stt("dve", s2, s2, 1.0 / 64, s3, ALU.mult, ALU.subtract, [st.b], [st.b])
        p.ts("dve", s2, s2, 64e-5, None, ALU.add, None, [st.b], [st.b])
        p.op("act", lambda e: e.activation(out=s2, in_=s2, func=AF.Ln), [st.b], [st.b])
        p.op("act", lambda e: e.activation(out=s2, in_=s2, func=AF.Exp, scale=-0.5), [st.b], [st.b])
        p.tt("dve", Y3, Y3, s1.unsqueeze(2).to_broadcast([64, 64, 64]), ALU.subtract, [Ytok.b, st.b], [Ytok.b])
        p.tt("dve", Y3, Y3, s2.unsqueeze(2).to_broadcast([64, 64, 64]), ALU.mult, [Ytok.b, st.b], [Ytok.b])
        for cc in range(4):
            py = self.next_pp()
            for c in range(8):
                p.tr(py[:, c * 64:(c + 1) * 64], Ytok[0:64, c, cc * 128:(cc + 1) * 128], cst[0:64, 0:64], [Ytok.b, cst.b], [py.b])
            p.act(e1, py[:], AF.Identity, [py.b, tmc.b], [tmpb], scale=tmc[:, j, 5, cc:cc + 1], bias=tmc[:, j, 6, cc:cc + 1])
            p.tt("dve", e1, e1, bonT[:, cc, :], ALU.add, [tmpb, bonT.b], [tmpb])
            p.tt("dve", self.yT[:, cc, :], e1, sgT[:, cc, :], ALU.mult, [tmpb, sgT.b], [self.yT.bufs[cc]])

    def build(self):
        if self.prefetch and self.wseq is None and not getattr(self, "_recording", False):
            rec = Builder(self.NL, self.NT, self.mixers, False)
            rec._recording = True
            rec.rw_stop = self.rw_stop
            rec.build()
            self.wseq = rec.wrec
        p = self.p
        self.declare()
        self.alloc()
        self.prologue()
        xr = self.x.rearrange("(t s p) d -> t p s d", p=128, s=4)
        orr = self.out.rearrange("(t s p) d -> t p s d", p=128, s=4)
        for t in range(self.NT):
            p.dma("sp", self.xs[0][:], xr[t], (), [self.xs[0].b])
            self.load_tile(t)
            for L in range(self.NL):
                if L % 2 == 0:
                    self.even_layer(L)
                else:
                    self.odd_layer(L)
            self.store_tile(t, orr)
        p.finish()
        return self.nc


def make_in_maps(inputs):
    consts = make_consts()
    maps = []
    for core in range(8):
        b = core % 4
        m = {}
        for k, v in inputs.items():
            v = np.asarray(v)
            if k == "x":
                m[k] = np.ascontiguousarray(v[b])
            elif k == "c":
                m[k] = np.ascontiguousarray(v[b].reshape(8, 128))
            elif k in ("tm_k_k", "tm_k_a", "tm_r_k"):
                m[k] = np.ascontiguousarray(v.reshape(2, 512))
            else:
                m[k] = np.ascontiguousarray(v)
        m["consts"] = consts
        maps.append(m)
    return maps


def kernel(**inputs):
    import os
    bld = Builder(mixers=os.environ.get("K_MIXERS", "ABCD"))
    bld.rw_stop = int(os.environ.get("K_RWSTOP", "0"))
    nc = bld.build()
    maps = make_in_maps(inputs)
    res = run_bass_kernel_spmd(nc, maps, core_ids=list(range(8)))
    out = np.stack([res.results[b]["out"] for b in range(4)], axis=0)
    return out.astype(np.float32)
```

```python
import numpy as np
import concourse.bass as bass
import concourse.mybir as mybir
from concourse.bass_utils import run_bass_kernel_spmd

F32 = mybir.dt.float32
BF16 = mybir.dt.bfloat16
AF = mybir.ActivationFunctionType
ALU = mybir.AluOpType

D = 1024
SEQ = 8192
TT = 512
NDS = 24
EVEN_COLS = 4224
ODD_COLS = 3080


class Buf:
    __slots__ = ("w", "r")

    def __init__(self):
        self.w = {}
        self.r = {}


class Tile:
    def __init__(self, h, nb=1):
        self.h = h
        self.bufs = [Buf() for _ in range(nb)]

    def __getitem__(self, k):
        return self.h[k]

    @property
    def b(self):
        return self.bufs[0]


class Prog:
    def __init__(self, nc):
        self.nc = nc
        self.eng = {"pe": nc.tensor, "act": nc.scalar, "dve": nc.vector, "pool": nc.gpsimd, "sp": nc.sync}
        self.sem = {}
        self.cnt = {}
        self.cur = {}
        self.qeng = {}
        self.epochs = {}
        for q in ("pe", "act", "dve", "pool"):
            self._new_epoch(q)
        for i in range(NDS):
            self.sem["d%d" % i] = nc.alloc_semaphore("d%d" % i)
            self.cnt["d%d" % i] = 0
        self.dnext = 0
        self.known = {e: {} for e in self.eng}
        self.snap = {}
        self.nwait = 0
        self.nins = 0
        import os
        self.raw_only = os.environ.get("K_RAWONLY", "0") == "1"

    SEM_LIMIT = 30000

    def _new_epoch(self, e):
        k = self.epochs.get(e, 0)
        self.epochs[e] = k + 1
        key = "%s#%d" % (e, k)
        self.sem[key] = self.nc.alloc_semaphore("s_%s_%d" % (e, k))
        self.cnt[key] = 0
        self.cur[e] = key
        self.qeng[key] = e

    def sb(self, name, shape, dt, nb=1):
        return Tile(self.nc.alloc_sbuf_tensor(name, list(shape), dt), nb)

    def ps(self, name, shape, dt=F32, nb=1):
        return Tile(self.nc.alloc_psum_tensor(name, list(shape), dt), nb)

    def _wait(self, e, evs, same_ok=True):
        kn = self.known[e]
        need = {}
        for (q, v) in evs:
            if same_ok and self.qeng.get(q) == e:
                continue
            if kn.get(q, 0) >= v:
                continue
            if need.get(q, 0) < v:
                need[q] = v
        for q, v in need.items():
            if kn.get(q, 0) >= v:
                continue
            self.eng[e].wait_ge(self.sem[q], v)
            self.nwait += 1
            kn[q] = v
            s = self.snap.get((q, v))
            if s:
                for q2, v2 in s.items():
                    if kn.get(q2, 0) < v2:
                        kn[q2] = v2

    def _deps(self, reads, writes):
        ev = []
        for b in reads:
            ev.extend(b.w.items())
        for b in writes:
            ev.extend(b.w.items())
            ev.extend(b.r.items())
        return ev

    def op(self, e, fn, reads=(), writes=()):
        if e != "pe" and self.raw_only:
            raw = []
            for b in reads:
                for q, v in b.w.items():
                    if self.qeng.get(q) == e:
                        raw.append((q, v))
            if raw:
                self._wait(e, raw, same_ok=False)
            self._wait(e, self._deps(reads, writes), same_ok=True)
        else:
            self._wait(e, self._deps(reads, writes), same_ok=(e == "pe"))
        ins = fn(self.eng[e])
        key = self.cur[e]
        self.cnt[key] += 1
        n = self.cnt[key]
        ins.then_inc(self.sem[key], 1)
        self.nins += 1
        self.snap[(key, n)] = dict(self.known[e])
        for b in reads:
            b.r[key] = n
        for b in writes:
            b.w[key] = n
            b.r = {}
        if n >= self.SEM_LIMIT:
            self._new_epoch(e)

    def dma(self, e, out, in_, reads=(), writes=(), **kw):
        i = self.dnext
        self.dnext = (i + 1) % NDS
        q = "d%d" % i
        ev = self._deps(reads, writes)
        if self.cnt[q] > 0:
            ev.append((q, self.cnt[q]))
        self._wait(e, ev, same_ok=False)
        self.eng[e].dma_start(out=out, in_=in_, **kw).then_inc(self.sem[q], 16)
        self.cnt[q] += 16
        n = self.cnt[q]
        self.nins += 1
        self.snap[(q, n)] = dict(self.known[e])
        for b in reads:
            b.r[q] = n
        for b in writes:
            b.w[q] = n
            b.r = {}

    def finish(self):
        self._wait("sp", [(q, v) for q, v in self.cnt.items() if v > 0], same_ok=False)

    def act(self, out, in_, func, reads, writes, **kw):
        self.op("act", lambda e: e.activation(out=out, in_=in_, func=func, **kw), reads, writes)

    def mm(self, out, lhsT, rhs, start, stop, reads, writes):
        self.op("pe", lambda e: e.matmul(out, lhsT=lhsT, rhs=rhs, start=start, stop=stop), reads, writes)

    def tr(self, out, in_, ident, reads, writes):
        self.op("pe", lambda e: e.transpose(out, in_, ident), reads, writes)

    def ts(self, eng, out, in0, s1, s2, op0, op1, reads, writes):
        if s2 is None:
            self.op(eng, lambda e: e.tensor_scalar(out=out, in0=in0, scalar1=s1, scalar2=None, op0=op0), reads, writes)
        else:
            self.op(eng, lambda e: e.tensor_scalar(out=out, in0=in0, scalar1=s1, scalar2=s2, op0=op0, op1=op1), reads, writes)

    def tt(self, eng, out, in0, in1, op, reads, writes):
        self.op(eng, lambda e: e.tensor_tensor(out=out, in0=in0, in1=in1, op=op), reads, writes)

    def stt(self, eng, out, in0, s, in1, op0, op1, reads, writes):
        self.op(eng, lambda e: e.scalar_tensor_tensor(out=out, in0=in0, scalar=s, in1=in1, op0=op0, op1=op1), reads, writes)

    def cp(self, eng, out, in_, reads, writes):
        if eng == "act":
            self.op(eng, lambda e: e.activation(out=out, in_=in_, func=AF.Copy), reads, writes)
        else:
            self.op(eng, lambda e: e.tensor_copy(out=out, in_=in_), reads, writes)

    def rsqrt(self, ap, tile):
        self.op("act", lambda e: e.activation(out=ap, in_=ap, func=AF.Ln), [tile.b], [tile.b])
        self.op("act", lambda e: e.activation(out=ap, in_=ap, func=AF.Exp, scale=-0.5), [tile.b], [tile.b])

    def ms(self, eng, ap, val, writes):
        self.op(eng, lambda e: e.memset(ap, val), (), writes)


CW = 2688
C_ID = 0
C_SU = 128
C_IU = 192
C_SL = 256
C_BLK = 384
C_ONE = 512
C_TRI2 = 640
C_MB2 = 768
C_SEL0 = 896
C_SEL1 = 1024
C_RST = 1152
CWF = 1664
C_PAR0 = 1664
C_PAR1 = 2176
CB_PAR0 = 640
CB_PAR1 = 1152


def make_consts():
    c = np.zeros((128, CW), np.float32)
    c[:, 0:128] = np.eye(128, dtype=np.float32)
    s = np.arange(64)[:, None]
    t = np.arange(64)[None, :]
    c[0:64, C_SU:C_SU + 64] = (s < t)
    c[0:64, C_IU:C_IU + 64] = (s <= t)
    c[0:64, C_SL:C_SL + 64] = (s > t)
    c[0:64, C_BLK:C_BLK + 64] = 1.0
    c[64:128, C_BLK + 64:C_BLK + 128] = 1.0
    c[:, C_ONE:C_ONE + 128] = 1.0
    r = np.arange(128)[:, None]
    q = np.arange(128)[None, :]
    same = (r // 64) == (q // 64)
    c[:, C_TRI2:C_TRI2 + 128] = (same & (r <= q))
    c[:, C_MB2:C_MB2 + 128] = np.where(same & (r <= q), 0.0, -1.0e5)
    c[0:64, C_SEL0:C_SEL0 + 128] = 1.0
    c[64:128, C_SEL1:C_SEL1 + 128] = 1.0
    tk = np.arange(512)
    c[:, C_PAR0:C_PAR0 + 512] = ((tk // 64) % 2 == 0)[None, :]
    c[:, C_PAR1:C_PAR1 + 512] = ((tk // 64) % 2 == 1)[None, :]
    c[:, C_RST:C_RST + 512] = (tk % 64 != 0)[None, :]
    return c


class Builder:
    def __init__(self, n_layers=4, n_tiles=16, mixers="ABCD", dbg=False):
        self.NL = n_layers
        self.NT = n_tiles
        self.mixers = mixers
        self.dbg = dbg
        self.INV_DT = BF16
        self.wseq = None
        self.wrec = []
        self.wptr = 0
        self.wissued = {}
        self.prefetch = True
        self.rw_stop = 0
        self.nc = bass.Bass("TRN2", target_bir_lowering=False)
        self.p = Prog(self.nc)

    def declare(self):
        nc = self.nc
        di = lambda name, shape: nc.dram_tensor(name, list(shape), F32, kind="ExternalInput").ap()
        self.x = di("x", (SEQ, D))
        self.c = di("c", (8, 128))
        self.consts = di("consts", (128, CW))
        self.ada_w = di("ada_w", (4, D, 3 * D))
        self.ada_b = di("ada_b", (4, 3 * D))
        self.norm_pre = di("norm_pre", (4, D))
        self.norm_post = di("norm_post", (4, D))
        self.ev_w_in = di("ev_w_in", (2, D, EVEN_COLS))
        self.ev_w_out = di("ev_w_out", (2, D, D))
        self.tm_mu = di("tm_mu", (2, 2176))
        self.tm_w0 = di("tm_w0", (2, 512))
        self.tm_w2 = di("tm_w2", (2, 64, 512))
        self.tm_a0 = di("tm_a0", (2, 512))
        self.tm_a2 = di("tm_a2", (2, 64, 512))
        self.tm_k_k = di("tm_k_k", (2, 512))
        self.tm_k_a = di("tm_k_a", (2, 512))
        self.tm_r_k = di("tm_r_k", (2, 512))
        self.tm_lnx_g = di("tm_lnx_g", (2, 512))
        self.tm_lnx_b = di("tm_lnx_b", (2, 512))
        self.sc_conv_w = di("sc_conv_w", (2, 3, 512))
        self.od_w_in = di("od_w_in", (2, D, ODD_COLS))
        self.od_w_out = di("od_w_out", (2, D, D))
        self.cf_conv_w = di("cf_conv_w", (2, 31, 512))
        self.cf_conv_b = di("cf_conv_b", (2, 512))
        self.cf_ln_g = di("cf_ln_g", (2, 512))
        self.cf_ln_b = di("cf_ln_b", (2, 512))
        self.ssd_conv_w = di("ssd_conv_w", (2, 4, 1024))
        self.ssd_conv_b = di("ssd_conv_b", (2, 1024))
        self.ssd_dt_bias = di("ssd_dt_bias", (2, 8))
        self.ssd_a_log = di("ssd_a_log", (2, 8))
        self.ssd_d = di("ssd_d", (2, 8))
        self.ssd_norm_g = di("ssd_norm_g", (2, 512))
        self.out = nc.dram_tensor("out", [SEQ, D], F32, kind="ExternalOutput").ap()
        self.wblocks = []
        for L in range(self.NL):
            ncols = EVEN_COLS if L % 2 == 0 else ODD_COLS
            blks = []
            c0 = 0
            bi = 0
            while c0 < ncols + D:
                if c0 < ncols:
                    n = min(512, ncols - c0)
                    src = (self.ev_w_in if L % 2 == 0 else self.od_w_in)[L // 2][:, c0:c0 + n]
                else:
                    n = 512
                    src = (self.ev_w_out if L % 2 == 0 else self.od_w_out)[L // 2][:, c0 - ncols:c0 - ncols + n]
                sc = nc.dram_tensor("wsc_%d_%d" % (L, bi), [128, 8 * 512], BF16).ap()
                blks.append((sc, n, Buf(), src))
                c0 += n
                bi += 1
            self.wblocks.append(blks)

    def alloc(self):
        p = self.p
        self.cst = p.sb("cst", (128, CWF), F32)
        self.cstb = p.sb("cstb", (128, 1664), BF16)
        self.xs = [p.sb("xs0", (128, 4, D), F32)]
        self.xs.append(self.xs[0])
        self.xT = p.sb("xT", (128, 8, TT), F32)
        self.yo = p.sb("yo", (128, 8, TT + 3), F32)
        self.sq = p.sb("sq", (128, 8, TT), BF16, nb=8)
        self.rb = p.sb("rb", (128, TT), F32)
        self.wst = self.yo
        self.g2c = p.sb("g2c", (128, 4, 8), F32)
        self.npo = p.sb("npo", (128, 4, 8), F32)
        self.hT = p.sb("hT", (128, 8, TT), BF16)
        self.yT = p.sb("yT", (128, 8, TT), BF16, nb=8)
        self.wring = [p.sb("wr%d" % i, (128, 8, 512), BF16) for i in range(2)]
        self.wnext = 0
        self.ss = p.sb("ss", (128, 8), F32)
        self.rstd = p.sb("rstd", (128, 8), F32)
        self.g1c = p.sb("g1c", (128, 4, 8), F32)
        self.shc = p.sb("shc", (128, 4, 8), F32)
        self.cT = p.sb("cT", (128, 8), F32)
        self.abc = p.sb("abc", (128, 4, 24), F32)
        self.npc = p.sb("npc", (128, 4, 8), F32)
        self.F = [p.sb("F%d" % i, (128, 4, TT), F32) for i in range(6)]
        self.cfw = p.sb("cfw", (128, 2, 4, 31), F32)
        self.cfp = p.sb("cfp", (128, 2, 3, 4), F32)
        self.cf_tail = p.sb("cf_tail", (128, 2, 4, 30), F32)
        self.uext = p.sb("uext", (128, 4, 30 + TT), F32)
        self.sdw = p.sb("sdw", (128, 2, 8, 4), F32)
        self.sdb = p.sb("sdb", (128, 2, 8), F32)
        self.sd_tail = p.sb("sd_tail", (128, 2, 8, 3), F32)
        self.sdng = p.sb("sdng", (128, 2, 4), F32)
        self.hrow = p.sb("hrow", (128, 2, 3, 8), F32)
        self.xext = self.yo
        self.xbcT = p.sb("xbcT", (128, 8, TT), BF16)
        self.ctm = p.sb("ctm", (128, 2, 2, TT), BF16)
        self.S = p.sb("S", (128, 2, 8, 64), F32)
        self.Sb = p.sb("Sb", (128, 2, 8, 64), BF16, nb=2)
        self.tokx = p.sb("tokx", (128, 512), F32)
        self.xdt = p.sb("xdt", (128, 8, 64), BF16)
        self.xdd = p.sb("xdd", (128, 8, 64), BF16)
        self.btok = p.sb("btok", (128, 256), BF16)
        self.dts = p.sb("dts", (128, 6, 8), F32)
        self.cdb = p.sb("cdb", (128, 2, 8), F32)
        self.Dall = p.sb("Dall", (128, 8, 128), F32)
        self.dec = p.sb("dec", (128, 8, 128), F32)
        self.cbt = p.sb("cbt", (128, 2, 128), F32)
        self.MT = p.sb("MT", (128, 8, 128), BF16)
        self.ytok = p.sb("ytok", (128, 512), F32)

        self.tmp = p.sb("tmp", (128, D), F32)
        self.pp = [p.ps("pp%d" % i, (128, 512), F32) for i in range(4)]
        self.ppn = 0
        self.pt = [Tile(self.nc.alloc_psum_tensor("ptb%d" % i, [128, 1024], BF16)[:, 0:512]) for i in range(2)]
        pcb_ = self.nc.alloc_psum_tensor("pcb", [128, 512], F32)
        self.pcb = Tile(pcb_[:, 0:256].rearrange("p (g n) -> p g n", g=2))
        psm_ = self.nc.alloc_psum_tensor("psm", [128, 512], F32)
        self.psm = Tile(psm_[:, 0:64])
        self.pp6 = list(self.pp) + [Tile(pcb_[:, :]), Tile(psm_[:, :])]
        self.pp6[4].bufs = self.pcb.bufs
        self.pp6[5].bufs = self.psm.bufs
        self.pp6n = 0
        self.ptn = 0
        self.mu = p.sb("mu", (128, 2, 17), F32)
        self.omu = p.sb("omu", (128, 2, 17), F32)
        self.scw = p.sb("scw", (128, 2, 4, 3), F32)
        self.sc_tail = p.sb("sc_tail", (128, 2, 4, 2), F32)
        self.tm_prev = p.sb("tm_prev", (128, 2, 17), F32)
        self.tmc = p.sb("tmc", (128, 2, 7, 4), F32)
        self.tmk1 = p.sb("tmk1", (128, 2, 4), F32)
        self.w2a2 = p.sb("w2a2", (128, 2, 512), BF16)
        self.Hs = p.sb("Hs", (128, 2, 4, 64), F32)
        self.Hb = p.sb("Hb", (128, 4, 64), BF16)
        self.gl = p.sb("gl", (128, 4, 8), F32)

    def next_pp(self):
        t = self.pp[self.ppn]
        self.ppn = (self.ppn + 1) % len(self.pp)
        return t

    def next_pp6(self):
        t = self.pp6[self.pp6n]
        self.pp6n = (self.pp6n + 1) % len(self.pp6)
        return t

    def next_pt(self):
        t = self.pt[self.ptn]
        self.ptn = (self.ptn + 1) % len(self.pt)
        return t

    def dump(self, name, ap, tile, n):
        if not self.dbg:
            return
        p = self.p
        d = self.nc.dram_tensor("dbg_" + name, [128, n], F32, kind="ExternalOutput").ap()
        sc = self.tmp
        p.cp("dve", sc[:, 0:n], ap, list(tile.bufs), [sc.b])
        p.dma("sp", d, sc[:, 0:n], [sc.b], ())

    def prologue(self):
        p = self.p
        nc = self.nc
        p.dma("sp", self.cst[:], self.consts[:, 0:CWF], (), [self.cst.b])
        p.cp("dve", self.cstb[:, 0:640], self.cst[:, 0:640], [self.cst.b], [self.cstb.b])
        p.dma("sp", self.tmp[:], self.consts[:, C_PAR0:C_PAR0 + 1024], (), [self.tmp.b])
        p.cp("dve", self.cstb[:, 640:1664], self.tmp[:], [self.tmp.b], [self.cstb.b])
        self.ident_b = self.cstb[:, 0:128]
        self.ones_b = self.cstb[:, C_ONE:C_ONE + 128]
        self.ident_f = self.cst[:, 0:128]
        k = 0
        for L in range(self.NL):
            for (sc, n, buf, src) in self.wblocks[L]:
                p.dma("sp", self.wst[:, :, 0:n], src.rearrange("(kc p) n -> p kc n", p=128), (), [self.wst.b])
                wr = self.wring[k % 2]
                eng = ("dve", "act", "pool")[k % 3]
                if eng == "act":
                    p.act(wr[:, :, 0:n], self.wst[:, :, 0:n], AF.Copy, [self.wst.b], [wr.b])
                else:
                    p.cp(eng, wr[:, :, 0:n], self.wst[:, :, 0:n], [self.wst.b], [wr.b])
                p.dma("act", sc.rearrange("p (kc n) -> p kc n", kc=8)[:, :, 0:n], wr[:, :, 0:n], [wr.b], [buf])
                k += 1
        stg = self.tmp

        def cols(src_rows, R, W, dst_fn, dst_tile):
            for w0 in range(0, W, 1024):
                wn = min(1024, W - w0)
                p.dma("sp", stg[0:R, 0:wn], src_rows[:, w0:w0 + wn], (), [stg.b])
                for c_ in range(wn // 128):
                    ps = self.next_pp()
                    p.tr(ps[:, 0:R], stg[0:R, c_ * 128:(c_ + 1) * 128], self.cst[0:R, 0:R], [stg.b, self.cst.b], [ps.b])
                    p.cp("dve", dst_fn(w0 // 128 + c_), ps[:, 0:R], [ps.b], [dst_tile.b])

        cols(self.c, 8, 128, lambda cc: self.cT[:, :], self.cT)
        p.act(self.cT[:], self.cT[:], AF.Silu, [self.cT.b], [self.cT.b])
        cols(self.ada_b, 4, 3072, lambda cc: self.abc[:, :, cc], self.abc)
        cols(self.norm_pre, 4, 1024, lambda cc: self.npc[:, :, cc], self.npc)
        cols(self.norm_post, 4, 1024, lambda cc: self.npo[:, :, cc], self.npo)
        cols(self.tm_mu, 2, 2176, lambda cc: self.mu[:, :, cc], self.mu)
        p.ts("dve", self.omu[:], self.mu[:], -1.0, 1.0, ALU.mult, ALU.add, [self.mu.b], [self.omu.b])
        for l in range(2):
            for i_, src in enumerate((self.tm_w0, self.tm_a0, self.tm_k_k, self.tm_k_a, self.tm_r_k, self.tm_lnx_g, self.tm_lnx_b)):
                cols(src[l:l + 1, :], 1, 512, lambda cc, l=l, i_=i_: self.tmc[:, l, i_, cc:cc + 1], self.tmc)
            p.ts("dve", self.tmk1[:, l, :], self.tmc[:, l, 3, :], -1.0, 1.0, ALU.mult, ALU.add, [self.tmc.b], [self.tmk1.b])
            p.dma("sp", stg[0:64, 0:512], self.tm_w2[l], (), [stg.b])
            p.dma("sp", stg[64:128, 0:512], self.tm_a2[l], (), [stg.b])
            p.cp("dve", self.w2a2[:, l, :], stg[:, 0:512], [stg.b], [self.w2a2.b])
            cols(self.sc_conv_w[l], 3, 512, lambda cc, l=l: self.scw[:, l, cc, :], self.scw)
            cols(self.cf_conv_w[l], 31, 512, lambda cc, l=l: self.cfw[:, l, cc, :], self.cfw)
            for i_, src in enumerate((self.cf_conv_b, self.cf_ln_g, self.cf_ln_b)):
                cols(src[l:l + 1, :], 1, 512, lambda cc, l=l, i_=i_: self.cfp[:, l, i_, cc:cc + 1], self.cfp)
            cols(self.ssd_conv_w[l], 4, 1024, lambda cc, l=l: self.sdw[:, l, cc, :], self.sdw)
            cols(self.ssd_conv_b[l:l + 1, :], 1, 1024, lambda cc, l=l: self.sdb[:, l, cc:cc + 1], self.sdb)
            cols(self.ssd_norm_g[l:l + 1, :], 1, 512, lambda cc, l=l: self.sdng[:, l, cc:cc + 1], self.sdng)
            for i_, src in enumerate((self.ssd_dt_bias, self.ssd_a_log, self.ssd_d)):
                p.dma("sp", stg[0:1, 0:8], src[l:l + 1, :], (), [stg.b])
                ps = self.next_pp()
                p.mm(ps[:, 0:8], self.cst[0:1, C_ONE:C_ONE + 128], stg[0:1, 0:8], True, True, [self.cst.b, stg.b], [ps.b])
                p.cp("dve", self.hrow[:, l, i_, :], ps[:, 0:8], [ps.b], [self.hrow.b])
            p.act(self.hrow[:, l, 1, :], self.hrow[:, l, 1, :], AF.Exp, [self.hrow.b], [self.hrow.b])
            p.ts("dve", self.hrow[:, l, 1, :], self.hrow[:, l, 1, :], -1.0, None, ALU.mult, None, [self.hrow.b], [self.hrow.b])
        p.ms("dve", self.Hs[:], 0.0, [self.Hs.b])
        p.ms("dve", self.cf_tail[:], 0.0, [self.cf_tail.b])
        p.ms("dve", self.sd_tail[:], 0.0, [self.sd_tail.b])
        p.ms("dve", self.S[:], 0.0, [self.S.b])
        p.ms("dve", self.sc_tail[:], 0.0, [self.sc_tail.b])
        p.ms("dve", self.tm_prev[:], 0.0, [self.tm_prev.b])
        for L in range(self.NL):
            aw = self.ada_w[L]
            pcol = self.next_pp()
            for blk in range(4):
                p.dma("sp", self.wst[:, :, 0:512], aw[:, blk * 512:(blk + 1) * 512].rearrange("(kc p) n -> p kc n", p=128), (), [self.wst.b])
                for jj in range(4):
                    j = blk * 4 + jj
                    for kc in range(8):
                        p.mm(pcol[:, j:j + 1], self.wst[:, kc, jj * 128:(jj + 1) * 128], self.cT[:, kc:kc + 1], kc == 0, kc == 7,
                             [self.wst.b, self.cT.b], [pcol.b])
            p.tt("dve", self.shc[:, L, :], pcol[:, 0:8], self.abc[:, L, 0:8], ALU.add, [pcol.b, self.abc.b], [self.shc.b])
            p.tt("dve", self.g1c[:, L, :], pcol[:, 8:16], self.abc[:, L, 8:16], ALU.add, [pcol.b, self.abc.b], [self.g1c.b])
            p.stt("dve", self.g1c[:, L, :], self.g1c[:, L, :], 1.0, self.npc[:, L, :], ALU.add, ALU.mult, [self.g1c.b, self.npc.b], [self.g1c.b])
            pcol2 = self.next_pp()
            for blk in range(2):
                p.dma("sp", self.wst[:, :, 0:512], aw[:, 2 * D + blk * 512:2 * D + (blk + 1) * 512].rearrange("(kc p) n -> p kc n", p=128), (), [self.wst.b])
                for jj in range(4):
                    j = blk * 4 + jj
                    for kc in range(8):
                        p.mm(pcol2[:, j:j + 1], self.wst[:, kc, jj * 128:(jj + 1) * 128], self.cT[:, kc:kc + 1], kc == 0, kc == 7,
                             [self.wst.b, self.cT.b], [pcol2.b])
            p.tt("dve", self.g2c[:, L, :], pcol2[:, 0:8], self.abc[:, L, 16:24], ALU.add, [pcol2.b, self.abc.b], [self.g2c.b])
            p.tt("dve", self.g2c[:, L, :], self.g2c[:, L, :], self.npo[:, L, :], ALU.mult, [self.g2c.b, self.npo.b], [self.g2c.b])

    def _issue_w(self, k):
        L, bi = self.wseq[k]
        sc, n, buf, _ = self.wblocks[L][bi]
        wr = self.wring[k % len(self.wring)]
        self.p.dma("sp", wr[:, :, 0:n], sc.rearrange("p (kc n) -> p kc n", kc=8)[:, :, 0:n], [buf], [wr.b])
        self.wissued[k] = wr

    def load_w(self, L, bi):
        if self.wseq is None:
            self.wrec.append((L, bi))
            sc, n, buf, _ = self.wblocks[L][bi]
            wr = self.wring[self.wnext]
            self.wnext = (self.wnext + 1) % len(self.wring)
            self.p.dma("sp", wr[:, :, 0:n], sc.rearrange("p (kc n) -> p kc n", kc=8)[:, :, 0:n], [buf], [wr.b])
            return wr
        k = self.wptr
        assert self.wseq[k] == (L, bi), (k, self.wseq[k], (L, bi))
        if k not in self.wissued:
            self._issue_w(k)
        wr = self.wissued.pop(k)
        self.wptr += 1
        if self.wptr < len(self.wseq):
            self._issue_w(self.wptr)
        return wr

    def load_tile(self, t):
        p = self.p
        xs = self.xs[t % 2]
        for kc in range(8):
            ps = self.next_pp()
            for s in range(4):
                p.tr(ps[:, s * 128:(s + 1) * 128], xs[:, s, kc * 128:(kc + 1) * 128], self.ident_f, [xs.b, self.cst.b], [ps.b])
            if kc % 2 == 0:
                p.act(self.xT[:, kc, :], ps[:], AF.Copy, [ps.b], [self.xT.b])
            else:
                p.cp("dve", self.xT[:, kc, :], ps[:], [ps.b], [self.xT.b])

    def store_tile(self, t, orr):
        p = self.p
        xs = self.xs[t % 2]
        for s in range(4):
            for half in range(2):
                ps = self.next_pp()
                for q in range(4):
                    kc = half * 4 + q
                    p.tr(ps[:, q * 128:(q + 1) * 128], self.xT[:, kc, s * 128:(s + 1) * 128], self.ident_f, [self.xT.b, self.cst.b], [ps.b])
                if half == 0:
                    p.act(xs[:, s, 0:512], ps[:], AF.Copy, [ps.b], [xs.b])
                else:
                    p.cp("dve", xs[:, s, 512:1024], ps[:], [ps.b], [xs.b])
        p.dma("sp", orr[t], xs[:], [xs.b], ())

    def rms_bcast(self, src):
        p = self.p
        for kc in range(8):
            p.act(self.sq[:, kc, :], src[:, kc, 0:TT], AF.Square, [src.b], [self.sq.bufs[kc]])
        ps = self.next_pp()
        for kc in range(8):
            p.mm(ps[:], self.ones_b, self.sq[:, kc, :], kc == 0, kc == 7, [self.cstb.b, self.sq.bufs[kc]], [ps.b])
        p.ts("dve", self.rb[:], ps[:], 1.0 / D, 1e-6, ALU.mult, ALU.add, [ps.b], [self.rb.b])
        p.op("act", lambda e: e.activation(out=self.rb[:], in_=self.rb[:], func=AF.Ln), [self.rb.b], [self.rb.b])
        p.op("act", lambda e: e.activation(out=self.rb[:], in_=self.rb[:], func=AF.Exp, scale=-0.5), [self.rb.b], [self.rb.b])

    def pre_norm(self, L):
        p = self.p
        if L == 0 and not getattr(self, "_d0", False):
            self.dump("xT0", self.xT[:, 0, :], self.xT, 512)
            self.dump("g1c", self.g1c[:, 0, :], self.g1c, 8)
            self.dump("shc", self.shc[:, 0, :], self.shc, 8)
            self.dump("cT", self.cT[:], self.cT, 8)
            self.dump("npc", self.npc[:, 0, :], self.npc, 8)
            self.dump("abc", self.abc[:, 0, :], self.abc, 24)
            self.dump("xT7", self.xT[:, 7, :], self.xT, 512)
        self.rms_bcast(self.xT)
        if L == 0 and not getattr(self, "_d0", False):
            self._d0 = True
            self._d0b = True
            self.dump("rb0", self.rb[:], self.rb, 512)
            self.dump("sq0", self.sq[:, 0, :], self.sq, 512)
        for kc in range(8):
            p.tt("dve", self.tmp[:, 0:512], self.xT[:, kc, :], self.rb[:], ALU.mult, [self.xT.b, self.rb.b], [self.tmp.b])
            p.ts("dve", self.hT[:, kc, :], self.tmp[:, 0:512], self.g1c[:, L, kc:kc + 1], self.shc[:, L, kc:kc + 1], ALU.mult, ALU.add,
                 [self.tmp.b, self.g1c.b, self.shc.b], [self.hT.b])
        if getattr(self, "_d0b", False):
            self._d0b = False
            self.dump("hT", self.hT[:, 0, :], self.hT, 512)
            self.dump("hT7", self.hT[:, 7, :], self.hT, 512)

    def proj_fm(self, wr, lc, ps):
        p = self.p
        for kc in range(8):
            p.mm(ps[:], wr[:, kc, lc * 128:(lc + 1) * 128], self.hT[:, kc, :], kc == 0, kc == 7, [wr.b, self.hT.b], [ps.b])

    def out_proj_post(self, L):
        p = self.p
        nb = len(self.wblocks[L])
        allY = self.yT.bufs
        for half in range(2):
            wr = self.load_w(L, nb - 2 + half)
            for q in range(4):
                dmc = half * 4 + q
                ps = self.next_pp()
                for cc in range(8):
                    p.mm(ps[:], wr[:, cc, q * 128:(q + 1) * 128], self.yT[:, cc, :], cc == 0, cc == 7, [wr.b] + allY, [ps.b])
                p.act(self.yo[:, dmc, 0:TT], ps[:], AF.Copy, [ps.b], [self.yo.b])
        if not getattr(self, "_d1", False):
            self._d1 = True
            self.dump("yo0", self.yo[:, 0, 0:TT], self.yo, 512)
            self.dump("yo7", self.yo[:, 7, 0:TT], self.yo, 512)
        self.rms_bcast(self.yo)
        for dmc in range(8):
            p.stt("dve", self.tmp[:, 0:512], self.yo[:, dmc, 0:TT], self.g2c[:, L, dmc:dmc + 1], self.rb[:], ALU.mult, ALU.mult,
                  [self.yo.b, self.g2c.b, self.rb.b], [self.tmp.b])
            p.tt("dve", self.xT[:, dmc, :], self.xT[:, dmc, :], self.tmp[:, 0:512], ALU.add, [self.xT.b, self.tmp.b], [self.xT.b])

    def even_layer(self, L):
        p = self.p
        j = L // 2
        self.pre_norm(L)
        wrs = {}
        loaded = {}

        def get(ch):
            bi = ch // 4
            if bi not in loaded:
                loaded[bi] = self.load_w(L, bi)
            return (loaded[bi], ch % 4)

        self._cur_blk = None
        if "B" in self.mixers:
            self.short_conv_stream(L, j)
        else:
            for cc in range(4, 8):
                p.ms("dve", self.yT[:, cc, :], 0.0, [self.yT.bufs[cc]])
        if "A" in self.mixers:
            self.rwkv(L, j)
        else:
            for cc in range(4):
                p.ms("dve", self.yT[:, cc, :], 0.0, [self.yT.bufs[cc]])
        if not getattr(self, "_yd", False):
            self._yd = True
            self.dump("yT4", self.yT[:, 4, :], self.yT, 512)
        self.out_proj_post(L)

    def short_conv_stream(self, L, j):
        p = self.p
        F = self.F
        dest = {}
        for cc in range(4):
            dest[17 + cc] = (F[3], cc, AF.Copy)
            dest[21 + cc] = (F[0], cc, AF.Copy)
            dest[25 + cc] = (F[1], cc, AF.Copy)
            dest[29 + cc] = (F[4], cc, AF.Silu)
        cur_bi = None
        wr = None
        for ch in range(17, 33):
            bi = ch // 4
            if bi != cur_bi:
                wr = self.load_w(L, bi)
                cur_bi = bi
            ps = self.next_pp()
            self.proj_fm(wr, ch % 4, ps)
            t, cc, fn = dest[ch]
            if ch % 2 == 0:
                p.act(t[:, cc, :], ps[:], fn, [ps.b], [t.b])
            else:
                if fn == AF.Copy:
                    p.cp("dve", t[:, cc, :], ps[:], [ps.b], [t.b])
                else:
                    p.act(t[:, cc, :], ps[:], fn, [ps.b], [t.b])
        u = F[0]
        acc = F[2]
        w = self.scw
        tl = self.sc_tail
        p.tt("dve", u[:], F[0][:], F[1][:], ALU.mult, [F[0].b, F[1].b], [u.b])
        for cc in range(4):
            p.ts("dve", acc[:, cc, :], u[:, cc, :], w[:, j, cc, 2:3], None, ALU.mult, None, [u.b, w.b], [acc.b])
            p.stt("dve", acc[:, cc, 1:TT], u[:, cc, 0:TT - 1], w[:, j, cc, 1:2], acc[:, cc, 1:TT], ALU.mult, ALU.add, [u.b, w.b, acc.b], [acc.b])
            p.stt("dve", acc[:, cc, 2:TT], u[:, cc, 0:TT - 2], w[:, j, cc, 0:1], acc[:, cc, 2:TT], ALU.mult, ALU.add, [u.b, w.b, acc.b], [acc.b])
            p.stt("dve", acc[:, cc, 0:1], tl[:, j, cc, 1:2], w[:, j, cc, 1:2], acc[:, cc, 0:1], ALU.mult, ALU.add, [tl.b, w.b, acc.b], [acc.b])
            p.stt("dve", acc[:, cc, 0:2], tl[:, j, cc, 0:2], w[:, j, cc, 0:1], acc[:, cc, 0:2], ALU.mult, ALU.add, [tl.b, w.b, acc.b], [acc.b])
            p.cp("dve", tl[:, j, cc, :], u[:, cc, TT - 2:TT], [u.b, acc.b], [tl.b])
        p.tt("dve", acc[:], acc[:], F[3][:], ALU.mult, [acc.b, F[3].b], [acc.b])
        for cc in range(4):
            p.tt("dve", self.yT[:, 4 + cc, :], acc[:, cc, :], F[4][:, cc, :], ALU.mult, [acc.b, F[4].b], [self.yT.bufs[4 + cc]])

    def proj_to(self, L, ch, dst_ap, dst_bufs, fn=None, eng="act"):
        p = self.p
        bi = ch // 4
        if self._cur_blk != (L, bi):
            self._cur_wr = self.load_w(L, bi)
            self._cur_blk = (L, bi)
        ps = self.next_pp()
        self.proj_fm(self._cur_wr, ch % 4, ps)
        if eng == "act":
            p.act(dst_ap, ps[:], fn or AF.Copy, [ps.b], dst_bufs)
        else:
            p.cp(eng, dst_ap, ps[:], [ps.b], dst_bufs)

    def conformer(self, L, j):
        p = self.p
        F = self.F
        ue = self.uext
        p.cp("dve", ue[:, :, 0:30], self.cf_tail[:, j, :, :], [self.cf_tail.b], [ue.b])
        for cc in range(4):
            self.proj_to(L, cc, F[0][:, cc, :], [F[0].b], eng="dve")
        for cc in range(4):
            self.proj_to(L, 4 + cc, F[1][:, cc, :], [F[1].b], AF.Sigmoid)
        p.tt("dve", ue[:, :, 30:30 + TT], F[0][:], F[1][:], ALU.mult, [F[0].b, F[1].b], [ue.b])
        p.cp("dve", self.cf_tail[:, j, :, :], ue[:, :, TT:TT + 30], [ue.b], [self.cf_tail.b])
        for cc in range(4):
            self.proj_to(L, 8 + cc, F[1][:, cc, :], [F[1].b], AF.Silu)
        yield "proj"
        acc = F[2]
        w = self.cfw
        ub = Tile(self.sq[:].rearrange("p a b -> p (a b)")[:, 0:4 * (30 + TT)].rearrange("p (c t) -> p c t", c=4))
        ub.bufs = list(self.sq.bufs)
        p.cp("act", ub[:], ue[:, :, 0:30 + TT], [ue.b], list(ub.bufs))
        ring = []
        for c4 in range(4):
            for i_ in range(4):
                r_ = Tile(self.yT[:, 4 + c4, i_ * 128:(i_ + 1) * 128])
                ring.append(r_)
        n_ = 0
        for cc in range(4):
            ps = self.next_pp()
            for k in range(31):
                dg = ring[n_ % 16]
                p.ts("dve", dg[:], self.ident_b, w[:, j, cc, k:k + 1], None, ALU.mult, None, [self.cstb.b, w.b], [dg.b])
                p.mm(ps[:], dg[:], ub[:, cc, k:k + TT], k == 0, k == 30, [dg.b] + list(ub.bufs), [ps.b])
                n_ += 1
            p.act(acc[:, cc, :], ps[:], AF.Identity, [ps.b, self.cfp.b], [acc.b], bias=self.cfp[:, j, 0, cc:cc + 1])
            yield "cc"
        for c4 in range(4):
            p.ms("dve", self.yT[0:1, 4 + c4, 0:1], 0.0, [r_.b for r_ in ring[c4 * 4:(c4 + 1) * 4]] + [self.yT.bufs[4 + c4]])
        yield "conv"
        ones_f = self.cst[:, C_ONE:C_ONE + 128]
        pm = self.next_pp()
        for cc in range(4):
            p.mm(pm[:], ones_f, acc[:, cc, :], cc == 0, cc == 3, [self.cst.b, acc.b], [pm.b])
        for cc in range(4):
            p.act(F[0][:, cc, :], acc[:, cc, :], AF.Square, [acc.b], [F[0].b])
        pq = self.next_pp()
        for cc in range(4):
            p.mm(pq[:], ones_f, F[0][:, cc, :], cc == 0, cc == 3, [self.cst.b, F[0].b], [pq.b])
        mean = F[3][:, 0, :]
        var = F[3][:, 1, :]
        yield "stats"
        p.ts("dve", mean, pm[:], 1.0 / 512, None, ALU.mult, None, [pm.b], [F[3].b])
        p.tt("dve", F[3][:, 2, :], mean, mean, ALU.mult, [F[3].b], [F[3].b])
        p.stt("dve", var, pq[:], 1.0 / 512, F[3][:, 2, :], ALU.mult, ALU.subtract, [pq.b, F[3].b], [F[3].b])
        p.ts("dve", var, var, 1e-5, None, ALU.add, None, [F[3].b], [F[3].b])
        p.op("act", lambda e: e.activation(out=var, in_=var, func=AF.Ln), [F[3].b], [F[3].b])
        p.op("act", lambda e: e.activation(out=var, in_=var, func=AF.Exp, scale=-0.5), [F[3].b], [F[3].b])
        if not getattr(self, "_cfd", False):
            self._cfd = True
            self.dump("cf_hT", self.hT[:, 0, :], self.hT, 512)
            self.dump("cf_xT", self.xT[:, 0, :], self.xT, 512)
            self.dump("cf_g1c", self.g1c[:, 1, :], self.g1c, 8)
            self.dump("cf_shc", self.shc[:, 1, :], self.shc, 8)
            self.dump("cf_rb", self.rb[:], self.rb, 512)
            self.dump("cf_u", ue[:, 0, 30:30 + TT], ue, 512)
            self.dump("cf_acc", acc[:, 0, :], acc, 512)
            self.dump("cf_mean", mean, F[3], 512)
            self.dump("cf_rstd", var, F[3], 512)
            self.dump("cf_sg", F[1][:, 0, :], F[1], 512)
        for cc in range(4):
            p.tt("dve", F[0][:, cc, :], acc[:, cc, :], mean, ALU.subtract, [acc.b, F[3].b], [F[0].b])
            p.tt("dve", F[0][:, cc, :], F[0][:, cc, :], var, ALU.mult, [F[0].b, F[3].b], [F[0].b])
            p.act(F[0][:, cc, :], F[0][:, cc, :], AF.Silu, [F[0].b, self.cfp.b], [F[0].b],
                  scale=self.cfp[:, j, 1, cc:cc + 1], bias=self.cfp[:, j, 2, cc:cc + 1])
            p.tt("dve", self.yT[:, cc, :], F[0][:, cc, :], F[1][:, cc, :], ALU.mult, [F[0].b, F[1].b], [self.yT.bufs[cc]])

    def ssd(self, L, j):
        p = self.p
        F = self.F
        xe = self.xext
        zs = F[4]
        for cc in range(4):
            self.proj_to(L, 12 + cc, zs[:, cc, :], [zs.b], AF.Silu)
        p.cp("dve", xe[:, :, 0:3], self.sd_tail[:, j, :, :], [self.sd_tail.b], [xe.b])
        for cc in range(8):
            self.proj_to(L, 16 + cc, xe[:, cc, 3:3 + TT], [xe.b], eng="act")
        p.cp("dve", self.sd_tail[:, j, :, :], xe[:, :, TT:TT + 3], [xe.b], [self.sd_tail.b])
        wdt = self.load_w(L, 6)
        self._cur_blk = None
        yield "proj"
        w = self.sdw
        for cc in range(8):
            eng = "dve"
            acc = F[5][:, cc % 4, :]
            p.ts(eng, acc, xe[:, cc, 3:3 + TT], w[:, j, cc, 3:4], None, ALU.mult, None, [xe.b, w.b], [F[5].bufs[0]])
            for k in range(3):
                p.stt(eng, acc, xe[:, cc, k:k + TT], w[:, j, cc, k:k + 1], acc, ALU.mult, ALU.add, [xe.b, w.b, F[5].b], [F[5].b])
            p.act(self.xbcT[:, cc, :], acc, AF.Silu, [F[5].b, self.sdb.b], [self.xbcT.b], bias=self.sdb[:, j, cc:cc + 1])
            if cc % 2 == 1:
                yield "conv"
        for g in range(2):
            for e in range(2):
                par = self.cstb[:, (CB_PAR0 if e == 0 else CB_PAR1):(CB_PAR0 if e == 0 else CB_PAR1) + TT]
                p.tt("dve", self.ctm[:, g, e, :], self.xbcT[:, 6 + g, :], par, ALU.mult, [self.xbcT.b, self.cstb.b], [self.ctm.b])
        y2T = Tile(self.yo[:, 0:4, 0:TT])
        y2T.bufs = self.yo.bufs
        for s in range(4):
            tsl = slice(s * 128, (s + 1) * 128)
            for kc in range(8):
                p.mm(self.psm[:, 0:8], self.hT[:, kc, tsl], wdt[:, kc, 0:8], kc == 0, kc == 7, [self.hT.b, wdt.b], [self.psm.b])
            d = self.dts
            p.tt("dve", d[:, 0, :], self.psm[:, 0:8], self.hrow[:, j, 0, :], ALU.add, [self.psm.b, self.hrow.b], [d.b])
            p.act(d[:, 0, :], d[:, 0, :], AF.Exp, [d.b], [d.b])
            p.ts("dve", d[:, 0, :], d[:, 0, :], 1.0, None, ALU.add, None, [d.b], [d.b])
            p.act(d[:, 0, :], d[:, 0, :], AF.Ln, [d.b], [d.b])
            p.tt("dve", d[:, 1, :], d[:, 0, :], self.hrow[:, j, 1, :], ALU.mult, [d.b, self.hrow.b], [d.b])
            p.mm(self.psm[:, 8:16], self.cst[:, C_TRI2:C_TRI2 + 128], d[:, 1, :], True, True, [self.cst.b, d.b], [self.psm.b])
            p.mm(self.psm[:, 16:24], self.cst[:, C_BLK:C_BLK + 128], d[:, 1, :], True, True, [self.cst.b, d.b], [self.psm.b])
            p.mm(self.psm[:, 24:32], self.cst[:, C_SEL0:C_SEL0 + 128], d[:, 1, :], True, True, [self.cst.b, d.b], [self.psm.b])
            p.mm(self.psm[:, 32:40], self.cst[:, C_SEL1:C_SEL1 + 128], d[:, 1, :], True, True, [self.cst.b, d.b], [self.psm.b])
            p.cp("dve", d[:, 2, :], self.psm[:, 8:16], [self.psm.b], [d.b])
            p.tt("dve", d[:, 4, :], self.psm[:, 16:24], d[:, 2, :], ALU.subtract, [self.psm.b, d.b], [d.b])
            p.act(d[:, 4, :], d[:, 4, :], AF.Exp, [d.b], [d.b])
            p.act(d[:, 5, :], d[:, 2, :], AF.Exp, [d.b], [d.b])
            p.act(self.cdb[:, 0, :], self.psm[:, 24:32], AF.Exp, [self.psm.b], [self.cdb.b])
            p.act(self.cdb[:, 1, :], self.psm[:, 32:40], AF.Exp, [self.psm.b], [self.cdb.b])
            p.ts("dve", d[:, 3, :], d[:, 2, :], -1.0, None, ALU.mult, None, [d.b], [d.b])
            yield "st"
            pt = self.next_pt()
            for cc in range(4):
                p.tr(pt[:, cc * 128:(cc + 1) * 128], self.xbcT[:, cc, tsl], self.ident_b, [self.xbcT.b, self.cstb.b], [pt.b])
            p.cp("act", self.tokx[:], pt[:], [pt.b], [self.tokx.b])
            pt2 = self.next_pt()
            for cc in range(2):
                p.tr(pt2[:, cc * 128:(cc + 1) * 128], self.xbcT[:, 4 + cc, tsl], self.ident_b, [self.xbcT.b, self.cstb.b], [pt2.b])
            p.cp("act", self.btok[:], pt2[:, 0:256], [pt2.b], [self.btok.b])
            tx3 = self.tokx[:].rearrange("p (h q) -> p h q", h=8)
            p.tt("dve", self.xdt[:], tx3, d[:, 0, :].unsqueeze(2).to_broadcast([128, 8, 64]), ALU.mult, [self.tokx.b, d.b], [self.xdt.b])
            p.tt("dve", self.xdd[:], self.xdt[:], d[:, 4, :].unsqueeze(2).to_broadcast([128, 8, 64]), ALU.mult, [self.xdt.b, d.b], [self.xdd.b])
            yield "st"
            for g in range(2):
                p.mm(self.pcb[:, g, :], self.xbcT[:, 4 + g, tsl], self.xbcT[:, 6 + g, tsl], True, True, [self.xbcT.b], [self.pcb.b])
            p.cp("act", self.cbt[:], self.pcb[:], [self.pcb.b], [self.cbt.b])
            p.cp("dve", self.Dall[:], d[:, 1, :].unsqueeze(2).to_broadcast([128, 8, 128]), [d.b], [self.Dall.b])
            for hh in range(2):
                pd = self.next_pp()
                for q in range(4):
                    h = hh * 4 + q
                    p.mm(pd[:, q * 128:(q + 1) * 128], self.Dall[:, h, :], self.cst[:, C_TRI2:C_TRI2 + 128], True, False, [self.Dall.b, self.cst.b], [pd.b])
                    p.mm(pd[:, q * 128:(q + 1) * 128], self.cst[:, C_ID:C_ID + 128], self.cst[:, C_MB2:C_MB2 + 128], False, True, [self.cst.b], [pd.b])
                for q in range(4):
                    h = hh * 4 + q
                    p.act(self.dec[:, h, :], pd[:, q * 128:(q + 1) * 128], AF.Exp, [pd.b, d.b], [self.dec.b], bias=d[:, 3, h:h + 1])
            for g in range(2):
                p.tt("dve", self.MT[:, g * 4:(g + 1) * 4, :], self.dec[:, g * 4:(g + 1) * 4, :],
                     self.cbt[:, g, :].unsqueeze(1).to_broadcast([128, 4, 128]), ALU.mult, [self.dec.b, self.cbt.b], [self.MT.b])
            yield "st"
            pyd = self.next_pp()
            for h in range(8):
                p.mm(pyd[:, h * 64:(h + 1) * 64], self.MT[:, h, :], self.xdt[:, h, :], True, True, [self.MT.b, self.xdt.b], [pyd.b])
            yield "st"
            for e in range(2):
                rs = slice(e * 64, (e + 1) * 64)
                p.cp("act", self.Sb[:, e, :, :], self.S[:, j, :, :], [self.S.b], [self.Sb.bufs[e]])
                pst = self.next_pp()
                for g in range(2):
                    p.mm(pst[:, g * 256:(g + 1) * 256], self.btok[rs, g * 128:(g + 1) * 128],
                         self.xdd[rs, g * 4:(g + 1) * 4, :].rearrange("p h q -> p (h q)"), True, True, [self.btok.b, self.xdd.b], [pst.b])
                p.tt("dve", self.S[:, j, :, :], self.S[:, j, :, :], self.cdb[:, e, :].unsqueeze(2).to_broadcast([128, 8, 64]), ALU.mult,
                     [self.S.b, self.cdb.b], [self.S.b])
                p.tt("dve", self.S[:, j, :, :], self.S[:, j, :, :], pst[:].rearrange("p (h q) -> p h q", h=8), ALU.add, [self.S.b, pst.b], [self.S.b])
            pyo = self.next_pp()
            for g in range(2):
                for e in range(2):
                    p.mm(pyo[:, g * 256:(g + 1) * 256], self.ctm[:, g, e, tsl], self.Sb[:, e, g * 4:(g + 1) * 4, :].rearrange("p h q -> p (h q)"),
                         e == 0, e == 1, [self.ctm.b, self.Sb.bufs[e]], [pyo.b])
            yt = self.ytok
            yt3 = yt[:].rearrange("p (h q) -> p h q", h=8)
            p.tt("dve", yt3, pyo[:].rearrange("p (h q) -> p h q", h=8), d[:, 5, :].unsqueeze(2).to_broadcast([128, 8, 64]), ALU.mult, [pyo.b, d.b], [yt.b])
            p.tt("dve", yt[:], yt[:], pyd[:], ALU.add, [yt.b, pyd.b], [yt.b])
            p.tt("dve", self.tokx[:].rearrange("p (h q) -> p h q", h=8), tx3, self.hrow[:, j, 2, :].unsqueeze(2).to_broadcast([128, 8, 64]), ALU.mult,
                 [self.tokx.b, self.hrow.b], [self.tokx.b])
            p.tt("dve", yt[:], yt[:], self.tokx[:], ALU.add, [yt.b, self.tokx.b], [yt.b])
            yield "st"
            py = self.next_pp()
            for cc in range(4):
                p.tr(py[:, cc * 128:(cc + 1) * 128], yt[:, cc * 128:(cc + 1) * 128], self.ident_f, [yt.b, self.cst.b], [py.b])
            p.tt("dve", y2T[:, :, tsl], py[:].rearrange("p (c t) -> p c t", c=4), zs[:, :, tsl], ALU.mult, [py.b, zs.b], [y2T.b])
        yield "loop"
        for cc in range(4):
            p.act(self.sq[:, cc, :], y2T[:, cc, :], AF.Square, [y2T.b], [self.sq.bufs[cc]])
        pr = self.next_pp()
        for cc in range(4):
            p.mm(pr[:], self.ones_b, self.sq[:, cc, :], cc == 0, cc == 3, [self.cstb.b, self.sq.bufs[cc]], [pr.b])
        p.ts("dve", self.rb[:], pr[:], 1.0 / 512, 1e-6, ALU.mult, ALU.add, [pr.b], [self.rb.b])
        p.op("act", lambda e: e.activation(out=self.rb[:], in_=self.rb[:], func=AF.Ln), [self.rb.b], [self.rb.b])
        p.op("act", lambda e: e.activation(out=self.rb[:], in_=self.rb[:], func=AF.Exp, scale=-0.5), [self.rb.b], [self.rb.b])
        for cc in range(4):
            p.stt("dve", self.yT[:, 4 + cc, :], y2T[:, cc, :], self.sdng[:, j, cc:cc + 1], self.rb[:], ALU.mult, ALU.mult,
                  [y2T.b, self.sdng.b, self.rb.b], [self.yT.bufs[4 + cc]])

    def odd_layer(self, L):
        p = self.p
        j = L // 2
        self.pre_norm(L)
        self._cur_blk = None
        gens = []
        if "C" in self.mixers:
            gc = self.conformer(L, j)
            next(gc)
            gens.append(gc)
        else:
            for cc in range(4):
                p.ms("dve", self.yT[:, cc, :], 0.0, [self.yT.bufs[cc]])
        if "D" in self.mixers:
            gd = self.ssd(L, j)
            next(gd)
            gens.append(gd)
        else:
            for cc in range(4, 8):
                p.ms("dve", self.yT[:, cc, :], 0.0, [self.yT.bufs[cc]])
        while gens:
            for g_ in list(gens):
                try:
                    next(g_)
                except StopIteration:
                    gens.remove(g_)
        self.out_proj_post(L)

    def rwkv(self, L, j):
        p = self.p
        F = self.F
        LW = -0.6065306597126334
        INV = self.INV_DT
        cst = self.cst
        rT, kT, vT, sgT, bonT, aT = F[0], F[1], F[2], F[3], F[4], F[5]
        yob = self.yo[:].rearrange("p a b -> p (a b)").bitcast(BF16)
        AR = Tile(yob[:, 0:4096].rearrange("p (c n q d) -> p c n q d", c=4, n=8, q=2))
        AR.bufs = self.yo.bufs
        BK = Tile(yob[:, 4096:8192].rearrange("p (c n q d) -> p c n q d", c=4, n=8, q=2))
        BK.bufs = self.yo.bufs
        Vt, Bt, Kt = self.hT, self.sq, self.xbcT
        Ytok = Tile(self.xs[0][:].rearrange("p s (a d) -> p (s a) d", a=2))
        Ytok.bufs = self.xs[0].bufs
        if INV == F32:
            Wp = [self.Dall, self.dec]
            NTp = [Tile(self.uext[:, 0, 0:512].rearrange("p (h d) -> p h d", h=8)), Tile(self.uext[:, 1, 0:512].rearrange("p (h d) -> p h d", h=8))]
            Xs = Tile(self.uext[:, 2, 0:512].rearrange("p (h d) -> p h d", h=8))
        else:
            Wp = []
            for t_ in (self.Dall, self.dec):
                w_ = Tile(t_[:].rearrange("p h d -> p (h d)").bitcast(BF16)[:, 0:1024].rearrange("p (h d) -> p h d", h=8))
                w_.bufs = t_.bufs
                Wp.append(w_)
            NTp = [Tile(self.uext[:, i_, 0:512].bitcast(BF16)[:, 0:512].rearrange("p (h d) -> p h d", h=8)) for i_ in range(2)]
            Xs = Tile(self.uext[:, 2, 0:512].bitcast(BF16)[:, 0:512].rearrange("p (h d) -> p h d", h=8))
        for t_ in NTp + [Xs]:
            t_.bufs = self.uext.bufs
        Us = self.xdd
        Srb = Tile(self.Sb[:, 0, :, :])
        Srb.bufs = [self.Sb.bufs[0]]
        Sk = self.MT
        wdad = Tile(self.xdt[:].rearrange("p h d -> p (h d)"))
        wdad.bufs = self.xdt.bufs
        sig, cs = self.tokx, self.ytok
        e1 = self.tmp[:, 0:512]
        e2 = self.tmp[:, 512:1024]
        tmpb = self.tmp.b
        kk = self.rb
        mu, omu, prev = self.mu, self.omu, self.tm_prev
        tmc = self.tmc

        def shifted(ch, dst_ap, dst_tile):
            bi = ch // 4
            if self._cur_blk != (L, bi):
                self._cur_wr = self.load_w(L, bi)
                self._cur_blk = (L, bi)
            ps = self.next_pp()
            self.proj_fm(self._cur_wr, ch % 4, ps)
            p.act(dst_ap, ps[:], AF.Copy, [ps.b, omu.b], [dst_tile.b], scale=omu[:, j, ch:ch + 1])
            p.stt("dve", dst_ap[:, 1:TT], ps[:, 0:TT - 1], mu[:, j, ch:ch + 1], dst_ap[:, 1:TT], ALU.mult, ALU.add, [ps.b, mu.b, dst_tile.b], [dst_tile.b])
            p.stt("dve", dst_ap[:, 0:1], prev[:, j, ch:ch + 1], mu[:, j, ch:ch + 1], dst_ap[:, 0:1], ALU.mult, ALU.add, [prev.b, mu.b, dst_tile.b], [dst_tile.b])
            p.cp("dve", prev[:, j, ch:ch + 1], ps[:, TT - 1:TT], [ps.b], [prev.b])

        for cc in range(4):
            shifted(cc, rT[:, cc, :], rT)
        for cc in range(4):
            shifted(4 + cc, kT[:, cc, :], kT)
        for cc in range(4):
            shifted(8 + cc, vT[:, cc, :], vT)
        for cc in range(4):
            shifted(12 + cc, sgT[:, cc, :], sgT)
        shifted(16, e1, self.tmp)
        p.act(wdad[0:64, :], e1[0:64, :], AF.Tanh, [tmpb], [wdad.b])
        p.cp("dve", wdad[64:128, :], e1[64:128, :], [tmpb], [wdad.b])
        for cc in range(4):
            p.act(sgT[:, cc, :], sgT[:, cc, :], AF.Silu, [sgT.b], [sgT.b])
        if self.rw_stop == 1:
            for cc in range(4):
                p.ms("dve", self.yT[:, cc, :], 0.0, [self.yT.bufs[cc]])
            return
        xs0 = self.xs[0]
        regs = [Tile(xs0[:, s_, h_ * 512:(h_ + 1) * 512]) for s_ in range(4) for h_ in range(2)] + [self.tokx, self.ytok]
        sets = [regs[0:5], regs[5:10]]
        own = [r_.b for r_ in regs[0:8]]
        p.ms("dve", xs0[0:1, 0, 0:1], 0.0, [xs0.b] + own)
        npp6 = self.next_pp6

        def ph2(cc, S):
            sig, cs, kk, e1t, e2t = S
            e1, e2 = e1t[:], e2t[:]
            csl = slice(cc * 128, (cc + 1) * 128)
            v3 = lambda ap: ap.rearrange("p (n d) -> p n d", d=64)
            pz = npp6()
            p.mm(pz[:], self.w2a2[0:64, j, csl], wdad[0:64, :], True, True, [self.w2a2.b, wdad.b], [pz.b])
            p.act(sig[:], pz[:], AF.Sigmoid, [pz.b, tmc.b], [sig.b], bias=tmc[:, j, 0, cc:cc + 1])
            pa = npp6()
            p.mm(pa[:], self.w2a2[64:128, j, csl], wdad[64:128, :], True, True, [self.w2a2.b, wdad.b], [pa.b])
            p.act(aT[:, cc, :], pa[:], AF.Sigmoid, [pa.b, tmc.b], [aT.b], bias=tmc[:, j, 1, cc:cc + 1])
            yield
            p.op("dve", lambda e: e.tensor_tensor_scan(out=cs[:], data0=cst[:, C_RST:C_RST + TT], data1=sig[:], initial=0.0,
                                                        op0=ALU.mult, op1=ALU.add), [cst.b, sig.b], [cs.b])
            p.ts("dve", kk[:], kT[:, cc, :], tmc[:, j, 2, cc:cc + 1], None, ALU.mult, None, [kT.b, tmc.b], [kk.b])
            p.tt("dve", e2, kk[:], kk[:], ALU.mult, [kk.b], [e2t.b])
            pn = npp6()
            p.mm(pn[:], cst[:, C_BLK:C_BLK + 128], e2, True, True, [cst.b, e2t.b], [pn.b])
            yield
            p.act(self.gl[:, cc, :], v3(cs[:])[:, :, 63], AF.Exp, [cs.b], [self.gl.b], scale=LW)
            p.act(e1, cs[:], AF.Exp, [cs.b], [e1t.b], scale=LW)
            p.tt("dve", AR[:, cc, :, 1, :], v3(rT[:, cc, :]), v3(e1), ALU.mult, [rT.b, e1t.b], [AR.b])
            yield
            p.ts("dve", e2, pn[:], 1e-24, None, ALU.max, None, [pn.b], [e2t.b])
            p.op("act", lambda e: e.activation(out=e2, in_=e2, func=AF.Ln), [e2t.b], [e2t.b])
            p.op("act", lambda e: e.activation(out=e2, in_=e2, func=AF.Exp, scale=-0.5), [e2t.b], [e2t.b])
            yield
            p.tt("dve", kk[:], kk[:], e2, ALU.mult, [kk.b, e2t.b], [kk.b])
            p.tt("dve", e2, cs[:], sig[:], ALU.subtract, [cs.b, sig.b], [e2t.b])
            p.act(e2, e2, AF.Exp, [e2t.b], [e2t.b], scale=LW)
            p.act(e1, cs[:], AF.Exp, [cs.b, AR.b], [e1t.b], scale=-LW)
            yield
            p.stt("dve", AR[:, cc, :, 0, :], v3(kk[:]), -1.0, v3(e2), ALU.mult, ALU.mult, [kk.b, e2t.b], [AR.b])
            p.tt("dve", e2, kk[:], aT[:, cc, :], ALU.mult, [kk.b, aT.b], [e2t.b])
            p.tt("dve", BK[:, cc, :, 0, :], v3(e2), v3(e1), ALU.mult, [e2t.b, e1t.b], [BK.b])
            yield
            p.ts("dve", e2, aT[:, cc, :], tmc[:, j, 3, cc:cc + 1], self.tmk1[:, j, cc:cc + 1], ALU.mult, ALU.add, [aT.b, tmc.b, self.tmk1.b], [e2t.b])
            p.tt("dve", kT[:, cc, :], kT[:, cc, :], e2, ALU.mult, [kT.b, e2t.b], [kT.b])
            p.tt("dve", BK[:, cc, :, 1, :], v3(kT[:, cc, :]), v3(e1), ALU.mult, [kT.b, e1t.b], [BK.b])
            yield
            p.stt("dve", e2, rT[:, cc, :], tmc[:, j, 4, cc:cc + 1], kT[:, cc, :], ALU.mult, ALU.mult, [rT.b, tmc.b, kT.b], [e2t.b])
            pbn = npp6()
            p.mm(pbn[:], cst[:, C_BLK:C_BLK + 128], e2, True, True, [cst.b, e2t.b], [pbn.b])
            yield
            p.tt("dve", bonT[:, cc, :], pbn[:], vT[:, cc, :], ALU.mult, [pbn.b, vT.b], [bonT.b])

        for pair in range(2):
            gens = [ph2(2 * pair, sets[0]), ph2(2 * pair + 1, sets[1])]
            while gens:
                for g_ in list(gens):
                    try:
                        next(g_)
                    except StopIteration:
                        gens.remove(g_)
        p.ms("dve", xs0[0:1, 0, 0:1], 0.0, [xs0.b] + own)
        if self.rw_stop == 2:
            for cc in range(4):
                p.ms("dve", self.yT[:, cc, :], 0.0, [self.yT.bufs[cc]])
            return
        vb = Tile(self.ctm[:].rearrange("p g e t -> p (g e) t"))
        vb.bufs = self.ctm.bufs
        p.cp("act", vb[:], vT[:], [vT.b], [vb.b])
        for c in range(8):
            tsl = slice(c * 64, (c + 1) * 64)
            for ii, (dst, getsrc, srct) in enumerate(((Vt, lambda cc: vb[:, cc, tsl], vb), (Bt, lambda cc: BK[:, cc, c, 0, :], BK), (Kt, lambda cc: BK[:, cc, c, 1, :], BK))):
                if self.rw_stop == 31 and ii > 0:
                    continue
                if self.rw_stop == 33:
                    continue
                if self.rw_stop == 32 and ii != 1:
                    continue
                pt = self.next_pp()
                for cc in range(4):
                    p.mm(pt[0:64, cc * 128:(cc + 1) * 128], getsrc(cc), self.ident_b, True, True, [srct.b, self.cstb.b], [pt.b])
                if c % 2 == 0:
                    p.cp("act", dst[0:64, c, :], pt[0:64, :], [pt.b], [dst.b] if dst is not self.sq else list(self.sq.bufs))
                else:
                    p.cp("dve", dst[0:64, c, :], pt[0:64, :], [pt.b], [dst.b] if dst is not self.sq else list(self.sq.bufs))
        if self.rw_stop in (3, 31, 32, 33):
            for cc in range(4):
                p.ms("dve", self.yT[:, cc, :], 0.0, [self.yT.bufs[cc]])
            return
        Btb = list(self.sq.bufs)
        p.cp("act", self.Hb[:], self.Hs[:, j, :, :], [self.Hs.b], [self.Hb.b])
        msk_su = cst[0:64, C_SU:C_SU + 64]
        msk_iu = cst[0:64, C_IU:C_IU + 64]
        msk_sl = cst[0:64, C_SL:C_SL + 64]
        idn = cst[0:64, 0:64]
        bc = lambda ap, n: ap.unsqueeze(1).to_broadcast([64, n, 64])
        npp = self.next_pp6
        SrbP = [Tile(self.Sb[:, 0, :, :]), Tile(self.Sb[:, 1, :, :])]
        SrbP[0].bufs = [self.Sb.bufs[0]]
        SrbP[1].bufs = [self.Sb.bufs[1]]
        SkP = [self.MT, Tile(self.Dall[:].rearrange("p h d -> p (h d)").bitcast(BF16)[:, 1024:2048].rearrange("p (h d) -> p h d", h=8))]
        TtP = [Tile(self.cbt[:].rearrange("p g n -> p (g n)").bitcast(BF16)[:, 0:512].rearrange("p (h d) -> p h d", h=8)),
               Tile(self.dec[:].rearrange("p h d -> p (h d)").bitcast(BF16)[:, 1024:1536].rearrange("p (h d) -> p h d", h=8))]
        TtP[0].bufs = self.cbt.bufs

        def prep_units(c):
            Srb, Sk, Tt = SrbP[c % 2], SkP[c % 2], TtP[c % 2]
            W0, NT0 = Wp[0], NTp[0]
            units = []

            def scores(g):
                hs = slice(g * 4, (g + 1) * 4)
                rows = slice(g * 64, (g + 1) * 64)
                Pb, Pk, Pa = npp(), npp(), npp()
                for q in range(4):
                    ar = AR[rows, q, c, :, :].rearrange("p q d -> p (q d)")
                    p.mm(Pb[0:64, q * 128:(q + 1) * 128], BK[rows, q, c, 0, :], ar, True, True, [BK.b, AR.b], [Pb.b])
                    p.mm(Pk[0:64, q * 128:(q + 1) * 128], BK[rows, q, c, 1, :], ar, True, True, [BK.b, AR.b], [Pk.b])
                    p.mm(Pa[0:64, q * 64:(q + 1) * 64], AR[rows, q, c, 0, :], BK[rows, q, c, 0, :], True, True, [BK.b, AR.b], [Pa.b])
                Pb3 = Pb[0:64, :].rearrange("p (h d) -> p h d", h=4)
                Pk3 = Pk[0:64, :].rearrange("p (h d) -> p h d", h=4)
                p.tt("dve", W0[0:64, hs, 0:64], Pb3[:, :, 0:64], bc(msk_su, 4), ALU.mult, [Pb.b, cst.b], [W0.b])
                p.tt("dve", Srb[0:64, hs, :], Pb3[:, :, 64:128], bc(msk_iu, 4), ALU.mult, [Pb.b, cst.b], [Srb.b])
                p.tt("dve", Sk[0:64, hs, 0:64], Pk3[:, :, 0:64], bc(msk_su, 4), ALU.mult, [Pk.b, cst.b], [Sk.b])
                p.tt("dve", Sk[0:64, hs, 64:128], Pk3[:, :, 64:128], bc(msk_iu, 4), ALU.mult, [Pk.b, cst.b], [Sk.b])
                p.tt("dve", NT0[0:64, hs, :], Pa[0:64, 0:256].rearrange("p (h d) -> p h d", h=4), bc(msk_sl, 4), ALU.mult, [Pa.b, cst.b], [NT0.b])

            def level(lvl, g):
                cur = lvl % 2
                Wc, NTc = Wp[cur], NTp[cur]
                Wn, NTn = Wp[1 - cur], NTp[1 - cur]
                last = (lvl == 5)
                hs = slice(g * 4, (g + 1) * 4)
                P1 = npp()
                if not last:
                    P2 = npp()
                for q in range(4):
                    h = g * 4 + q
                    if lvl == 0:
                        p.mm(P1[0:64, q * 128:q * 128 + 64], NTc[0:64, h, :], Wc[0:64, h, 0:64], True, True, [NTc.b, Wc.b], [P1.b])
                    elif not last:
                        p.mm(P1[0:64, q * 128:(q + 1) * 128], NTc[0:64, h, :], Wc[0:64, h, :], True, True, [NTc.b, Wc.b], [P1.b])
                    else:
                        p.mm(P1[0:64, q * 128 + 64:(q + 1) * 128], NTc[0:64, h, :], Wc[0:64, h, 64:128], True, True, [NTc.b, Wc.b], [P1.b])
                    if not last:
                        p.mm(P2[0:64, q * 64:(q + 1) * 64], Wc[0:64, h, 0:64], NTc[0:64, h, :], True, True, [NTc.b, Wc.b], [P2.b])
                P13 = P1[0:64, :].rearrange("p (h d) -> p h d", h=4)
                if not last:
                    p.cp("act", Wn[0:64, hs, 0:64], P13[:, :, 0:64], [P1.b], [Wn.b])
                    p.cp("act", NTn[0:64, hs, :], P2[0:64, 0:256].rearrange("p (h d) -> p h d", h=4), [P2.b], [NTn.b])
                if lvl == 0:
                    p.tt("dve", Wn[0:64, hs, 64:128], Wc[0:64, hs, 0:64], bc(idn, 4), ALU.add, [Wc.b, cst.b], [Wn.b])
                elif not last:
                    p.tt("dve", Wn[0:64, hs, 64:128], P13[:, :, 64:128], Wc[0:64, hs, 64:128], ALU.add, [P1.b, Wc.b], [Wn.b])
                else:
                    p.tt("dve", Tt[0:64, hs, :], P13[:, :, 64:128], Wc[0:64, hs, 64:128], ALU.add, [P1.b, Wc.b], [Tt.b])

            for g in range(2):
                units.append(lambda g=g: scores(g))
            for lvl in range(6):
                for g in range(2):
                    units.append(lambda lvl=lvl, g=g: level(lvl, g))
            return units

        def chain_units(c):
            Srb, Sk, Tt = SrbP[c % 2], SkP[c % 2], TtP[c % 2]
            st = {}

            def hterm(PS, qd):
                for q in range(4):
                    p.mm(PS[0:64, q * 64:(q + 1) * 64], AR[64:128, q, c, qd, :], self.Hb[64:128, q, :], True, True, [AR.b, self.Hb.b], [PS.b])

            def ux():
                PX, PXo = npp(), npp()
                hterm(PXo, 0)
                for hh in range(8):
                    g, q = hh // 4, hh % 4
                    h = 2 * q + g
                    o = PX[0:64, hh * 64:(hh + 1) * 64]
                    if g == 0:
                        p.mm(o, AR[0:64, q, c, 0, :], self.Hb[0:64, q, :], True, False, [AR.b, self.Hb.b], [PX.b])
                    p.mm(o, Sk[0:64, hh, 0:64], Vt[0:64, c, h * 64:(h + 1) * 64], g == 1, True, [Sk.b, Vt.b], [PX.b])
                p.cp("act", Xs[0:64, 0:4, :], PX[0:64, 0:256].rearrange("p (h d) -> p h d", h=4), [PX.b], [Xs.b])
                p.cp("act", Xs[0:64, 4:8, :], PXo[0:64, 0:256].rearrange("p (h d) -> p h d", h=4), [PXo.b], [Xs.b])
                p.tt("dve", Xs[0:64, 4:8, :], Xs[0:64, 4:8, :], PX[0:64, 256:512].rearrange("p (h d) -> p h d", h=4), ALU.add, [Xs.b, PX.b], [Xs.b])

            def uu():
                PU = npp()
                for hh in range(8):
                    p.mm(PU[0:64, hh * 64:(hh + 1) * 64], Tt[0:64, hh, :], Xs[0:64, hh, :], True, True, [Tt.b, Xs.b], [PU.b])
                p.cp("act", Us[0:64, :, :].rearrange("p (q g) d -> p g q d", g=2), PU[0:64, :].rearrange("p (g q d) -> p g q d", g=2, q=4), [PU.b], [Us.b])

            def uy():
                PY, PYo = npp(), npp()
                hterm(PYo, 1)
                for hh in range(8):
                    g, q = hh // 4, hh % 4
                    h = 2 * q + g
                    o = PY[0:64, hh * 64:(hh + 1) * 64]
                    if g == 0:
                        p.mm(o, AR[0:64, q, c, 1, :], self.Hb[0:64, q, :], True, False, [AR.b, self.Hb.b], [PY.b])
                    p.mm(o, Srb[0:64, hh, :], Us[0:64, h, :], g == 1, False, [Srb.b, Us.b], [PY.b])
                    p.mm(o, Sk[0:64, hh, 64:128], Vt[0:64, c, h * 64:(h + 1) * 64], False, True, [Sk.b, Vt.b], [PY.b])
                Yc = Ytok[0:64, c, :].rearrange("p (q g d) -> p g q d", g=2, d=64)
                p.cp("dve", Yc, PY[0:64, :].rearrange("p (g q d) -> p g q d", g=2, q=4), [PY.b], [Ytok.b])
                p.tt("dve", Yc[:, 1, :, :], Yc[:, 1, :, :], PYo[0:64, 0:256].rearrange("p (q d) -> p q d", q=4), ALU.add, [Ytok.b, PYo.b], [Ytok.b])

            def uh():
                PH = npp()
                for cc in range(4):
                    o = PH[:, cc * 128:(cc + 1) * 128]
                    p.mm(o, Bt[0:64, c, cc * 128:(cc + 1) * 128], Us[0:64, 2 * cc:2 * cc + 2, :].rearrange("p h d -> p (h d)"), True, False, Btb + [Us.b], [PH.b])
                    p.mm(o, Kt[0:64, c, cc * 128:(cc + 1) * 128], Vt[0:64, c, cc * 128:(cc + 1) * 128], False, True, [Kt.b, Vt.b], [PH.b])
                PH3 = PH[:].rearrange("p (c d) -> p c d", c=4)
                for hl in range(2):
                    rws = slice(hl * 64, (hl + 1) * 64)
                    p.tt("dve", self.Hs[rws, j, :, :], self.Hs[rws, j, :, :], PH3[rws, :, hl * 64:(hl + 1) * 64], ALU.add, [self.Hs.b, PH.b], [self.Hs.b])
                    p.tt("dve", self.Hs[rws, j, :, :], self.Hs[rws, j, :, :], self.gl[rws, :, c].unsqueeze(2).to_broadcast([64, 4, 64]), ALU.mult,
                         [self.Hs.b, self.gl.b], [self.Hs.b])
                p.cp("act", self.Hb[:], self.Hs[:, j, :, :], [self.Hs.b], [self.Hb.b])

            return [ux, uu, uy, uh]

        prev_chain = []
        for c in range(9):
            pu = prep_units(c) if c < 8 else []
            cu = prev_chain
            i_ = j_ = 0
            while i_ < len(pu) or j_ < len(cu):
                for _ in range(4):
                    if i_ < len(pu):
                        pu[i_]()
                        i_ += 1
                if j_ < len(cu):
                    cu[j_]()
                    j_ += 1
            prev_chain = chain_units(c) if c < 8 else []
        if self.rw_stop in (4, 5, 6):
            for cc in range(4):
                p.ms("dve", self.yT[:, cc, :], 0.0, [self.yT.bufs[cc]])
            return
        Y3 = Ytok[0:64, :, :].rearrange("p c (h d) -> p (c h) d", d=64)
        st = Tile(self.uext[:, 3, 0:512])
        st.bufs = self.uext.bufs
        s1 = st[0:64, 0:64]
        s2 = st[0:64, 64:128]
        s3 = st[0:64, 128:192]
        p.op("dve", lambda e: e.reduce_sum(out=s1, in_=Y3, axis=mybir.AxisListType.X), [Ytok.b], [st.b])
        sqt = Tile(self.F[5][:].rearrange("p c t -> p (c t)")[:, 0:2048].rearrange("p (a b) -> p a b", b=64))
        sqt.bufs = self.F[5].bufs
        for hf in range(2):
            ysl = Y3[:, hf * 32:(hf + 1) * 32, :]
            p.tt("dve", sqt[0:64, :, :], ysl, ysl, ALU.mult, [Ytok.b], [sqt.b])
            p.op("dve", lambda e, hf=hf: e.reduce_sum(out=s2[:, hf * 32:(hf + 1) * 32], in_=sqt[0:64, :, :], axis=mybir.AxisListType.X), [sqt.b], [st.b])
        p.ts("dve", s1, s1, 1.0 / 64, None, ALU.mult, None, [st.b], [st.b])
        p.tt("dve", s3, s1, s1, ALU.mult, [st.b], [st.b])
        p.stt("dve", s2, s2, 1.0 / 64, s3, ALU.mult, ALU.subtract, [st.b], [st.b])
        p.ts("dve", s2, s2, 64e-5, None, ALU.add, None, [st.b], [st.b])
        p.op("act", lambda e: e.activation(out=s2, in_=s2, func=AF.Ln), [st.b], [st.b])
        p.op("act", lambda e: e.activation(out=s2, in_=s2, func=AF.Exp, scale=-0.5), [st.b], [st.b])
        p.tt("dve", Y3, Y3, s1.unsqueeze(2).to_broadcast([64, 64, 64]), ALU.subtract, [Ytok.b, st.b], [Ytok.b])
        p.tt("dve", Y3, Y3, s2.unsqueeze(2).to_broadcast([64, 64, 64]), ALU.mult, [Ytok.b, st.b], [Ytok.b])
        for cc in range(4):
            py = self.next_pp()
            for c in range(8):
                p.tr(py[:, c * 64:(c + 1) * 64], Ytok[0:64, c, cc * 128:(cc + 1) * 128], cst[0:64, 0:64], [Ytok.b, cst.b], [py.b])
            p.act(e1, py[:], AF.Identity, [py.b, tmc.b], [tmpb], scale=tmc[:, j, 5, cc:cc + 1], bias=tmc[:, j, 6, cc:cc + 1])
            p.tt("dve", e1, e1, bonT[:, cc, :], ALU.add, [tmpb, bonT.b], [tmpb])
            p.tt("dve", self.yT[:, cc, :], e1, sgT[:, cc, :], ALU.mult, [tmpb, sgT.b], [self.yT.bufs[cc]])

    def build(self):
        if self.prefetch and self.wseq is None and not getattr(self, "_recording", False):
            rec = Builder(self.NL, self.NT, self.mixers, False)
            rec._recording = True
            rec.rw_stop = self.rw_stop
            rec.build()
            self.wseq = rec.wrec
        p = self.p
        self.declare()
        self.alloc()
        self.prologue()
        xr = self.x.rearrange("(t s p) d -> t p s d", p=128, s=4)
        orr = self.out.rearrange("(t s p) d -> t p s d", p=128, s=4)
        for t in range(self.NT):
            p.dma("sp", self.xs[0][:], xr[t], (), [self.xs[0].b])
            self.load_tile(t)
            for L in range(self.NL):
                if L % 2 == 0:
                    self.even_layer(L)
                else:
                    self.odd_layer(L)
            self.store_tile(t, orr)
        p.finish()
        return self.nc


def make_in_maps(inputs):
    consts = make_consts()
    maps = []
    for core in range(8):
        b = core % 4
        m = {}
        for k, v in inputs.items():
            v = np.asarray(v)
            if k == "x":
                m[k] = np.ascontiguousarray(v[b])
            elif k == "c":
                m[k] = np.ascontiguousarray(v[b].reshape(8, 128))
            elif k in ("tm_k_k", "tm_k_a", "tm_r_k"):
                m[k] = np.ascontiguousarray(v.reshape(2, 512))
            else:
                m[k] = np.ascontiguousarray(v)
        m["consts"] = consts
        maps.append(m)
    return maps


def kernel(**inputs):
    import os
    bld = Builder(mixers=os.environ.get("K_MIXERS", "ABCD"))
    bld.rw_stop = int(os.environ.get("K_RWSTOP", "0"))
    nc = bld.build()
    maps = make_in_maps(inputs)
    res = run_bass_kernel_spmd(nc, maps, core_ids=list(range(8)))
    out = np.stack([res.results[b]["out"] for b in range(4)], axis=0)
    return out.astype(np.float32)
```

```python
import numpy as np
import concourse.bass as bass
import concourse.mybir as mybir
from concourse.bass_utils import run_bass_kernel_spmd

F32 = mybir.dt.float32
BF16 = mybir.dt.bfloat16
AF = mybir.ActivationFunctionType
ALU = mybir.AluOpType

D = 1024
SEQ = 8192
TT = 512
NDS = 24
EVEN_COLS = 4224
ODD_COLS = 3080


class Buf:
    __slots__ = ("w", "r")

    def __init__(self):
        self.w = {}
        self.r = {}


class Tile:
    def __init__(self, h, nb=1):
        self.h = h
        self.bufs = [Buf() for _ in range(nb)]

    def __getitem__(self, k):
        return self.h[k]

    @property
    def b(self):
        return self.bufs[0]


class Prog:
    def __init__(self, nc):
        self.nc = nc
        self.eng = {"pe": nc.tensor, "act": nc.scalar, "dve": nc.vector, "pool": nc.gpsimd, "sp": nc.sync}
        self.sem = {}
        self.cnt = {}
        self.cur = {}
        self.qeng = {}
        self.epochs = {}
        for q in ("pe", "act", "dve", "pool"):
            self._new_epoch(q)
        for i in range(NDS):
            self.sem["d%d" % i] = nc.alloc_semaphore("d%d" % i)
            self.cnt["d%d" % i] = 0
        self.dnext = 0
        self.known = {e: {} for e in self.eng}
        self.snap = {}
        self.nwait = 0
        self.nins = 0
        import os
        self.raw_only = os.environ.get("K_RAWONLY", "0") == "1"

    SEM_LIMIT = 30000

    def _new_epoch(self, e):
        k = self.epochs.get(e, 0)
        self.epochs[e] = k + 1
        key = "%s#%d" % (e, k)
        self.sem[key] = self.nc.alloc_semaphore("s_%s_%d" % (e, k))
        self.cnt[key] = 0
        self.cur[e] = key
        self.qeng[key] = e

    def sb(self, name, shape, dt, nb=1):
        return Tile(self.nc.alloc_sbuf_tensor(name, list(shape), dt), nb)

    def ps(self, name, shape, dt=F32, nb=1):
        return Tile(self.nc.alloc_psum_tensor(name, list(shape), dt), nb)

    def _wait(self, e, evs, same_ok=True):
        kn = self.known[e]
        need = {}
        for (q, v) in evs:
            if same_ok and self.qeng.get(q) == e:
                continue
            if kn.get(q, 0) >= v:
                continue
            if need.get(q, 0) < v:
                need[q] = v
        for q, v in need.items():
            if kn.get(q, 0) >= v:
                continue
            self.eng[e].wait_ge(self.sem[q], v)
            self.nwait += 1
            kn[q] = v
            s = self.snap.get((q, v))
            if s:
                for q2, v2 in s.items():
                    if kn.get(q2, 0) < v2:
                        kn[q2] = v2

    def _deps(self, reads, writes):
        ev = []
        for b in reads:
            ev.extend(b.w.items())
        for b in writes:
            ev.extend(b.w.items())
            ev.extend(b.r.items())
        return ev

    def op(self, e, fn, reads=(), writes=()):
        if e != "pe" and self.raw_only:
            raw = []
            for b in reads:
                for q, v in b.w.items():
                    if self.qeng.get(q) == e:
                        raw.append((q, v))
            if raw:
                self._wait(e, raw, same_ok=False)
            self._wait(e, self._deps(reads, writes), same_ok=True)
        else:
            self._wait(e, self._deps(reads, writes), same_ok=(e == "pe"))
        ins = fn(self.eng[e])
        key = self.cur[e]
        self.cnt[key] += 1
        n = self.cnt[key]
        ins.then_inc(self.sem[key], 1)
        self.nins += 1
        self.snap[(key, n)] = dict(self.known[e])
        for b in reads:
            b.r[key] = n
        for b in writes:
            b.w[key] = n
            b.r = {}
        if n >= self.SEM_LIMIT:
            self._new_epoch(e)

    def dma(self, e, out, in_, reads=(), writes=(), **kw):
        i = self.dnext
        self.dnext = (i + 1) % NDS
        q = "d%d" % i
        ev = self._deps(reads, writes)
        if self.cnt[q] > 0:
            ev.append((q, self.cnt[q]))
        self._wait(e, ev, same_ok=False)
        self.eng[e].dma_start(out=out, in_=in_, **kw).then_inc(self.sem[q], 16)
        self.cnt[q] += 16
        n = self.cnt[q]
        self.nins += 1
        self.snap[(q, n)] = dict(self.known[e])
        for b in reads:
            b.r[q] = n
        for b in writes:
            b.w[q] = n
            b.r = {}

    def finish(self):
        self._wait("sp", [(q, v) for q, v in self.cnt.items() if v > 0], same_ok=False)

    def act(self, out, in_, func, reads, writes, **kw):
        self.op("act", lambda e: e.activation(out=out, in_=in_, func=func, **kw), reads, writes)

    def mm(self, out, lhsT, rhs, start, stop, reads, writes):
        self.op("pe", lambda e: e.matmul(out, lhsT=lhsT, rhs=rhs, start=start, stop=stop), reads, writes)

    def tr(self, out, in_, ident, reads, writes):
        self.op("pe", lambda e: e.transpose(out, in_, ident), reads, writes)

    def ts(self, eng, out, in0, s1, s2, op0, op1, reads, writes):
        if s2 is None:
            self.op(eng, lambda e: e.tensor_scalar(out=out, in0=in0, scalar1=s1, scalar2=None, op0=op0), reads, writes)
        else:
            self.op(eng, lambda e: e.tensor_scalar(out=out, in0=in0, scalar1=s1, scalar2=s2, op0=op0, op1=op1), reads, writes)

    def tt(self, eng, out, in0, in1, op, reads, writes):
        self.op(eng, lambda e: e.tensor_tensor(out=out, in0=in0, in1=in1, op=op), reads, writes)

    def stt(self, eng, out, in0, s, in1, op0, op1, reads, writes):
        self.op(eng, lambda e: e.scalar_tensor_tensor(out=out, in0=in0, scalar=s, in1=in1, op0=op0, op1=op1), reads, writes)

    def cp(self, eng, out, in_, reads, writes):
        if eng == "act":
            self.op(eng, lambda e: e.activation(out=out, in_=in_, func=AF.Copy), reads, writes)
        else:
            self.op(eng, lambda e: e.tensor_copy(out=out, in_=in_), reads, writes)

    def rsqrt(self, ap, tile):
        self.op("act", lambda e: e.activation(out=ap, in_=ap, func=AF.Ln), [tile.b], [tile.b])
        self.op("act", lambda e: e.activation(out=ap, in_=ap, func=AF.Exp, scale=-0.5), [tile.b], [tile.b])

    def ms(self, eng, ap, val, writes):
        self.op(eng, lambda e: e.memset(ap, val), (), writes)


CW = 2688
C_ID = 0
C_SU = 128
C_IU = 192
C_SL = 256
C_BLK = 384
C_ONE = 512
C_TRI2 = 640
C_MB2 = 768
C_SEL0 = 896
C_SEL1 = 1024
C_RST = 1152
CWF = 1664
C_PAR0 = 1664
C_PAR1 = 2176
CB_PAR0 = 640
CB_PAR1 = 1152


def make_consts():
    c = np.zeros((128, CW), np.float32)
    c[:, 0:128] = np.eye(128, dtype=np.float32)
    s = np.arange(64)[:, None]
    t = np.arange(64)[None, :]
    c[0:64, C_SU:C_SU + 64] = (s < t)
    c[0:64, C_IU:C_IU + 64] = (s <= t)
    c[0:64, C_SL:C_SL + 64] = (s > t)
    c[0:64, C_BLK:C_BLK + 64] = 1.0
    c[64:128, C_BLK + 64:C_BLK + 128] = 1.0
    c[:, C_ONE:C_ONE + 128] = 1.0
    r = np.arange(128)[:, None]
    q = np.arange(128)[None, :]
    same = (r // 64) == (q // 64)
    c[:, C_TRI2:C_TRI2 + 128] = (same & (r <= q))
    c[:, C_MB2:C_MB2 + 128] = np.where(same & (r <= q), 0.0, -1.0e5)
    c[0:64, C_SEL0:C_SEL0 + 128] = 1.0
    c[64:128, C_SEL1:C_SEL1 + 128] = 1.0
    tk = np.arange(512)
    c[:, C_PAR0:C_PAR0 + 512] = ((tk // 64) % 2 == 0)[None, :]
    c[:, C_PAR1:C_PAR1 + 512] = ((tk // 64) % 2 == 1)[None, :]
    c[:, C_RST:C_RST + 512] = (tk % 64 != 0)[None, :]
    return c


class Builder:
    def __init__(self, n_layers=4, n_tiles=16, mixers="ABCD", dbg=False):
        self.NL = n_layers
        self.NT = n_tiles
        self.mixers = mixers
        self.dbg = dbg
        self.INV_DT = BF16
        self.wseq = None
        self.wrec = []
        self.wptr = 0
        self.wissued = {}
        self.prefetch = True
        self.rw_stop = 0
        self.nc = bass.Bass("TRN2", target_bir_lowering=False)
        self.p = Prog(self.nc)

    def declare(self):
        nc = self.nc
        di = lambda name, shape: nc.dram_tensor(name, list(shape), F32, kind="ExternalInput").ap()
        self.x = di("x", (SEQ, D))
        self.c = di("c", (8, 128))
        self.consts = di("consts", (128, CW))
        self.ada_w = di("ada_w", (4, D, 3 * D))
        self.ada_b = di("ada_b", (4, 3 * D))
        self.norm_pre = di("norm_pre", (4, D))
        self.norm_post = di("norm_post", (4, D))
        self.ev_w_in = di("ev_w_in", (2, D, EVEN_COLS))
        self.ev_w_out = di("ev_w_out", (2, D, D))
        self.tm_mu = di("tm_mu", (2, 2176))
        self.tm_w0 = di("tm_w0", (2, 512))
        self.tm_w2 = di("tm_w2", (2, 64, 512))
        self.tm_a0 = di("tm_a0", (2, 512))
        self.tm_a2 = di("tm_a2", (2, 64, 512))
        self.tm_k_k = di("tm_k_k", (2, 512))
        self.tm_k_a = di("tm_k_a", (2, 512))
        self.tm_r_k = di("tm_r_k", (2, 512))
        self.tm_lnx_g = di("tm_lnx_g", (2, 512))
        self.tm_lnx_b = di("tm_lnx_b", (2, 512))
        self.sc_conv_w = di("sc_conv_w", (2, 3, 512))
        self.od_w_in = di("od_w_in", (2, D, ODD_COLS))
        self.od_w_out = di("od_w_out", (2, D, D))
        self.cf_conv_w = di("cf_conv_w", (2, 31, 512))
        self.cf_conv_b = di("cf_conv_b", (2, 512))
        self.cf_ln_g = di("cf_ln_g", (2, 512))
        self.cf_ln_b = di("cf_ln_b", (2, 512))
        self.ssd_conv_w = di("ssd_conv_w", (2, 4, 1024))
        self.ssd_conv_b = di("ssd_conv_b", (2, 1024))
        self.ssd_dt_bias = di("ssd_dt_bias", (2, 8))
        self.ssd_a_log = di("ssd_a_log", (2, 8))
        self.ssd_d = di("ssd_d", (2, 8))
        self.ssd_norm_g = di("ssd_norm_g", (2, 512))
        self.out = nc.dram_tensor("out", [SEQ, D], F32, kind="ExternalOutput").ap()
        self.wblocks = []
        for L in range(self.NL):
            ncols = EVEN_COLS if L % 2 == 0 else ODD_COLS
            blks = []
            c0 = 0
            bi = 0
            while c0 < ncols + D:
                if c0 < ncols:
                    n = min(512, ncols - c0)
                    src = (self.ev_w_in if L % 2 == 0 else self.od_w_in)[L // 2][:, c0:c0 + n]
                else:
                    n = 512
                    src = (self.ev_w_out if L % 2 == 0 else self.od_w_out)[L // 2][:, c0 - ncols:c0 - ncols + n]
                sc = nc.dram_tensor("wsc_%d_%d" % (L, bi), [128, 8 * 512], BF16).ap()
                blks.append((sc, n, Buf(), src))
                c0 += n
                bi += 1
            self.wblocks.append(blks)

    def alloc(self):
        p = self.p
        self.cst = p.sb("cst", (128, CWF), F32)
        self.cstb = p.sb("cstb", (128, 1664), BF16)
        self.xs = [p.sb("xs0", (128, 4, D), F32)]
        self.xs.append(self.xs[0])
        self.xT = p.sb("xT", (128, 8, TT), F32)
        self.yo = p.sb("yo", (128, 8, TT + 3), F32)
        self.sq = p.sb("sq", (128, 8, TT), BF16, nb=8)
        self.rb = p.sb("rb", (128, TT), F32)
        self.wst = self.yo
        self.g2c = p.sb("g2c", (128, 4, 8), F32)
        self.npo = p.sb("npo", (128, 4, 8), F32)
        self.hT = p.sb("hT", (128, 8, TT), BF16)
        self.yT = p.sb("yT", (128, 8, TT), BF16, nb=8)
        self.wring = [p.sb("wr%d" % i, (128, 8, 512), BF16) for i in range(2)]
        self.wnext = 0
        self.ss = p.sb("ss", (128, 8), F32)
        self.rstd = p.sb("rstd", (128, 8), F32)
        self.g1c = p.sb("g1c", (128, 4, 8), F32)
        self.shc = p.sb("shc", (128, 4, 8), F32)
        self.cT = p.sb("cT", (128, 8), F32)
        self.abc = p.sb("abc", (128, 4, 24), F32)
        self.npc = p.sb("npc", (128, 4, 8), F32)
        self.F = [p.sb("F%d" % i, (128, 4, TT), F32) for i in range(6)]
        self.cfw = p.sb("cfw", (128, 2, 4, 31), F32)
        self.cfp = p.sb("cfp", (128, 2, 3, 4), F32)
        self.cf_tail = p.sb("cf_tail", (128, 2, 4, 30), F32)
        self.uext = p.sb("uext", (128, 4, 30 + TT), F32)
        self.sdw = p.sb("sdw", (128, 2, 8, 4), F32)
        self.sdb = p.sb("sdb", (128, 2, 8), F32)
        self.sd_tail = p.sb("sd_tail", (128, 2, 8, 3), F32)
        self.sdng = p.sb("sdng", (128, 2, 4), F32)
        self.hrow = p.sb("hrow", (128, 2, 3, 8), F32)
        self.xext = self.yo
        self.xbcT = p.sb("xbcT", (128, 8, TT), BF16)
        self.ctm = p.sb("ctm", (128, 2, 2, TT), BF16)
        self.S = p.sb("S", (128, 2, 8, 64), F32)
        self.Sb = p.sb("Sb", (128, 2, 8, 64), BF16, nb=2)
        self.tokx = p.sb("tokx", (128, 512), F32)
        self.xdt = p.sb("xdt", (128, 8, 64), BF16)
        self.xdd = p.sb("xdd", (128, 8, 64), BF16)
        self.btok = p.sb("btok", (128, 256), BF16)
        self.dts = p.sb("dts", (128, 6, 8), F32)
        self.cdb = p.sb("cdb", (128, 2, 8), F32)
        self.Dall = p.sb("Dall", (128, 8, 128), F32)
        self.dec = p.sb("dec", (128, 8, 128), F32)
        self.cbt = p.sb("cbt", (128, 2, 128), F32)
        self.MT = p.sb("MT", (128, 8, 128), BF16)
        self.ytok = p.sb("ytok", (128, 512), F32)

        self.tmp = p.sb("tmp", (128, D), F32)
        self.pp = [p.ps("pp%d" % i, (128, 512), F32) for i in range(4)]
        self.ppn = 0
        self.pt = [Tile(self.nc.alloc_psum_tensor("ptb%d" % i, [128, 1024], BF16)[:, 0:512]) for i in range(2)]
        pcb_ = self.nc.alloc_psum_tensor("pcb", [128, 512], F32)
        self.pcb = Tile(pcb_[:, 0:256].rearrange("p (g n) -> p g n", g=2))
        psm_ = self.nc.alloc_psum_tensor("psm", [128, 512], F32)
        self.psm = Tile(psm_[:, 0:64])
        self.pp6 = list(self.pp) + [Tile(pcb_[:, :]), Tile(psm_[:, :])]
        self.pp6[4].bufs = self.pcb.bufs
        self.pp6[5].bufs = self.psm.bufs
        self.pp6n = 0
        self.ptn = 0
        self.mu = p.sb("mu", (128, 2, 17), F32)
        self.omu = p.sb("omu", (128, 2, 17), F32)
        self.scw = p.sb("scw", (128, 2, 4, 3), F32)
        self.sc_tail = p.sb("sc_tail", (128, 2, 4, 2), F32)
        self.tm_prev = p.sb("tm_prev", (128, 2, 17), F32)
        self.tmc = p.sb("tmc", (128, 2, 7, 4), F32)
        self.tmk1 = p.sb("tmk1", (128, 2, 4), F32)
        self.w2a2 = p.sb("w2a2", (128, 2, 512), BF16)
        self.Hs = p.sb("Hs", (128, 2, 4, 64), F32)
        self.Hb = p.sb("Hb", (128, 4, 64), BF16)
        self.gl = p.sb("gl", (128, 4, 8), F32)

    def next_pp(self):
        t = self.pp[self.ppn]
        self.ppn = (self.ppn + 1) % len(self.pp)
        return t

    def next_pp6(self):
        t = self.pp6[self.pp6n]
        self.pp6n = (self.pp6n + 1) % len(self.pp6)
        return t

    def next_pt(self):
        t = self.pt[self.ptn]
        self.ptn = (self.ptn + 1) % len(self.pt)
        return t

    def dump(self, name, ap, tile, n):
        if not self.dbg:
            return
        p = self.p
        d = self.nc.dram_tensor("dbg_" + name, [128, n], F32, kind="ExternalOutput").ap()
        sc = self.tmp
        p.cp("dve", sc[:, 0:n], ap, list(tile.bufs), [sc.b])
        p.dma("sp", d, sc[:, 0:n], [sc.b], ())

    def prologue(self):
        p = self.p
        nc = self.nc
        p.dma("sp", self.cst[:], self.consts[:, 0:CWF], (), [self.cst.b])
        p.cp("dve", self.cstb[:, 0:640], self.cst[:, 0:640], [self.cst.b], [self.cstb.b])
        p.dma("sp", self.tmp[:], self.consts[:, C_PAR0:C_PAR0 + 1024], (), [self.tmp.b])
        p.cp("dve", self.cstb[:, 640:1664], self.tmp[:], [self.tmp.b], [self.cstb.b])
        self.ident_b = self.cstb[:, 0:128]
        self.ones_b = self.cstb[:, C_ONE:C_ONE + 128]
        self.ident_f = self.cst[:, 0:128]
        k = 0
        for L in range(self.NL):
            for (sc, n, buf, src) in self.wblocks[L]:
                p.dma("sp", self.wst[:, :, 0:n], src.rearrange("(kc p) n -> p kc n", p=128), (), [self.wst.b])
                wr = self.wring[k % 2]
                eng = ("dve", "act", "pool")[k % 3]
                if eng == "act":
                    p.act(wr[:, :, 0:n], self.wst[:, :, 0:n], AF.Copy, [self.wst.b], [wr.b])
                else:
                    p.cp(eng, wr[:, :, 0:n], self.wst[:, :, 0:n], [self.wst.b], [wr.b])
                p.dma("act", sc.rearrange("p (kc n) -> p kc n", kc=8)[:, :, 0:n], wr[:, :, 0:n], [wr.b], [buf])
                k += 1
        stg = self.tmp

        def cols(src_rows, R, W, dst_fn, dst_tile):
            for w0 in range(0, W, 1024):
                wn = min(1024, W - w0)
                p.dma("sp", stg[0:R, 0:wn], src_rows[:, w0:w0 + wn], (), [stg.b])
                for c_ in range(wn // 128):
                    ps = self.next_pp()
                    p.tr(ps[:, 0:R], stg[0:R, c_ * 128:(c_ + 1) * 128], self.cst[0:R, 0:R], [stg.b, self.cst.b], [ps.b])
                    p.cp("dve", dst_fn(w0 // 128 + c_), ps[:, 0:R], [ps.b], [dst_tile.b])

        cols(self.c, 8, 128, lambda cc: self.cT[:, :], self.cT)
        p.act(self.cT[:], self.cT[:], AF.Silu, [self.cT.b], [self.cT.b])
        cols(self.ada_b, 4, 3072, lambda cc: self.abc[:, :, cc], self.abc)
        cols(self.norm_pre, 4, 1024, lambda cc: self.npc[:, :, cc], self.npc)
        cols(self.norm_post, 4, 1024, lambda cc: self.npo[:, :, cc], self.npo)
        cols(self.tm_mu, 2, 2176, lambda cc: self.mu[:, :, cc], self.mu)
        p.ts("dve", self.omu[:], self.mu[:], -1.0, 1.0, ALU.mult, ALU.add, [self.mu.b], [self.omu.b])
        for l in range(2):
            for i_, src in enumerate((self.tm_w0, self.tm_a0, self.tm_k_k, self.tm_k_a, self.tm_r_k, self.tm_lnx_g, self.tm_lnx_b)):
                cols(src[l:l + 1, :], 1, 512, lambda cc, l=l, i_=i_: self.tmc[:, l, i_, cc:cc + 1], self.tmc)
            p.ts("dve", self.tmk1[:, l, :], self.tmc[:, l, 3, :], -1.0, 1.0, ALU.mult, ALU.add, [self.tmc.b], [self.tmk1.b])
            p.dma("sp", stg[0:64, 0:512], self.tm_w2[l], (), [stg.b])
            p.dma("sp", stg[64:128, 0:512], self.tm_a2[l], (), [stg.b])
            p.cp("dve", self.w2a2[:, l, :], stg[:, 0:512], [stg.b], [self.w2a2.b])
            cols(self.sc_conv_w[l], 3, 512, lambda cc, l=l: self.scw[:, l, cc, :], self.scw)
            cols(self.cf_conv_w[l], 31, 512, lambda cc, l=l: self.cfw[:, l, cc, :], self.cfw)
            for i_, src in enumerate((self.cf_conv_b, self.cf_ln_g, self.cf_ln_b)):
                cols(src[l:l + 1, :], 1, 512, lambda cc, l=l, i_=i_: self.cfp[:, l, i_, cc:cc + 1], self.cfp)
            cols(self.ssd_conv_w[l], 4, 1024, lambda cc, l=l: self.sdw[:, l, cc, :], self.sdw)
            cols(self.ssd_conv_b[l:l + 1, :], 1, 1024, lambda cc, l=l: self.sdb[:, l, cc:cc + 1], self.sdb)
            cols(self.ssd_norm_g[l:l + 1, :], 1, 512, lambda cc, l=l: self.sdng[:, l, cc:cc + 1], self.sdng)
            for i_, src in enumerate((self.ssd_dt_bias, self.ssd_a_log, self.ssd_d)):
                p.dma("sp", stg[0:1, 0:8], src[l:l + 1, :], (), [stg.b])
                ps = self.next_pp()
                p.mm(ps[:, 0:8], self.cst[0:1, C_ONE:C_ONE + 128], stg[0:1, 0:8], True, True, [self.cst.b, stg.b], [ps.b])
                p.cp("dve", self.hrow[:, l, i_, :], ps[:, 0:8], [ps.b], [self.hrow.b])
            p.act(self.hrow[:, l, 1, :], self.hrow[:, l, 1, :], AF.Exp, [self.hrow.b], [self.hrow.b])
            p.ts("dve", self.hrow[:, l, 1, :], self.hrow[:, l, 1, :], -1.0, None, ALU.mult, None, [self.hrow.b], [self.hrow.b])
        p.ms("dve", self.Hs[:], 0.0, [self.Hs.b])
        p.ms("dve", self.cf_tail[:], 0.0, [self.cf_tail.b])
        p.ms("dve", self.sd_tail[:], 0.0, [self.sd_tail.b])
        p.ms("dve", self.S[:], 0.0, [self.S.b])
        p.ms("dve", self.sc_tail[:], 0.0, [self.sc_tail.b])
        p.ms("dve", self.tm_prev[:], 0.0, [self.tm_prev.b])
        for L in range(self.NL):
            aw = self.ada_w[L]
            pcol = self.next_pp()
            for blk in range(4):
                p.dma("sp", self.wst[:, :, 0:512], aw[:, blk * 512:(blk + 1) * 512].rearrange("(kc p) n -> p kc n", p=128), (), [self.wst.b])
                for jj in range(4):
                    j = blk * 4 + jj
                    for kc in range(8):
                        p.mm(pcol[:, j:j + 1], self.wst[:, kc, jj * 128:(jj + 1) * 128], self.cT[:, kc:kc + 1], kc == 0, kc == 7,
                             [self.wst.b, self.cT.b], [pcol.b])
            p.tt("dve", self.shc[:, L, :], pcol[:, 0:8], self.abc[:, L, 0:8], ALU.add, [pcol.b, self.abc.b], [self.shc.b])
            p.tt("dve", self.g1c[:, L, :], pcol[:, 8:16], self.abc[:, L, 8:16], ALU.add, [pcol.b, self.abc.b], [self.g1c.b])
            p.stt("dve", self.g1c[:, L, :], self.g1c[:, L, :], 1.0, self.npc[:, L, :], ALU.add, ALU.mult, [self.g1c.b, self.npc.b], [self.g1c.b])
            pcol2 = self.next_pp()
            for blk in range(2):
                p.dma("sp", self.wst[:, :, 0:512], aw[:, 2 * D + blk * 512:2 * D + (blk + 1) * 512].rearrange("(kc p) n -> p kc n", p=128), (), [self.wst.b])
                for jj in range(4):
                    j = blk * 4 + jj
                    for kc in range(8):
                        p.mm(pcol2[:, j:j + 1], self.wst[:, kc, jj * 128:(jj + 1) * 128], self.cT[:, kc:kc + 1], kc == 0, kc == 7,
                             [self.wst.b, self.cT.b], [pcol2.b])
            p.tt("dve", self.g2c[:, L, :], pcol2[:, 0:8], self.abc[:, L, 16:24], ALU.add, [pcol2.b, self.abc.b], [self.g2c.b])
            p.tt("dve", self.g2c[:, L, :], self.g2c[:, L, :], self.npo[:, L, :], ALU.mult, [self.g2c.b, self.npo.b], [self.g2c.b])

    def _issue_w(self, k):
        L, bi = self.wseq[k]
        sc, n, buf, _ = self.wblocks[L][bi]
        wr = self.wring[k % len(self.wring)]
        self.p.dma("sp", wr[:, :, 0:n], sc.rearrange("p (kc n) -> p kc n", kc=8)[:, :, 0:n], [buf], [wr.b])
        self.wissued[k] = wr

    def load_w(self, L, bi):
        if self.wseq is None:
            self.wrec.append((L, bi))
            sc, n, buf, _ = self.wblocks[L][bi]
            wr = self.wring[self.wnext]
            self.wnext = (self.wnext + 1) % len(self.wring)
            self.p.dma("sp", wr[:, :, 0:n], sc.rearrange("p (kc n) -> p kc n", kc=8)[:, :, 0:n], [buf], [wr.b])
            return wr
        k = self.wptr
        assert self.wseq[k] == (L, bi), (k, self.wseq[k], (L, bi))
        if k not in self.wissued:
            self._issue_w(k)
        wr = self.wissued.pop(k)
        self.wptr += 1
        if self.wptr < len(self.wseq):
            self._issue_w(self.wptr)
        return wr

    def load_tile(self, t):
        p = self.p
        xs = self.xs[t % 2]
        for kc in range(8):
            ps = self.next_pp()
            for s in range(4):
                p.tr(ps[:, s * 128:(s + 1) * 128], xs[:, s, kc * 128:(kc + 1) * 128], self.ident_f, [xs.b, self.cst.b], [ps.b])
            if kc % 2 == 0:
                p.act(self.xT[:, kc, :], ps[:], AF.Copy, [ps.b], [self.xT.b])
            else:
                p.cp("dve", self.xT[:, kc, :], ps[:], [ps.b], [self.xT.b])

    def store_tile(self, t, orr):
        p = self.p
        xs = self.xs[t % 2]
        for s in range(4):
            for half in range(2):
                ps = self.next_pp()
                for q in range(4):
                    kc = half * 4 + q
                    p.tr(ps[:, q * 128:(q + 1) * 128], self.xT[:, kc, s * 128:(s + 1) * 128], self.ident_f, [self.xT.b, self.cst.b], [ps.b])
                if half == 0:
                    p.act(xs[:, s, 0:512], ps[:], AF.Copy, [ps.b], [xs.b])
                else:
                    p.cp("dve", xs[:, s, 512:1024], ps[:], [ps.b], [xs.b])
        p.dma("sp", orr[t], xs[:], [xs.b], ())

    def rms_bcast(self, src):
        p = self.p
        for kc in range(8):
            p.act(self.sq[:, kc, :], src[:, kc, 0:TT], AF.Square, [src.b], [self.sq.bufs[kc]])
        ps = self.next_pp()
        for kc in range(8):
            p.mm(ps[:], self.ones_b, self.sq[:, kc, :], kc == 0, kc == 7, [self.cstb.b, self.sq.bufs[kc]], [ps.b])
        p.ts("dve", self.rb[:], ps[:], 1.0 / D, 1e-6, ALU.mult, ALU.add, [ps.b], [self.rb.b])
        p.op("act", lambda e: e.activation(out=self.rb[:], in_=self.rb[:], func=AF.Ln), [self.rb.b], [self.rb.b])
        p.op("act", lambda e: e.activation(out=self.rb[:], in_=self.rb[:], func=AF.Exp, scale=-0.5), [self.rb.b], [self.rb.b])

    def pre_norm(self, L):
        p = self.p
        if L == 0 and not getattr(self, "_d0", False):
            self.dump("xT0", self.xT[:, 0, :], self.xT, 512)
            self.dump("g1c", self.g1c[:, 0, :], self.g1c, 8)
            self.dump("shc", self.shc[:, 0, :], self.shc, 8)
            self.dump("cT", self.cT[:], self.cT, 8)
            self.dump("npc", self.npc[:, 0, :], self.npc, 8)
            self.dump("abc", self.abc[:, 0, :], self.abc, 24)
            self.dump("xT7", self.xT[:, 7, :], self.xT, 512)
        self.rms_bcast(self.xT)
        if L == 0 and not getattr(self, "_d0", False):
            self._d0 = True
            self._d0b = True
            self.dump("rb0", self.rb[:], self.rb, 512)
            self.dump("sq0", self.sq[:, 0, :], self.sq, 512)
        for kc in range(8):
            p.tt("dve", self.tmp[:, 0:512], self.xT[:, kc, :], self.rb[:], ALU.mult, [self.xT.b, self.rb.b], [self.tmp.b])
            p.ts("dve", self.hT[:, kc, :], self.tmp[:, 0:512], self.g1c[:, L, kc:kc + 1], self.shc[:, L, kc:kc + 1], ALU.mult, ALU.add,
                 [self.tmp.b, self.g1c.b, self.shc.b], [self.hT.b])
        if getattr(self, "_d0b", False):
            self._d0b = False
            self.dump("hT", self.hT[:, 0, :], self.hT, 512)
            self.dump("hT7", self.hT[:, 7, :], self.hT, 512)

    def proj_fm(self, wr, lc, ps):
        p = self.p
        for kc in range(8):
            p.mm(ps[:], wr[:, kc, lc * 128:(lc + 1) * 128], self.hT[:, kc, :], kc == 0, kc == 7, [wr.b, self.hT.b], [ps.b])

    def out_proj_post(self, L):
        p = self.p
        nb = len(self.wblocks[L])
        allY = self.yT.bufs
        for half in range(2):
            wr = self.load_w(L, nb - 2 + half)
            for q in range(4):
                dmc = half * 4 + q
                ps = self.next_pp()
                for cc in range(8):
                    p.mm(ps[:], wr[:, cc, q * 128:(q + 1) * 128], self.yT[:, cc, :], cc == 0, cc == 7, [wr.b] + allY, [ps.b])
                p.act(self.yo[:, dmc, 0:TT], ps[:], AF.Copy, [ps.b], [self.yo.b])
        if not getattr(self, "_d1", False):
            self._d1 = True
            self.dump("yo0", self.yo[:, 0, 0:TT], self.yo, 512)
            self.dump("yo7", self.yo[:, 7, 0:TT], self.yo, 512)
        self.rms_bcast(self.yo)
        for dmc in range(8):
            p.stt("dve", self.tmp[:, 0:512], self.yo[:, dmc, 0:TT], self.g2c[:, L, dmc:dmc + 1], self.rb[:], ALU.mult, ALU.mult,
                  [self.yo.b, self.g2c.b, self.rb.b], [self.tmp.b])
            p.tt("dve", self.xT[:, dmc, :], self.xT[:, dmc, :], self.tmp[:, 0:512], ALU.add, [self.xT.b, self.tmp.b], [self.xT.b])

    def even_layer(self, L):
        p = self.p
        j = L // 2
        self.pre_norm(L)
        wrs = {}
        loaded = {}

        def get(ch):
            bi = ch // 4
            if bi not in loaded:
                loaded[bi] = self.load_w(L, bi)
            return (loaded[bi], ch % 4)

        self._cur_blk = None
        if "B" in self.mixers:
            self.short_conv_stream(L, j)
        else:
            for cc in range(4, 8):
                p.ms("dve", self.yT[:, cc, :], 0.0, [self.yT.bufs[cc]])
        if "A" in self.mixers:
            self.rwkv(L, j)
        else:
            for cc in range(4):
                p.ms("dve", self.yT[:, cc, :], 0.0, [self.yT.bufs[cc]])
        if not getattr(self, "_yd", False):
            self._yd = True
            self.dump("yT4", self.yT[:, 4, :], self.yT, 512)
        self.out_proj_post(L)

    def short_conv_stream(self, L, j):
        p = self.p
        F = self.F
        dest = {}
        for cc in range(4):
            dest[17 + cc] = (F[3], cc, AF.Copy)
            dest[21 + cc] = (F[0], cc, AF.Copy)
            dest[25 + cc] = (F[1], cc, AF.Copy)
            dest[29 + cc] = (F[4], cc, AF.Silu)
        cur_bi = None
        wr = None
        for ch in range(17, 33):
            bi = ch // 4
            if bi != cur_bi:
                wr = self.load_w(L, bi)
                cur_bi = bi
            ps = self.next_pp()
            self.proj_fm(wr, ch % 4, ps)
            t, cc, fn = dest[ch]
            if ch % 2 == 0:
                p.act(t[:, cc, :], ps[:], fn, [ps.b], [t.b])
            else:
                if fn == AF.Copy:
                    p.cp("dve", t[:, cc, :], ps[:], [ps.b], [t.b])
                else:
                    p.act(t[:, cc, :], ps[:], fn, [ps.b], [t.b])
        u = F[0]
        acc = F[2]
        w = self.scw
        tl = self.sc_tail
        p.tt("dve", u[:], F[0][:], F[1][:], ALU.mult, [F[0].b, F[1].b], [u.b])
        for cc in range(4):
            p.ts("dve", acc[:, cc, :], u[:, cc, :], w[:, j, cc, 2:3], None, ALU.mult, None, [u.b, w.b], [acc.b])
            p.stt("dve", acc[:, cc, 1:TT], u[:, cc, 0:TT - 1], w[:, j, cc, 1:2], acc[:, cc, 1:TT], ALU.mult, ALU.add, [u.b, w.b, acc.b], [acc.b])
            p.stt("dve", acc[:, cc, 2:TT], u[:, cc, 0:TT - 2], w[:, j, cc, 0:1], acc[:, cc, 2:TT], ALU.mult, ALU.add, [u.b, w.b, acc.b], [acc.b])
            p.stt("dve", acc[:, cc, 0:1], tl[:, j, cc, 1:2], w[:, j, cc, 1:2], acc[:, cc, 0:1], ALU.mult, ALU.add, [tl.b, w.b, acc.b], [acc.b])
            p.stt("dve", acc[:, cc, 0:2], tl[:, j, cc, 0:2], w[:, j, cc, 0:1], acc[:, cc, 0:2], ALU.mult, ALU.add, [tl.b, w.b, acc.b], [acc.b])
            p.cp("dve", tl[:, j, cc, :], u[:, cc, TT - 2:TT], [u.b, acc.b], [tl.b])
        p.tt("dve", acc[:], acc[:], F[3][:], ALU.mult, [acc.b, F[3].b], [acc.b])
        for cc in range(4):
            p.tt("dve", self.yT[:, 4 + cc, :], acc[:, cc, :], F[4][:, cc, :], ALU.mult, [acc.b, F[4].b], [self.yT.bufs[4 + cc]])

    def proj_to(self, L, ch, dst_ap, dst_bufs, fn=None, eng="act"):
        p = self.p
        bi = ch // 4
        if self._cur_blk != (L, bi):
            self._cur_wr = self.load_w(L, bi)
            self._cur_blk = (L, bi)
        ps = self.next_pp()
        self.proj_fm(self._cur_wr, ch % 4, ps)
        if eng == "act":
            p.act(dst_ap, ps[:], fn or AF.Copy, [ps.b], dst_bufs)
        else:
            p.cp(eng, dst_ap, ps[:], [ps.b], dst_bufs)

    def conformer(self, L, j):
        p = self.p
        F = self.F
        ue = self.uext
        p.cp("dve", ue[:, :, 0:30], self.cf_tail[:, j, :, :], [self.cf_tail.b], [ue.b])
        for cc in range(4):
            self.proj_to(L, cc, F[0][:, cc, :], [F[0].b], eng="dve")
        for cc in range(4):
            self.proj_to(L, 4 + cc, F[1][:, cc, :], [F[1].b], AF.Sigmoid)
        p.tt("dve", ue[:, :, 30:30 + TT], F[0][:], F[1][:], ALU.mult, [F[0].b, F[1].b], [ue.b])
        p.cp("dve", self.cf_tail[:, j, :, :], ue[:, :, TT:TT + 30], [ue.b], [self.cf_tail.b])
        for cc in range(4):
            self.proj_to(L, 8 + cc, F[1][:, cc, :], [F[1].b], AF.Silu)
        yield "proj"
        acc = F[2]
        w = self.cfw
        ub = Tile(self.sq[:].rearrange("p a b -> p (a b)")[:, 0:4 * (30 + TT)].rearrange("p (c t) -> p c t", c=4))
        ub.bufs = list(self.sq.bufs)
        p.cp("act", ub[:], ue[:, :, 0:30 + TT], [ue.b], list(ub.bufs))
        ring = []
        for c4 in range(4):
            for i_ in range(4):
                r_ = Tile(self.yT[:, 4 + c4, i_ * 128:(i_ + 1) * 128])
                ring.append(r_)
        n_ = 0
        for cc in range(4):
            ps = self.next_pp()
            for k in range(31):
                dg = ring[n_ % 16]
                p.ts("dve", dg[:], self.ident_b, w[:, j, cc, k:k + 1], None, ALU.mult, None, [self.cstb.b, w.b], [dg.b])
                p.mm(ps[:], dg[:], ub[:, cc, k:k + TT], k == 0, k == 30, [dg.b] + list(ub.bufs), [ps.b])
                n_ += 1
            p.act(acc[:, cc, :], ps[:], AF.Identity, [ps.b, self.cfp.b], [acc.b], bias=self.cfp[:, j, 0, cc:cc + 1])
            yield "cc"
        for c4 in range(4):
            p.ms("dve", self.yT[0:1, 4 + c4, 0:1], 0.0, [r_.b for r_ in ring[c4 * 4:(c4 + 1) * 4]] + [self.yT.bufs[4 + c4]])
        yield "conv"
        ones_f = self.cst[:, C_ONE:C_ONE + 128]
        pm = self.next_pp()
        for cc in range(4):
            p.mm(pm[:], ones_f, acc[:, cc, :], cc == 0, cc == 3, [self.cst.b, acc.b], [pm.b])
        for cc in range(4):
            p.act(F[0][:, cc, :], acc[:, cc, :], AF.Square, [acc.b], [F[0].b])
        pq = self.next_pp()
        for cc in range(4):
            p.mm(pq[:], ones_f, F[0][:, cc, :], cc == 0, cc == 3, [self.cst.b, F[0].b], [pq.b])
        mean = F[3][:, 0, :]
        var = F[3][:, 1, :]
        yield "stats"
        p.ts("dve", mean, pm[:], 1.0 / 512, None, ALU.mult, None, [pm.b], [F[3].b])
        p.tt("dve", F[3][:, 2, :], mean, mean, ALU.mult, [F[3].b], [F[3].b])
        p.stt("dve", var, pq[:], 1.0 / 512, F[3][:, 2, :], ALU.mult, ALU.subtract, [pq.b, F[3].b], [F[3].b])
        p.ts("dve", var, var, 1e-5, None, ALU.add, None, [F[3].b], [F[3].b])
        p.op("act", lambda e: e.activation(out=var, in_=var, func=AF.Ln), [F[3].b], [F[3].b])
        p.op("act", lambda e: e.activation(out=var, in_=var, func=AF.Exp, scale=-0.5), [F[3].b], [F[3].b])
        if not getattr(self, "_cfd", False):
            self._cfd = True
            self.dump("cf_hT", self.hT[:, 0, :], self.hT, 512)
            self.dump("cf_xT", self.xT[:, 0, :], self.xT, 512)
            self.dump("cf_g1c", self.g1c[:, 1, :], self.g1c, 8)
            self.dump("cf_shc", self.shc[:, 1, :], self.shc, 8)
            self.dump("cf_rb", self.rb[:], self.rb, 512)
            self.dump("cf_u", ue[:, 0, 30:30 + TT], ue, 512)
            self.dump("cf_acc", acc[:, 0, :], acc, 512)
            self.dump("cf_mean", mean, F[3], 512)
            self.dump("cf_rstd", var, F[3], 512)
            self.dump("cf_sg", F[1][:, 0, :], F[1], 512)
        for cc in range(4):
            p.tt("dve", F[0][:, cc, :], acc[:, cc, :], mean, ALU.subtract, [acc.b, F[3].b], [F[0].b])
            p.tt("dve", F[0][:, cc, :], F[0][:, cc, :], var, ALU.mult, [F[0].b, F[3].b], [F[0].b])
            p.act(F[0][:, cc, :], F[0][:, cc, :], AF.Silu, [F[0].b, self.cfp.b], [F[0].b],
                  scale=self.cfp[:, j, 1, cc:cc + 1], bias=self.cfp[:, j, 2, cc:cc + 1])
            p.tt("dve", self.yT[:, cc, :], F[0][:, cc, :], F[1][:, cc, :], ALU.mult, [F[0].b, F[1].b], [self.yT.bufs[cc]])

    def ssd(self, L, j):
        p = self.p
        F = self.F
        xe = self.xext
        zs = F[4]
        for cc in range(4):
            self.proj_to(L, 12 + cc, zs[:, cc, :], [zs.b], AF.Silu)
        p.cp("dve", xe[:, :, 0:3], self.sd_tail[:, j, :, :], [self.sd_tail.b], [xe.b])
        for cc in range(8):
            self.proj_to(L, 16 + cc, xe[:, cc, 3:3 + TT], [xe.b], eng="act")
        p.cp("dve", self.sd_tail[:, j, :, :], xe[:, :, TT:TT + 3], [xe.b], [self.sd_tail.b])
        wdt = self.load_w(L, 6)
        self._cur_blk = None
        yield "proj"
        w = self.sdw
        for cc in range(8):
            eng = "dve"
            acc = F[5][:, cc % 4, :]
            p.ts(eng, acc, xe[:, cc, 3:3 + TT], w[:, j, cc, 3:4], None, ALU.mult, None, [xe.b, w.b], [F[5].bufs[0]])
            for k in range(3):
                p.stt(eng, acc, xe[:, cc, k:k + TT], w[:, j, cc, k:k + 1], acc, ALU.mult, ALU.add, [xe.b, w.b, F[5].b], [F[5].b])
            p.act(self.xbcT[:, cc, :], acc, AF.Silu, [F[5].b, self.sdb.b], [self.xbcT.b], bias=self.sdb[:, j, cc:cc + 1])
            if cc % 2 == 1:
                yield "conv"
        for g in range(2):
            for e in range(2):
                par = self.cstb[:, (CB_PAR0 if e == 0 else CB_PAR1):(CB_PAR0 if e == 0 else CB_PAR1) + TT]
                p.tt("dve", self.ctm[:, g, e, :], self.xbcT[:, 6 + g, :], par, ALU.mult, [self.xbcT.b, self.cstb.b], [self.ctm.b])
        y2T = Tile(self.yo[:, 0:4, 0:TT])
        y2T.bufs = self.yo.bufs
        for s in range(4):
            tsl = slice(s * 128, (s + 1) * 128)
            for kc in range(8):
                p.mm(self.psm[:, 0:8], self.hT[:, kc, tsl], wdt[:, kc, 0:8], kc == 0, kc == 7, [self.hT.b, wdt.b], [self.psm.b])
            d = self.dts
            p.tt("dve", d[:, 0, :], self.psm[:, 0:8], self.hrow[:, j, 0, :], ALU.add, [self.psm.b, self.hrow.b], [d.b])
            p.act(d[:, 0, :], d[:, 0, :], AF.Exp, [d.b], [d.b])
            p.ts("dve", d[:, 0, :], d[:, 0, :], 1.0, None, ALU.add, None, [d.b], [d.b])
            p.act(d[:, 0, :], d[:, 0, :], AF.Ln, [d.b], [d.b])
            p.tt("dve", d[:, 1, :], d[:, 0, :], self.hrow[:, j, 1, :], ALU.mult, [d.b, self.hrow.b], [d.b])
            p.mm(self.psm[:, 8:16], self.cst[:, C_TRI2:C_TRI2 + 128], d[:, 1, :], True, True, [self.cst.b, d.b], [self.psm.b])
            p.mm(self.psm[:, 16:24], self.cst[:, C_BLK:C_BLK + 128], d[:, 1, :], True, True, [self.cst.b, d.b], [self.psm.b])
            p.mm(self.psm[:, 24:32], self.cst[:, C_SEL0:C_SEL0 + 128], d[:, 1, :], True, True, [self.cst.b, d.b], [self.psm.b])
            p.mm(self.psm[:, 32:40], self.cst[:, C_SEL1:C_SEL1 + 128], d[:, 1, :], True, True, [self.cst.b, d.b], [self.psm.b])
            p.cp("dve", d[:, 2, :], self.psm[:, 8:16], [self.psm.b], [d.b])
            p.tt("dve", d[:, 4, :], self.psm[:, 16:24], d[:, 2, :], ALU.subtract, [self.psm.b, d.b], [d.b])
            p.act(d[:, 4, :], d[:, 4, :], AF.Exp, [d.b], [d.b])
            p.act(d[:, 5, :], d[:, 2, :], AF.Exp, [d.b], [d.b])
            p.act(self.cdb[:, 0, :], self.psm[:, 24:32], AF.Exp, [self.psm.b], [self.cdb.b])
            p.act(self.cdb[:, 1, :], self.psm[:, 32:40], AF.Exp, [self.psm.b], [self.cdb.b])
            p.ts("dve", d[:, 3, :], d[:, 2, :], -1.0, None, ALU.mult, None, [d.b], [d.b])
            yield "st"
            pt = self.next_pt()
            for cc in range(4):
                p.tr(pt[:, cc * 128:(cc + 1) * 128], self.xbcT[:, cc, tsl], self.ident_b, [self.xbcT.b, self.cstb.b], [pt.b])
            p.cp("act", self.tokx[:], pt[:], [pt.b], [self.tokx.b])
            pt2 = self.next_pt()
            for cc in range(2):
                p.tr(pt2[:, cc * 128:(cc + 1) * 128], self.xbcT[:, 4 + cc, tsl], self.ident_b, [self.xbcT.b, self.cstb.b], [pt2.b])
            p.cp("act", self.btok[:], pt2[:, 0:256], [pt2.b], [self.btok.b])
            tx3 = self.tokx[:].rearrange("p (h q) -> p h q", h=8)
            p.tt("dve", self.xdt[:], tx3, d[:, 0, :].unsqueeze(2).to_broadcast([128, 8, 64]), ALU.mult, [self.tokx.b, d.b], [self.xdt.b])
            p.tt("dve", self.xdd[:], self.xdt[:], d[:, 4, :].unsqueeze(2).to_broadcast([128, 8, 64]), ALU.mult, [self.xdt.b, d.b], [self.xdd.b])
            yield "st"
            for g in range(2):
                p.mm(self.pcb[:, g, :], self.xbcT[:, 4 + g, tsl], self.xbcT[:, 6 + g, tsl], True, True, [self.xbcT.b], [self.pcb.b])
            p.cp("act", self.cbt[:], self.pcb[:], [self.pcb.b], [self.cbt.b])
            p.tt("dve", self.Dall[:], self.cst[:, C_TRI2:C_TRI2 + 128].unsqueeze(1).to_broadcast([128, 8, 128]),
                 d[:, 1, :].unsqueeze(2).to_broadcast([128, 8, 128]), ALU.mult, [self.cst.b, d.b], [self.Dall.b])
            for hh in range(2):
                pd = self.next_pp()
                p.mm(pd[:], self.cst[:, C_ONE:C_ONE + 128], self.Dall[:, hh * 4:(hh + 1) * 4, :].rearrange("p h l -> p (h l)"), True, False,
                     [self.Dall.b, self.cst.b], [pd.b])
                p.mm(pd[:].rearrange("p (h l) -> p h l", h=4), self.cst[:, C_ID:C_ID + 128],
                     self.cst[:, C_MB2:C_MB2 + 128].unsqueeze(1).to_broadcast([128, 4, 128]), False, True, [self.cst.b], [pd.b])
                for q in range(4):
                    h = hh * 4 + q
                    p.act(self.dec[:, h, :], pd[:, q * 128:(q + 1) * 128], AF.Exp, [pd.b, d.b], [self.dec.b], bias=d[:, 3, h:h + 1])
            for g in range(2):
                p.tt("dve", self.MT[:, g * 4:(g + 1) * 4, :], self.dec[:, g * 4:(g + 1) * 4, :],
                     self.cbt[:, g, :].unsqueeze(1).to_broadcast([128, 4, 128]), ALU.mult, [self.dec.b, self.cbt.b], [self.MT.b])
            yield "st"
            pyd = self.next_pp()
            for h in range(8):
                p.mm(pyd[:, h * 64:(h + 1) * 64], self.MT[:, h, :], self.xdt[:, h, :], True, True, [self.MT.b, self.xdt.b], [pyd.b])
            yield "st"
            for e in range(2):
                rs = slice(e * 64, (e + 1) * 64)
                p.cp("act", self.Sb[:, e, :, :], self.S[:, j, :, :], [self.S.b], [self.Sb.bufs[e]])
                pst = self.next_pp()
                for g in range(2):
                    p.mm(pst[:, g * 256:(g + 1) * 256], self.btok[rs, g * 128:(g + 1) * 128],
                         self.xdd[rs, g * 4:(g + 1) * 4, :].rearrange("p h q -> p (h q)"), True, True, [self.btok.b, self.xdd.b], [pst.b])
                p.tt("dve", self.S[:, j, :, :], self.S[:, j, :, :], self.cdb[:, e, :].unsqueeze(2).to_broadcast([128, 8, 64]), ALU.mult,
                     [self.S.b, self.cdb.b], [self.S.b])
                p.tt("dve", self.S[:, j, :, :], self.S[:, j, :, :], pst[:].rearrange("p (h q) -> p h q", h=8), ALU.add, [self.S.b, pst.b], [self.S.b])
            pyo = self.next_pp()
            for g in range(2):
                for e in range(2):
                    p.mm(pyo[:, g * 256:(g + 1) * 256], self.ctm[:, g, e, tsl], self.Sb[:, e, g * 4:(g + 1) * 4, :].rearrange("p h q -> p (h q)"),
                         e == 0, e == 1, [self.ctm.b, self.Sb.bufs[e]], [pyo.b])
            yt = self.ytok
            yt3 = yt[:].rearrange("p (h q) -> p h q", h=8)
            p.tt("dve", yt3, pyo[:].rearrange("p (h q) -> p h q", h=8), d[:, 5, :].unsqueeze(2).to_broadcast([128, 8, 64]), ALU.mult, [pyo.b, d.b], [yt.b])
            p.tt("dve", yt[:], yt[:], pyd[:], ALU.add, [yt.b, pyd.b], [yt.b])
            p.tt("dve", self.tokx[:].rearrange("p (h q) -> p h q", h=8), tx3, self.hrow[:, j, 2, :].unsqueeze(2).to_broadcast([128, 8, 64]), ALU.mult,
                 [self.tokx.b, self.hrow.b], [self.tokx.b])
            p.tt("dve", yt[:], yt[:], self.tokx[:], ALU.add, [yt.b, self.tokx.b], [yt.b])
            yield "st"
            py = self.next_pp()
            for cc in range(4):
                p.tr(py[:, cc * 128:(cc + 1) * 128], yt[:, cc * 128:(cc + 1) * 128], self.ident_f, [yt.b, self.cst.b], [py.b])
            p.tt("dve", y2T[:, :, tsl], py[:].rearrange("p (c t) -> p c t", c=4), zs[:, :, tsl], ALU.mult, [py.b, zs.b], [y2T.b])
        yield "loop"
        for cc in range(4):
            p.act(self.sq[:, cc, :], y2T[:, cc, :], AF.Square, [y2T.b], [self.sq.bufs[cc]])
        pr = self.next_pp()
        for cc in range(4):
            p.mm(pr[:], self.ones_b, self.sq[:, cc, :], cc == 0, cc == 3, [self.cstb.b, self.sq.bufs[cc]], [pr.b])
        p.ts("dve", self.rb[:], pr[:], 1.0 / 512, 1e-6, ALU.mult, ALU.add, [pr.b], [self.rb.b])
        p.op("act", lambda e: e.activation(out=self.rb[:], in_=self.rb[:], func=AF.Ln), [self.rb.b], [self.rb.b])
        p.op("act", lambda e: e.activation(out=self.rb[:], in_=self.rb[:], func=AF.Exp, scale=-0.5), [self.rb.b], [self.rb.b])
        for cc in range(4):
            p.stt("dve", self.yT[:, 4 + cc, :], y2T[:, cc, :], self.sdng[:, j, cc:cc + 1], self.rb[:], ALU.mult, ALU.mult,
                  [y2T.b, self.sdng.b, self.rb.b], [self.yT.bufs[4 + cc]])

    def odd_layer(self, L):
        p = self.p
        j = L // 2
        self.pre_norm(L)
        self._cur_blk = None
        gens = []
        if "C" in self.mixers:
            gc = self.conformer(L, j)
            next(gc)
            gens.append(gc)
        else:
            for cc in range(4):
                p.ms("dve", self.yT[:, cc, :], 0.0, [self.yT.bufs[cc]])
        if "D" in self.mixers:
            gd = self.ssd(L, j)
            next(gd)
            gens.append(gd)
        else:
            for cc in range(4, 8):
                p.ms("dve", self.yT[:, cc, :], 0.0, [self.yT.bufs[cc]])
        while gens:
            for g_ in list(gens):
                try:
                    next(g_)
                except StopIteration:
                    gens.remove(g_)
        self.out_proj_post(L)

    def rwkv(self, L, j):
        p = self.p
        F = self.F
        LW = -0.6065306597126334
        INV = self.INV_DT
        cst = self.cst
        rT, kT, vT, sgT, bonT, aT = F[0], F[1], F[2], F[3], F[4], F[5]
        yob = self.yo[:].rearrange("p a b -> p (a b)").bitcast(BF16)
        AR = Tile(yob[:, 0:4096].rearrange("p (c n q d) -> p c n q d", c=4, n=8, q=2))
        AR.bufs = self.yo.bufs
        BK = Tile(yob[:, 4096:8192].rearrange("p (c n q d) -> p c n q d", c=4, n=8, q=2))
        BK.bufs = self.yo.bufs
        Vt, Bt, Kt = self.hT, self.sq, self.xbcT
        Ytok = Tile(self.xs[0][:].rearrange("p s (a d) -> p (s a) d", a=2))
        Ytok.bufs = self.xs[0].bufs
        if INV == F32:
            Wp = [self.Dall, self.dec]
            NTp = [Tile(self.uext[:, 0, 0:512].rearrange("p (h d) -> p h d", h=8)), Tile(self.uext[:, 1, 0:512].rearrange("p (h d) -> p h d", h=8))]
            Xs = Tile(self.uext[:, 2, 0:512].rearrange("p (h d) -> p h d", h=8))
        else:
            Wp = []
            for t_ in (self.Dall, self.dec):
                w_ = Tile(t_[:].rearrange("p h d -> p (h d)").bitcast(BF16)[:, 0:1024].rearrange("p (h d) -> p h d", h=8))
                w_.bufs = t_.bufs
                Wp.append(w_)
            NTp = [Tile(self.uext[:, i_, 0:512].bitcast(BF16)[:, 0:512].rearrange("p (h d) -> p h d", h=8)) for i_ in range(2)]
            Xs = Tile(self.uext[:, 2, 0:512].bitcast(BF16)[:, 0:512].rearrange("p (h d) -> p h d", h=8))
        for t_ in NTp + [Xs]:
            t_.bufs = self.uext.bufs
        Us = self.xdd
        Srb = Tile(self.Sb[:, 0, :, :])
        Srb.bufs = [self.Sb.bufs[0]]
        Sk = self.MT
        wdad = Tile(self.xdt[:].rearrange("p h d -> p (h d)"))
        wdad.bufs = self.xdt.bufs
        sig, cs = self.tokx, self.ytok
        e1 = self.tmp[:, 0:512]
        e2 = self.tmp[:, 512:1024]
        tmpb = self.tmp.b
        kk = self.rb
        mu, omu, prev = self.mu, self.omu, self.tm_prev
        tmc = self.tmc

        def shifted(ch, dst_ap, dst_tile):
            bi = ch // 4
            if self._cur_blk != (L, bi):
                self._cur_wr = self.load_w(L, bi)
                self._cur_blk = (L, bi)
            ps = self.next_pp()
            self.proj_fm(self._cur_wr, ch % 4, ps)
            p.act(dst_ap, ps[:], AF.Copy, [ps.b, omu.b], [dst_tile.b], scale=omu[:, j, ch:ch + 1])
            p.stt("dve", dst_ap[:, 1:TT], ps[:, 0:TT - 1], mu[:, j, ch:ch + 1], dst_ap[:, 1:TT], ALU.mult, ALU.add, [ps.b, mu.b, dst_tile.b], [dst_tile.b])
            p.stt("dve", dst_ap[:, 0:1], prev[:, j, ch:ch + 1], mu[:, j, ch:ch + 1], dst_ap[:, 0:1], ALU.mult, ALU.add, [prev.b, mu.b, dst_tile.b], [dst_tile.b])
            p.cp("dve", prev[:, j, ch:ch + 1], ps[:, TT - 1:TT], [ps.b], [prev.b])

        for cc in range(4):
            shifted(cc, rT[:, cc, :], rT)
        for cc in range(4):
            shifted(4 + cc, kT[:, cc, :], kT)
        for cc in range(4):
            shifted(8 + cc, vT[:, cc, :], vT)
        for cc in range(4):
            shifted(12 + cc, sgT[:, cc, :], sgT)
        shifted(16, e1, self.tmp)
        p.act(wdad[0:64, :], e1[0:64, :], AF.Tanh, [tmpb], [wdad.b])
        p.cp("dve", wdad[64:128, :], e1[64:128, :], [tmpb], [wdad.b])
        for cc in range(4):
            p.act(sgT[:, cc, :], sgT[:, cc, :], AF.Silu, [sgT.b], [sgT.b])
        if self.rw_stop == 1:
            for cc in range(4):
                p.ms("dve", self.yT[:, cc, :], 0.0, [self.yT.bufs[cc]])
            return
        xs0 = self.xs[0]
        regs = [Tile(xs0[:, s_, h_ * 512:(h_ + 1) * 512]) for s_ in range(4) for h_ in range(2)] + [self.tokx, self.ytok]
        sets = [regs[0:5], regs[5:10]]
        own = [r_.b for r_ in regs[0:8]]
        p.ms("dve", xs0[0:1, 0, 0:1], 0.0, [xs0.b] + own)
        npp6 = self.next_pp6

        def ph2(cc, S):
            sig, cs, kk, e1t, e2t = S
            e1, e2 = e1t[:], e2t[:]
            csl = slice(cc * 128, (cc + 1) * 128)
            v3 = lambda ap: ap.rearrange("p (n d) -> p n d", d=64)
            pz = npp6()
            p.mm(pz[:], self.w2a2[0:64, j, csl], wdad[0:64, :], True, True, [self.w2a2.b, wdad.b], [pz.b])
            p.act(sig[:], pz[:], AF.Sigmoid, [pz.b, tmc.b], [sig.b], bias=tmc[:, j, 0, cc:cc + 1])
            pa = npp6()
            p.mm(pa[:], self.w2a2[64:128, j, csl], wdad[64:128, :], True, True, [self.w2a2.b, wdad.b], [pa.b])
            p.act(aT[:, cc, :], pa[:], AF.Sigmoid, [pa.b, tmc.b], [aT.b], bias=tmc[:, j, 1, cc:cc + 1])
            yield
            p.op("dve", lambda e: e.tensor_tensor_scan(out=cs[:], data0=cst[:, C_RST:C_RST + TT], data1=sig[:], initial=0.0,
                                                        op0=ALU.mult, op1=ALU.add), [cst.b, sig.b], [cs.b])
            p.ts("dve", kk[:], kT[:, cc, :], tmc[:, j, 2, cc:cc + 1], None, ALU.mult, None, [kT.b, tmc.b], [kk.b])
            p.tt("dve", e2, kk[:], kk[:], ALU.mult, [kk.b], [e2t.b])
            pn = npp6()
            p.mm(pn[:], cst[:, C_BLK:C_BLK + 128], e2, True, True, [cst.b, e2t.b], [pn.b])
            yield
            p.act(self.gl[:, cc, :], v3(cs[:])[:, :, 63], AF.Exp, [cs.b], [self.gl.b], scale=LW)
            p.act(e1, cs[:], AF.Exp, [cs.b], [e1t.b], scale=LW)
            p.tt("dve", AR[:, cc, :, 1, :], v3(rT[:, cc, :]), v3(e1), ALU.mult, [rT.b, e1t.b], [AR.b])
            yield
            p.ts("dve", e2, pn[:], 1e-24, None, ALU.max, None, [pn.b], [e2t.b])
            p.op("act", lambda e: e.activation(out=e2, in_=e2, func=AF.Ln), [e2t.b], [e2t.b])
            p.op("act", lambda e: e.activation(out=e2, in_=e2, func=AF.Exp, scale=-0.5), [e2t.b], [e2t.b])
            yield
            p.tt("dve", kk[:], kk[:], e2, ALU.mult, [kk.b, e2t.b], [kk.b])
            p.tt("dve", e2, cs[:], sig[:], ALU.subtract, [cs.b, sig.b], [e2t.b])
            p.act(e2, e2, AF.Exp, [e2t.b], [e2t.b], scale=LW)
            p.act(e1, cs[:], AF.Exp, [cs.b, AR.b], [e1t.b], scale=-LW)
            yield
            p.stt("dve", AR[:, cc, :, 0, :], v3(kk[:]), -1.0, v3(e2), ALU.mult, ALU.mult, [kk.b, e2t.b], [AR.b])
            p.tt("dve", e2, kk[:], aT[:, cc, :], ALU.mult, [kk.b, aT.b], [e2t.b])
            p.tt("dve", BK[:, cc, :, 0, :], v3(e2), v3(e1), ALU.mult, [e2t.b, e1t.b], [BK.b])
            yield
            p.ts("dve", e2, aT[:, cc, :], tmc[:, j, 3, cc:cc + 1], self.tmk1[:, j, cc:cc + 1], ALU.mult, ALU.add, [aT.b, tmc.b, self.tmk1.b], [e2t.b])
            p.tt("dve", kT[:, cc, :], kT[:, cc, :], e2, ALU.mult, [kT.b, e2t.b], [kT.b])
            p.tt("dve", BK[:, cc, :, 1, :], v3(kT[:, cc, :]), v3(e1), ALU.mult, [kT.b, e1t.b], [BK.b])
            yield
            p.stt("dve", e2, rT[:, cc, :], tmc[:, j, 4, cc:cc + 1], kT[:, cc, :], ALU.mult, ALU.mult, [rT.b, tmc.b, kT.b], [e2t.b])
            pbn = npp6()
            p.mm(pbn[:], cst[:, C_BLK:C_BLK + 128], e2, True, True, [cst.b, e2t.b], [pbn.b])
            yield
            p.tt("dve", bonT[:, cc, :], pbn[:], vT[:, cc, :], ALU.mult, [pbn.b, vT.b], [bonT.b])

        for pair in range(2):
            gens = [ph2(2 * pair, sets[0]), ph2(2 * pair + 1, sets[1])]
            while gens:
                for g_ in list(gens):
                    try:
                        next(g_)
                    except StopIteration:
                        gens.remove(g_)
        p.ms("dve", xs0[0:1, 0, 0:1], 0.0, [xs0.b] + own)
        if self.rw_stop == 2:
            for cc in range(4):
                p.ms("dve", self.yT[:, cc, :], 0.0, [self.yT.bufs[cc]])
            return
        vb = Tile(self.ctm[:].rearrange("p g e t -> p (g e) t"))
        vb.bufs = self.ctm.bufs
        p.cp("act", vb[:], vT[:], [vT.b], [vb.b])
        for c in range(8):
            tsl = slice(c * 64, (c + 1) * 64)
            for ii, (dst, getsrc, srct) in enumerate(((Vt, lambda cc: vb[:, cc, tsl], vb), (Bt, lambda cc: BK[:, cc, c, 0, :], BK), (Kt, lambda cc: BK[:, cc, c, 1, :], BK))):
                if self.rw_stop == 31 and ii > 0:
                    continue
                if self.rw_stop == 33:
                    continue
                if self.rw_stop == 32 and ii != 1:
                    continue
                pt = self.next_pp()
                for cc in range(4):
                    p.mm(pt[0:64, cc * 128:(cc + 1) * 128], getsrc(cc), self.ident_b, True, True, [srct.b, self.cstb.b], [pt.b])
                if c % 2 == 0:
                    p.cp("act", dst[0:64, c, :], pt[0:64, :], [pt.b], [dst.b] if dst is not self.sq else list(self.sq.bufs))
                else:
                    p.cp("dve", dst[0:64, c, :], pt[0:64, :], [pt.b], [dst.b] if dst is not self.sq else list(self.sq.bufs))
        if self.rw_stop in (3, 31, 32, 33):
            for cc in range(4):
                p.ms("dve", self.yT[:, cc, :], 0.0, [self.yT.bufs[cc]])
            return
        Btb = list(self.sq.bufs)
        p.cp("act", self.Hb[:], self.Hs[:, j, :, :], [self.Hs.b], [self.Hb.b])
        msk_su = cst[0:64, C_SU:C_SU + 64]
        msk_iu = cst[0:64, C_IU:C_IU + 64]
        msk_sl = cst[0:64, C_SL:C_SL + 64]
        idn = cst[0:64, 0:64]
        bc = lambda ap, n: ap.unsqueeze(1).to_broadcast([64, n, 64])
        npp = self.next_pp6
        SrbP = [Tile(self.Sb[:, 0, :, :]), Tile(self.Sb[:, 1, :, :])]
        SrbP[0].bufs = [self.Sb.bufs[0]]
        SrbP[1].bufs = [self.Sb.bufs[1]]
        SkP = [self.MT, Tile(self.Dall[:].rearrange("p h d -> p (h d)").bitcast(BF16)[:, 1024:2048].rearrange("p (h d) -> p h d", h=8))]
        TtP = [Tile(self.cbt[:].rearrange("p g n -> p (g n)").bitcast(BF16)[:, 0:512].rearrange("p (h d) -> p h d", h=8)),
               Tile(self.dec[:].rearrange("p h d -> p (h d)").bitcast(BF16)[:, 1024:1536].rearrange("p (h d) -> p h d", h=8))]
        TtP[0].bufs = self.cbt.bufs

        def prep_units(c):
            Srb, Sk, Tt = SrbP[c % 2], SkP[c % 2], TtP[c % 2]
            W0, NT0 = Wp[0], NTp[0]
            units = []

            def scores(g):
                hs = slice(g * 4, (g + 1) * 4)
                rows = slice(g * 64, (g + 1) * 64)
                Pb, Pk, Pa = npp(), npp(), npp()
                for q in range(4):
                    ar = AR[rows, q, c, :, :].rearrange("p q d -> p (q d)")
                    p.mm(Pb[0:64, q * 128:(q + 1) * 128], BK[rows, q, c, 0, :], ar, True, True, [BK.b, AR.b], [Pb.b])
                    p.mm(Pk[0:64, q * 128:(q + 1) * 128], BK[rows, q, c, 1, :], ar, True, True, [BK.b, AR.b], [Pk.b])
                    p.mm(Pa[0:64, q * 64:(q + 1) * 64], AR[rows, q, c, 0, :], BK[rows, q, c, 0, :], True, True, [BK.b, AR.b], [Pa.b])
                Pb3 = Pb[0:64, :].rearrange("p (h d) -> p h d", h=4)
                Pk3 = Pk[0:64, :].rearrange("p (h d) -> p h d", h=4)
                p.tt("dve", W0[0:64, hs, 0:64], Pb3[:, :, 0:64], bc(msk_su, 4), ALU.mult, [Pb.b, cst.b], [W0.b])
                p.tt("dve", Srb[0:64, hs, :], Pb3[:, :, 64:128], bc(msk_iu, 4), ALU.mult, [Pb.b, cst.b], [Srb.b])
                p.tt("dve", Sk[0:64, hs, 0:64], Pk3[:, :, 0:64], bc(msk_su, 4), ALU.mult, [Pk.b, cst.b], [Sk.b])
                p.tt("dve", Sk[0:64, hs, 64:128], Pk3[:, :, 64:128], bc(msk_iu, 4), ALU.mult, [Pk.b, cst.b], [Sk.b])
                p.tt("dve", NT0[0:64, hs, :], Pa[0:64, 0:256].rearrange("p (h d) -> p h d", h=4), bc(msk_sl, 4), ALU.mult, [Pa.b, cst.b], [NT0.b])

            def level(lvl, g):
                cur = lvl % 2
                Wc, NTc = Wp[cur], NTp[cur]
                Wn, NTn = Wp[1 - cur], NTp[1 - cur]
                last = (lvl == 5)
                hs = slice(g * 4, (g + 1) * 4)
                P1 = npp()
                if not last:
                    P2 = npp()
                for q in range(4):
                    h = g * 4 + q
                    if lvl == 0:
                        p.mm(P1[0:64, q * 128:q * 128 + 64], NTc[0:64, h, :], Wc[0:64, h, 0:64], True, True, [NTc.b, Wc.b], [P1.b])
                    elif not last:
                        p.mm(P1[0:64, q * 128:(q + 1) * 128], NTc[0:64, h, :], Wc[0:64, h, :], True, True, [NTc.b, Wc.b], [P1.b])
                    else:
                        p.mm(P1[0:64, q * 128 + 64:(q + 1) * 128], NTc[0:64, h, :], Wc[0:64, h, 64:128], True, True, [NTc.b, Wc.b], [P1.b])
                    if not last:
                        p.mm(P2[0:64, q * 64:(q + 1) * 64], Wc[0:64, h, 0:64], NTc[0:64, h, :], True, True, [NTc.b, Wc.b], [P2.b])
                P13 = P1[0:64, :].rearrange("p (h d) -> p h d", h=4)
                if not last:
                    p.cp("act", Wn[0:64, hs, 0:64], P13[:, :, 0:64], [P1.b], [Wn.b])
                    p.cp("act", NTn[0:64, hs, :], P2[0:64, 0:256].rearrange("p (h d) -> p h d", h=4), [P2.b], [NTn.b])
                if lvl == 0:
                    p.tt("dve", Wn[0:64, hs, 64:128], Wc[0:64, hs, 0:64], bc(idn, 4), ALU.add, [Wc.b, cst.b], [Wn.b])
                elif not last:
                    p.tt("dve", Wn[0:64, hs, 64:128], P13[:, :, 64:128], Wc[0:64, hs, 64:128], ALU.add, [P1.b, Wc.b], [Wn.b])
                else:
                    p.tt("dve", Tt[0:64, hs, :], P13[:, :, 64:128], Wc[0:64, hs, 64:128], ALU.add, [P1.b, Wc.b], [Tt.b])

            for g in range(2):
                units.append(lambda g=g: scores(g))
            for lvl in range(6):
                for g in range(2):
                    units.append(lambda lvl=lvl, g=g: level(lvl, g))
            return units

        def chain_units(c):
            Srb, Sk, Tt = SrbP[c % 2], SkP[c % 2], TtP[c % 2]
            st = {}

            def hterm(PS, qd):
                for q in range(4):
                    p.mm(PS[0:64, q * 64:(q + 1) * 64], AR[64:128, q, c, qd, :], self.Hb[64:128, q, :], True, True, [AR.b, self.Hb.b], [PS.b])

            def ux():
                PX, PXo = npp(), npp()
                hterm(PXo, 0)
                for hh in range(8):
                    g, q = hh // 4, hh % 4
                    h = 2 * q + g
                    o = PX[0:64, hh * 64:(hh + 1) * 64]
                    if g == 0:
                        p.mm(o, AR[0:64, q, c, 0, :], self.Hb[0:64, q, :], True, False, [AR.b, self.Hb.b], [PX.b])
                    p.mm(o, Sk[0:64, hh, 0:64], Vt[0:64, c, h * 64:(h + 1) * 64], g == 1, True, [Sk.b, Vt.b], [PX.b])
                p.cp("act", Xs[0:64, 0:4, :], PX[0:64, 0:256].rearrange("p (h d) -> p h d", h=4), [PX.b], [Xs.b])
                p.cp("act", Xs[0:64, 4:8, :], PXo[0:64, 0:256].rearrange("p (h d) -> p h d", h=4), [PXo.b], [Xs.b])
                p.tt("dve", Xs[0:64, 4:8, :], Xs[0:64, 4:8, :], PX[0:64, 256:512].rearrange("p (h d) -> p h d", h=4), ALU.add, [Xs.b, PX.b], [Xs.b])

            def uu():
                PU = npp()
                for hh in range(8):
                    p.mm(PU[0:64, hh * 64:(hh + 1) * 64], Tt[0:64, hh, :], Xs[0:64, hh, :], True, True, [Tt.b, Xs.b], [PU.b])
                p.cp("act", Us[0:64, :, :].rearrange("p (q g) d -> p g q d", g=2), PU[0:64, :].rearrange("p (g q d) -> p g q d", g=2, q=4), [PU.b], [Us.b])

            def uy():
                PY, PYo = npp(), npp()
                hterm(PYo, 1)
                for hh in range(8):
                    g, q = hh // 4, hh % 4
                    h = 2 * q + g
                    o = PY[0:64, hh * 64:(hh + 1) * 64]
                    if g == 0:
                        p.mm(o, AR[0:64, q, c, 1, :], self.Hb[0:64, q, :], True, False, [AR.b, self.Hb.b], [PY.b])
                    p.mm(o, Srb[0:64, hh, :], Us[0:64, h, :], g == 1, False, [Srb.b, Us.b], [PY.b])
                    p.mm(o, Sk[0:64, hh, 64:128], Vt[0:64, c, h * 64:(h + 1) * 64], False, True, [Sk.b, Vt.b], [PY.b])
                Yc = Ytok[0:64, c, :].rearrange("p (q g d) -> p g q d", g=2, d=64)
                p.cp("dve", Yc, PY[0:64, :].rearrange("p (g q d) -> p g q d", g=2, q=4), [PY.b], [Ytok.b])
                p.tt("dve", Yc[:, 1, :, :], Yc[:, 1, :, :], PYo[0:64, 0:256].rearrange("p (q d) -> p q d", q=4), ALU.add, [Ytok.b, PYo.b], [Ytok.b])

            def uh():
                PH = npp()
                for cc in range(4):
                    o = PH[:, cc * 128:(cc + 1) * 128]
                    p.mm(o, Bt[0:64, c, cc * 128:(cc + 1) * 128], Us[0:64, 2 * cc:2 * cc + 2, :].rearrange("p h d -> p (h d)"), True, False, Btb + [Us.b], [PH.b])
                    p.mm(o, Kt[0:64, c, cc * 128:(cc + 1) * 128], Vt[0:64, c, cc * 128:(cc + 1) * 128], False, True, [Kt.b, Vt.b], [PH.b])
                PH3 = PH[:].rearrange("p (c d) -> p c d", c=4)
                for hl in range(2):
                    rws = slice(hl * 64, (hl + 1) * 64)
                    p.tt("dve", self.Hs[rws, j, :, :], self.Hs[rws, j, :, :], PH3[rws, :, hl * 64:(hl + 1) * 64], ALU.add, [self.Hs.b, PH.b], [self.Hs.b])
                    p.tt("dve", self.Hs[rws, j, :, :], self.Hs[rws, j, :, :], self.gl[rws, :, c].unsqueeze(2).to_broadcast([64, 4, 64]), ALU.mult,
                         [self.Hs.b, self.gl.b], [self.Hs.b])
                p.cp("act", self.Hb[:], self.Hs[:, j, :, :], [self.Hs.b], [self.Hb.b])

            return [ux, uu, uy, uh]

        prev_chain = []
        for c in range(9):
            pu = prep_units(c) if c < 8 else []
            cu = prev_chain
            i_ = j_ = 0
            while i_ < len(pu) or j_ < len(cu):
                for _ in range(4):
                    if i_ < len(pu):
                        pu[i_]()
                        i_ += 1
                if j_ < len(cu):
                    cu[j_]()
                    j_ += 1
            prev_chain = chain_units(c) if c < 8 else []
        if self.rw_stop in (4, 5, 6):
            for cc in range(4):
                p.ms("dve", self.yT[:, cc, :], 0.0, [self.yT.bufs[cc]])
            return
        Y3 = Ytok[0:64, :, :].rearrange("p c (h d) -> p (c h) d", d=64)
        st = Tile(self.uext[:, 3, 0:512])
        st.bufs = self.uext.bufs
        s1 = st[0:64, 0:64]
        s2 = st[0:64, 64:128]
        s3 = st[0:64, 128:192]
        p.op("dve", lambda e: e.reduce_sum(out=s1, in_=Y3, axis=mybir.AxisListType.X), [Ytok.b], [st.b])
        sqt = Tile(self.F[5][:].rearrange("p c t -> p (c t)")[:, 0:2048].rearrange("p (a b) -> p a b", b=64))
        sqt.bufs = self.F[5].bufs
        for hf in range(2):
            ysl = Y3[:, hf * 32:(hf + 1) * 32, :]
            p.tt("dve", sqt[0:64, :, :], ysl, ysl, ALU.mult, [Ytok.b], [sqt.b])
            p.op("dve", lambda e, hf=hf: e.reduce_sum(out=s2[:, hf * 32:(hf + 1) * 32], in_=sqt[0:64, :, :], axis=mybir.AxisListType.X), [sqt.b], [st.b])
        p.ts("dve", s1, s1, 1.0 / 64, None, ALU.mult, None, [st.b], [st.b])
        p.tt("dve", s3, s1, s1, ALU.mult, [st.b], [st.b])
        p.stt("dve", s2, s2, 1.0 / 64, s3, ALU.mult, ALU.subtract, [st.b], [st.b])
        p.ts("dve", s2, s2, 64e-5, None, ALU.add, None, [st.b], [st.b])
        p.op("act", lambda e: e.activation(out=s2, in_=s2, func=AF.Ln), [st.b], [st.b])
        p.op("act", lambda e: e.activation(out=s2, in_=s2, func=AF.Exp, scale=-0.5), [st.b], [st.b])
        p.tt("dve", Y3, Y3, s1.unsqueeze(2).to_broadcast([64, 64, 64]), ALU.subtract, [Ytok.b, st.b], [Ytok.b])
        p.tt("dve", Y3, Y3, s2.unsqueeze(2).to_broadcast([64, 64, 64]), ALU.mult, [Ytok.b, st.b], [Ytok.b])
        for cc in range(4):
            py = self.next_pp()
            for c in range(8):
                p.tr(py[:, c * 64:(c + 1) * 64], Ytok[0:64, c, cc * 128:(cc + 1) * 128], cst[0:64, 0:64], [Ytok.b, cst.b], [py.b])
            p.act(e1, py[:], AF.Identity, [py.b, tmc.b], [tmpb], scale=tmc[:, j, 5, cc:cc + 1], bias=tmc[:, j, 6, cc:cc + 1])
            p.tt("dve", e1, e1, bonT[:, cc, :], ALU.add, [tmpb, bonT.b], [tmpb])
            p.tt("dve", self.yT[:, cc, :], e1, sgT[:, cc, :], ALU.mult, [tmpb, sgT.b], [self.yT.bufs[cc]])

    def build(self):
        if self.prefetch and self.wseq is None and not getattr(self, "_recording", False):
            rec = Builder(self.NL, self.NT, self.mixers, False)
            rec._recording = True
            rec.rw_stop = self.rw_stop
            rec.build()
            self.wseq = rec.wrec
        p = self.p
        self.declare()
        self.alloc()
        self.prologue()
        xr = self.x.rearrange("(t s p) d -> t p s d", p=128, s=4)
        orr = self.out.rearrange("(t s p) d -> t p s d", p=128, s=4)
        for t in range(self.NT):
            p.dma("sp", self.xs[0][:], xr[t], (), [self.xs[0].b])
            self.load_tile(t)
            for L in range(self.NL):
                if L % 2 == 0:
                    self.even_layer(L)
                else:
                    self.odd_layer(L)
            self.store_tile(t, orr)
        p.finish()
        return self.nc


def make_in_maps(inputs):
    consts = make_consts()
    maps = []
    for core in range(8):
        b = core % 4
        m = {}
        for k, v in inputs.items():
            v = np.asarray(v)
            if k == "x":
                m[k] = np.ascontiguousarray(v[b])
            elif k == "c":
                m[k] = np.ascontiguousarray(v[b].reshape(8, 128))
            elif k in ("tm_k_k", "tm_k_a", "tm_r_k"):
                m[k] = np.ascontiguousarray(v.reshape(2, 512))
            else:
                m[k] = np.ascontiguousarray(v)
        m["consts"] = consts
        maps.append(m)
    return maps


def kernel(**inputs):
    import os
    bld = Builder(mixers=os.environ.get("K_MIXERS", "ABCD"))
    bld.rw_stop = int(os.environ.get("K_RWSTOP", "0"))
    nc = bld.build()
    maps = make_in_maps(inputs)
    res = run_bass_kernel_spmd(nc, maps, core_ids=list(range(8)))
    out = np.stack([res.results[b]["out"] for b in range(4)], axis=0)
    return out.astype(np.float32)
```

```python
import numpy as np
import concourse.bass as bass
import concourse.mybir as mybir
from concourse.bass_utils import run_bass_kernel_spmd

F32 = mybir.dt.float32
BF16 = mybir.dt.bfloat16
AF = mybir.ActivationFunctionType
ALU = mybir.AluOpType

D = 1024
SEQ = 8192
TT = 512
NDS = 24
EVEN_COLS = 4224
ODD_COLS = 3080


class Buf:
    __slots__ = ("w", "r")

    def __init__(self):
        self.w = {}
        self.r = {}


class Tile:
    def __init__(self, h, nb=1):
        self.h = h
        self.bufs = [Buf() for _ in range(nb)]

    def __getitem__(self, k):
        return self.h[k]

    @property
    def b(self):
        return self.bufs[0]


class Prog:
    def __init__(self, nc):
        self.nc = nc
        self.eng = {"pe": nc.tensor, "act": nc.scalar, "dve": nc.vector, "pool": nc.gpsimd, "sp": nc.sync}
        self.sem = {}
        self.cnt = {}
        self.cur = {}
        self.qeng = {}
        self.epochs = {}
        for q in ("pe", "act", "dve", "pool"):
            self._new_epoch(q)
        for i in range(NDS):
            self.sem["d%d" % i] = nc.alloc_semaphore("d%d" % i)
            self.cnt["d%d" % i] = 0
        self.dnext = 0
        self.known = {e: {} for e in self.eng}
        self.snap = {}
        self.nwait = 0
        self.nins = 0
        import os
        self.raw_only = os.environ.get("K_RAWONLY", "0") == "1"

    SEM_LIMIT = 30000

    def _new_epoch(self, e):
        k = self.epochs.get(e, 0)
        self.epochs[e] = k + 1
        key = "%s#%d" % (e, k)
        self.sem[key] = self.nc.alloc_semaphore("s_%s_%d" % (e, k))
        self.cnt[key] = 0
        self.cur[e] = key
        self.qeng[key] = e

    def sb(self, name, shape, dt, nb=1):
        return Tile(self.nc.alloc_sbuf_tensor(name, list(shape), dt), nb)

    def ps(self, name, shape, dt=F32, nb=1):
        return Tile(self.nc.alloc_psum_tensor(name, list(shape), dt), nb)

    def _wait(self, e, evs, same_ok=True):
        kn = self.known[e]
        need = {}
        for (q, v) in evs:
            if same_ok and self.qeng.get(q) == e:
                continue
            if kn.get(q, 0) >= v:
                continue
            if need.get(q, 0) < v:
                need[q] = v
        for q, v in need.items():
            if kn.get(q, 0) >= v:
                continue
            self.eng[e].wait_ge(self.sem[q], v)
            self.nwait += 1
            kn[q] = v
            s = self.snap.get((q, v))
            if s:
                for q2, v2 in s.items():
                    if kn.get(q2, 0) < v2:
                        kn[q2] = v2

    def _deps(self, reads, writes):
        ev = []
        for b in reads:
            ev.extend(b.w.items())
        for b in writes:
            ev.extend(b.w.items())
            ev.extend(b.r.items())
        return ev

    def op(self, e, fn, reads=(), writes=()):
        if e != "pe" and self.raw_only:
            raw = []
            for b in reads:
                for q, v in b.w.items():
                    if self.qeng.get(q) == e:
                        raw.append((q, v))
            if raw:
                self._wait(e, raw, same_ok=False)
            self._wait(e, self._deps(reads, writes), same_ok=True)
        else:
            self._wait(e, self._deps(reads, writes), same_ok=(e == "pe"))
        ins = fn(self.eng[e])
        key = self.cur[e]
        self.cnt[key] += 1
        n = self.cnt[key]
        ins.then_inc(self.sem[key], 1)
        self.nins += 1
        self.snap[(key, n)] = dict(self.known[e])
        for b in reads:
            b.r[key] = n
        for b in writes:
            b.w[key] = n
            b.r = {}
        if n >= self.SEM_LIMIT:
            self._new_epoch(e)

    def dma(self, e, out, in_, reads=(), writes=(), **kw):
        i = self.dnext
        self.dnext = (i + 1) % NDS
        q = "d%d" % i
        ev = self._deps(reads, writes)
        if self.cnt[q] > 0:
            ev.append((q, self.cnt[q]))
        self._wait(e, ev, same_ok=False)
        self.eng[e].dma_start(out=out, in_=in_, **kw).then_inc(self.sem[q], 16)
        self.cnt[q] += 16
        n = self.cnt[q]
        self.nins += 1
        self.snap[(q, n)] = dict(self.known[e])
        for b in reads:
            b.r[q] = n
        for b in writes:
            b.w[q] = n
            b.r = {}

    def finish(self):
        self._wait("sp", [(q, v) for q, v in self.cnt.items() if v > 0], same_ok=False)

    def act(self, out, in_, func, reads, writes, **kw):
        self.op("act", lambda e: e.activation(out=out, in_=in_, func=func, **kw), reads, writes)

    def mm(self, out, lhsT, rhs, start, stop, reads, writes):
        self.op("pe", lambda e: e.matmul(out, lhsT=lhsT, rhs=rhs, start=start, stop=stop), reads, writes)

    def tr(self, out, in_, ident, reads, writes):
        self.op("pe", lambda e: e.transpose(out, in_, ident), reads, writes)

    def ts(self, eng, out, in0, s1, s2, op0, op1, reads, writes):
        if s2 is None:
            self.op(eng, lambda e: e.tensor_scalar(out=out, in0=in0, scalar1=s1, scalar2=None, op0=op0), reads, writes)
        else:
            self.op(eng, lambda e: e.tensor_scalar(out=out, in0=in0, scalar1=s1, scalar2=s2, op0=op0, op1=op1), reads, writes)

    def tt(self, eng, out, in0, in1, op, reads, writes):
        self.op(eng, lambda e: e.tensor_tensor(out=out, in0=in0, in1=in1, op=op), reads, writes)

    def stt(self, eng, out, in0, s, in1, op0, op1, reads, writes):
        self.op(eng, lambda e: e.scalar_tensor_tensor(out=out, in0=in0, scalar=s, in1=in1, op0=op0, op1=op1), reads, writes)

    def cp(self, eng, out, in_, reads, writes):
        if eng == "act":
            self.op(eng, lambda e: e.activation(out=out, in_=in_, func=AF.Copy), reads, writes)
        else:
            self.op(eng, lambda e: e.tensor_copy(out=out, in_=in_), reads, writes)

    def rsqrt(self, ap, tile):
        self.op("act", lambda e: e.activation(out=ap, in_=ap, func=AF.Ln), [tile.b], [tile.b])
        self.op("act", lambda e: e.activation(out=ap, in_=ap, func=AF.Exp, scale=-0.5), [tile.b], [tile.b])

    def ms(self, eng, ap, val, writes):
        self.op(eng, lambda e: e.memset(ap, val), (), writes)


CW = 2688
C_ID = 0
C_SU = 128
C_IU = 192
C_SL = 256
C_BLK = 384
C_ONE = 512
C_TRI2 = 640
C_MB2 = 768
C_SEL0 = 896
C_SEL1 = 1024
C_RST = 1152
CWF = 1664
C_PAR0 = 1664
C_PAR1 = 2176
CB_PAR0 = 640
CB_PAR1 = 1152


def make_consts():
    c = np.zeros((128, CW), np.float32)
    c[:, 0:128] = np.eye(128, dtype=np.float32)
    s = np.arange(64)[:, None]
    t = np.arange(64)[None, :]
    c[0:64, C_SU:C_SU + 64] = (s < t)
    c[0:64, C_IU:C_IU + 64] = (s <= t)
    c[0:64, C_SL:C_SL + 64] = (s > t)
    c[0:64, C_BLK:C_BLK + 64] = 1.0
    c[64:128, C_BLK + 64:C_BLK + 128] = 1.0
    c[:, C_ONE:C_ONE + 128] = 1.0
    r = np.arange(128)[:, None]
    q = np.arange(128)[None, :]
    same = (r // 64) == (q // 64)
    c[:, C_TRI2:C_TRI2 + 128] = (same & (r <= q))
    c[:, C_MB2:C_MB2 + 128] = np.where(same & (r <= q), 0.0, -1.0e5)
    c[0:64, C_SEL0:C_SEL0 + 128] = 1.0
    c[64:128, C_SEL1:C_SEL1 + 128] = 1.0
    tk = np.arange(512)
    c[:, C_PAR0:C_PAR0 + 512] = ((tk // 64) % 2 == 0)[None, :]
    c[:, C_PAR1:C_PAR1 + 512] = ((tk // 64) % 2 == 1)[None, :]
    c[:, C_RST:C_RST + 512] = (tk % 64 != 0)[None, :]
    return c


class Builder:
    def __init__(self, n_layers=4, n_tiles=16, mixers="ABCD", dbg=False):
        self.NL = n_layers
        self.NT = n_tiles
        self.mixers = mixers
        self.dbg = dbg
        self.INV_DT = BF16
        self.wseq = None
        self.wrec = []
        self.wptr = 0
        self.wissued = {}
        self.prefetch = True
        self.rw_stop = 0
        self.nc = bass.Bass("TRN2", target_bir_lowering=False)
        self.p = Prog(self.nc)

    def declare(self):
        nc = self.nc
        di = lambda name, shape: nc.dram_tensor(name, list(shape), F32, kind="ExternalInput").ap()
        self.x = di("x", (SEQ, D))
        self.c = di("c", (8, 128))
        self.consts = di("consts", (128, CW))
        self.ada_w = di("ada_w", (4, D, 3 * D))
        self.ada_b = di("ada_b", (4, 3 * D))
        self.norm_pre = di("norm_pre", (4, D))
        self.norm_post = di("norm_post", (4, D))
        self.ev_w_in = di("ev_w_in", (2, D, EVEN_COLS))
        self.ev_w_out = di("ev_w_out", (2, D, D))
        self.tm_mu = di("tm_mu", (2, 2176))
        self.tm_w0 = di("tm_w0", (2, 512))
        self.tm_w2 = di("tm_w2", (2, 64, 512))
        self.tm_a0 = di("tm_a0", (2, 512))
        self.tm_a2 = di("tm_a2", (2, 64, 512))
        self.tm_k_k = di("tm_k_k", (2, 512))
        self.tm_k_a = di("tm_k_a", (2, 512))
        self.tm_r_k = di("tm_r_k", (2, 512))
        self.tm_lnx_g = di("tm_lnx_g", (2, 512))
        self.tm_lnx_b = di("tm_lnx_b", (2, 512))
        self.sc_conv_w = di("sc_conv_w", (2, 3, 512))
        self.od_w_in = di("od_w_in", (2, D, ODD_COLS))
        self.od_w_out = di("od_w_out", (2, D, D))
        self.cf_conv_w = di("cf_conv_w", (2, 31, 512))
        self.cf_conv_b = di("cf_conv_b", (2, 512))
        self.cf_ln_g = di("cf_ln_g", (2, 512))
        self.cf_ln_b = di("cf_ln_b", (2, 512))
        self.ssd_conv_w = di("ssd_conv_w", (2, 4, 1024))
        self.ssd_conv_b = di("ssd_conv_b", (2, 1024))
        self.ssd_dt_bias = di("ssd_dt_bias", (2, 8))
        self.ssd_a_log = di("ssd_a_log", (2, 8))
        self.ssd_d = di("ssd_d", (2, 8))
        self.ssd_norm_g = di("ssd_norm_g", (2, 512))
        self.out = nc.dram_tensor("out", [SEQ, D], F32, kind="ExternalOutput").ap()
        self.wblocks = []
        for L in range(self.NL):
            ncols = EVEN_COLS if L % 2 == 0 else ODD_COLS
            blks = []
            c0 = 0
            bi = 0
            while c0 < ncols + D:
                if c0 < ncols:
                    n = min(512, ncols - c0)
                    src = (self.ev_w_in if L % 2 == 0 else self.od_w_in)[L // 2][:, c0:c0 + n]
                else:
                    n = 512
                    src = (self.ev_w_out if L % 2 == 0 else self.od_w_out)[L // 2][:, c0 - ncols:c0 - ncols + n]
                sc = nc.dram_tensor("wsc_%d_%d" % (L, bi), [128, 8 * 512], BF16).ap()
                blks.append((sc, n, Buf(), src))
                c0 += n
                bi += 1
            self.wblocks.append(blks)

    def alloc(self):
        p = self.p
        self.cst = p.sb("cst", (128, CWF), F32)
        self.cstb = p.sb("cstb", (128, 1664), BF16)
        self.xs = [p.sb("xs0", (128, 4, D), F32)]
        self.xs.append(self.xs[0])
        self.xT = p.sb("xT", (128, 8, TT), F32)
        self.yo = p.sb("yo", (128, 8, TT + 3), F32)
        self.sq = p.sb("sq", (128, 8, TT), BF16, nb=8)
        self.rb = p.sb("rb", (128, TT), F32)
        self.wst = self.yo
        self.g2c = p.sb("g2c", (128, 4, 8), F32)
        self.npo = p.sb("npo", (128, 4, 8), F32)
        self.hT = p.sb("hT", (128, 8, TT), BF16)
        self.yT = p.sb("yT", (128, 8, TT), BF16, nb=8)
        self.wring = [p.sb("wr%d" % i, (128, 8, 512), BF16) for i in range(2)]
        self.wnext = 0
        self.ss = p.sb("ss", (128, 8), F32)
        self.rstd = p.sb("rstd", (128, 8), F32)
        self.g1c = p.sb("g1c", (128, 4, 8), F32)
        self.shc = p.sb("shc", (128, 4, 8), F32)
        self.cT = p.sb("cT", (128, 8), F32)
        self.abc = p.sb("abc", (128, 4, 24), F32)
        self.npc = p.sb("npc", (128, 4, 8), F32)
        self.F = [p.sb("F%d" % i, (128, 4, TT), F32) for i in range(6)]
        self.cfw = p.sb("cfw", (128, 2, 4, 31), F32)
        self.cfp = p.sb("cfp", (128, 2, 3, 4), F32)
        self.cf_tail = p.sb("cf_tail", (128, 2, 4, 30), F32)
        self.uext = p.sb("uext", (128, 4, 30 + TT), F32)
        self.sdw = p.sb("sdw", (128, 2, 8, 4), F32)
        self.sdb = p.sb("sdb", (128, 2, 8), F32)
        self.sd_tail = p.sb("sd_tail", (128, 2, 8, 3), F32)
        self.sdng = p.sb("sdng", (128, 2, 4), F32)
        self.hrow = p.sb("hrow", (128, 2, 3, 8), F32)
        self.xext = self.yo
        self.xbcT = p.sb("xbcT", (128, 8, TT), BF16)
        self.ctm = p.sb("ctm", (128, 2, 2, TT), BF16)
        self.S = p.sb("S", (128, 2, 8, 64), F32)
        self.Sb = p.sb("Sb", (128, 2, 8, 64), BF16, nb=2)
        self.tokx = p.sb("tokx", (128, 512), F32)
        self.xdt = p.sb("xdt", (128, 8, 64), BF16)
        self.xdd = p.sb("xdd", (128, 8, 64), BF16)
        self.btok = p.sb("btok", (128, 256), BF16)
        self.dts = p.sb("dts", (128, 6, 8), F32)
        self.cdb = p.sb("cdb", (128, 2, 8), F32)
        self.Dall = p.sb("Dall", (128, 8, 128), F32)
        self.dec = p.sb("dec", (128, 8, 128), F32)
        self.cbt = p.sb("cbt", (128, 2, 128), F32)
        self.MT = p.sb("MT", (128, 8, 128), BF16)
        self.ytok = p.sb("ytok", (128, 512), F32)

        self.tmp = p.sb("tmp", (128, D), F32)
        self.pp = [p.ps("pp%d" % i, (128, 512), F32) for i in range(4)]
        self.ppn = 0
        self.pt = [Tile(self.nc.alloc_psum_tensor("ptb%d" % i, [128, 1024], BF16)[:, 0:512]) for i in range(2)]
        pcb_ = self.nc.alloc_psum_tensor("pcb", [128, 512], F32)
        self.pcb = Tile(pcb_[:, 0:256].rearrange("p (g n) -> p g n", g=2))
        psm_ = self.nc.alloc_psum_tensor("psm", [128, 512], F32)
        self.psm = Tile(psm_[:, 0:64])
        self.pp6 = list(self.pp) + [Tile(pcb_[:, :]), Tile(psm_[:, :])]
        self.pp6[4].bufs = self.pcb.bufs
        self.pp6[5].bufs = self.psm.bufs
        self.pp6n = 0
        self.ptn = 0
        self.mu = p.sb("mu", (128, 2, 17), F32)
        self.omu = p.sb("omu", (128, 2, 17), F32)
        self.scw = p.sb("scw", (128, 2, 4, 3), F32)
        self.sc_tail = p.sb("sc_tail", (128, 2, 4, 2), F32)
        self.tm_prev = p.sb("tm_prev", (128, 2, 17), F32)
        self.tmc = p.sb("tmc", (128, 2, 7, 4), F32)
        self.tmk1 = p.sb("tmk1", (128, 2, 4), F32)
        self.w2a2 = p.sb("w2a2", (128, 2, 512), BF16)
        self.Hs = p.sb("Hs", (128, 2, 4, 64), F32)
        self.Hb = p.sb("Hb", (128, 4, 64), BF16)
        self.gl = p.sb("gl", (128, 4, 8), F32)

    def next_pp(self):
        t = self.pp[self.ppn]
        self.ppn = (self.ppn + 1) % len(self.pp)
        return t

    def next_pp6(self):
        t = self.pp6[self.pp6n]
        self.pp6n = (self.pp6n + 1) % len(self.pp6)
        return t

    def next_pt(self):
        t = self.pt[self.ptn]
        self.ptn = (self.ptn + 1) % len(self.pt)
        return t

    def dump(self, name, ap, tile, n):
        if not self.dbg:
            return
        p = self.p
        d = self.nc.dram_tensor("dbg_" + name, [128, n], F32, kind="ExternalOutput").ap()
        sc = self.tmp
        p.cp("dve", sc[:, 0:n], ap, list(tile.bufs), [sc.b])
        p.dma("sp", d, sc[:, 0:n], [sc.b], ())

    def prologue(self):
        p = self.p
        nc = self.nc
        p.dma("sp", self.cst[:], self.consts[:, 0:CWF], (), [self.cst.b])
        p.cp("dve", self.cstb[:, 0:640], self.cst[:, 0:640], [self.cst.b], [self.cstb.b])
        p.dma("sp", self.tmp[:], self.consts[:, C_PAR0:C_PAR0 + 1024], (), [self.tmp.b])
        p.cp("dve", self.cstb[:, 640:1664], self.tmp[:], [self.tmp.b], [self.cstb.b])
        self.ident_b = self.cstb[:, 0:128]
        self.ones_b = self.cstb[:, C_ONE:C_ONE + 128]
        self.ident_f = self.cst[:, 0:128]
        k = 0
        for L in range(self.NL):
            for (sc, n, buf, src) in self.wblocks[L]:
                p.dma("sp", self.wst[:, :, 0:n], src.rearrange("(kc p) n -> p kc n", p=128), (), [self.wst.b])
                wr = self.wring[k % 2]
                eng = ("dve", "act", "pool")[k % 3]
                if eng == "act":
                    p.act(wr[:, :, 0:n], self.wst[:, :, 0:n], AF.Copy, [self.wst.b], [wr.b])
                else:
                    p.cp(eng, wr[:, :, 0:n], self.wst[:, :, 0:n], [self.wst.b], [wr.b])
                p.dma("act", sc.rearrange("p (kc n) -> p kc n", kc=8)[:, :, 0:n], wr[:, :, 0:n], [wr.b], [buf])
                k += 1
        stg = self.tmp

        def cols(src_rows, R, W, dst_fn, dst_tile):
            for w0 in range(0, W, 1024):
                wn = min(1024, W - w0)
                p.dma("sp", stg[0:R, 0:wn], src_rows[:, w0:w0 + wn], (), [stg.b])
                for c_ in range(wn // 128):
                    ps = self.next_pp()
                    p.tr(ps[:, 0:R], stg[0:R, c_ * 128:(c_ + 1) * 128], self.cst[0:R, 0:R], [stg.b, self.cst.b], [ps.b])
                    p.cp("dve", dst_fn(w0 // 128 + c_), ps[:, 0:R], [ps.b], [dst_tile.b])

        cols(self.c, 8, 128, lambda cc: self.cT[:, :], self.cT)
        p.act(self.cT[:], self.cT[:], AF.Silu, [self.cT.b], [self.cT.b])
        cols(self.ada_b, 4, 3072, lambda cc: self.abc[:, :, cc], self.abc)
        cols(self.norm_pre, 4, 1024, lambda cc: self.npc[:, :, cc], self.npc)
        cols(self.norm_post, 4, 1024, lambda cc: self.npo[:, :, cc], self.npo)
        cols(self.tm_mu, 2, 2176, lambda cc: self.mu[:, :, cc], self.mu)
        p.ts("dve", self.omu[:], self.mu[:], -1.0, 1.0, ALU.mult, ALU.add, [self.mu.b], [self.omu.b])
        for l in range(2):
            for i_, src in enumerate((self.tm_w0, self.tm_a0, self.tm_k_k, self.tm_k_a, self.tm_r_k, self.tm_lnx_g, self.tm_lnx_b)):
                cols(src[l:l + 1, :], 1, 512, lambda cc, l=l, i_=i_: self.tmc[:, l, i_, cc:cc + 1], self.tmc)
            p.ts("dve", self.tmk1[:, l, :], self.tmc[:, l, 3, :], -1.0, 1.0, ALU.mult, ALU.add, [self.tmc.b], [self.tmk1.b])
            p.dma("sp", stg[0:64, 0:512], self.tm_w2[l], (), [stg.b])
            p.dma("sp", stg[64:128, 0:512], self.tm_a2[l], (), [stg.b])
            p.cp("dve", self.w2a2[:, l, :], stg[:, 0:512], [stg.b], [self.w2a2.b])
            cols(self.sc_conv_w[l], 3, 512, lambda cc, l=l: self.scw[:, l, cc, :], self.scw)
            cols(self.cf_conv_w[l], 31, 512, lambda cc, l=l: self.cfw[:, l, cc, :], self.cfw)
            for i_, src in enumerate((self.cf_conv_b, self.cf_ln_g, self.cf_ln_b)):
                cols(src[l:l + 1, :], 1, 512, lambda cc, l=l, i_=i_: self.cfp[:, l, i_, cc:cc + 1], self.cfp)
            cols(self.ssd_conv_w[l], 4, 1024, lambda cc, l=l: self.sdw[:, l, cc, :], self.sdw)
            cols(self.ssd_conv_b[l:l + 1, :], 1, 1024, lambda cc, l=l: self.sdb[:, l, cc:cc + 1], self.sdb)
            cols(self.ssd_norm_g[l:l + 1, :], 1, 512, lambda cc, l=l: self.sdng[:, l, cc:cc + 1], self.sdng)
            for i_, src in enumerate((self.ssd_dt_bias, self.ssd_a_log, self.ssd_d)):
                p.dma("sp", stg[0:1, 0:8], src[l:l + 1, :], (), [stg.b])
                ps = self.next_pp()
                p.mm(ps[:, 0:8], self.cst[0:1, C_ONE:C_ONE + 128], stg[0:1, 0:8], True, True, [self.cst.b, stg.b], [ps.b])
                p.cp("dve", self.hrow[:, l, i_, :], ps[:, 0:8], [ps.b], [self.hrow.b])
            p.act(self.hrow[:, l, 1, :], self.hrow[:, l, 1, :], AF.Exp, [self.hrow.b], [self.hrow.b])
            p.ts("dve", self.hrow[:, l, 1, :], self.hrow[:, l, 1, :], -1.0, None, ALU.mult, None, [self.hrow.b], [self.hrow.b])
        p.ms("dve", self.Hs[:], 0.0, [self.Hs.b])
        p.ms("dve", self.cf_tail[:], 0.0, [self.cf_tail.b])
        p.ms("dve", self.sd_tail[:], 0.0, [self.sd_tail.b])
        p.ms("dve", self.S[:], 0.0, [self.S.b])
        p.ms("dve", self.sc_tail[:], 0.0, [self.sc_tail.b])
        p.ms("dve", self.tm_prev[:], 0.0, [self.tm_prev.b])
        one11 = self.cst[0:1, C_ONE:C_ONE + 1]
        for L in range(self.NL):
            aw = self.ada_w[L]
            pcols = []
            for grp in range(3):
                for blk in range(2):
                    c0 = grp * D + blk * 512
                    p.dma("sp", self.wst[:, :, 0:512], aw[:, c0:c0 + 512].rearrange("(kc p) n -> p kc n", p=128), (), [self.wst.b])
                    prow = self.next_pp()
                    for kc in range(8):
                        p.mm(prow[0:1, :], self.cT[:, kc:kc + 1], self.wst[:, kc, 0:512], kc == 0, kc == 7, [self.wst.b, self.cT.b], [prow.b])
                    p.cp("act", self.tmp[0:1, blk * 512:(blk + 1) * 512], prow[0:1, :], [prow.b], [self.tmp.b])
                pcol = self.next_pp()
                for jj in range(8):
                    p.mm(pcol[:, jj:jj + 1], self.tmp[0:1, jj * 128:(jj + 1) * 128], one11, True, True, [self.tmp.b, self.cst.b], [pcol.b])
                pcols.append(pcol)
                if grp == 0:
                    p.tt("dve", self.shc[:, L, :], pcol[:, 0:8], self.abc[:, L, 0:8], ALU.add, [pcol.b, self.abc.b], [self.shc.b])
                elif grp == 1:
                    p.tt("dve", self.g1c[:, L, :], pcol[:, 0:8], self.abc[:, L, 8:16], ALU.add, [pcol.b, self.abc.b], [self.g1c.b])
                    p.stt("dve", self.g1c[:, L, :], self.g1c[:, L, :], 1.0, self.npc[:, L, :], ALU.add, ALU.mult, [self.g1c.b, self.npc.b], [self.g1c.b])
                else:
                    p.tt("dve", self.g2c[:, L, :], pcol[:, 0:8], self.abc[:, L, 16:24], ALU.add, [pcol.b, self.abc.b], [self.g2c.b])
                    p.tt("dve", self.g2c[:, L, :], self.g2c[:, L, :], self.npo[:, L, :], ALU.mult, [self.g2c.b, self.npo.b], [self.g2c.b])

    def _issue_w(self, k):
        L, bi = self.wseq[k]
        sc, n, buf, _ = self.wblocks[L][bi]
        wr = self.wring[k % len(self.wring)]
        self.p.dma("sp", wr[:, :, 0:n], sc.rearrange("p (kc n) -> p kc n", kc=8)[:, :, 0:n], [buf], [wr.b])
        self.wissued[k] = wr

    def load_w(self, L, bi):
        if self.wseq is None:
            self.wrec.append((L, bi))
            sc, n, buf, _ = self.wblocks[L][bi]
            wr = self.wring[self.wnext]
            self.wnext = (self.wnext + 1) % len(self.wring)
            self.p.dma("sp", wr[:, :, 0:n], sc.rearrange("p (kc n) -> p kc n", kc=8)[:, :, 0:n], [buf], [wr.b])
            return wr
        k = self.wptr
        assert self.wseq[k] == (L, bi), (k, self.wseq[k], (L, bi))
        if k not in self.wissued:
            self._issue_w(k)
        wr = self.wissued.pop(k)
        self.wptr += 1
        if self.wptr < len(self.wseq):
            self._issue_w(self.wptr)
        return wr

    def load_tile(self, t):
        p = self.p
        xs = self.xs[t % 2]
        for kc in range(8):
            ps = self.next_pp()
            for s in range(4):
                p.tr(ps[:, s * 128:(s + 1) * 128], xs[:, s, kc * 128:(kc + 1) * 128], self.ident_f, [xs.b, self.cst.b], [ps.b])
            if kc % 2 == 0:
                p.act(self.xT[:, kc, :], ps[:], AF.Copy, [ps.b], [self.xT.b])
            else:
                p.cp("dve", self.xT[:, kc, :], ps[:], [ps.b], [self.xT.b])

    def store_tile(self, t, orr):
        p = self.p
        xs = self.xs[t % 2]
        for s in range(4):
            for half in range(2):
                ps = self.next_pp()
                for q in range(4):
                    kc = half * 4 + q
                    p.tr(ps[:, q * 128:(q + 1) * 128], self.xT[:, kc, s * 128:(s + 1) * 128], self.ident_f, [self.xT.b, self.cst.b], [ps.b])
                if half == 0:
                    p.act(xs[:, s, 0:512], ps[:], AF.Copy, [ps.b], [xs.b])
                else:
                    p.cp("dve", xs[:, s, 512:1024], ps[:], [ps.b], [xs.b])
        p.dma("sp", orr[t], xs[:], [xs.b], ())

    def rms_bcast(self, src):
        p = self.p
        for kc in range(8):
            p.act(self.sq[:, kc, :], src[:, kc, 0:TT], AF.Square, [src.b], [self.sq.bufs[kc]])
        ps = self.next_pp()
        for kc in range(8):
            p.mm(ps[:], self.ones_b, self.sq[:, kc, :], kc == 0, kc == 7, [self.cstb.b, self.sq.bufs[kc]], [ps.b])
        p.ts("dve", self.rb[:], ps[:], 1.0 / D, 1e-6, ALU.mult, ALU.add, [ps.b], [self.rb.b])
        p.op("act", lambda e: e.activation(out=self.rb[:], in_=self.rb[:], func=AF.Ln), [self.rb.b], [self.rb.b])
        p.op("act", lambda e: e.activation(out=self.rb[:], in_=self.rb[:], func=AF.Exp, scale=-0.5), [self.rb.b], [self.rb.b])

    def pre_norm(self, L):
        p = self.p
        if L == 0 and not getattr(self, "_d0", False):
            self.dump("xT0", self.xT[:, 0, :], self.xT, 512)
            self.dump("g1c", self.g1c[:, 0, :], self.g1c, 8)
            self.dump("shc", self.shc[:, 0, :], self.shc, 8)
            self.dump("cT", self.cT[:], self.cT, 8)
            self.dump("npc", self.npc[:, 0, :], self.npc, 8)
            self.dump("abc", self.abc[:, 0, :], self.abc, 24)
            self.dump("xT7", self.xT[:, 7, :], self.xT, 512)
        self.rms_bcast(self.xT)
        if L == 0 and not getattr(self, "_d0", False):
            self._d0 = True
            self._d0b = True
            self.dump("rb0", self.rb[:], self.rb, 512)
            self.dump("sq0", self.sq[:, 0, :], self.sq, 512)
        for kc in range(8):
            p.tt("dve", self.tmp[:, 0:512], self.xT[:, kc, :], self.rb[:], ALU.mult, [self.xT.b, self.rb.b], [self.tmp.b])
            p.ts("dve", self.hT[:, kc, :], self.tmp[:, 0:512], self.g1c[:, L, kc:kc + 1], self.shc[:, L, kc:kc + 1], ALU.mult, ALU.add,
                 [self.tmp.b, self.g1c.b, self.shc.b], [self.hT.b])
        if getattr(self, "_d0b", False):
            self._d0b = False
            self.dump("hT", self.hT[:, 0, :], self.hT, 512)
            self.dump("hT7", self.hT[:, 7, :], self.hT, 512)

    def proj_fm(self, wr, lc, ps):
        p = self.p
        for kc in range(8):
            p.mm(ps[:], wr[:, kc, lc * 128:(lc + 1) * 128], self.hT[:, kc, :], kc == 0, kc == 7, [wr.b, self.hT.b], [ps.b])

    def out_proj_post(self, L):
        p = self.p
        nb = len(self.wblocks[L])
        allY = self.yT.bufs
        for half in range(2):
            wr = self.load_w(L, nb - 2 + half)
            for q in range(4):
                dmc = half * 4 + q
                ps = self.next_pp()
                for cc in range(8):
                    p.mm(ps[:], wr[:, cc, q * 128:(q + 1) * 128], self.yT[:, cc, :], cc == 0, cc == 7, [wr.b] + allY, [ps.b])
                p.act(self.yo[:, dmc, 0:TT], ps[:], AF.Copy, [ps.b], [self.yo.b])
        if not getattr(self, "_d1", False):
            self._d1 = True
            self.dump("yo0", self.yo[:, 0, 0:TT], self.yo, 512)
            self.dump("yo7", self.yo[:, 7, 0:TT], self.yo, 512)
        self.rms_bcast(self.yo)
        for dmc in range(8):
            p.stt("dve", self.tmp[:, 0:512], self.yo[:, dmc, 0:TT], self.g2c[:, L, dmc:dmc + 1], self.rb[:], ALU.mult, ALU.mult,
                  [self.yo.b, self.g2c.b, self.rb.b], [self.tmp.b])
            p.tt("dve", self.xT[:, dmc, :], self.xT[:, dmc, :], self.tmp[:, 0:512], ALU.add, [self.xT.b, self.tmp.b], [self.xT.b])

    def even_layer(self, L):
        p = self.p
        j = L // 2
        self.pre_norm(L)
        wrs = {}
        loaded = {}

        def get(ch):
            bi = ch // 4
            if bi not in loaded:
                loaded[bi] = self.load_w(L, bi)
            return (loaded[bi], ch % 4)

        self._cur_blk = None
        if "B" in self.mixers:
            self.short_conv_stream(L, j)
        else:
            for cc in range(4, 8):
                p.ms("dve", self.yT[:, cc, :], 0.0, [self.yT.bufs[cc]])
        if "A" in self.mixers:
            self.rwkv(L, j)
        else:
            for cc in range(4):
                p.ms("dve", self.yT[:, cc, :], 0.0, [self.yT.bufs[cc]])
        if not getattr(self, "_yd", False):
            self._yd = True
            self.dump("yT4", self.yT[:, 4, :], self.yT, 512)
        self.out_proj_post(L)

    def short_conv_stream(self, L, j):
        p = self.p
        F = self.F
        dest = {}
        for cc in range(4):
            dest[17 + cc] = (F[3], cc, AF.Copy)
            dest[21 + cc] = (F[0], cc, AF.Copy)
            dest[25 + cc] = (F[1], cc, AF.Copy)
            dest[29 + cc] = (F[4], cc, AF.Silu)
        cur_bi = None
        wr = None
        for ch in range(17, 33):
            bi = ch // 4
            if bi != cur_bi:
                wr = self.load_w(L, bi)
                cur_bi = bi
            ps = self.next_pp()
            self.proj_fm(wr, ch % 4, ps)
            t, cc, fn = dest[ch]
            if ch % 2 == 0:
                p.act(t[:, cc, :], ps[:], fn, [ps.b], [t.b])
            else:
                if fn == AF.Copy:
                    p.cp("dve", t[:, cc, :], ps[:], [ps.b], [t.b])
                else:
                    p.act(t[:, cc, :], ps[:], fn, [ps.b], [t.b])
        u = F[0]
        acc = F[2]
        w = self.scw
        tl = self.sc_tail
        p.tt("dve", u[:], F[0][:], F[1][:], ALU.mult, [F[0].b, F[1].b], [u.b])
        for cc in range(4):
            p.ts("dve", acc[:, cc, :], u[:, cc, :], w[:, j, cc, 2:3], None, ALU.mult, None, [u.b, w.b], [acc.b])
            p.stt("dve", acc[:, cc, 1:TT], u[:, cc, 0:TT - 1], w[:, j, cc, 1:2], acc[:, cc, 1:TT], ALU.mult, ALU.add, [u.b, w.b, acc.b], [acc.b])
            p.stt("dve", acc[:, cc, 2:TT], u[:, cc, 0:TT - 2], w[:, j, cc, 0:1], acc[:, cc, 2:TT], ALU.mult, ALU.add, [u.b, w.b, acc.b], [acc.b])
            p.stt("dve", acc[:, cc, 0:1], tl[:, j, cc, 1:2], w[:, j, cc, 1:2], acc[:, cc, 0:1], ALU.mult, ALU.add, [tl.b, w.b, acc.b], [acc.b])
            p.stt("dve", acc[:, cc, 0:2], tl[:, j, cc, 0:2], w[:, j, cc, 0:1], acc[:, cc, 0:2], ALU.mult, ALU.add, [tl.b, w.b, acc.b], [acc.b])
            p.cp("dve", tl[:, j, cc, :], u[:, cc, TT - 2:TT], [u.b, acc.b], [tl.b])
        p.tt("dve", acc[:], acc[:], F[3][:], ALU.mult, [acc.b, F[3].b], [acc.b])
        for cc in range(4):
            p.tt("dve", self.yT[:, 4 + cc, :], acc[:, cc, :], F[4][:, cc, :], ALU.mult, [acc.b, F[4].b], [self.yT.bufs[4 + cc]])

    def proj_to(self, L, ch, dst_ap, dst_bufs, fn=None, eng="act"):
        p = self.p
        bi = ch // 4
        if self._cur_blk != (L, bi):
            self._cur_wr = self.load_w(L, bi)
            self._cur_blk = (L, bi)
        ps = self.next_pp()
        self.proj_fm(self._cur_wr, ch % 4, ps)
        if eng == "act":
            p.act(dst_ap, ps[:], fn or AF.Copy, [ps.b], dst_bufs)
        else:
            p.cp(eng, dst_ap, ps[:], [ps.b], dst_bufs)

    def conformer(self, L, j):
        p = self.p
        F = self.F
        ue = self.uext
        p.cp("dve", ue[:, :, 0:30], self.cf_tail[:, j, :, :], [self.cf_tail.b], [ue.b])
        for cc in range(4):
            self.proj_to(L, cc, F[0][:, cc, :], [F[0].b], eng="dve")
        for cc in range(4):
            self.proj_to(L, 4 + cc, F[1][:, cc, :], [F[1].b], AF.Sigmoid)
        p.tt("dve", ue[:, :, 30:30 + TT], F[0][:], F[1][:], ALU.mult, [F[0].b, F[1].b], [ue.b])
        p.cp("dve", self.cf_tail[:, j, :, :], ue[:, :, TT:TT + 30], [ue.b], [self.cf_tail.b])
        for cc in range(4):
            self.proj_to(L, 8 + cc, F[1][:, cc, :], [F[1].b], AF.Silu)
        yield "proj"
        acc = F[2]
        w = self.cfw
        ub = Tile(self.sq[:].rearrange("p a b -> p (a b)")[:, 0:4 * (30 + TT)].rearrange("p (c t) -> p c t", c=4))
        ub.bufs = list(self.sq.bufs)
        p.cp("act", ub[:], ue[:, :, 0:30 + TT], [ue.b], list(ub.bufs))
        ring = []
        for c4 in range(4):
            for i_ in range(4):
                r_ = Tile(self.yT[:, 4 + c4, i_ * 128:(i_ + 1) * 128])
                ring.append(r_)
        n_ = 0
        for cc in range(4):
            ps = self.next_pp()
            for k in range(31):
                dg = ring[n_ % 16]
                p.ts("dve", dg[:], self.ident_b, w[:, j, cc, k:k + 1], None, ALU.mult, None, [self.cstb.b, w.b], [dg.b])
                p.mm(ps[:], dg[:], ub[:, cc, k:k + TT], k == 0, k == 30, [dg.b] + list(ub.bufs), [ps.b])
                n_ += 1
            p.act(acc[:, cc, :], ps[:], AF.Identity, [ps.b, self.cfp.b], [acc.b], bias=self.cfp[:, j, 0, cc:cc + 1])
            yield "cc"
        for c4 in range(4):
            p.ms("dve", self.yT[0:1, 4 + c4, 0:1], 0.0, [r_.b for r_ in ring[c4 * 4:(c4 + 1) * 4]] + [self.yT.bufs[4 + c4]])
        yield "conv"
        ones_f = self.cst[:, C_ONE:C_ONE + 128]
        pm = self.next_pp()
        for cc in range(4):
            p.mm(pm[:], ones_f, acc[:, cc, :], cc == 0, cc == 3, [self.cst.b, acc.b], [pm.b])
        for cc in range(4):
            p.act(F[0][:, cc, :], acc[:, cc, :], AF.Square, [acc.b], [F[0].b])
        pq = self.next_pp()
        for cc in range(4):
            p.mm(pq[:], ones_f, F[0][:, cc, :], cc == 0, cc == 3, [self.cst.b, F[0].b], [pq.b])
        mean = F[3][:, 0, :]
        var = F[3][:, 1, :]
        yield "stats"
        p.ts("dve", mean, pm[:], 1.0 / 512, None, ALU.mult, None, [pm.b], [F[3].b])
        p.tt("dve", F[3][:, 2, :], mean, mean, ALU.mult, [F[3].b], [F[3].b])
        p.stt("dve", var, pq[:], 1.0 / 512, F[3][:, 2, :], ALU.mult, ALU.subtract, [pq.b, F[3].b], [F[3].b])
        p.ts("dve", var, var, 1e-5, None, ALU.add, None, [F[3].b], [F[3].b])
        p.op("act", lambda e: e.activation(out=var, in_=var, func=AF.Ln), [F[3].b], [F[3].b])
        p.op("act", lambda e: e.activation(out=var, in_=var, func=AF.Exp, scale=-0.5), [F[3].b], [F[3].b])
        if not getattr(self, "_cfd", False):
            self._cfd = True
            self.dump("cf_hT", self.hT[:, 0, :], self.hT, 512)
            self.dump("cf_xT", self.xT[:, 0, :], self.xT, 512)
            self.dump("cf_g1c", self.g1c[:, 1, :], self.g1c, 8)
            self.dump("cf_shc", self.shc[:, 1, :], self.shc, 8)
            self.dump("cf_rb", self.rb[:], self.rb, 512)
            self.dump("cf_u", ue[:, 0, 30:30 + TT], ue, 512)
            self.dump("cf_acc", acc[:, 0, :], acc, 512)
            self.dump("cf_mean", mean, F[3], 512)
            self.dump("cf_rstd", var, F[3], 512)
            self.dump("cf_sg", F[1][:, 0, :], F[1], 512)
        for cc in range(4):
            p.tt("dve", F[0][:, cc, :], acc[:, cc, :], mean, ALU.subtract, [acc.b, F[3].b], [F[0].b])
            p.tt("dve", F[0][:, cc, :], F[0][:, cc, :], var, ALU.mult, [F[0].b, F[3].b], [F[0].b])
            p.act(F[0][:, cc, :], F[0][:, cc, :], AF.Silu, [F[0].b, self.cfp.b], [F[0].b],
                  scale=self.cfp[:, j, 1, cc:cc + 1], bias=self.cfp[:, j, 2, cc:cc + 1])
            p.tt("dve", self.yT[:, cc, :], F[0][:, cc, :], F[1][:, cc, :], ALU.mult, [F[0].b, F[1].b], [self.yT.bufs[cc]])

    def ssd(self, L, j):
        p = self.p
        F = self.F
        xe = self.xext
        zs = F[4]
        for cc in range(4):
            self.proj_to(L, 12 + cc, zs[:, cc, :], [zs.b], AF.Silu)
        p.cp("dve", xe[:, :, 0:3], self.sd_tail[:, j, :, :], [self.sd_tail.b], [xe.b])
        for cc in range(8):
            self.proj_to(L, 16 + cc, xe[:, cc, 3:3 + TT], [xe.b], eng="act")
        p.cp("dve", self.sd_tail[:, j, :, :], xe[:, :, TT:TT + 3], [xe.b], [self.sd_tail.b])
        wdt = self.load_w(L, 6)
        self._cur_blk = None
        yield "proj"
        w = self.sdw
        for cc in range(8):
            eng = "dve"
            acc = F[5][:, cc % 4, :]
            p.ts(eng, acc, xe[:, cc, 3:3 + TT], w[:, j, cc, 3:4], None, ALU.mult, None, [xe.b, w.b], [F[5].bufs[0]])
            for k in range(3):
                p.stt(eng, acc, xe[:, cc, k:k + TT], w[:, j, cc, k:k + 1], acc, ALU.mult, ALU.add, [xe.b, w.b, F[5].b], [F[5].b])
            p.act(self.xbcT[:, cc, :], acc, AF.Silu, [F[5].b, self.sdb.b], [self.xbcT.b], bias=self.sdb[:, j, cc:cc + 1])
            if cc % 2 == 1:
                yield "conv"
        for g in range(2):
            for e in range(2):
                par = self.cstb[:, (CB_PAR0 if e == 0 else CB_PAR1):(CB_PAR0 if e == 0 else CB_PAR1) + TT]
                p.tt("dve", self.ctm[:, g, e, :], self.xbcT[:, 6 + g, :], par, ALU.mult, [self.xbcT.b, self.cstb.b], [self.ctm.b])
        y2T = Tile(self.yo[:, 0:4, 0:TT])
        y2T.bufs = self.yo.bufs
        for s in range(4):
            tsl = slice(s * 128, (s + 1) * 128)
            for kc in range(8):
                p.mm(self.psm[:, 0:8], self.hT[:, kc, tsl], wdt[:, kc, 0:8], kc == 0, kc == 7, [self.hT.b, wdt.b], [self.psm.b])
            d = self.dts
            p.tt("dve", d[:, 0, :], self.psm[:, 0:8], self.hrow[:, j, 0, :], ALU.add, [self.psm.b, self.hrow.b], [d.b])
            p.act(d[:, 0, :], d[:, 0, :], AF.Exp, [d.b], [d.b])
            p.ts("dve", d[:, 0, :], d[:, 0, :], 1.0, None, ALU.add, None, [d.b], [d.b])
            p.act(d[:, 0, :], d[:, 0, :], AF.Ln, [d.b], [d.b])
            p.tt("dve", d[:, 1, :], d[:, 0, :], self.hrow[:, j, 1, :], ALU.mult, [d.b, self.hrow.b], [d.b])
            p.mm(self.psm[:, 8:16], self.cst[:, C_TRI2:C_TRI2 + 128], d[:, 1, :], True, True, [self.cst.b, d.b], [self.psm.b])
            p.mm(self.psm[:, 16:24], self.cst[:, C_BLK:C_BLK + 128], d[:, 1, :], True, True, [self.cst.b, d.b], [self.psm.b])
            p.mm(self.psm[:, 24:32], self.cst[:, C_SEL0:C_SEL0 + 128], d[:, 1, :], True, True, [self.cst.b, d.b], [self.psm.b])
            p.mm(self.psm[:, 32:40], self.cst[:, C_SEL1:C_SEL1 + 128], d[:, 1, :], True, True, [self.cst.b, d.b], [self.psm.b])
            p.cp("dve", d[:, 2, :], self.psm[:, 8:16], [self.psm.b], [d.b])
            p.tt("dve", d[:, 4, :], self.psm[:, 16:24], d[:, 2, :], ALU.subtract, [self.psm.b, d.b], [d.b])
            p.act(d[:, 4, :], d[:, 4, :], AF.Exp, [d.b], [d.b])
            p.act(d[:, 5, :], d[:, 2, :], AF.Exp, [d.b], [d.b])
            p.act(self.cdb[:, 0, :], self.psm[:, 24:32], AF.Exp, [self.psm.b], [self.cdb.b])
            p.act(self.cdb[:, 1, :], self.psm[:, 32:40], AF.Exp, [self.psm.b], [self.cdb.b])
            p.ts("dve", d[:, 3, :], d[:, 2, :], -1.0, None, ALU.mult, None, [d.b], [d.b])
            yield "st"
            pt = self.next_pt()
            for cc in range(4):
                p.tr(pt[:, cc * 128:(cc + 1) * 128], self.xbcT[:, cc, tsl], self.ident_b, [self.xbcT.b, self.cstb.b], [pt.b])
            p.cp("act", self.tokx[:], pt[:], [pt.b], [self.tokx.b])
            pt2 = self.next_pt()
            for cc in range(2):
                p.tr(pt2[:, cc * 128:(cc + 1) * 128], self.xbcT[:, 4 + cc, tsl], self.ident_b, [self.xbcT.b, self.cstb.b], [pt2.b])
            p.cp("act", self.btok[:], pt2[:, 0:256], [pt2.b], [self.btok.b])
            tx3 = self.tokx[:].rearrange("p (h q) -> p h q", h=8)
            p.tt("dve", self.xdt[:], tx3, d[:, 0, :].unsqueeze(2).to_broadcast([128, 8, 64]), ALU.mult, [self.tokx.b, d.b], [self.xdt.b])
            p.tt("dve", self.xdd[:], self.xdt[:], d[:, 4, :].unsqueeze(2).to_broadcast([128, 8, 64]), ALU.mult, [self.xdt.b, d.b], [self.xdd.b])
            yield "st"
            for g in range(2):
                p.mm(self.pcb[:, g, :], self.xbcT[:, 4 + g, tsl], self.xbcT[:, 6 + g, tsl], True, True, [self.xbcT.b], [self.pcb.b])
            p.cp("act", self.cbt[:], self.pcb[:], [self.pcb.b], [self.cbt.b])
            p.tt("dve", self.Dall[:], self.cst[:, C_TRI2:C_TRI2 + 128].unsqueeze(1).to_broadcast([128, 8, 128]),
                 d[:, 1, :].unsqueeze(2).to_broadcast([128, 8, 128]), ALU.mult, [self.cst.b, d.b], [self.Dall.b])
            for hh in range(2):
                pd = self.next_pp()
                p.mm(pd[:], self.cst[:, C_ONE:C_ONE + 128], self.Dall[:, hh * 4:(hh + 1) * 4, :].rearrange("p h l -> p (h l)"), True, False,
                     [self.Dall.b, self.cst.b], [pd.b])
                p.mm(pd[:].rearrange("p (h l) -> p h l", h=4), self.cst[:, C_ID:C_ID + 128],
                     self.cst[:, C_MB2:C_MB2 + 128].unsqueeze(1).to_broadcast([128, 4, 128]), False, True, [self.cst.b], [pd.b])
                for q in range(4):
                    h = hh * 4 + q
                    p.act(self.dec[:, h, :], pd[:, q * 128:(q + 1) * 128], AF.Exp, [pd.b, d.b], [self.dec.b], bias=d[:, 3, h:h + 1])
            for g in range(2):
                p.tt("dve", self.MT[:, g * 4:(g + 1) * 4, :], self.dec[:, g * 4:(g + 1) * 4, :],
                     self.cbt[:, g, :].unsqueeze(1).to_broadcast([128, 4, 128]), ALU.mult, [self.dec.b, self.cbt.b], [self.MT.b])
            yield "st"
            pyd = self.next_pp()
            for h in range(8):
                p.mm(pyd[:, h * 64:(h + 1) * 64], self.MT[:, h, :], self.xdt[:, h, :], True, True, [self.MT.b, self.xdt.b], [pyd.b])
            yield "st"
            for e in range(2):
                rs = slice(e * 64, (e + 1) * 64)
                p.cp("act", self.Sb[:, e, :, :], self.S[:, j, :, :], [self.S.b], [self.Sb.bufs[e]])
                pst = self.next_pp()
                for g in range(2):
                    p.mm(pst[:, g * 256:(g + 1) * 256], self.btok[rs, g * 128:(g + 1) * 128],
                         self.xdd[rs, g * 4:(g + 1) * 4, :].rearrange("p h q -> p (h q)"), True, True, [self.btok.b, self.xdd.b], [pst.b])
                p.tt("dve", self.S[:, j, :, :], self.S[:, j, :, :], self.cdb[:, e, :].unsqueeze(2).to_broadcast([128, 8, 64]), ALU.mult,
                     [self.S.b, self.cdb.b], [self.S.b])
                p.tt("dve", self.S[:, j, :, :], self.S[:, j, :, :], pst[:].rearrange("p (h q) -> p h q", h=8), ALU.add, [self.S.b, pst.b], [self.S.b])
            pyo = self.next_pp()
            for g in range(2):
                for e in range(2):
                    p.mm(pyo[:, g * 256:(g + 1) * 256], self.ctm[:, g, e, tsl], self.Sb[:, e, g * 4:(g + 1) * 4, :].rearrange("p h q -> p (h q)"),
                         e == 0, e == 1, [self.ctm.b, self.Sb.bufs[e]], [pyo.b])
            yt = self.ytok
            yt3 = yt[:].rearrange("p (h q) -> p h q", h=8)
            p.tt("dve", yt3, pyo[:].rearrange("p (h q) -> p h q", h=8), d[:, 5, :].unsqueeze(2).to_broadcast([128, 8, 64]), ALU.mult, [pyo.b, d.b], [yt.b])
            p.tt("dve", yt[:], yt[:], pyd[:], ALU.add, [yt.b, pyd.b], [yt.b])
            p.tt("dve", self.tokx[:].rearrange("p (h q) -> p h q", h=8), tx3, self.hrow[:, j, 2, :].unsqueeze(2).to_broadcast([128, 8, 64]), ALU.mult,
                 [self.tokx.b, self.hrow.b], [self.tokx.b])
            p.tt("dve", yt[:], yt[:], self.tokx[:], ALU.add, [yt.b, self.tokx.b], [yt.b])
            yield "st"
            py = self.next_pp()
            for cc in range(4):
                p.tr(py[:, cc * 128:(cc + 1) * 128], yt[:, cc * 128:(cc + 1) * 128], self.ident_f, [yt.b, self.cst.b], [py.b])
            p.tt("dve", y2T[:, :, tsl], py[:].rearrange("p (c t) -> p c t", c=4), zs[:, :, tsl], ALU.mult, [py.b, zs.b], [y2T.b])
        yield "loop"
        for cc in range(4):
            p.act(self.sq[:, cc, :], y2T[:, cc, :], AF.Square, [y2T.b], [self.sq.bufs[cc]])
        pr = self.next_pp()
        for cc in range(4):
            p.mm(pr[:], self.ones_b, self.sq[:, cc, :], cc == 0, cc == 3, [self.cstb.b, self.sq.bufs[cc]], [pr.b])
        p.ts("dve", self.rb[:], pr[:], 1.0 / 512, 1e-6, ALU.mult, ALU.add, [pr.b], [self.rb.b])
        p.op("act", lambda e: e.activation(out=self.rb[:], in_=self.rb[:], func=AF.Ln), [self.rb.b], [self.rb.b])
        p.op("act", lambda e: e.activation(out=self.rb[:], in_=self.rb[:], func=AF.Exp, scale=-0.5), [self.rb.b], [self.rb.b])
        for cc in range(4):
            p.stt("dve", self.yT[:, 4 + cc, :], y2T[:, cc, :], self.sdng[:, j, cc:cc + 1], self.rb[:], ALU.mult, ALU.mult,
                  [y2T.b, self.sdng.b, self.rb.b], [self.yT.bufs[4 + cc]])

    def odd_layer(self, L):
        p = self.p
        j = L // 2
        self.pre_norm(L)
        self._cur_blk = None
        gens = []
        if "C" in self.mixers:
            gc = self.conformer(L, j)
            next(gc)
            gens.append(gc)
        else:
            for cc in range(4):
                p.ms("dve", self.yT[:, cc, :], 0.0, [self.yT.bufs[cc]])
        if "D" in self.mixers:
            gd = self.ssd(L, j)
            next(gd)
            gens.append(gd)
        else:
            for cc in range(4, 8):
                p.ms("dve", self.yT[:, cc, :], 0.0, [self.yT.bufs[cc]])
        while gens:
            for g_ in list(gens):
                try:
                    next(g_)
                except StopIteration:
                    gens.remove(g_)
        self.out_proj_post(L)

    def rwkv(self, L, j):
        p = self.p
        F = self.F
        LW = -0.6065306597126334
        INV = self.INV_DT
        cst = self.cst
        rT, kT, vT, sgT, bonT, aT = F[0], F[1], F[2], F[3], F[4], F[5]
        yob = self.yo[:].rearrange("p a b -> p (a b)").bitcast(BF16)
        AR = Tile(yob[:, 0:4096].rearrange("p (c n q d) -> p c n q d", c=4, n=8, q=2))
        AR.bufs = self.yo.bufs
        BK = Tile(yob[:, 4096:8192].rearrange("p (c n q d) -> p c n q d", c=4, n=8, q=2))
        BK.bufs = self.yo.bufs
        Vt, Bt, Kt = self.hT, self.sq, self.xbcT
        Ytok = Tile(self.xs[0][:].rearrange("p s (a d) -> p (s a) d", a=2))
        Ytok.bufs = self.xs[0].bufs
        if INV == F32:
            Wp = [self.Dall, self.dec]
            NTp = [Tile(self.uext[:, 0, 0:512].rearrange("p (h d) -> p h d", h=8)), Tile(self.uext[:, 1, 0:512].rearrange("p (h d) -> p h d", h=8))]
            Xs = Tile(self.uext[:, 2, 0:512].rearrange("p (h d) -> p h d", h=8))
        else:
            Wp = []
            for t_ in (self.Dall, self.dec):
                w_ = Tile(t_[:].rearrange("p h d -> p (h d)").bitcast(BF16)[:, 0:1024].rearrange("p (h d) -> p h d", h=8))
                w_.bufs = t_.bufs
                Wp.append(w_)
            NTp = [Tile(self.uext[:, i_, 0:512].bitcast(BF16)[:, 0:512].rearrange("p (h d) -> p h d", h=8)) for i_ in range(2)]
            Xs = Tile(self.uext[:, 2, 0:512].bitcast(BF16)[:, 0:512].rearrange("p (h d) -> p h d", h=8))
        for t_ in NTp + [Xs]:
            t_.bufs = self.uext.bufs
        Us = self.xdd
        Srb = Tile(self.Sb[:, 0, :, :])
        Srb.bufs = [self.Sb.bufs[0]]
        Sk = self.MT
        wdad = Tile(self.xdt[:].rearrange("p h d -> p (h d)"))
        wdad.bufs = self.xdt.bufs
        sig, cs = self.tokx, self.ytok
        e1 = self.tmp[:, 0:512]
        e2 = self.tmp[:, 512:1024]
        tmpb = self.tmp.b
        kk = self.rb
        mu, omu, prev = self.mu, self.omu, self.tm_prev
        tmc = self.tmc

        def shifted(ch, dst_ap, dst_tile):
            bi = ch // 4
            if self._cur_blk != (L, bi):
                self._cur_wr = self.load_w(L, bi)
                self._cur_blk = (L, bi)
            ps = self.next_pp()
            self.proj_fm(self._cur_wr, ch % 4, ps)
            p.act(dst_ap, ps[:], AF.Copy, [ps.b, omu.b], [dst_tile.b], scale=omu[:, j, ch:ch + 1])
            p.stt("dve", dst_ap[:, 1:TT], ps[:, 0:TT - 1], mu[:, j, ch:ch + 1], dst_ap[:, 1:TT], ALU.mult, ALU.add, [ps.b, mu.b, dst_tile.b], [dst_tile.b])
            p.stt("dve", dst_ap[:, 0:1], prev[:, j, ch:ch + 1], mu[:, j, ch:ch + 1], dst_ap[:, 0:1], ALU.mult, ALU.add, [prev.b, mu.b, dst_tile.b], [dst_tile.b])
            p.cp("dve", prev[:, j, ch:ch + 1], ps[:, TT - 1:TT], [ps.b], [prev.b])

        for cc in range(4):
            shifted(cc, rT[:, cc, :], rT)
        for cc in range(4):
            shifted(4 + cc, kT[:, cc, :], kT)
        for cc in range(4):
            shifted(8 + cc, vT[:, cc, :], vT)
        for cc in range(4):
            shifted(12 + cc, sgT[:, cc, :], sgT)
        shifted(16, e1, self.tmp)
        p.act(wdad[0:64, :], e1[0:64, :], AF.Tanh, [tmpb], [wdad.b])
        p.cp("dve", wdad[64:128, :], e1[64:128, :], [tmpb], [wdad.b])
        for cc in range(4):
            p.act(sgT[:, cc, :], sgT[:, cc, :], AF.Silu, [sgT.b], [sgT.b])
        if self.rw_stop == 1:
            for cc in range(4):
                p.ms("dve", self.yT[:, cc, :], 0.0, [self.yT.bufs[cc]])
            return
        xs0 = self.xs[0]
        regs = [Tile(xs0[:, s_, h_ * 512:(h_ + 1) * 512]) for s_ in range(4) for h_ in range(2)] + [self.tokx, self.ytok]
        sets = [regs[0:5], regs[5:10]]
        own = [r_.b for r_ in regs[0:8]]
        p.ms("dve", xs0[0:1, 0, 0:1], 0.0, [xs0.b] + own)
        npp6 = self.next_pp6

        def ph2(cc, S):
            sig, cs, kk, e1t, e2t = S
            e1, e2 = e1t[:], e2t[:]
            csl = slice(cc * 128, (cc + 1) * 128)
            v3 = lambda ap: ap.rearrange("p (n d) -> p n d", d=64)
            pz = npp6()
            p.mm(pz[:], self.w2a2[0:64, j, csl], wdad[0:64, :], True, True, [self.w2a2.b, wdad.b], [pz.b])
            p.act(sig[:], pz[:], AF.Sigmoid, [pz.b, tmc.b], [sig.b], bias=tmc[:, j, 0, cc:cc + 1])
            pa = npp6()
            p.mm(pa[:], self.w2a2[64:128, j, csl], wdad[64:128, :], True, True, [self.w2a2.b, wdad.b], [pa.b])
            p.act(aT[:, cc, :], pa[:], AF.Sigmoid, [pa.b, tmc.b], [aT.b], bias=tmc[:, j, 1, cc:cc + 1])
            yield
            p.op("dve", lambda e: e.tensor_tensor_scan(out=cs[:], data0=cst[:, C_RST:C_RST + TT], data1=sig[:], initial=0.0,
                                                        op0=ALU.mult, op1=ALU.add), [cst.b, sig.b], [cs.b])
            p.ts("dve", kk[:], kT[:, cc, :], tmc[:, j, 2, cc:cc + 1], None, ALU.mult, None, [kT.b, tmc.b], [kk.b])
            p.tt("dve", e2, kk[:], kk[:], ALU.mult, [kk.b], [e2t.b])
            pn = npp6()
            p.mm(pn[:], cst[:, C_BLK:C_BLK + 128], e2, True, True, [cst.b, e2t.b], [pn.b])
            yield
            p.act(self.gl[:, cc, :], v3(cs[:])[:, :, 63], AF.Exp, [cs.b], [self.gl.b], scale=LW)
            p.act(e1, cs[:], AF.Exp, [cs.b], [e1t.b], scale=LW)
            p.tt("dve", AR[:, cc, :, 1, :], v3(rT[:, cc, :]), v3(e1), ALU.mult, [rT.b, e1t.b], [AR.b])
            yield
            p.ts("dve", e2, pn[:], 1e-24, None, ALU.max, None, [pn.b], [e2t.b])
            p.op("act", lambda e: e.activation(out=e2, in_=e2, func=AF.Ln), [e2t.b], [e2t.b])
            p.op("act", lambda e: e.activation(out=e2, in_=e2, func=AF.Exp, scale=-0.5), [e2t.b], [e2t.b])
            yield
            p.tt("dve", kk[:], kk[:], e2, ALU.mult, [kk.b, e2t.b], [kk.b])
            p.tt("dve", e2, cs[:], sig[:], ALU.subtract, [cs.b, sig.b], [e2t.b])
            p.act(e2, e2, AF.Exp, [e2t.b], [e2t.b], scale=LW)
            p.act(e1, cs[:], AF.Exp, [cs.b, AR.b], [e1t.b], scale=-LW)
            yield
            p.stt("dve", AR[:, cc, :, 0, :], v3(kk[:]), -1.0, v3(e2), ALU.mult, ALU.mult, [kk.b, e2t.b], [AR.b])
            p.tt("dve", e2, kk[:], aT[:, cc, :], ALU.mult, [kk.b, aT.b], [e2t.b])
            p.tt("dve", BK[:, cc, :, 0, :], v3(e2), v3(e1), ALU.mult, [e2t.b, e1t.b], [BK.b])
            yield
            p.ts("dve", e2, aT[:, cc, :], tmc[:, j, 3, cc:cc + 1], self.tmk1[:, j, cc:cc + 1], ALU.mult, ALU.add, [aT.b, tmc.b, self.tmk1.b], [e2t.b])
            p.tt("dve", kT[:, cc, :], kT[:, cc, :], e2, ALU.mult, [kT.b, e2t.b], [kT.b])
            p.tt("dve", BK[:, cc, :, 1, :], v3(kT[:, cc, :]), v3(e1), ALU.mult, [kT.b, e1t.b], [BK.b])
            yield
            p.stt("dve", e2, rT[:, cc, :], tmc[:, j, 4, cc:cc + 1], kT[:, cc, :], ALU.mult, ALU.mult, [rT.b, tmc.b, kT.b], [e2t.b])
            pbn = npp6()
            p.mm(pbn[:], cst[:, C_BLK:C_BLK + 128], e2, True, True, [cst.b, e2t.b], [pbn.b])
            yield
            p.tt("dve", bonT[:, cc, :], pbn[:], vT[:, cc, :], ALU.mult, [pbn.b, vT.b], [bonT.b])

        for pair in range(2):
            gens = [ph2(2 * pair, sets[0]), ph2(2 * pair + 1, sets[1])]
            while gens:
                for g_ in list(gens):
                    try:
                        next(g_)
                    except StopIteration:
                        gens.remove(g_)
        p.ms("dve", xs0[0:1, 0, 0:1], 0.0, [xs0.b] + own)
        if self.rw_stop == 2:
            for cc in range(4):
                p.ms("dve", self.yT[:, cc, :], 0.0, [self.yT.bufs[cc]])
            return
        vb = Tile(self.ctm[:].rearrange("p g e t -> p (g e) t"))
        vb.bufs = self.ctm.bufs
        p.cp("act", vb[:], vT[:], [vT.b], [vb.b])
        for c in range(8):
            tsl = slice(c * 64, (c + 1) * 64)
            for ii, (dst, getsrc, srct) in enumerate(((Vt, lambda cc: vb[:, cc, tsl], vb), (Bt, lambda cc: BK[:, cc, c, 0, :], BK), (Kt, lambda cc: BK[:, cc, c, 1, :], BK))):
                if self.rw_stop == 31 and ii > 0:
                    continue
                if self.rw_stop == 33:
                    continue
                if self.rw_stop == 32 and ii != 1:
                    continue
                pt = self.next_pp()
                for cc in range(4):
                    p.mm(pt[0:64, cc * 128:(cc + 1) * 128], getsrc(cc), self.ident_b, True, True, [srct.b, self.cstb.b], [pt.b])
                if c % 2 == 0:
                    p.cp("act", dst[0:64, c, :], pt[0:64, :], [pt.b], [dst.b] if dst is not self.sq else list(self.sq.bufs))
                else:
                    p.cp("dve", dst[0:64, c, :], pt[0:64, :], [pt.b], [dst.b] if dst is not self.sq else list(self.sq.bufs))
        if self.rw_stop in (3, 31, 32, 33):
            for cc in range(4):
                p.ms("dve", self.yT[:, cc, :], 0.0, [self.yT.bufs[cc]])
            return
        Btb = list(self.sq.bufs)
        p.cp("act", self.Hb[:], self.Hs[:, j, :, :], [self.Hs.b], [self.Hb.b])
        msk_su = cst[0:64, C_SU:C_SU + 64]
        msk_iu = cst[0:64, C_IU:C_IU + 64]
        msk_sl = cst[0:64, C_SL:C_SL + 64]
        idn = cst[0:64, 0:64]
        bc = lambda ap, n: ap.unsqueeze(1).to_broadcast([64, n, 64])
        npp = self.next_pp6
        SrbP = [Tile(self.Sb[:, 0, :, :]), Tile(self.Sb[:, 1, :, :])]
        SrbP[0].bufs = [self.Sb.bufs[0]]
        SrbP[1].bufs = [self.Sb.bufs[1]]
        SkP = [self.MT, Tile(self.Dall[:].rearrange("p h d -> p (h d)").bitcast(BF16)[:, 1024:2048].rearrange("p (h d) -> p h d", h=8))]
        TtP = [Tile(self.cbt[:].rearrange("p g n -> p (g n)").bitcast(BF16)[:, 0:512].rearrange("p (h d) -> p h d", h=8)),
               Tile(self.dec[:].rearrange("p h d -> p (h d)").bitcast(BF16)[:, 1024:1536].rearrange("p (h d) -> p h d", h=8))]
        TtP[0].bufs = self.cbt.bufs

        def prep_units(c):
            Srb, Sk, Tt = SrbP[c % 2], SkP[c % 2], TtP[c % 2]
            W0, NT0 = Wp[0], NTp[0]
            units = []

            def scores(g):
                hs = slice(g * 4, (g + 1) * 4)
                rows = slice(g * 64, (g + 1) * 64)
                Pb, Pk, Pa = npp(), npp(), npp()
                for q in range(4):
                    ar = AR[rows, q, c, :, :].rearrange("p q d -> p (q d)")
                    p.mm(Pb[0:64, q * 128:(q + 1) * 128], BK[rows, q, c, 0, :], ar, True, True, [BK.b, AR.b], [Pb.b])
                    p.mm(Pk[0:64, q * 128:(q + 1) * 128], BK[rows, q, c, 1, :], ar, True, True, [BK.b, AR.b], [Pk.b])
                    p.mm(Pa[0:64, q * 64:(q + 1) * 64], AR[rows, q, c, 0, :], BK[rows, q, c, 0, :], True, True, [BK.b, AR.b], [Pa.b])
                Pb3 = Pb[0:64, :].rearrange("p (h d) -> p h d", h=4)
                Pk3 = Pk[0:64, :].rearrange("p (h d) -> p h d", h=4)
                p.tt("dve", W0[0:64, hs, 0:64], Pb3[:, :, 0:64], bc(msk_su, 4), ALU.mult, [Pb.b, cst.b], [W0.b])
                p.tt("dve", Srb[0:64, hs, :], Pb3[:, :, 64:128], bc(msk_iu, 4), ALU.mult, [Pb.b, cst.b], [Srb.b])
                p.tt("dve", Sk[0:64, hs, 0:64], Pk3[:, :, 0:64], bc(msk_su, 4), ALU.mult, [Pk.b, cst.b], [Sk.b])
                p.tt("dve", Sk[0:64, hs, 64:128], Pk3[:, :, 64:128], bc(msk_iu, 4), ALU.mult, [Pk.b, cst.b], [Sk.b])
                p.tt("dve", NT0[0:64, hs, :], Pa[0:64, 0:256].rearrange("p (h d) -> p h d", h=4), bc(msk_sl, 4), ALU.mult, [Pa.b, cst.b], [NT0.b])

            def level(lvl, g):
                cur = lvl % 2
                Wc, NTc = Wp[cur], NTp[cur]
                Wn, NTn = Wp[1 - cur], NTp[1 - cur]
                last = (lvl == 5)
                hs = slice(g * 4, (g + 1) * 4)
                P1 = npp()
                if not last:
                    P2 = npp()
                for q in range(4):
                    h = g * 4 + q
                    if lvl == 0:
                        p.mm(P1[0:64, q * 128:q * 128 + 64], NTc[0:64, h, :], Wc[0:64, h, 0:64], True, True, [NTc.b, Wc.b], [P1.b])
                    elif not last:
                        p.mm(P1[0:64, q * 128:(q + 1) * 128], NTc[0:64, h, :], Wc[0:64, h, :], True, True, [NTc.b, Wc.b], [P1.b])
                    else:
                        p.mm(P1[0:64, q * 128 + 64:(q + 1) * 128], NTc[0:64, h, :], Wc[0:64, h, 64:128], True, True, [NTc.b, Wc.b], [P1.b])
                    if not last:
                        p.mm(P2[0:64, q * 64:(q + 1) * 64], Wc[0:64, h, 0:64], NTc[0:64, h, :], True, True, [NTc.b, Wc.b], [P2.b])
                P13 = P1[0:64, :].rearrange("p (h d) -> p h d", h=4)
                if not last:
                    p.cp("act", Wn[0:64, hs, 0:64], P13[:, :, 0:64], [P1.b], [Wn.b])
                    p.cp("act", NTn[0:64, hs, :], P2[0:64, 0:256].rearrange("p (h d) -> p h d", h=4), [P2.b], [NTn.b])
                if lvl == 0:
                    p.tt("dve", Wn[0:64, hs, 64:128], Wc[0:64, hs, 0:64], bc(idn, 4), ALU.add, [Wc.b, cst.b], [Wn.b])
                elif not last:
                    p.tt("dve", Wn[0:64, hs, 64:128], P13[:, :, 64:128], Wc[0:64, hs, 64:128], ALU.add, [P1.b, Wc.b], [Wn.b])
                else:
                    p.tt("dve", Tt[0:64, hs, :], P13[:, :, 64:128], Wc[0:64, hs, 64:128], ALU.add, [P1.b, Wc.b], [Tt.b])

            for g in range(2):
                units.append(lambda g=g: scores(g))
            for lvl in range(6):
                for g in range(2):
                    units.append(lambda lvl=lvl, g=g: level(lvl, g))
            return units

        def chain_units(c):
            Srb, Sk, Tt = SrbP[c % 2], SkP[c % 2], TtP[c % 2]
            st = {}

            def hterm(PS, qd):
                for q in range(4):
                    p.mm(PS[0:64, q * 64:(q + 1) * 64], AR[64:128, q, c, qd, :], self.Hb[64:128, q, :], True, True, [AR.b, self.Hb.b], [PS.b])

            def ux():
                PX, PXo = npp(), npp()
                hterm(PXo, 0)
                for hh in range(8):
                    g, q = hh // 4, hh % 4
                    h = 2 * q + g
                    o = PX[0:64, hh * 64:(hh + 1) * 64]
                    if g == 0:
                        p.mm(o, AR[0:64, q, c, 0, :], self.Hb[0:64, q, :], True, False, [AR.b, self.Hb.b], [PX.b])
                    p.mm(o, Sk[0:64, hh, 0:64], Vt[0:64, c, h * 64:(h + 1) * 64], g == 1, True, [Sk.b, Vt.b], [PX.b])
                p.cp("act", Xs[0:64, 0:4, :], PX[0:64, 0:256].rearrange("p (h d) -> p h d", h=4), [PX.b], [Xs.b])
                p.cp("act", Xs[0:64, 4:8, :], PXo[0:64, 0:256].rearrange("p (h d) -> p h d", h=4), [PXo.b], [Xs.b])
                p.tt("dve", Xs[0:64, 4:8, :], Xs[0:64, 4:8, :], PX[0:64, 256:512].rearrange("p (h d) -> p h d", h=4), ALU.add, [Xs.b, PX.b], [Xs.b])

            def uu():
                PU = npp()
                for hh in range(8):
                    p.mm(PU[0:64, hh * 64:(hh + 1) * 64], Tt[0:64, hh, :], Xs[0:64, hh, :], True, True, [Tt.b, Xs.b], [PU.b])
                p.cp("act", Us[0:64, :, :].rearrange("p (q g) d -> p g q d", g=2), PU[0:64, :].rearrange("p (g q d) -> p g q d", g=2, q=4), [PU.b], [Us.b])

            def uy():
                PY, PYo = npp(), npp()
                hterm(PYo, 1)
                for hh in range(8):
                    g, q = hh // 4, hh % 4
                    h = 2 * q + g
                    o = PY[0:64, hh * 64:(hh + 1) * 64]
                    if g == 0:
                        p.mm(o, AR[0:64, q, c, 1, :], self.Hb[0:64, q, :], True, False, [AR.b, self.Hb.b], [PY.b])
                    p.mm(o, Srb[0:64, hh, :], Us[0:64, h, :], g == 1, False, [Srb.b, Us.b], [PY.b])
                    p.mm(o, Sk[0:64, hh, 64:128], Vt[0:64, c, h * 64:(h + 1) * 64], False, True, [Sk.b, Vt.b], [PY.b])
                Yc = Ytok[0:64, c, :].rearrange("p (q g d) -> p g q d", g=2, d=64)
                p.cp("dve", Yc, PY[0:64, :].rearrange("p (g q d) -> p g q d", g=2, q=4), [PY.b], [Ytok.b])
                p.tt("dve", Yc[:, 1, :, :], Yc[:, 1, :, :], PYo[0:64, 0:256].rearrange("p (q d) -> p q d", q=4), ALU.add, [Ytok.b, PYo.b], [Ytok.b])

            def uh():
                PH = npp()
                for cc in range(4):
                    o = PH[:, cc * 128:(cc + 1) * 128]
                    p.mm(o, Bt[0:64, c, cc * 128:(cc + 1) * 128], Us[0:64, 2 * cc:2 * cc + 2, :].rearrange("p h d -> p (h d)"), True, False, Btb + [Us.b], [PH.b])
                    p.mm(o, Kt[0:64, c, cc * 128:(cc + 1) * 128], Vt[0:64, c, cc * 128:(cc + 1) * 128], False, True, [Kt.b, Vt.b], [PH.b])
                PH3 = PH[:].rearrange("p (c d) -> p c d", c=4)
                for hl in range(2):
                    rws = slice(hl * 64, (hl + 1) * 64)
                    p.tt("dve", self.Hs[rws, j, :, :], self.Hs[rws, j, :, :], PH3[rws, :, hl * 64:(hl + 1) * 64], ALU.add, [self.Hs.b, PH.b], [self.Hs.b])
                    p.tt("dve", self.Hs[rws, j, :, :], self.Hs[rws, j, :, :], self.gl[rws, :, c].unsqueeze(2).to_broadcast([64, 4, 64]), ALU.mult,
                         [self.Hs.b, self.gl.b], [self.Hs.b])
                p.cp("act", self.Hb[:], self.Hs[:, j, :, :], [self.Hs.b], [self.Hb.b])

            return [ux, uu, uy, uh]

        prev_chain = []
        for c in range(9):
            pu = prep_units(c) if c < 8 else []
            cu = prev_chain
            i_ = j_ = 0
            while i_ < len(pu) or j_ < len(cu):
                for _ in range(4):
                    if i_ < len(pu):
                        pu[i_]()
                        i_ += 1
                if j_ < len(cu):
                    cu[j_]()
                    j_ += 1
            prev_chain = chain_units(c) if c < 8 else []
        if self.rw_stop in (4, 5, 6):
            for cc in range(4):
                p.ms("dve", self.yT[:, cc, :], 0.0, [self.yT.bufs[cc]])
            return
        Y3 = Ytok[0:64, :, :].rearrange("p c (h d) -> p (c h) d", d=64)
        st = Tile(self.uext[:, 3, 0:512])
        st.bufs = self.uext.bufs
        s1 = st[0:64, 0:64]
        s2 = st[0:64, 64:128]
        s3 = st[0:64, 128:192]
        p.op("dve", lambda e: e.reduce_sum(out=s1, in_=Y3, axis=mybir.AxisListType.X), [Ytok.b], [st.b])
        sqt = Tile(self.F[5][:].rearrange("p c t -> p (c t)")[:, 0:2048].rearrange("p (a b) -> p a b", b=64))
        sqt.bufs = self.F[5].bufs
        for hf in range(2):
            ysl = Y3[:, hf * 32:(hf + 1) * 32, :]
            p.tt("dve", sqt[0:64, :, :], ysl, ysl, ALU.mult, [Ytok.b], [sqt.b])
            p.op("dve", lambda e, hf=hf: e.reduce_sum(out=s2[:, hf * 32:(hf + 1) * 32], in_=sqt[0:64, :, :], axis=mybir.AxisListType.X), [sqt.b], [st.b])
        p.ts("dve", s1, s1, 1.0 / 64, None, ALU.mult, None, [st.b], [st.b])
        p.tt("dve", s3, s1, s1, ALU.mult, [st.b], [st.b])
        p.stt("dve", s2, s2, 1.0 / 64, s3, ALU.mult, ALU.subtract, [st.b], [st.b])
        p.ts("dve", s2, s2, 64e-5, None, ALU.add, None, [st.b], [st.b])
        p.op("act", lambda e: e.activation(out=s2, in_=s2, func=AF.Ln), [st.b], [st.b])
        p.op("act", lambda e: e.activation(out=s2, in_=s2, func=AF.Exp, scale=-0.5), [st.b], [st.b])
        p.tt("dve", Y3, Y3, s1.unsqueeze(2).to_broadcast([64, 64, 64]), ALU.subtract, [Ytok.b, st.b], [Ytok.b])
        p.tt("dve", Y3, Y3, s2.unsqueeze(2).to_broadcast([64, 64, 64]), ALU.mult, [Ytok.b, st.b], [Ytok.b])
        for cc in range(4):
            py = self.next_pp()
            for c in range(8):
                p.tr(py[:, c * 64:(c + 1) * 64], Ytok[0:64, c, cc * 128:(cc + 1) * 128], cst[0:64, 0:64], [Ytok.b, cst.b], [py.b])
            p.act(e1, py[:], AF.Identity, [py.b, tmc.b], [tmpb], scale=tmc[:, j, 5, cc:cc + 1], bias=tmc[:, j, 6, cc:cc + 1])
            p.tt("dve", e1, e1, bonT[:, cc, :], ALU.add, [tmpb, bonT.b], [tmpb])
            p.tt("dve", self.yT[:, cc, :], e1, sgT[:, cc, :], ALU.mult, [tmpb, sgT.b], [self.yT.bufs[cc]])

    def build(self):
        if self.prefetch and self.wseq is None and not getattr(self, "_recording", False):
            rec = Builder(self.NL, self.NT, self.mixers, False)
            rec._recording = True
            rec.rw_stop = self.rw_stop
            rec.build()
            self.wseq = rec.wrec
        p = self.p
        self.declare()
        self.alloc()
        self.prologue()
        xr = self.x.rearrange("(t s p) d -> t p s d", p=128, s=4)
        orr = self.out.rearrange("(t s p) d -> t p s d", p=128, s=4)
        for t in range(self.NT):
            p.dma("sp", self.xs[0][:], xr[t], (), [self.xs[0].b])
            self.load_tile(t)
            for L in range(self.NL):
                if L % 2 == 0:
                    self.even_layer(L)
                else:
                    self.odd_layer(L)
            self.store_tile(t, orr)
        p.finish()
        return self.nc


def make_in_maps(inputs):
    consts = make_consts()
    maps = []
    for core in range(8):
        b = core % 4
        m = {}
        for k, v in inputs.items():
            v = np.asarray(v)
            if k == "x":
                m[k] = np.ascontiguousarray(v[b])
            elif k == "c":
                m[k] = np.ascontiguousarray(v[b].reshape(8, 128))
            elif k in ("tm_k_k", "tm_k_a", "tm_r_k"):
                m[k] = np.ascontiguousarray(v.reshape(2, 512))
            else:
                m[k] = np.ascontiguousarray(v)
        m["consts"] = consts
        maps.append(m)
    return maps


def kernel(**inputs):
    import os
    bld = Builder(mixers=os.environ.get("K_MIXERS", "ABCD"))
    bld.rw_stop = int(os.environ.get("K_RWSTOP", "0"))
    nc = bld.build()
    maps = make_in_maps(inputs)
    res = run_bass_kernel_spmd(nc, maps, core_ids=list(range(8)))
    out = np.stack([res.results[b]["out"] for b in range(4)], axis=0)
    return out.astype(np.float32)
```
